# Optimizing a Trainium2 kernel written in Bass

```python
import math
import jax, jax.numpy as jnp
from jax import lax
import numpy as np

D_MODEL = 1024
BATCH = 2
SEQ = 8192
DEPTH = 1
DEC_BATCH = 128
DEC_SEQ = 4
PAST_LEN = 16384
PAGE_SIZE = 128

HEAD_DIM = 64
MIX_WIDTH = D_MODEL
ATT_WIDTH = MIX_WIDTH // 2
CONV_DIM = MIX_WIDTH - ATT_WIDTH
N_Q_HEADS = ATT_WIDTH // HEAD_DIM
N_KV_HEADS = 2
GQA_GROUP = N_Q_HEADS // N_KV_HEADS
CONV_GROUPS = CONV_DIM // HEAD_DIM
CONV_WIDTH = 3
WINDOW = 128
BLOCK = 128
N_BUCKETS = 32
MAX_DISTANCE = 128
D_FF = 2816
D_PLE = 256
RMS_EPS = 1e-6
IN_COLS = ATT_WIDTH + 2 * N_KV_HEADS * HEAD_DIM + 3 * CONV_DIM
NEG_INF = -1e30

kernel_name = "hybrid_swa_shortconv_macaron_decoder_step"


def _rms(x, g):
    x32 = x.astype(jnp.float32)
    y = x32 * lax.rsqrt(jnp.mean(x32 * x32, axis=-1, keepdims=True) + RMS_EPS)
    return y.astype(x.dtype) * g


def _swiglu(h, w_gate, w_up, w_down):
    return (jax.nn.silu(h @ w_gate) * (h @ w_up)) @ w_down


def _rel_bucket(dist):
    max_exact = N_BUCKETS // 2
    d = jnp.maximum(dist, 0)
    df = jnp.maximum(d, 1).astype(jnp.float32)
    large = max_exact + (jnp.log(df / max_exact) / math.log(MAX_DISTANCE / max_exact)
                         * (N_BUCKETS - max_exact)).astype(jnp.int32)
    large = jnp.minimum(large, N_BUCKETS - 1)
    return jnp.where(d < max_exact, d, large)


def _rel_bias(table, dist):
    b = table[_rel_bucket(dist)]
    qn, kn = dist.shape
    return jnp.transpose(b, (2, 0, 1)).reshape(N_KV_HEADS, GQA_GROUP, qn, kn)


def _sink_attention(q, k, v, bias, mask, sinks):
    s = jnp.einsum('...qhgd,...khd->...hgqk', q, k).astype(jnp.float32) * (HEAD_DIM ** -0.5)
    s = jnp.where(mask, s + bias.astype(jnp.float32), NEG_INF)
    sink = sinks.astype(jnp.float32)[..., None, None]
    m = jnp.maximum(jnp.max(s, axis=-1, keepdims=True), sink)
    e = jnp.exp(s - m)
    w = e / (jnp.sum(e, axis=-1, keepdims=True) + jnp.exp(sink - m))
    return jnp.einsum('...hgqk,...khd->...qhgd', w.astype(v.dtype), v)


def _attn_prompt(q, k, v, table, sinks):
    B, T = q.shape[:2]
    nb = T // BLOCK
    qb = q.reshape(B, nb, BLOCK, N_KV_HEADS, GQA_GROUP, HEAD_DIM)
    kb = k.reshape(B, nb, BLOCK, N_KV_HEADS, HEAD_DIM)
    vb = v.reshape(B, nb, BLOCK, N_KV_HEADS, HEAD_DIM)
    pad = jnp.zeros_like(kb[:, :1])
    kc = jnp.concatenate([jnp.concatenate([pad, kb[:, :-1]], axis=1), kb], axis=2)
    vc = jnp.concatenate([jnp.concatenate([pad, vb[:, :-1]], axis=1), vb], axis=2)
    qi = jnp.arange(BLOCK)[:, None]
    kj = jnp.arange(2 * BLOCK)[None, :]
    dist = qi + BLOCK - kj
    valid_key = (jnp.arange(nb)[:, None, None] * BLOCK - BLOCK + kj[None]) >= 0
    mask = (dist >= 0)[None] & (dist < WINDOW)[None] & valid_key
    bias = _rel_bias(table, dist)
    o = _sink_attention(qb, kc, vc, bias, mask[None, :, None, None], sinks)
    return o.reshape(B, T, ATT_WIDTH)


def _attn_sample(q, k_new, v_new, k_cache, v_cache, table, sinks):
    Bd, S = q.shape[:2]
    W = k_cache.shape[1]
    kc = jnp.concatenate([k_cache, k_new], axis=1)
    vc = jnp.concatenate([v_cache, v_new], axis=1)
    dist = jnp.arange(S)[:, None] + W - jnp.arange(W + S)[None, :]
    mask = (dist >= 0) & (dist < WINDOW)
    bias = _rel_bias(table, dist)
    qg = q.reshape(Bd, S, N_KV_HEADS, GQA_GROUP, HEAD_DIM)
    o = _sink_attention(qg, kc, vc, bias, mask, sinks)
    return o.reshape(Bd, S, ATT_WIDTH), kc[:, -W:], vc[:, -W:]


def _short_conv(u, buf, w_conv):
    T = u.shape[1]
    full = jnp.concatenate([buf, u], axis=1)
    y = w_conv[0] * full[:, 0:T]
    for j in range(1, CONV_WIDTH):
        y = y + w_conv[j] * full[:, j:j + T]
    return y, full[:, -(CONV_WIDTH - 1):]


def _layer(x, pe, lw, rel_bias, k_cache=None, v_cache=None, conv_buf=None):
    (g1, w1g, w1u, w1d, gm, w_in, qn, kn, sinks, w_conv, w_out,
     g2, w2g, w2u, w2d, gp, wpg, wpp) = lw
    B, T, _ = x.shape
    x = x + 0.5 * _swiglu(_rms(x, g1), w1g, w1u, w1d)
    z = _rms(x, gm) @ w_in
    o1 = ATT_WIDTH
    o2 = o1 + N_KV_HEADS * HEAD_DIM
    o3 = o2 + N_KV_HEADS * HEAD_DIM
    o4 = o3 + CONV_DIM
    o5 = o4 + CONV_DIM
    q = _rms(z[..., :o1].reshape(B, T, N_Q_HEADS, HEAD_DIM), qn)
    k = _rms(z[..., o1:o2].reshape(B, T, N_KV_HEADS, HEAD_DIM), kn)
    v = z[..., o2:o3].reshape(B, T, N_KV_HEADS, HEAD_DIM)
    gate_b = z[..., o3:o4]
    gate_c = z[..., o4:o5]
    h = z[..., o5:]
    sinks_g = sinks.reshape(N_KV_HEADS, GQA_GROUP)
    if k_cache is None:
        att = _attn_prompt(q, k, v, rel_bias, sinks_g)
        w = min(WINDOW, T)
        k_state, v_state = k[:, T - w:], v[:, T - w:]
        conv_buf = jnp.zeros((B, CONV_WIDTH - 1, CONV_DIM), x.dtype)
    else:
        att, k_state, v_state = _attn_sample(q, k, v, k_cache, v_cache, rel_bias, sinks_g)
    cv, conv_state = _short_conv(gate_c * h, conv_buf, w_conv)
    x = x + jnp.concatenate([att, gate_b * cv], axis=-1) @ w_out
    x = x + 0.5 * _swiglu(_rms(x, g2), w2g, w2u, w2d)
    x = x + jax.nn.sigmoid(_rms(x, gp) @ wpg) * (pe @ wpp)
    return x, k_state, v_state, conv_state


def setup_inputs(seed: int = 0) -> dict:
    key = jax.random.key(seed)
    ks = jax.random.split(key, 32)
    f32 = jnp.float32

    def nrm(k, shape, scale):
        return jax.random.normal(k, shape, f32) * scale

    def gain(k, shape):
        return 1.0 + 0.05 * jax.random.normal(k, shape, f32)

    win = min(WINDOW, PAST_LEN)
    D = D_MODEL
    return {
        "x_prompt": nrm(ks[0], (BATCH, SEQ, D), 1.0),
        "x_sample": nrm(ks[1], (DEC_BATCH, DEC_SEQ, D), 1.0),
        "p_prompt": nrm(ks[2], (DEPTH, BATCH, SEQ, D_PLE), 1.0),
        "p_sample": nrm(ks[3], (DEPTH, DEC_BATCH, DEC_SEQ, D_PLE), 1.0),
        "cache_k": nrm(ks[4], (DEPTH, DEC_BATCH, win, N_KV_HEADS, HEAD_DIM), 1.0),
        "cache_v": nrm(ks[5], (DEPTH, DEC_BATCH, win, N_KV_HEADS, HEAD_DIM), 1.0),
        "state_conv": nrm(ks[6], (DEPTH, DEC_BATCH, CONV_WIDTH - 1, CONV_DIM), 1.0),
        "rel_bias": nrm(ks[7], (N_BUCKETS, N_Q_HEADS), 0.5),
        "g_ffn1": gain(ks[8], (DEPTH, D)),
        "w1_gate": nrm(ks[9], (DEPTH, D, D_FF), D ** -0.5),
        "w1_up": nrm(ks[10], (DEPTH, D, D_FF), D ** -0.5),
        "w1_down": nrm(ks[11], (DEPTH, D_FF, D), D_FF ** -0.5),
        "g_mix": gain(ks[12], (DEPTH, D)),
        "w_in": nrm(ks[13], (DEPTH, D, IN_COLS), D ** -0.5),
        "q_norm": gain(ks[14], (DEPTH, HEAD_DIM)),
        "k_norm": gain(ks[15], (DEPTH, HEAD_DIM)),
        "sinks": nrm(ks[16], (DEPTH, N_Q_HEADS), 1.0),
        "w_conv": nrm(ks[17], (DEPTH, CONV_WIDTH, CONV_DIM), CONV_WIDTH ** -0.5),
        "w_out": nrm(ks[18], (DEPTH, MIX_WIDTH, D), MIX_WIDTH ** -0.5),
        "g_ffn2": gain(ks[19], (DEPTH, D)),
        "w2_gate": nrm(ks[20], (DEPTH, D, D_FF), D ** -0.5),
        "w2_up": nrm(ks[21], (DEPTH, D, D_FF), D ** -0.5),
        "w2_down": nrm(ks[22], (DEPTH, D_FF, D), D_FF ** -0.5),
        "g_ple": gain(ks[23], (DEPTH, D)),
        "w_ple_gate": nrm(ks[24], (DEPTH, D, D), D ** -0.5),
        "w_ple_proj": nrm(ks[25], (DEPTH, D_PLE, D), D_PLE ** -0.5),
    }


def reference(x_prompt, x_sample, p_prompt, p_sample, cache_k, cache_v, state_conv, rel_bias,
              g_ffn1, w1_gate, w1_up, w1_down, g_mix, w_in, q_norm, k_norm, sinks, w_conv, w_out,
              g_ffn2, w2_gate, w2_up, w2_down, g_ple, w_ple_gate, w_ple_proj):
    xp, xs = x_prompt, x_sample
    kp_l, vp_l, cp_l, ks_l, vs_l, cs_l = [], [], [], [], [], []
    for i in range(DEPTH):
        lw = (g_ffn1[i], w1_gate[i], w1_up[i], w1_down[i], g_mix[i], w_in[i], q_norm[i], k_norm[i],
              sinks[i], w_conv[i], w_out[i], g_ffn2[i], w2_gate[i], w2_up[i], w2_down[i],
              g_ple[i], w_ple_gate[i], w_ple_proj[i])
        xp, kp, vp, cp = _layer(xp, p_prompt[i], lw, rel_bias)
        xs, ksn, vsn, csn = _layer(xs, p_sample[i], lw, rel_bias,
                                   k_cache=cache_k[i], v_cache=cache_v[i], conv_buf=state_conv[i])
        kp_l.append(kp); vp_l.append(vp); cp_l.append(cp)
        ks_l.append(ksn); vs_l.append(vsn); cs_l.append(csn)
    k_win_prompt = jnp.stack(kp_l)
    v_win_prompt = jnp.stack(vp_l)
    conv_prompt = jnp.stack(cp_l)
    k_win_sample = jnp.stack(ks_l)
    v_win_sample = jnp.stack(vs_l)
    conv_sample = jnp.stack(cs_l)
    return (xp, xs, k_win_prompt, v_win_prompt, conv_prompt, k_win_sample, v_win_sample, conv_sample)
```

```python
import math
from contextlib import ExitStack

import numpy as np
import concourse.bass as bass
import concourse.mybir as mybir
from concourse.bass_utils import run_bass_kernel_spmd

F32 = mybir.dt.float32
BF16 = mybir.dt.bfloat16
AF = mybir.ActivationFunctionType
ALU = mybir.AluOpType
AX = mybir.AxisListType

NCORES = 8
D = 1024
DFF = 2816
NFF = 22
DPLE = 256
INC = 2304
HALO = 128
NPR = 2048
NSM = 64
NT = HALO + NPR + NSM
NOUT = NPR + NSM
CS = HALO + NPR
MT = 256
EPS = 1e-6
NSEQ = 16
HPERM = [0, 1, 4, 5, 2, 3, 6, 7]

FFN_GROUPS = [(0, 4), (4, 8), (8, 12), (12, 16), (16, 20), (20, 22)]
FFN1_TILES = [(0, 128), (128, 640), (640, 1152), (1152, 1664), (1664, 2176), (2176, 2240)]
FFN2_TILES = FFN1_TILES[1:]


class Eng:
    def __init__(self, eng, sem, is_pe=False):
        self.eng = eng
        self.sem = sem
        self.cnt = 0
        self.waited = {}
        self.is_pe = is_pe

    def wait(self, tok):
        if tok is None:
            return
        s, v = tok
        if self.is_pe and s is self.sem:
            return
        if self.waited.get(s.num, 0) >= v:
            return
        self.eng.wait_ge(s, v)
        self.waited[s.num] = v

    def mark(self, inst):
        self.cnt += 1
        inst.then_inc(self.sem, 1)
        return (self.sem, self.cnt)


class _Stop(Exception):
    pass


class Buf:
    __slots__ = ("w", "r", "excl")

    def __init__(self, excl=False):
        self.w = None
        self.r = {}
        self.excl = excl


def PBuf():
    return Buf(excl=True)


def _pre(E, reads, writes):
    for b in reads:
        E.wait(b.w)
        if b.excl:
            for t in list(b.r.values()):
                if t[0] is not E.sem:
                    E.wait(t)
    for b in writes:
        E.wait(b.w)
        for t in list(b.r.values()):
            E.wait(t)


def _commit(tok, reads, writes):
    for b in reads:
        cur = b.r.get(tok[0].num)
        if cur is None or cur[1] < tok[1]:
            b.r[tok[0].num] = tok
    for b in writes:
        b.w = tok
        b.r = {}


def emit(E, fn, reads=(), writes=()):
    _pre(E, reads, writes)
    tok = E.mark(fn())
    _commit(tok, reads, writes)
    return tok


class DmaQ:
    def __init__(self, E, sems):
        self.E = E
        self.slots = [[s, 0] for s in sems]
        self.i = 0

    def start(self, out, in_, reads=(), writes=()):
        E = self.E
        _pre(E, reads, writes)
        slot = self.slots[self.i % len(self.slots)]
        self.i += 1
        if slot[1]:
            E.wait((slot[0], slot[1]))
        inst = E.eng.dma_start(out=out, in_=in_)
        slot[1] += 16
        inst.then_inc(slot[0], 16)
        tok = (slot[0], slot[1])
        _commit(tok, reads, writes)
        return tok

    def outstanding(self):
        return [(s, v) for s, v in self.slots if v]


def build_nc(stop=None):
    nc = bass.Bass("TRN2", target_bir_lowering=False)

    def din(name, shape):
        return nc.dram_tensor(name, list(shape), F32, kind="ExternalInput").ap()

    def dout(name, shape):
        return nc.dram_tensor(name, list(shape), F32, kind="ExternalOutput").ap()

    xT = din("xT", [D, NT])
    pT_d = din("pT", [DPLE, NOUT])
    ckT_d = din("ckT", [128, NSEQ, 128])
    ck_nat = din("ck_nat", [NSEQ, 128, 128])
    cvt_d = din("cvt", [128, NSEQ, 128])
    cv_nat = din("cv_nat", [NSEQ, 128, 128])
    scT_d = din("scT", [128, 4, NSEQ, 2])
    tabp_d = din("tabp", [32, 8])
    E2_d = din("E2", [33, 383])
    onesD_d = din("onesD", [128, 128])
    bd64_d = din("bd64", [128, 128])
    ident_d = din("ident", [128, 128])
    gv_d = din("gv", [128, 4, 8])
    qkg_d = din("qkg", [128, 2])
    sinkb_d = din("sinkb", [128, 8])
    sinks_d = din("sinks_s", [16, 2])
    wconv_d = din("wconv", [128, 4, 3])
    hflag_d = din("hflag", [128, 1])
    w1g = din("w1g", [D, DFF])
    w1u = din("w1u", [D, DFF])
    w1d = din("w1d", [DFF, D])
    win_d = din("win", [D, INC])
    wout_d = din("wout", [D, D])
    w2g = din("w2g", [D, DFF])
    w2u = din("w2u", [D, DFF])
    w2d = din("w2d", [DFF, D])
    wpg_d = din("wpg", [D, D])
    wpp_d = din("wpp", [DPLE, D])

    yT = dout("yT", [D, NOUT])
    kTl_o = dout("kTl", [128, 128])
    vl_o = dout("vl", [128, 128])
    cl_o = dout("cl", [128, 4, 2])
    kws_o = dout("kws_old", [NSEQ, 124, 128])
    vws_o = dout("vws_old", [NSEQ, 124, 128])
    ksn_o = dout("ksn", [128, NSM])
    vsn_o = dout("vsn", [4, NSEQ, 128])
    csn_o = dout("csn", [128, 4, NSEQ, 2])

    Ubc_t = nc.dram_tensor("Ubc", [128, 8, 383], F32, kind="Internal")
    Ubc = Ubc_t.ap()

    uid = [0]

    with ExitStack() as top:
        def sem(name):
            return top.enter_context(nc.semaphore(name))

        PE = Eng(nc.tensor, sem("s_pe"), is_pe=True)
        ACT = Eng(nc.scalar, sem("s_act"))
        DVE = Eng(nc.vector, sem("s_dve"))
        POOL = Eng(nc.gpsimd, sem("s_pool"))
        SP = Eng(nc.sync, sem("s_sp"))
        import os
        NSQ = int(os.environ.get("KNSQ", "10"))
        KSKIP = os.environ.get("KSKIP", "").split(",")
        CONV_ON_POOL = os.environ.get("KCONVPOOL", "0") == "1"
        NORM_ADD_POOL = os.environ.get("KNORMPOOL", "1") == "1"
        NWARM = int(os.environ.get("KNWARM", "0"))
        KNPOS = int(os.environ.get("KNPOS", "10"))
        SLOT = [int(x) for x in os.environ.get("KSLOT", "2,2,1,2").split(",")]
        QS = DmaQ(SP, [sem("qs%d" % i) for i in range(NSQ)])
        QP = DmaQ(POOL, [sem("qp%d" % i) for i in range(NSQ)])
        ENGS = (PE, ACT, DVE, POOL, SP)

        def sb(stack, shape, dt, name="t"):
            uid[0] += 1
            return stack.enter_context(nc.sbuf_tensor("%s_%d" % (name, uid[0]), list(shape), dt))

        def ps(stack, shape, dt, name="p"):
            uid[0] += 1
            return stack.enter_context(nc.psum_tensor("%s_%d" % (name, uid[0]), list(shape), dt))

        QA_REF = []

        def barrier(queues=None):
            toks = [(E.sem, E.cnt) for E in (PE, ACT, DVE, POOL) if E.cnt > 0]
            for q_ in (queues if queues is not None else (QS, QP)):
                toks += q_.outstanding()
            if queues is None and QA_REF:
                toks += QA_REF[0].outstanding()
            for E in ENGS:
                for t in toks:
                    E.wait(t)

        def pe_group(mms, reads=(), writes=()):
            _pre(PE, reads, writes)
            n = len(mms)
            inst = None
            for i, (o, l, r) in enumerate(mms):
                inst = nc.tensor.matmul(o, l, r, start=(i == 0), stop=(i == n - 1))
            tok = PE.mark(inst)
            _commit(tok, reads, writes)
            return tok

        def pe_transposes(trs, ident_ap_fn, reads=(), writes=()):
            _pre(PE, reads, writes)
            inst = None
            for (o, i_) in trs:
                inst = nc.tensor.transpose(o, i_, ident_ap_fn(i_))
            tok = PE.mark(inst)
            _commit(tok, reads, writes)
            return tok

        xres = sb(top, [128, 8, NT], F32, "xres")
        xb = [[Buf() for _ in range(18)] for _ in range(8)]

        def xbufs(c, a, b):
            return [xb[c][k] for k in range(a // 128, (b + 127) // 128)]

        def xbufs_all(a, b):
            r = []
            for c in range(8):
                r += xbufs(c, a, b)
            return r

        onesD = sb(top, [128, 128], BF16, "onesD")
        bd64 = sb(top, [128, 128], BF16, "bd64")
        ident = sb(top, [128, 128], BF16, "ident")
        gv = sb(top, [128, 4, 8], F32, "gv")
        qkg = sb(top, [128, 2], F32, "qkg")
        sinkb = sb(top, [128, 8], F32, "sinkb")
        nsinkb = sb(top, [128, 8], F32, "nsinkb")
        wconv = sb(top, [128, 4, 3], F32, "wconv")
        hflag = sb(top, [128, 1], F32, "hflag")
        epst = sb(top, [128, 1], F32, "epst")
        NS = {}

        def alloc_norm(stack, Wn):
            NS["sq"] = [sb(stack, [128, Wn], BF16, "sq") for _ in range(3)]
            NS["sqb"] = [Buf() for _ in range(3)]
            NS["rt"] = [sb(stack, [128, Wn], F32, "rt") for _ in range(2)]
            NS["rtb"] = [Buf() for _ in range(2)]
            NS["rstd"] = [sb(stack, [128, Wn], F32, "rstd") for _ in range(2)]
            NS["rstdb"] = [Buf() for _ in range(2)]
        CB = []

        def newcb():
            b_ = Buf()
            CB.append(b_)
            return b_
        cnt = {"sq": 0, "rt": 0}

        xTv = xT.rearrange("(c p) t -> p c t", p=128)
        QA = DmaQ(ACT, [sem("qa%d" % i) for i in range(8)])
        QA_REF.append(QA)
        for (a_, b_) in ((0, 1152), (1152, NT)):
            for c in range(8):
                q_ = QS if c % 2 == 0 else QA
                q_.start(xres[:, c, a_:b_], xTv[:, c, a_:b_], writes=xbufs(c, a_, b_))
        for dst, src in ((gv, gv_d), (qkg, qkg_d), (sinkb, sinkb_d), (wconv, wconv_d), (hflag, hflag_d)):
            QS.start(dst[:], src, writes=[newcb()])
        for dst, src in ((onesD, onesD_d), (bd64, bd64_d), (ident, ident_d)):
            QP.start(dst[:], src, writes=[newcb()])
        emit(DVE, lambda: nc.vector.memset(epst[:], EPS), writes=[newcb()])
        qkg8 = sb(top, [128, 1], F32, "qkg8")
        emit(DVE, lambda: nc.vector.tensor_scalar(out=qkg8[:], in0=qkg[:, 0:1], scalar1=0.125, scalar2=None,
                                                  op0=ALU.mult), reads=list(CB), writes=[newcb()])
        emit(DVE, lambda: nc.vector.tensor_scalar(out=nsinkb[:], in0=sinkb[:], scalar1=-1.0, scalar2=None,
                                                  op0=ALU.mult), reads=list(CB), writes=[newcb()])

        def norm_tile(a, b, gsel, out_fn, out_bufs, pstat, pstat_b):
            sq, sqb, rt, rtb, rstd, rstdb = NS["sq"], NS["sqb"], NS["rt"], NS["rtb"], NS["rstd"], NS["rstdb"]
            W = b - a
            for c in range(8):
                i = cnt["sq"] % 3
                cnt["sq"] += 1
                emit(ACT, lambda: nc.scalar.activation(out=sq[i][:, :W], in_=xres[:, c, a:b], func=AF.Square),
                     reads=xbufs(c, a, b), writes=[sqb[i]])
                _pre(PE, [sqb[i]] + CB, [pstat_b] if c == 0 else [])
                inst = nc.tensor.matmul(pstat[:, :W], onesD[:], sq[i][:, :W], start=(c == 0), stop=(c == 7))
                tok = PE.mark(inst)
                _commit(tok, [sqb[i]] + CB, [pstat_b] if c == 7 else [])
            j = cnt["rt"] % 2
            cnt["rt"] += 1
            emit(ACT, lambda: nc.scalar.activation(out=rt[j][:, :W], in_=pstat[:, :W], func=AF.Ln,
                                                   bias=epst[:, 0:1], scale=1.0),
                 reads=[pstat_b] + CB, writes=[rtb[j]])
            emit(ACT, lambda: nc.scalar.activation(out=rstd[j][:, :W], in_=rt[j][:, :W], func=AF.Exp, scale=-0.5),
                 reads=[rtb[j]], writes=[rstdb[j]])
            for c in range(8):
                emit(DVE, lambda: nc.vector.scalar_tensor_tensor(
                    out=out_fn(c), in0=xres[:, c, a:b], scalar=gv[:, gsel, c:c + 1], in1=rstd[j][:, :W],
                    op0=ALU.mult, op1=ALU.mult),
                    reads=xbufs(c, a, b) + [rstdb[j]] + CB, writes=[out_bufs[c]])

        def ffn_phase(wg_d, wu_d, wd_d, gsel, tiles):
            with ExitStack() as ph:
                alloc_norm(ph, 512)
                hb = sb(ph, [128, 8, NT], BF16, "hb")
                hbb = [[Buf() for _ in range(8)] for _ in tiles]
                Wg = [sb(ph, [128, 8, 512], BF16, "Wg") for _ in range(2)]
                Wu = [sb(ph, [128, 8, 512], BF16, "Wu") for _ in range(2)]
                Wd = [sb(ph, [128, 4, 1024], BF16, "Wd") for _ in range(2)]
                Wgb = [Buf() for _ in range(2)]
                Wub = [Buf() for _ in range(2)]
                Wdb = [Buf() for _ in range(2)]
                A = [sb(ph, [128, 4, 512], BF16, "A") for _ in range(2)]
                Ab = [[Buf() for _ in range(4)] for _ in range(2)]
                S = [sb(ph, [128, 512], F32, "S") for _ in range(2)]
                Sb = [Buf() for _ in range(2)]
                pG = [ps(ph, [128, 512], F32, "pG") for _ in range(2)]
                pU = [ps(ph, [128, 512], F32, "pU") for _ in range(2)]
                pY = [ps(ph, [128, 512], F32, "pY") for _ in range(2)]
                pstat = ps(ph, [128, 512], F32, "pstat")
                pGb = [PBuf() for _ in range(2)]
                pUb = [PBuf() for _ in range(2)]
                pYb = [PBuf() for _ in range(2)]
                pstat_b = PBuf()

                def load_group(gi):
                    c0, c1 = FFN_GROUPS[gi]
                    s = gi % 2
                    gw = (c1 - c0) * 128
                    QP.start(Wg[s][:, :, 0:gw], wg_d[:, c0 * 128:c1 * 128].rearrange("(kc p) n -> p kc n", p=128),
                             writes=[Wgb[s]])
                    QP.start(Wu[s][:, :, 0:gw], wu_d[:, c0 * 128:c1 * 128].rearrange("(kc p) n -> p kc n", p=128),
                             writes=[Wub[s]])
                    QP.start(Wd[s][:, 0:c1 - c0, :], wd_d[c0 * 128:c1 * 128, :].rearrange("(g p) n -> p g n", p=128),
                             writes=[Wdb[s]])

                load_group(0)
                load_group(1)
                def ffn_norm(ti):
                    a_, b_ = tiles[ti]
                    norm_tile(a_, b_, gsel, lambda c: hb[:, c, a_:b_], hbb[ti], pstat, pstat_b)
                ffn_norm(0)
                if len(tiles) > 1:
                    ffn_norm(1)

                items = [(gi, ti) for gi in range(len(FFN_GROUPS)) for ti in range(len(tiles))]
                kc_cnt = [0]
                y_cnt = [0]

                def stage1(idx):
                    gi, ti = items[idx]
                    if gi == 0 and ti + 2 < len(tiles):
                        ffn_norm(ti + 2)
                    a, b = tiles[ti]
                    W = b - a
                    c0, c1 = FFN_GROUPS[gi]
                    s = gi % 2
                    ai = idx % 2
                    for cl in range(c1 - c0):
                        k = kc_cnt[0] % 2
                        kc_cnt[0] += 1
                        pe_group([(pG[k][:, :W], Wg[s][:, kc, cl * 128:(cl + 1) * 128], hb[:, kc, a:b]) for kc in range(8)],
                                 reads=[Wgb[s]] + hbb[ti], writes=[pGb[k]])
                        pe_group([(pU[k][:, :W], Wu[s][:, kc, cl * 128:(cl + 1) * 128], hb[:, kc, a:b]) for kc in range(8)],
                                 reads=[Wub[s]] + hbb[ti], writes=[pUb[k]])
                        emit(ACT, lambda: nc.scalar.activation(out=S[k][:, :W], in_=pG[k][:, :W], func=AF.Silu),
                             reads=[pGb[k]], writes=[Sb[k]])
                        emit(DVE, lambda: nc.vector.tensor_tensor(out=A[ai][:, cl, :W], in0=S[k][:, :W], in1=pU[k][:, :W],
                                                                  op=ALU.mult),
                             reads=[Sb[k], pUb[k]], writes=[Ab[ai][cl]])

                def stage2(idx):
                    gi, ti = items[idx]
                    a, b = tiles[ti]
                    W = b - a
                    c0, c1 = FFN_GROUPS[gi]
                    s = gi % 2
                    ai = idx % 2
                    n = c1 - c0
                    for j in range(8):
                        k = y_cnt[0] % 2
                        y_cnt[0] += 1
                        pe_group([(pY[k][:, :W], Wd[s][:, cl, j * 128:(j + 1) * 128], A[ai][:, cl, :W]) for cl in range(n)],
                                 reads=[Wdb[s]] + Ab[ai][:n], writes=[pYb[k]])
                        emit(DVE, lambda: nc.vector.scalar_tensor_tensor(
                            out=xres[:, j, a:b], in0=pY[k][:, :W], scalar=0.5, in1=xres[:, j, a:b],
                            op0=ALU.mult, op1=ALU.add),
                            reads=[pYb[k]] + xbufs(j, a, b), writes=xbufs(j, a, b))
                    if ti == len(tiles) - 1 and gi + 2 < len(FFN_GROUPS):
                        load_group(gi + 2)

                stage1(0)
                for idx in range(len(items)):
                    if idx + 1 < len(items):
                        stage1(idx + 1)
                    stage2(idx)
                barrier()

        def mixer_phase():
            out_toks = []
            with ExitStack() as ph:
                alloc_norm(ph, MT)
                Win = sb(ph, [128, 8, INC], BF16, "Win")
                Wout = sb(ph, [128, 8, D], BF16, "Wout")
                WinSeg = [(0, 512), (512, 768), (768, 1280), (1280, 1792), (1792, 2304)]
                Winb = [Buf() for _ in WinSeg]
                Woutb = Buf()
                Bias_s = sb(ph, [16, 2, 132], F32, "Bias_s")
                sinks_s = sb(ph, [16, 2], F32, "sinks_s")
                sconst = Buf()
                sconst2 = Buf()
                kT_all = sb(ph, [128, NT], BF16, "kT_all")
                kTb = [Buf() for _ in range(18)]
                hbm = [sb(ph, [128, 8, MT], BF16, "hbm") for _ in range(2)]
                hbmb = [[Buf() for _ in range(8)] for _ in range(2)]
                zq = sb(ph, [128, 4, MT], F32, "zq")
                zqb = [Buf() for _ in range(4)]
                zk = sb(ph, [128, MT], F32, "zk")
                zkb = Buf()
                cv = sb(ph, [128, 4, MT], F32, "cv")
                cvb = [Buf() for _ in range(4)]
                qnb2 = [sb(ph, [128, 4, MT], BF16, "qnb") for _ in range(2)]
                qnbb2 = [[Buf() for _ in range(4)] for _ in range(2)]
                knf = sb(ph, [128, MT], F32, "knf")
                knfb = Buf()
                mixb2 = [sb(ph, [128, 8, MT], BF16, "mixb") for _ in range(2)]
                mixbb2 = [[Buf() for _ in range(8)] for _ in range(2)]

                pA = ps(ph, [128, 2048], F32, "pA")
                bA = [PBuf() for _ in range(4)]
                pstat = ps(ph, [128, 512], F32, "pstat")
                pstat_b = PBuf()
                pT = ps(ph, [128, 1024], F32, "pT")
                pTb = PBuf()
                pO = ps(ph, [128, 512], F32, "pO")
                pOb = PBuf()
                pZ = [pA[:, 0:512], pA[:, 512:1024]]
                pZb = [bA[0], bA[1]]
                pS = pA[:, 1024:2048]
                pSb = [bA[2], bA[3]]

                winv = win_d.rearrange("(kc p) n -> p kc n", p=128)
                for si in (1, 3, 4, 0, 2):
                    lo, hi = WinSeg[si]
                    QP.start(Win[:, :, lo:hi], winv[:, :, lo:hi], writes=[Winb[si]])
                QP.start(Wout[:], wout_d.rearrange("(kc p) n -> p kc n", p=128), writes=[Woutb])
                QS.start(sinks_s[:], sinks_d, writes=[sconst2])
                out_toks.append(QS.start(kws_o, ck_nat[:, 4:128, :]))
                out_toks.append(QS.start(vws_o, cv_nat[:, 4:128, :]))

                state = {"ti": 0, "z": 0, "wprev": None}
                hsq = [sb(ph, [128, MT], BF16, "hsq") for _ in range(5)]
                hsqb = [Buf() for _ in range(5)]

                def presq(slot, src_ap, W, src_bufs):
                    emit(ACT, lambda: nc.scalar.activation(out=hsq[slot][:, :W], in_=src_ap, func=AF.Square),
                         reads=src_bufs, writes=[hsqb[slot]])

                def zmm(wcol, seg, hs, hsb, W):
                    k = state["z"] % 2
                    state["z"] += 1
                    pe_group([(pZ[k][:, :W], Win[:, kc, wcol:wcol + 128], hs[:, kc, :W]) for kc in range(8)],
                             reads=[Winb[seg]] + hsb, writes=[pZb[k]])
                    return k

                def head_norm(src_ap, W, gap, out_ap, out_bufs, src_bufs, slot=None):
                    sq, sqb, rt, rtb, rstd, rstdb = NS["sq"], NS["sqb"], NS["rt"], NS["rtb"], NS["rstd"], NS["rstdb"]
                    pe_group([(pstat[:, :W], bd64[:], hsq[slot][:, :W])], reads=[hsqb[slot]] + CB, writes=[pstat_b])
                    j = cnt["rt"] % 2
                    cnt["rt"] += 1
                    emit(ACT, lambda: nc.scalar.activation(out=rt[j][:, :W], in_=pstat[:, :W], func=AF.Ln,
                                                           bias=epst[:, 0:1], scale=1.0),
                         reads=[pstat_b] + CB, writes=[rtb[j]])
                    emit(ACT, lambda: nc.scalar.activation(out=rstd[j][:, :W], in_=rt[j][:, :W], func=AF.Exp, scale=-0.5),
                         reads=[rtb[j]], writes=[rstdb[j]])
                    emit(DVE, lambda: nc.vector.scalar_tensor_tensor(
                        out=out_ap, in0=src_ap, scalar=gap, in1=rstd[j][:, :W],
                        op0=ALU.mult, op1=ALU.mult),
                        reads=src_bufs + [rstdb[j]] + CB, writes=out_bufs)

                def norm_only(a, b, ti):
                    W = b - a
                    hs = hbm[ti % 2]
                    norm_tile(a, b, 1, lambda c: hs[:, c, :W], hbmb[ti % 2], pstat, pstat_b)

                def kchunk(a, b, ti):
                    W = b - a
                    hs = hbm[ti % 2]
                    hsb = hbmb[ti % 2]
                    k = zmm(512, 1, hs, hsb, W)
                    emit(ACT, lambda: nc.scalar.copy(out=zk[:, :W], in_=pZ[k][:, :W]), reads=[pZb[k]], writes=[zkb])
                    presq(4, zk[:, :W], W, [zkb])
                    return hs, hsb

                def front0(a, b):
                    ti = state["ti"]
                    state["ti"] += 1
                    norm_only(a, b, ti)
                    return kchunk(a, b, ti)

                def q_chunk1(hs, hsb, W, i):
                    k = zmm(i * 128, 0, hs, hsb, W)
                    emit(ACT, lambda: nc.scalar.copy(out=zq[:, i, :W], in_=pZ[k][:, :W]), reads=[pZb[k]],
                         writes=[zqb[i]])
                    presq(i, zq[:, i, :W], W, [zqb[i]])

                def q_chunks(hs, hsb, W):
                    for i in range(4):
                        q_chunk1(hs, hsb, W, i)

                def u_c1(hs, hsb, W, uview, pzview, ubw, c):
                    k = zmm(1280 + c * 128, 3, hs, hsb, W)
                    emit(ACT, lambda: nc.scalar.copy(out=uview(c), in_=pzview(k)), reads=[pZb[k]], writes=ubw[c])

                def u_h1(hs, hsb, W, uview, pzview, ubw, c):
                    k = zmm(1792 + c * 128, 4, hs, hsb, W)
                    emit(DVE, lambda: nc.vector.tensor_tensor(out=uview(c), in0=uview(c), in1=pzview(k), op=ALU.mult),
                         reads=[pZb[k]] + ubw[c], writes=ubw[c])

                def u_chunks(hs, hsb, W, uview, pzview, ubw):
                    for c in range(4):
                        u_c1(hs, hsb, W, uview, pzview, ubw, c)
                    for c in range(4):
                        u_h1(hs, hsb, W, uview, pzview, ubw, c)

                def k_norm(a, b, kind):
                    W = b - a
                    head_norm(zk[:, :W], W, qkg[:, 1:2], knf[:, :W], [knfb], [zkb], slot=4)
                    blks = [17] if kind == "sample" else list(range(a // 128, b // 128))
                    emit(ACT, lambda: nc.scalar.copy(out=kT_all[:, a:b], in_=knf[:, :W]), reads=[knfb],
                         writes=[kTb[x] for x in blks])

                def conv_only(W, cvo_fn, taps_fn, rb_fn, cs=range(4)):
                    for c in cs:
                        cvo = cvo_fn(c)
                        taps = taps_fn(c)
                        rb = rb_fn(c)
                        CE, ce = (POOL, nc.gpsimd) if CONV_ON_POOL else (DVE, nc.vector)
                        emit(CE, lambda: ce.tensor_scalar(out=cvo, in0=taps[2], scalar1=wconv[:, c, 2:3],
                                                          scalar2=None, op0=ALU.mult),
                             reads=rb + CB, writes=[cvb[c]])
                        for j in (1, 0):
                            emit(CE, lambda: ce.scalar_tensor_tensor(
                                out=cvo, in0=taps[j], scalar=wconv[:, c, j:j + 1], in1=cvo, op0=ALU.mult, op1=ALU.add),
                                reads=rb + CB + [cvb[c]], writes=[cvb[c]])

                def gate_only(hs, hsb, W, par, cs=range(4)):
                    for c in cs:
                        k = zmm(768 + c * 128, 2, hs, hsb, W)
                        emit(DVE, lambda: nc.vector.tensor_tensor(out=mixb2[par][:, 4 + c, :W], in0=pZ[k][:, :W],
                                                                  in1=cv[:, c, :W], op=ALU.mult),
                             reads=[pZb[k], cvb[c]], writes=[mixbb2[par][4 + c]])

                def out_proj(a, b, par, js):
                    W = b - a
                    for j in js:
                        k = state["z"] % 2
                        state["z"] += 1
                        pe_group([(pZ[k][:, :W], Wout[:, m, j * 128:(j + 1) * 128], mixb2[par][:, m, :W]) for m in range(8)],
                                 reads=[Woutb] + mixbb2[par], writes=[pZb[k]])
                        emit(DVE, lambda: nc.vector.tensor_tensor(out=xres[:, j, a:b], in0=pZ[k][:, :W], in1=xres[:, j, a:b],
                                                                  op=ALU.add),
                             reads=[pZb[k]] + xbufs(j, a, b), writes=xbufs(j, a, b))

                with ExitStack() as P:
                    Bhi = sb(P, [128, 8, 256], BF16, "Bhi")
                    Blo = sb(P, [128, 8, 256], BF16, "Blo")
                    Hm = sb(P, [128, 256], BF16, "Hm")
                    Bhib, Blob, Hmb = Buf(), Buf(), Buf()
                    Biasb = Buf()
                    Vt = sb(P, [128, 17, 192], BF16, "Vt")
                    Vtb = [Buf() for _ in range(17)]
                    emit(DVE, lambda: nc.vector.memset(Vt[:], 0.0), writes=Vtb)
                    with ExitStack() as su:
                        tab_sb = sb(su, [33, 8], F32, "tab_sb")
                        tabB = sb(su, [33, 8, 128], F32, "tabB")
                        E2_sb = sb(su, [33, 383], F32, "E2_sb")
                        Ubc_sb = sb(su, [128, 8, 383], F32, "Ubc_sb")
                        Bias = sb(su, [128, 8, 256], F32, "Bias")
                        tb = Buf()
                        eb = Buf()
                        ub = Buf()
                        emit(DVE, lambda: nc.vector.memset(tab_sb[:], 1.0), writes=[tb])
                        QS.start(tab_sb[0:32, :], tabp_d, writes=[tb])
                        QS.start(E2_sb[:], E2_d, writes=[eb])
                        emit(DVE, lambda: nc.vector.tensor_copy(out=tabB[:], in_=tab_sb[:].unsqueeze(2).to_broadcast([33, 8, 128])),
                             reads=[tb], writes=[tb])
                        for h in range(8):
                            k = h % 2
                            pe_group([(pZ[k][:, 0:383], tabB[:, h, :], E2_sb[:])], reads=[tb, eb], writes=[pZb[k]])
                            emit(ACT, lambda: nc.scalar.copy(out=Ubc_sb[:, h, :], in_=pZ[k][:, 0:383]), reads=[pZb[k]], writes=[ub])
                        dsc = Buf()
                        QS.start(Ubc, Ubc_sb[:], reads=[ub], writes=[dsc])
                        src = bass.AP(Ubc_t, 127, [[8 * 383 - 1, 128], [383, 8], [1, 256]])
                        QS.start(Bias[:], src, reads=[dsc], writes=[Biasb])
                        emit(DVE, lambda: nc.vector.tensor_copy(out=Bhi[:], in_=Bias[:]), reads=[Biasb], writes=[Bhib])
                        emit(DVE, lambda: nc.vector.tensor_tensor(out=Blo[:], in0=Bias[:], in1=Bhi[:], op=ALU.subtract),
                             reads=[Biasb, Bhib], writes=[Blob])
                        emit(DVE, lambda: nc.vector.memset(Hm[:], 0.0), writes=[Hmb])
                        emit(DVE, lambda: nc.vector.tensor_scalar(out=Hm[:, 0:128], in0=Hm[:, 0:128], scalar1=hflag[:, 0:1],
                                                                  scalar2=None, op0=ALU.add), reads=[Hmb] + CB, writes=[Hmb])
                        for kv in range(2):
                            for g in range(4):
                                ph_ = 4 * (g // 2) + 2 * kv + (g % 2)
                                src = bass.AP(Ubc_t, ph_ * 383 + 127, [[8 * 383 - 1, 4], [1, 132]])
                                QS.start(Bias_s[g * 4:(g + 1) * 4, kv, :], src, reads=[dsc], writes=[sconst])
                        barrier(queues=(QS,))
                    ubuf = sb(P, [128, 4, MT + 2], F32, "ubuf")
                    ubb = [Buf() for _ in range(4)]
                    vlast = sb(P, [128, 128], F32, "vlast")
                    vlastb = Buf()
                    P2 = [sb(P, [128, 4, 256], BF16, "Pexp") for _ in range(2)]
                    P2b = [[Buf() for _ in range(4)] for _ in range(2)]
                    D2 = [sb(P, [128, 4, 128], BF16, "Dn") for _ in range(2)]
                    D2b = [Buf() for _ in range(2)]
                    PTs2 = [sb(P, [128, 1024], BF16, "PTs") for _ in range(2)]
                    PTsb2 = [Buf() for _ in range(2)]
                    sm2 = [{n: sb(P, [128, 4], F32, n) for n in ("mx", "negm", "tmp4", "es4", "rs4", "den4", "rden4")}
                           for _ in range(2)]
                    smb2 = [{n: Buf() for n in sm2[0]} for _ in range(2)]

                    def attn_A(bi, o, hp, par, hb_):
                        Pe, Peb, sm, smb = P2[hb_], P2b[hb_], sm2[hb_], smb2[hb_]
                        qnb, qnbb = qnb2[par], qnbb2[par]
                        kc0 = 128 * (bi - 1)
                        for ci in range(2):
                            i = 2 * hp + ci
                            for kv in range(2):
                                sl = kv * 2 + ci
                                dst = pS[:, sl * 256:(sl + 1) * 256]
                                mms = [(dst, qnb[kv * 64:(kv + 1) * 64, i, o:o + 128], kT_all[kv * 64:(kv + 1) * 64, kc0:kc0 + 256]),
                                       (dst, ident[:], Bhi[:, 4 * hp + sl, :]),
                                       (dst, ident[:], Blo[:, 4 * hp + sl, :])]
                                rd = [qnbb[i], kTb[bi - 1], kTb[bi], Bhib, Blob] + CB
                                if bi == 1:
                                    mms.append((dst, ident[:], Hm[:]))
                                    rd.append(Hmb)
                                pe_group(mms, reads=rd, writes=[pSb[sl // 2]])
                        emit(DVE, lambda: nc.vector.reduce_max(out=sm["mx"][:], in_=pS.rearrange("p (a b) -> p a b", b=256),
                                                               axis=AX.X),
                             reads=[pSb[0], pSb[1]], writes=[smb["mx"]])
                        emit(DVE, lambda: nc.vector.scalar_tensor_tensor(
                            out=sm["negm"][:], in0=sm["mx"][:], scalar=-1.0, in1=nsinkb[:, 4 * hp:4 * hp + 4],
                            op0=ALU.mult, op1=ALU.min), reads=[smb["mx"]] + CB, writes=[smb["negm"]])
                        emit(DVE, lambda: nc.vector.tensor_tensor(
                            out=sm["tmp4"][:], in0=sinkb[:, 4 * hp:4 * hp + 4], in1=sm["negm"][:], op=ALU.add),
                            reads=[smb["negm"]] + CB, writes=[smb["tmp4"]])
                        for sl in range(4):
                            emit(ACT, lambda: nc.scalar.activation(
                                out=Pe[:, sl, :], in_=pS[:, sl * 256:(sl + 1) * 256], func=AF.Exp,
                                bias=sm["negm"][:, sl:sl + 1], scale=1.0, accum_out=sm["rs4"][:, sl:sl + 1]),
                                reads=[pSb[sl // 2], smb["negm"]], writes=[Peb[sl], smb["rs4"]])
                        emit(ACT, lambda: nc.scalar.activation(out=sm["es4"][:], in_=sm["tmp4"][:], func=AF.Exp),
                             reads=[smb["tmp4"]], writes=[smb["es4"]])

                    def attn_B(bi, o, hp, par, hb_):
                        Pe, Peb, sm, smb = P2[hb_], P2b[hb_], sm2[hb_], smb2[hb_]
                        Dn, Dnb, PTs, PTsb = D2[hb_], D2b[hb_], PTs2[hb_], PTsb2[hb_]
                        emit(DVE, lambda: nc.vector.tensor_tensor(out=sm["den4"][:], in0=sm["rs4"][:], in1=sm["es4"][:],
                                                                  op=ALU.add),
                             reads=[smb["rs4"], smb["es4"]], writes=[smb["den4"]])
                        emit(DVE, lambda: nc.vector.reciprocal(out=sm["rden4"][:], in_=sm["den4"][:]),
                             reads=[smb["den4"]], writes=[smb["rden4"]])
                        emit(DVE, lambda: nc.vector.tensor_tensor(
                            out=Dn[:], in0=ident[:].unsqueeze(1).to_broadcast([128, 4, 128]),
                            in1=sm["rden4"][:].unsqueeze(2).to_broadcast([128, 4, 128]), op=ALU.mult),
                            reads=[smb["rden4"]] + CB, writes=[Dnb])
                        rd = Peb + [Dnb]
                        _pre(PE, rd, [pTb])
                        inst = None
                        for sl in range(4):
                            for kh in range(2):
                                idx = sl * 2 + kh
                                inst = nc.tensor.matmul(pT[:, idx * 128:(idx + 1) * 128], Pe[:, sl, kh * 128:(kh + 1) * 128],
                                                        Dn[:, sl, :], start=True, stop=True)
                        tok_ = PE.mark(inst)
                        _commit(tok_, rd, [pTb])
                        emit(ACT, lambda: nc.scalar.copy(out=PTs[:], in_=pT[:]), reads=[pTb], writes=[PTsb])

                    def attn_B2(bi, o, hp, par, hb_):
                        PTs, PTsb = PTs2[hb_], PTsb2[hb_]
                        for ci in range(2):
                            i = 2 * hp + ci
                            mms = []
                            for kv in range(2):
                                for kh in range(2):
                                    idx = (kv * 2 + ci) * 2 + kh
                                    mms.append((pO[:, i * 128:(i + 1) * 128], Vt[:, bi - 1 + kh, kv * 64:kv * 64 + 128],
                                                PTs[:, idx * 128:(idx + 1) * 128]))
                            pe_group(mms, reads=[PTsb, Vtb[bi - 1], Vtb[bi]], writes=[pOb])

                    def attn_fin(o, par):
                        emit(ACT, lambda: nc.scalar.copy(out=mixb2[par][:, 0:4, o:o + 128],
                                                         in_=pO[:].rearrange("p (a b) -> p a b", b=128)),
                             reads=[pOb], writes=mixbb2[par][0:4])

                    def front_steps(kind, a, b, par, tidx=None, nxt=None):
                        W = b - a
                        box = {}

                        def s0():
                            box["hs"], box["hsb"] = kchunk(a, b, tidx)

                        def s_next():
                            if nxt is not None:
                                norm_only(nxt[1], nxt[2], tidx + 1)

                        def s1():
                            hs, hsb = box["hs"], box["hsb"]
                            for bo in range(W // 128):
                                bi = a // 128 + bo
                                kz = state["z"] % 2
                                state["z"] += 1
                                pe_group([(pZ[kz][:, 0:128], hs[:, kc, bo * 128:(bo + 1) * 128], Win[:, kc, 640:768])
                                          for kc in range(8)], reads=[Winb[1]] + hsb, writes=[pZb[kz]])
                                emit(ACT, lambda: nc.scalar.copy(out=Vt[:, bi, 0:64], in_=pZ[kz][:, 0:64]),
                                     reads=[pZb[kz]], writes=[Vtb[bi]])
                                emit(ACT, lambda: nc.scalar.copy(out=Vt[:, bi, 128:192], in_=pZ[kz][:, 64:128]),
                                     reads=[pZb[kz]], writes=[Vtb[bi]])
                                if bi == 16:
                                    emit(ACT, lambda: nc.scalar.copy(out=vlast[:], in_=pZ[kz][:, 0:128]),
                                         reads=[pZb[kz]], writes=[vlastb])
                                    out_toks.append(QS.start(vl_o, vlast[:], reads=[vlastb]))

                        uv = lambda c: ubuf[:, c, 2:2 + W]
                        pzv = lambda k: pZ[k][:, :W]
                        ubw_ = [[ubb[c]] for c in range(4)]

                        def s3a():
                            if state["wprev"] is not None:
                                wp = state["wprev"]
                                emit(DVE, lambda: nc.vector.tensor_copy(out=ubuf[:, :, 0:2], in_=ubuf[:, :, wp:wp + 2]),
                                     reads=ubb, writes=ubb)
                            else:
                                emit(DVE, lambda: nc.vector.memset(ubuf[:, :, 0:2], 0.0), writes=ubb)
                            state["wprev"] = W

                        def s4():
                            k_norm(a, b, kind)
                            if b == CS:
                                out_toks.append(QS.start(kTl_o, knf[:, W - 128:W], reads=[knfb]))
                                out_toks.append(QS.start(cl_o, ubuf[:, :, W:W + 2], reads=ubb))

                        def mk(f, *args):
                            return lambda: f(*args)
                        steps = [s0, s1]
                        if kind != "halo":
                            steps += [mk(lambda i: q_chunk1(box["hs"], box["hsb"], W, i), i) for i in range(4)]
                        steps.append(s3a)
                        steps += [mk(lambda c: u_c1(box["hs"], box["hsb"], W, uv, pzv, ubw_, c), c) for c in range(4)]
                        steps += [mk(lambda c: u_h1(box["hs"], box["hsb"], W, uv, pzv, ubw_, c), c) for c in range(4)]
                        steps.append(s4)
                        if kind == "halo":
                            return steps + [s_next]
                        steps += [mk(lambda i: head_norm(zq[:, i, :W], W, qkg8[:, 0:1], qnb2[par][:, i, :W], [qnbb2[par][i]], [zqb[i]], slot=i), i)
                                  for i in range(4)]
                        steps += [mk(lambda c: conv_only(W, lambda c_: cv[:, c_, :W],
                                                         lambda c_: [ubuf[:, c_, j:j + W] for j in range(3)],
                                                         lambda c_: [ubb[c_]], cs=[c]), c) for c in range(4)]
                        steps += [mk(lambda c: gate_only(box["hs"], box["hsb"], W, par, cs=[c]), c) for c in range(4)]
                        steps.insert(min(len(steps), KNPOS), s_next)
                        return steps

                    hpc = [0]

                    def back_steps(a, b, par):
                        W = b - a
                        passes = []
                        for bo in range(W // 128):
                            for hp in range(2):
                                passes.append((a // 128 + bo, bo * 128, hp, hpc[0] % 2))
                                hpc[0] += 1
                        steps = []
                        n = len(passes)

                        def mkA(p):
                            return lambda: attn_A(p[0], p[1], p[2], par, p[3])

                        def mkB1(idx):
                            p = passes[idx]
                            return lambda: attn_B(p[0], p[1], p[2], par, p[3])

                        def mkB2(idx):
                            p = passes[idx]

                            def f():
                                attn_B2(p[0], p[1], p[2], par, p[3])
                                if p[2] == 1:
                                    attn_fin(p[1], par)
                            return f
                        steps.append((mkA(passes[0]), SLOT[0]))
                        for idx in range(n):
                            if idx + 1 < n:
                                steps.append((mkA(passes[idx + 1]), SLOT[0]))
                            steps.append((mkB1(idx), SLOT[1]))
                            steps.append((mkB2(idx), SLOT[2]))
                        for j0 in range(0, 8, 2):
                            steps.append(((lambda j0_: (lambda: out_proj(a, b, par, range(j0_, j0_ + 2))))(j0), SLOT[3]))
                        return steps

                    pW = None

                    def warm():
                        for _ in range(NWARM):
                            nc.tensor.matmul(pW[:, :], ident[:], kT_all[:, 0:512], start=True, stop=True)

                    def interleave(bs, fs):
                        out = []
                        j = 0
                        for (st, k) in bs:
                            if NWARM:
                                out.append(warm)
                            out.append(st)
                            for _ in range(k):
                                if j < len(fs):
                                    out.append(fs[j]); j += 1
                        out += fs[j:]
                        return out

                    tiles = [("halo", 0, HALO)] + [("prompt", HALO + i * MT, HALO + (i + 1) * MT) for i in range(NPR // MT)]
                    def nx(t):
                        return tiles[t + 1] if t + 1 < len(tiles) else None
                    norm_only(tiles[0][1], tiles[0][2], 0)
                    for st_ in front_steps(*tiles[0], 0, tidx=0, nxt=nx(0)):
                        st_()
                    for st_ in front_steps(*tiles[1], 1, tidx=1, nxt=nx(1)):
                        st_()
                    for t in range(1, len(tiles)):
                        bs = back_steps(tiles[t][1], tiles[t][2], t % 2)
                        fs = front_steps(*tiles[t + 1], (t + 1) % 2, tidx=t + 1, nxt=nx(t + 1)) if t + 1 < len(tiles) else []
                        for st_ in interleave(bs, fs):
                            st_()
                    barrier()

                with ExitStack() as S_:
                    qnb, qnbb, mixb, mixbb = qnb2[0], qnbb2[0], mixb2[0], mixbb2[0]
                    kTs = sb(S_, [128, NSEQ, 128], BF16, "kTs")
                    Vc = sb(S_, [128, NSEQ, 192], BF16, "Vc")
                    Vn = sb(S_, [4, NSEQ, 192], BF16, "Vn")
                    vsnf = sb(S_, [4, 8, 128], F32, "vsnf")
                    vsnfb = Buf()
                    Vnb = Buf()
                    ubs = sb(S_, [128, 4, NSEQ, 6], F32, "ubs")
                    ubsb = Buf()
                    qs = sb(S_, [128, NSEQ, 16], BF16, "qs")
                    qsb = Buf()
                    NQ = 4
                    NSL = 2 * NQ
                    Ss = sb(S_, [16, NSL, 132], F32, "Ss")
                    Ssb_s = Buf()
                    Pns = sb(S_, [16, NSL, 132], BF16, "Pns")
                    Pnsb = Buf()
                    PTcs = sb(S_, [128, NSL * 16], BF16, "PTcs")
                    PTns = sb(S_, [4, NSL * 16], BF16, "PTns")
                    PTcsb = Buf()
                    ss = {n: sb(S_, [16, NSL], F32, "s" + n) for n in ("mx", "m", "rs", "tmp", "es", "den", "rden")}
                    ssb = {n: Buf() for n in ss}
                    kTsb = Buf()
                    Vcb = Buf()
                    emit(DVE, lambda: nc.vector.memset(Vc[:, :, 64:128], 0.0), writes=[Vcb])
                    emit(DVE, lambda: nc.vector.memset(Vn[:], 0.0), writes=[Vnb])
                    QP.start(kTs[:], ckT_d, writes=[kTsb])
                    QP.start(Vc[:, :, 0:64], cvt_d[:, :, 0:64], writes=[Vcb])
                    QP.start(Vc[:, :, 128:192], cvt_d[:, :, 64:128], writes=[Vcb])
                    st_c = sb(S_, [128, 128], F32, "st_c")
                    st_cb = Buf()
                    cs_c = sb(S_, [128, 128], F32, "cs_c")
                    cs_cb = Buf()
                    ubs_st = Buf()
                    QS.start(st_c[:], scT_d.rearrange("p c q r -> p (c q r)"), writes=[st_cb])
                    emit(DVE, lambda: nc.vector.tensor_copy(out=ubs[:, :, :, 0:2],
                                                            in_=st_c[:].rearrange("p (c q r) -> p c q r", c=4, r=2)),
                         reads=[st_cb], writes=[ubs_st])

                    a, b = CS, NT
                    W = NSM
                    state["ti"] = 0
                    hs, hsb = front0(a, b)
                    pV = pA[0:4, :]
                    for seq in range(NSEQ):
                        pe_group([(pV[:, seq * 128:(seq + 1) * 128], hs[:, kc, seq * 4:(seq + 1) * 4],
                                   Win[:, kc, 640:768]) for kc in range(8)],
                                 reads=[Winb[1]] + hsb, writes=[bA[seq // 4]])
                    pVv = pV.rearrange("p (q n) -> p q n", n=128)
                    emit(ACT, lambda: nc.scalar.copy(out=Vn[:, :, 0:64], in_=pVv[:, :, 0:64]), reads=bA, writes=[Vnb])
                    emit(ACT, lambda: nc.scalar.copy(out=Vn[:, :, 128:192], in_=pVv[:, :, 64:128]), reads=bA, writes=[Vnb])
                    for hh in range(2):
                        emit(ACT, lambda: nc.scalar.copy(out=vsnf[:], in_=pVv[:, hh * 8:(hh + 1) * 8, :]),
                             reads=bA, writes=[vsnfb])
                        out_toks.append(QS.start(vsn_o[:, hh * 8:(hh + 1) * 8, :], vsnf[:], reads=[vsnfb]))
                    q_chunks(hs, hsb, W)
                    u_chunks(hs, hsb, W, lambda c: ubs[:, c, :, 2:6],
                             lambda k: pZ[k][:, 0:NSM].rearrange("p (q s) -> p q s", s=4), [[ubsb]] * 4)
                    k_norm(a, b, "sample")
                    out_toks.append(QS.start(ksn_o, knf[:, :W], reads=[knfb]))
                    emit(DVE, lambda: nc.vector.tensor_copy(out=cs_c[:].rearrange("p (c q r) -> p c q r", c=4, r=2),
                                                            in_=ubs[:, :, :, 4:6]), reads=[ubsb], writes=[cs_cb])
                    out_toks.append(QS.start(csn_o.rearrange("p c q r -> p (c q r)"), cs_c[:], reads=[cs_cb]))
                    for i in range(4):
                        head_norm(zq[:, i, :W], W, qkg8[:, 0:1], qnb[:, i, :W], [qnbb[i]], [zqb[i]], slot=i)
                    conv_only(W, lambda c: cv[:, c, 0:NSM].rearrange("p (q s) -> p q s", s=4),
                              lambda c: [ubs[:, c, :, j:j + 4] for j in range(3)], lambda c: [ubsb, ubs_st])
                    gate_only(hs, hsb, W, 0)
                    emit(DVE, lambda: nc.vector.tensor_copy(
                        out=qs[:].rearrange("p q (g s) -> p q g s", s=4),
                        in_=qnb[:, :, 0:NSM].rearrange("p g (q s) -> p q g s", s=4)),
                        reads=qnbb, writes=[qsb])
                    pSc = pA[0:16, 0:NSL * 128]
                    nbank = (NSL * 128) // 512
                    for part in range(NSEQ // NQ):
                        for si in range(NQ):
                            seq = part * NQ + si
                            for kv in range(2):
                                slot = kv * NQ + si
                                pe_group([(pSc[:, slot * 128:(slot + 1) * 128], qs[kv * 64:(kv + 1) * 64, seq, :],
                                           kTs[kv * 64:(kv + 1) * 64, seq, :])],
                                         reads=[qsb, kTsb], writes=[bA[slot // 4]])
                                pnew, pnewb = ((pO, pOb), (pstat, pstat_b))[kv]
                                pe_group([(pnew[0:16, si * 4:(si + 1) * 4], qs[kv * 64:(kv + 1) * 64, seq, :],
                                           kT_all[kv * 64:(kv + 1) * 64, CS + seq * 4:CS + seq * 4 + 4])],
                                         reads=[qsb, kTb[17]], writes=[pnewb])
                        emit(DVE, lambda: nc.vector.tensor_scalar(
                            out=Ss[:, :, 0:128], in0=pSc.rearrange("p (q n) -> p q n", n=128), scalar1=1.0,
                            scalar2=None, op0=ALU.mult), reads=bA[0:nbank], writes=[Ssb_s])
                        for kv in range(2):
                            pnew, pnewb = ((pO, pOb), (pstat, pstat_b))[kv]
                            emit(DVE, lambda: nc.vector.tensor_scalar(
                                out=Ss[:, kv * NQ:(kv + 1) * NQ, 128:132],
                                in0=pnew[0:16, 0:NQ * 4].rearrange("p (q n) -> p q n", n=4),
                                scalar1=1.0, scalar2=None, op0=ALU.mult), reads=[pnewb], writes=[Ssb_s])
                        for kv in range(2):
                            sv = Ss[:, kv * NQ:(kv + 1) * NQ, :]
                            emit(DVE, lambda: nc.vector.tensor_tensor(
                                out=sv, in0=sv, in1=Bias_s[:, kv, :].unsqueeze(1).to_broadcast([16, NQ, 132]),
                                op=ALU.add), reads=[Ssb_s, sconst], writes=[Ssb_s])
                        emit(DVE, lambda: nc.vector.reduce_max(out=ss["mx"][:], in_=Ss[:], axis=AX.X),
                             reads=[Ssb_s], writes=[ssb["mx"]])
                        sbc = sinks_s[:].unsqueeze(2).to_broadcast([16, 2, NQ])
                        emit(DVE, lambda: nc.vector.tensor_tensor(
                            out=ss["m"][:].rearrange("p (k q) -> p k q", k=2),
                            in0=ss["mx"][:].rearrange("p (k q) -> p k q", k=2), in1=sbc, op=ALU.max),
                            reads=[ssb["mx"], sconst2], writes=[ssb["m"]])
                        emit(DVE, lambda: nc.vector.tensor_tensor(
                            out=Ss[:], in0=Ss[:], in1=ss["m"][:].unsqueeze(2).to_broadcast([16, NSL, 132]),
                            op=ALU.subtract), reads=[Ssb_s, ssb["m"]], writes=[Ssb_s])
                        emit(ACT, lambda: nc.scalar.activation(out=Ss[:], in_=Ss[:], func=AF.Exp),
                             reads=[Ssb_s], writes=[Ssb_s])
                        emit(DVE, lambda: nc.vector.reduce_sum(out=ss["rs"][:], in_=Ss[:], axis=AX.X),
                             reads=[Ssb_s], writes=[ssb["rs"]])
                        emit(DVE, lambda: nc.vector.tensor_tensor(
                            out=ss["tmp"][:].rearrange("p (k q) -> p k q", k=2), in0=sbc,
                            in1=ss["m"][:].rearrange("p (k q) -> p k q", k=2), op=ALU.subtract),
                            reads=[ssb["m"], sconst2], writes=[ssb["tmp"]])
                        emit(ACT, lambda: nc.scalar.activation(out=ss["es"][:], in_=ss["tmp"][:], func=AF.Exp),
                             reads=[ssb["tmp"]], writes=[ssb["es"]])
                        emit(DVE, lambda: nc.vector.tensor_tensor(out=ss["den"][:], in0=ss["rs"][:], in1=ss["es"][:],
                                                                  op=ALU.add),
                             reads=[ssb["rs"], ssb["es"]], writes=[ssb["den"]])
                        emit(DVE, lambda: nc.vector.reciprocal(out=ss["rden"][:], in_=ss["den"][:]),
                             reads=[ssb["den"]], writes=[ssb["rden"]])
                        emit(DVE, lambda: nc.vector.tensor_tensor(
                            out=Pns[:], in0=Ss[:], in1=ss["rden"][:].unsqueeze(2).to_broadcast([16, NSL, 132]),
                            op=ALU.mult), reads=[Ssb_s, ssb["rden"]], writes=[Pnsb])
                        trs = []
                        for slot in range(NSL):
                            trs.append((pT[:, slot * 16:(slot + 1) * 16], Pns[0:16, slot, 0:128]))
                            trs.append((pT[0:4, 512 + slot * 16:512 + (slot + 1) * 16], Pns[0:16, slot, 128:132]))
                        _pre(PE, [Pnsb] + CB, [pTb])
                        inst = None
                        for (o_, i_) in trs:
                            inst = nc.tensor.matmul(o_, i_, ident[0:16, 0:16], start=True, stop=True)
                        tok_ = PE.mark(inst)
                        _commit(tok_, [Pnsb] + CB, [pTb])
                        emit(ACT, lambda: nc.scalar.copy(out=PTcs[:], in_=pT[:, 0:NSL * 16]), reads=[pTb], writes=[PTcsb])
                        emit(ACT, lambda: nc.scalar.copy(out=PTns[:], in_=pT[0:4, 512:512 + NSL * 16]), reads=[pTb],
                             writes=[PTcsb])
                        for si in range(NQ):
                            seq = part * NQ + si
                            mms = []
                            for kv in range(2):
                                slot = kv * NQ + si
                                mms.append((pstat[:, si * 16:(si + 1) * 16], Vc[:, seq, kv * 64:kv * 64 + 128],
                                            PTcs[:, slot * 16:(slot + 1) * 16]))
                                mms.append((pstat[:, si * 16:(si + 1) * 16], Vn[0:4, seq, kv * 64:kv * 64 + 128],
                                            PTns[0:4, slot * 16:(slot + 1) * 16]))
                            pe_group(mms, reads=[PTcsb, Vcb, Vnb], writes=[pstat_b])
                        emit(DVE, lambda: nc.vector.tensor_copy(
                            out=mixb[:, 0:4, part * NQ * 4:(part + 1) * NQ * 4].rearrange("p g (q s) -> p q g s", s=4),
                            in_=pstat[:, 0:NQ * 16].rearrange("p (q g s) -> p q g s", g=4, s=4)),
                            reads=[pstat_b], writes=mixbb[0:4])
                    out_proj(a, b, 0, range(8))
                    barrier()
            return out_toks

        def ple_phase(out_toks):
            with ExitStack() as ph:
                alloc_norm(ph, 512)
                Wpg = sb(ph, [128, 8, D], BF16, "Wpg")
                Wpp = sb(ph, [128, 2, D], BF16, "Wpp")
                pe_b = sb(ph, [128, 2, NOUT], BF16, "pe_b")
                wb_ = Buf()
                hb2 = [sb(ph, [128, 8, 512], BF16, "hb2") for _ in range(2)]
                hb2b = [[Buf() for _ in range(8)] for _ in range(2)]
                sg = [sb(ph, [128, 512], F32, "sg") for _ in range(2)]
                sgb = [Buf() for _ in range(2)]
                tp = [sb(ph, [128, 512], F32, "tp") for _ in range(2)]
                tpb = [Buf() for _ in range(2)]
                pG = [ps(ph, [128, 512], F32, "pG") for _ in range(2)]
                pP = [ps(ph, [128, 512], F32, "pP") for _ in range(2)]
                pstat = ps(ph, [128, 512], F32, "pstat")
                pGb = [PBuf() for _ in range(2)]
                pPb = [PBuf() for _ in range(2)]
                pstat_b = PBuf()
                wb2_ = Buf()
                wb3_ = Buf()
                QP.start(Wpg[:], wpg_d.rearrange("(kc p) n -> p kc n", p=128), writes=[wb_])
                QP.start(Wpp[:], wpp_d.rearrange("(kc p) n -> p kc n", p=128), writes=[wb2_])
                QP.start(pe_b[:], pT_d.rearrange("(kc p) n -> p kc n", p=128), writes=[wb3_])
                yTv = yT.rearrange("(c p) t -> p c t", p=128)
                kk = 0
                def ple_norm(ti):
                    a_, b_ = FFN2_TILES[ti]
                    hs_ = hb2[ti % 2]
                    norm_tile(a_, b_, 3, lambda c: hs_[:, c, :b_ - a_], hb2b[ti % 2], pstat, pstat_b)
                ple_norm(0)
                for ti, (a, b) in enumerate(FFN2_TILES):
                    W = b - a
                    hs = hb2[ti % 2]
                    hsb = hb2b[ti % 2]
                    if ti + 1 < len(FFN2_TILES):
                        ple_norm(ti + 1)
                    for j in range(8):
                        k = kk % 2
                        kk += 1
                        pe_group([(pG[k][:, :W], Wpg[:, kc, j * 128:(j + 1) * 128], hs[:, kc, :W]) for kc in range(8)],
                                 reads=[wb_] + hsb, writes=[pGb[k]])
                        pe_group([(pP[k][:, :W], Wpp[:, m, j * 128:(j + 1) * 128], pe_b[:, m, a - HALO:b - HALO])
                                  for m in range(2)], reads=[wb2_, wb3_], writes=[pPb[k]])
                        emit(ACT, lambda: nc.scalar.activation(out=sg[k][:, :W], in_=pG[k][:, :W], func=AF.Sigmoid),
                             reads=[pGb[k]], writes=[sgb[k]])
                        emit(DVE, lambda: nc.vector.tensor_tensor(out=tp[k][:, :W], in0=sg[k][:, :W], in1=pP[k][:, :W],
                                                                  op=ALU.mult),
                             reads=[sgb[k], pPb[k]], writes=[tpb[k]])
                        emit(DVE, lambda: nc.vector.tensor_tensor(out=xres[:, j, a:b], in0=tp[k][:, :W], in1=xres[:, j, a:b],
                                                                  op=ALU.add),
                             reads=[tpb[k]] + xbufs(j, a, b), writes=xbufs(j, a, b))
                    out_toks.append(QS.start(yTv[:, :, a - HALO:b - HALO], xres[:, :, a:b], reads=xbufs_all(a, b)))
                for t in out_toks:
                    SP.wait(t)
                barrier()

        def dbg_finish():
            yTv = yT.rearrange("(c p) t -> p c t", p=128)
            t = QS.start(yTv, xres[:, :, HALO:NT], reads=xbufs_all(0, NT))
            SP.wait(t)

        if stop == "load":
            dbg_finish()
            return nc
        ffn_phase(w1g, w1u, w1d, 0, FFN1_TILES)
        if stop == "ffn1":
            dbg_finish()
            return nc
        toks = mixer_phase()
        if stop == "mixer":
            dbg_finish()
            return nc
        ffn_phase(w2g, w2u, w2d, 2, FFN2_TILES)
        if stop == "ffn2":
            dbg_finish()
            return nc
        ple_phase(toks)
    return nc


def _rel_bucket_np(d):
    max_exact = 16
    df = np.maximum(d, 1).astype(np.float32)
    val = (np.log(df / np.float32(max_exact)) / np.float32(math.log(128 / 16)) * np.float32(32 - max_exact))
    large = max_exact + val.astype(np.int32)
    large = np.minimum(large, 31)
    return np.where(d < max_exact, d, large)


def _consts():
    E2 = np.zeros((33, 383), np.float32)
    for m in range(383):
        d = 255 - m
        if 0 <= d < 128:
            E2[int(_rel_bucket_np(np.array([d]))[0]), m] = 1.0
        else:
            E2[32, m] = -1e30
    onesD = np.full((128, 128), 1.0 / D, np.float32)
    bd64 = np.zeros((128, 128), np.float32)
    bd64[:64, :64] = 1.0 / 64
    bd64[64:, 64:] = 1.0 / 64
    ident = np.eye(128, dtype=np.float32)
    return E2, onesD, bd64, ident


_NC_CACHE = {}


def kernel(x_prompt, x_sample, p_prompt, p_sample, cache_k, cache_v, state_conv, rel_bias,
           g_ffn1, w1_gate, w1_up, w1_down, g_mix, w_in, q_norm, k_norm, sinks, w_conv, w_out,
           g_ffn2, w2_gate, w2_up, w2_down, g_ple, w_ple_gate, w_ple_proj):
    f = lambda a: np.ascontiguousarray(np.asarray(a, dtype=np.float32))
    x_prompt, x_sample, p_prompt, p_sample = f(x_prompt), f(x_sample), f(p_prompt), f(p_sample)
    cache_k, cache_v, state_conv, rel_bias = f(cache_k), f(cache_v), f(state_conv), f(rel_bias)
    E2, onesD, bd64, ident = _consts()
    qperm = np.concatenate([np.r_[i * 64:(i + 1) * 64, (4 + i) * 64:(5 + i) * 64] for i in range(4)])
    win = f(w_in)[0]
    win_p = f(np.concatenate([win[:, qperm], win[:, 512:]], axis=1))
    wout = f(w_out)[0]
    wout_p = f(np.concatenate([wout[qperm, :], wout[512:, :]], axis=0))
    gv = f(np.stack([f(g_ffn1)[0], f(g_mix)[0], f(g_ffn2)[0], f(g_ple)[0]]).reshape(4, 8, 128).transpose(2, 0, 1))
    qkg = f(np.stack([np.tile(f(q_norm)[0], 2), np.tile(f(k_norm)[0], 2)], axis=1))
    sk = f(sinks)[0]
    sinkb = f(np.broadcast_to(sk[HPERM][None, :], (128, 8)))
    sinks_s = np.zeros((16, 2), np.float32)
    for g in range(4):
        for kv in range(2):
            sinks_s[g * 4:(g + 1) * 4, kv] = sk[kv * 4 + g]
    tabp = f(rel_bias[:, HPERM])
    wconv = f(f(w_conv)[0].reshape(3, 4, 128).transpose(2, 1, 0))
    shared = {
        "tabp": tabp, "E2": E2, "onesD": onesD, "bd64": bd64, "ident": ident, "gv": gv, "qkg": qkg,
        "sinkb": sinkb, "sinks_s": sinks_s, "wconv": wconv,
        "w1g": f(w1_gate)[0], "w1u": f(w1_up)[0], "w1d": f(w1_down)[0], "win": win_p, "wout": wout_p,
        "w2g": f(w2_gate)[0], "w2u": f(w2_up)[0], "w2d": f(w2_down)[0], "wpg": f(w_ple_gate)[0],
        "wpp": f(w_ple_proj)[0],
    }
    in_maps = []
    for c in range(NCORES):
        b, j = divmod(c, 4)
        t0 = j * NPR
        halo = x_prompt[b, t0 - HALO:t0] if j > 0 else np.zeros((HALO, D), np.float32)
        sq = slice(c * NSEQ, (c + 1) * NSEQ)
        xs = x_sample[sq].reshape(NSM, D)
        xT = f(np.concatenate([halo, x_prompt[b, t0:t0 + NPR], xs], axis=0).T)
        pT = f(np.concatenate([p_prompt[0, b, t0:t0 + NPR], p_sample[0, sq].reshape(NSM, DPLE)], axis=0).T)
        ck = cache_k[0, sq].reshape(NSEQ, 128, 128)
        cvv = cache_v[0, sq].reshape(NSEQ, 128, 128)
        sc = state_conv[0, sq]
        scT = f(sc.transpose(2, 0, 1).reshape(4, 128, NSEQ, 2).transpose(1, 0, 2, 3))
        m = dict(shared)
        m.update({
            "xT": xT, "pT": pT, "ckT": f(ck.transpose(2, 0, 1)), "ck_nat": f(ck),
            "cvt": f(cvv.transpose(1, 0, 2)), "cv_nat": f(cvv), "scT": scT,
            "hflag": np.full((128, 1), -1e30 if j == 0 else 0.0, np.float32),
        })
        in_maps.append(m)

    if "nc" not in _NC_CACHE:
        _NC_CACHE["nc"] = build_nc()
    nc = _NC_CACHE["nc"]
    res = run_bass_kernel_spmd(nc, in_maps, core_ids=list(range(NCORES)))
    R = res.results

    B, T = x_prompt.shape[0], x_prompt.shape[1]
    y_prompt = np.zeros((B, T, D), np.float32)
    y_sample = np.zeros((x_sample.shape[0], 4, D), np.float32)
    kwp = np.zeros((1, B, 128, 2, 64), np.float32)
    vwp = np.zeros((1, B, 128, 2, 64), np.float32)
    cvp = np.zeros((1, B, 2, 512), np.float32)
    kws = np.zeros((1, x_sample.shape[0], 128, 2, 64), np.float32)
    vws = np.zeros((1, x_sample.shape[0], 128, 2, 64), np.float32)
    cvs = np.zeros((1, x_sample.shape[0], 2, 512), np.float32)
    for c in range(NCORES):
        b, j = divmod(c, 4)
        t0 = j * NPR
        r = R[c]
        sq = slice(c * NSEQ, (c + 1) * NSEQ)
        y = np.asarray(r["yT"])
        y_prompt[b, t0:t0 + NPR] = y[:, :NPR].T
        y_sample[sq] = y[:, NPR:].T.reshape(NSEQ, 4, D)
        if j == 3:
            kwp[0, b] = np.asarray(r["kTl"]).T.reshape(128, 2, 64)
            vwp[0, b] = np.asarray(r["vl"]).reshape(128, 2, 64)
            cvp[0, b] = np.asarray(r["cl"]).transpose(2, 1, 0).reshape(2, 512)
        kws[0, sq, 0:124] = np.asarray(r["kws_old"]).reshape(NSEQ, 124, 2, 64)
        kws[0, sq, 124:128] = np.asarray(r["ksn"]).T.reshape(NSEQ, 4, 2, 64)
        vws[0, sq, 0:124] = np.asarray(r["vws_old"]).reshape(NSEQ, 124, 2, 64)
        vws[0, sq, 124:128] = np.asarray(r["vsn"]).transpose(1, 0, 2).reshape(NSEQ, 4, 2, 64)
        cvs[0, sq] = np.asarray(r["csn"]).transpose(2, 3, 1, 0).reshape(NSEQ, 2, 512)
    return (y_prompt, y_sample, kwp, vwp, cvp, kws, vws, cvs)
```

```python
import math
from contextlib import ExitStack

import numpy as np
import concourse.bass as bass
import concourse.mybir as mybir
from concourse.bass_utils import run_bass_kernel_spmd

F32 = mybir.dt.float32
BF16 = mybir.dt.bfloat16
AF = mybir.ActivationFunctionType
ALU = mybir.AluOpType
AX = mybir.AxisListType

NCORES = 8
D = 1024
DFF = 2816
NFF = 22
DPLE = 256
INC = 2304
HALO = 128
NPR = 2048
NSM = 64
NT = HALO + NPR + NSM
NOUT = NPR + NSM
CS = HALO + NPR
MT = 256
EPS = 1e-6
NSEQ = 16
HPERM = [0, 1, 4, 5, 2, 3, 6, 7]

FFN_GROUPS = [(0, 5), (5, 10), (10, 14), (14, 18), (18, 22)]
FFN1_TILES = [(0, 128), (128, 640), (640, 1152), (1152, 1664), (1664, 2176), (2176, 2240)]
FFN2_TILES = FFN1_TILES[1:]


class Eng:
    def __init__(self, eng, sem, is_pe=False):
        self.eng = eng
        self.sem = sem
        self.cnt = 0
        self.waited = {}
        self.is_pe = is_pe

    def wait(self, tok):
        if tok is None:
            return
        s, v = tok
        if self.is_pe and s is self.sem:
            return
        if self.waited.get(s.num, 0) >= v:
            return
        self.eng.wait_ge(s, v)
        self.waited[s.num] = v

    def mark(self, inst):
        self.cnt += 1
        inst.then_inc(self.sem, 1)
        return (self.sem, self.cnt)


class _Stop(Exception):
    pass


class Buf:
    __slots__ = ("w", "r", "excl")

    def __init__(self, excl=False):
        self.w = None
        self.r = {}
        self.excl = excl


def PBuf():
    return Buf(excl=True)


def _pre(E, reads, writes):
    for b in reads:
        E.wait(b.w)
        if b.excl:
            for t in list(b.r.values()):
                if t[0] is not E.sem:
                    E.wait(t)
    for b in writes:
        E.wait(b.w)
        for t in list(b.r.values()):
            E.wait(t)


def _commit(tok, reads, writes):
    for b in reads:
        cur = b.r.get(tok[0].num)
        if cur is None or cur[1] < tok[1]:
            b.r[tok[0].num] = tok
    for b in writes:
        b.w = tok
        b.r = {}


def emit(E, fn, reads=(), writes=()):
    _pre(E, reads, writes)
    tok = E.mark(fn())
    _commit(tok, reads, writes)
    return tok


class DmaQ:
    def __init__(self, E, sems):
        self.E = E
        self.slots = [[s, 0] for s in sems]
        self.i = 0

    def start(self, out, in_, reads=(), writes=()):
        E = self.E
        _pre(E, reads, writes)
        slot = self.slots[self.i % len(self.slots)]
        self.i += 1
        if slot[1]:
            E.wait((slot[0], slot[1]))
        inst = E.eng.dma_start(out=out, in_=in_)
        slot[1] += 16
        inst.then_inc(slot[0], 16)
        tok = (slot[0], slot[1])
        _commit(tok, reads, writes)
        return tok

    def outstanding(self):
        return [(s, v) for s, v in self.slots if v]


def build_nc(stop=None):
    nc = bass.Bass("TRN2", target_bir_lowering=False)

    def din(name, shape):
        return nc.dram_tensor(name, list(shape), F32, kind="ExternalInput").ap()

    def dout(name, shape):
        return nc.dram_tensor(name, list(shape), F32, kind="ExternalOutput").ap()

    xT = din("xT", [D, NT])
    pT_d = din("pT", [DPLE, NOUT])
    ckT_d = din("ckT", [128, NSEQ, 128])
    ck_nat = din("ck_nat", [NSEQ, 128, 128])
    cvt_d = din("cvt", [128, NSEQ, 128])
    cv_nat = din("cv_nat", [NSEQ, 128, 128])
    scT_d = din("scT", [128, 4, NSEQ, 2])
    tabp_d = din("tabp", [32, 8])
    E2_d = din("E2", [33, 383])
    onesD_d = din("onesD", [128, 128])
    bd64_d = din("bd64", [128, 128])
    ident_d = din("ident", [128, 128])
    gv_d = din("gv", [128, 4, 8])
    qkg_d = din("qkg", [128, 2])
    sinkb_d = din("sinkb", [128, 8])
    sinks_d = din("sinks_s", [16, 2])
    wconv_d = din("wconv", [128, 4, 3])
    hflag_d = din("hflag", [128, 1])
    w1g = din("w1g", [D, DFF])
    w1u = din("w1u", [D, DFF])
    w1d = din("w1d", [DFF, D])
    win_d = din("win", [D, INC])
    wout_d = din("wout", [D, D])
    w2g = din("w2g", [D, DFF])
    w2u = din("w2u", [D, DFF])
    w2d = din("w2d", [DFF, D])
    wpg_d = din("wpg", [D, D])
    wpp_d = din("wpp", [DPLE, D])

    yT = dout("yT", [D, NOUT])
    kTl_o = dout("kTl", [128, 128])
    vl_o = dout("vl", [128, 128])
    cl_o = dout("cl", [128, 4, 2])
    kws_o = dout("kws_old", [NSEQ, 124, 128])
    vws_o = dout("vws_old", [NSEQ, 124, 128])
    ksn_o = dout("ksn", [128, NSM])
    vsn_o = dout("vsn", [4, NSEQ, 128])
    csn_o = dout("csn", [128, 4, NSEQ, 2])

    Ubc_t = nc.dram_tensor("Ubc", [128, 8, 383], F32, kind="Internal")
    Ubc = Ubc_t.ap()

    uid = [0]

    with ExitStack() as top:
        def sem(name):
            return top.enter_context(nc.semaphore(name))

        PE = Eng(nc.tensor, sem("s_pe"), is_pe=True)
        ACT = Eng(nc.scalar, sem("s_act"))
        DVE = Eng(nc.vector, sem("s_dve"))
        POOL = Eng(nc.gpsimd, sem("s_pool"))
        SP = Eng(nc.sync, sem("s_sp"))
        import os
        NSQ = int(os.environ.get("KNSQ", "10"))
        KSKIP = os.environ.get("KSKIP", "").split(",")
        CONV_ON_POOL = os.environ.get("KCONVPOOL", "0") == "1"
        NORM_ADD_POOL = os.environ.get("KNORMPOOL", "1") == "1"
        NWARM = int(os.environ.get("KNWARM", "0"))
        KNPOS = int(os.environ.get("KNPOS", "10"))
        SLOT = [int(x) for x in os.environ.get("KSLOT", "2,2,1,2").split(",")]
        QS = DmaQ(SP, [sem("qs%d" % i) for i in range(NSQ)])
        QP = DmaQ(POOL, [sem("qp%d" % i) for i in range(NSQ)])
        ENGS = (PE, ACT, DVE, POOL, SP)

        def sb(stack, shape, dt, name="t"):
            uid[0] += 1
            return stack.enter_context(nc.sbuf_tensor("%s_%d" % (name, uid[0]), list(shape), dt))

        def ps(stack, shape, dt, name="p"):
            uid[0] += 1
            return stack.enter_context(nc.psum_tensor("%s_%d" % (name, uid[0]), list(shape), dt))

        QA_REF = []

        def barrier(queues=None):
            toks = [(E.sem, E.cnt) for E in (PE, ACT, DVE, POOL) if E.cnt > 0]
            for q_ in (queues if queues is not None else (QS, QP)):
                toks += q_.outstanding()
            if queues is None and QA_REF:
                toks += QA_REF[0].outstanding()
            for E in ENGS:
                for t in toks:
                    E.wait(t)

        def pe_group(mms, reads=(), writes=()):
            _pre(PE, reads, writes)
            n = len(mms)
            inst = None
            for i, (o, l, r) in enumerate(mms):
                inst = nc.tensor.matmul(o, l, r, start=(i == 0), stop=(i == n - 1))
            tok = PE.mark(inst)
            _commit(tok, reads, writes)
            return tok

        def pe_transposes(trs, ident_ap_fn, reads=(), writes=()):
            _pre(PE, reads, writes)
            inst = None
            for (o, i_) in trs:
                inst = nc.tensor.transpose(o, i_, ident_ap_fn(i_))
            tok = PE.mark(inst)
            _commit(tok, reads, writes)
            return tok

        xres = sb(top, [128, 8, NT], F32, "xres")
        xb = [[Buf() for _ in range(18)] for _ in range(8)]

        def xbufs(c, a, b):
            return [xb[c][k] for k in range(a // 128, (b + 127) // 128)]

        def xbufs_all(a, b):
            r = []
            for c in range(8):
                r += xbufs(c, a, b)
            return r

        onesD = sb(top, [128, 128], BF16, "onesD")
        bd64 = sb(top, [128, 128], BF16, "bd64")
        ident = sb(top, [128, 128], BF16, "ident")
        gv = sb(top, [128, 4, 8], F32, "gv")
        qkg = sb(top, [128, 2], F32, "qkg")
        sinkb = sb(top, [128, 8], F32, "sinkb")
        nsinkb = sb(top, [128, 8], F32, "nsinkb")
        wconv = sb(top, [128, 4, 3], F32, "wconv")
        hflag = sb(top, [128, 1], F32, "hflag")
        epst = sb(top, [128, 1], F32, "epst")
        NS = {}

        def alloc_norm(stack, Wn):
            NS["sq"] = [sb(stack, [128, Wn], BF16, "sq") for _ in range(3)]
            NS["sqb"] = [Buf() for _ in range(3)]
            NS["rt"] = [sb(stack, [128, Wn], F32, "rt") for _ in range(2)]
            NS["rtb"] = [Buf() for _ in range(2)]
            NS["rstd"] = [sb(stack, [128, Wn], F32, "rstd") for _ in range(2)]
            NS["rstdb"] = [Buf() for _ in range(2)]
        CB = []

        def newcb():
            b_ = Buf()
            CB.append(b_)
            return b_
        cnt = {"sq": 0, "rt": 0}

        xTv = xT.rearrange("(c p) t -> p c t", p=128)
        QA = DmaQ(ACT, [sem("qa%d" % i) for i in range(8)])
        QA_REF.append(QA)
        for (a_, b_) in ((0, 1152), (1152, NT)):
            for c in range(8):
                q_ = QS if c % 2 == 0 else QA
                q_.start(xres[:, c, a_:b_], xTv[:, c, a_:b_], writes=xbufs(c, a_, b_))
        for dst, src in ((gv, gv_d), (qkg, qkg_d), (sinkb, sinkb_d), (wconv, wconv_d), (hflag, hflag_d)):
            QS.start(dst[:], src, writes=[newcb()])
        for dst, src in ((onesD, onesD_d), (bd64, bd64_d), (ident, ident_d)):
            QP.start(dst[:], src, writes=[newcb()])
        emit(DVE, lambda: nc.vector.memset(epst[:], EPS), writes=[newcb()])
        qkg8 = sb(top, [128, 1], F32, "qkg8")
        emit(DVE, lambda: nc.vector.tensor_scalar(out=qkg8[:], in0=qkg[:, 0:1], scalar1=0.125, scalar2=None,
                                                  op0=ALU.mult), reads=list(CB), writes=[newcb()])
        emit(DVE, lambda: nc.vector.tensor_scalar(out=nsinkb[:], in0=sinkb[:], scalar1=-1.0, scalar2=None,
                                                  op0=ALU.mult), reads=list(CB), writes=[newcb()])

        def norm_tile(a, b, gsel, out_fn, out_bufs, pstat, pstat_b):
            sq, sqb, rt, rtb, rstd, rstdb = NS["sq"], NS["sqb"], NS["rt"], NS["rtb"], NS["rstd"], NS["rstdb"]
            W = b - a
            for c in range(8):
                i = cnt["sq"] % 3
                cnt["sq"] += 1
                emit(ACT, lambda: nc.scalar.activation(out=sq[i][:, :W], in_=xres[:, c, a:b], func=AF.Square),
                     reads=xbufs(c, a, b), writes=[sqb[i]])
                _pre(PE, [sqb[i]] + CB, [pstat_b] if c == 0 else [])
                inst = nc.tensor.matmul(pstat[:, :W], onesD[:], sq[i][:, :W], start=(c == 0), stop=(c == 7))
                tok = PE.mark(inst)
                _commit(tok, [sqb[i]] + CB, [pstat_b] if c == 7 else [])
            j = cnt["rt"] % 2
            cnt["rt"] += 1
            emit(ACT, lambda: nc.scalar.activation(out=rt[j][:, :W], in_=pstat[:, :W], func=AF.Ln,
                                                   bias=epst[:, 0:1], scale=1.0),
                 reads=[pstat_b] + CB, writes=[rtb[j]])
            emit(ACT, lambda: nc.scalar.activation(out=rstd[j][:, :W], in_=rt[j][:, :W], func=AF.Exp, scale=-0.5),
                 reads=[rtb[j]], writes=[rstdb[j]])
            for c in range(8):
                emit(DVE, lambda: nc.vector.scalar_tensor_tensor(
                    out=out_fn(c), in0=xres[:, c, a:b], scalar=gv[:, gsel, c:c + 1], in1=rstd[j][:, :W],
                    op0=ALU.mult, op1=ALU.mult),
                    reads=xbufs(c, a, b) + [rstdb[j]] + CB, writes=[out_bufs[c]])

        def ffn_phase(wg_d, wu_d, wd_d, gsel, tiles):
            with ExitStack() as ph:
                alloc_norm(ph, 512)
                hb = sb(ph, [128, 8, NT], BF16, "hb")
                hbb = [[Buf() for _ in range(8)] for _ in tiles]
                Wg = [sb(ph, [128, 8, 640], BF16, "Wg") for _ in range(2)]
                Wu = [sb(ph, [128, 8, 640], BF16, "Wu") for _ in range(2)]
                Wd = [sb(ph, [128, 5, 1024], BF16, "Wd") for _ in range(2)]
                Wgb = [Buf() for _ in range(2)]
                Wub = [Buf() for _ in range(2)]
                Wdb = [Buf() for _ in range(2)]
                A = [sb(ph, [128, 5, 512], BF16, "A") for _ in range(2)]
                Ab = [[Buf() for _ in range(5)] for _ in range(2)]
                S = [sb(ph, [128, 512], F32, "S") for _ in range(2)]
                Sb = [Buf() for _ in range(2)]
                pG = [ps(ph, [128, 512], F32, "pG") for _ in range(2)]
                pU = [ps(ph, [128, 512], F32, "pU") for _ in range(2)]
                pY = [ps(ph, [128, 512], F32, "pY") for _ in range(2)]
                pstat = ps(ph, [128, 512], F32, "pstat")
                pGb = [PBuf() for _ in range(2)]
                pUb = [PBuf() for _ in range(2)]
                pYb = [PBuf() for _ in range(2)]
                pstat_b = PBuf()

                def load_group(gi):
                    c0, c1 = FFN_GROUPS[gi]
                    s = gi % 2
                    gw = (c1 - c0) * 128
                    QP.start(Wg[s][:, :, 0:gw], wg_d[:, c0 * 128:c1 * 128].rearrange("(kc p) n -> p kc n", p=128),
                             writes=[Wgb[s]])
                    QP.start(Wu[s][:, :, 0:gw], wu_d[:, c0 * 128:c1 * 128].rearrange("(kc p) n -> p kc n", p=128),
                             writes=[Wub[s]])
                    QP.start(Wd[s][:, 0:c1 - c0, :], wd_d[c0 * 128:c1 * 128, :].rearrange("(g p) n -> p g n", p=128),
                             writes=[Wdb[s]])

                load_group(0)
                load_group(1)
                def ffn_norm(ti):
                    a_, b_ = tiles[ti]
                    norm_tile(a_, b_, gsel, lambda c: hb[:, c, a_:b_], hbb[ti], pstat, pstat_b)
                ffn_norm(0)
                if len(tiles) > 1:
                    ffn_norm(1)

                items = [(gi, ti) for gi in range(len(FFN_GROUPS)) for ti in range(len(tiles))]
                kc_cnt = [0]
                y_cnt = [0]

                def stage1(idx):
                    gi, ti = items[idx]
                    if gi == 0 and ti + 2 < len(tiles):
                        ffn_norm(ti + 2)
                    a, b = tiles[ti]
                    W = b - a
                    c0, c1 = FFN_GROUPS[gi]
                    s = gi % 2
                    ai = idx % 2
                    for cl in range(c1 - c0):
                        k = kc_cnt[0] % 2
                        kc_cnt[0] += 1
                        pe_group([(pG[k][:, :W], Wg[s][:, kc, cl * 128:(cl + 1) * 128], hb[:, kc, a:b]) for kc in range(8)],
                                 reads=[Wgb[s]] + hbb[ti], writes=[pGb[k]])
                        pe_group([(pU[k][:, :W], Wu[s][:, kc, cl * 128:(cl + 1) * 128], hb[:, kc, a:b]) for kc in range(8)],
                                 reads=[Wub[s]] + hbb[ti], writes=[pUb[k]])
                        emit(ACT, lambda: nc.scalar.activation(out=S[k][:, :W], in_=pG[k][:, :W], func=AF.Silu),
                             reads=[pGb[k]], writes=[Sb[k]])
                        emit(DVE, lambda: nc.vector.tensor_tensor(out=A[ai][:, cl, :W], in0=S[k][:, :W], in1=pU[k][:, :W],
                                                                  op=ALU.mult),
                             reads=[Sb[k], pUb[k]], writes=[Ab[ai][cl]])

                def stage2(idx):
                    gi, ti = items[idx]
                    a, b = tiles[ti]
                    W = b - a
                    c0, c1 = FFN_GROUPS[gi]
                    s = gi % 2
                    ai = idx % 2
                    n = c1 - c0
                    for j in range(8):
                        k = y_cnt[0] % 2
                        y_cnt[0] += 1
                        pe_group([(pY[k][:, :W], Wd[s][:, cl, j * 128:(j + 1) * 128], A[ai][:, cl, :W]) for cl in range(n)],
                                 reads=[Wdb[s]] + Ab[ai][:n], writes=[pYb[k]])
                        emit(DVE, lambda: nc.vector.scalar_tensor_tensor(
                            out=xres[:, j, a:b], in0=pY[k][:, :W], scalar=0.5, in1=xres[:, j, a:b],
                            op0=ALU.mult, op1=ALU.add),
                            reads=[pYb[k]] + xbufs(j, a, b), writes=xbufs(j, a, b))
                    if ti == len(tiles) - 1 and gi + 2 < len(FFN_GROUPS):
                        load_group(gi + 2)

                stage1(0)
                for idx in range(len(items)):
                    if idx + 1 < len(items):
                        stage1(idx + 1)
                    stage2(idx)
                barrier()

        def mixer_phase():
            out_toks = []
            with ExitStack() as ph:
                alloc_norm(ph, MT)
                Win = sb(ph, [128, 8, INC], BF16, "Win")
                Wout = sb(ph, [128, 8, D], BF16, "Wout")
                WinSeg = [(0, 512), (512, 768), (768, 1280), (1280, 1792), (1792, 2304)]
                Winb = [Buf() for _ in WinSeg]
                Woutb = Buf()
                Bias_s = sb(ph, [16, 2, 132], F32, "Bias_s")
                sinks_s = sb(ph, [16, 2], F32, "sinks_s")
                sconst = Buf()
                sconst2 = Buf()
                kT_all = sb(ph, [128, NT], BF16, "kT_all")
                kTb = [Buf() for _ in range(18)]
                hbm = [sb(ph, [128, 8, MT], BF16, "hbm") for _ in range(2)]
                hbmb = [[Buf() for _ in range(8)] for _ in range(2)]
                zq = sb(ph, [128, 4, MT], F32, "zq")
                zqb = [Buf() for _ in range(4)]
                zk = sb(ph, [128, MT], F32, "zk")
                zkb = Buf()
                cv = sb(ph, [128, 4, MT], F32, "cv")
                cvb = [Buf() for _ in range(4)]
                qnb2 = [sb(ph, [128, 4, MT], BF16, "qnb") for _ in range(2)]
                qnbb2 = [[Buf() for _ in range(4)] for _ in range(2)]
                knf = sb(ph, [128, MT], F32, "knf")
                knfb = Buf()
                mixb2 = [sb(ph, [128, 8, MT], BF16, "mixb") for _ in range(2)]
                mixbb2 = [[Buf() for _ in range(8)] for _ in range(2)]

                pA = ps(ph, [128, 2048], F32, "pA")
                bA = [PBuf() for _ in range(4)]
                pstat = ps(ph, [128, 512], F32, "pstat")
                pstat_b = PBuf()
                pT = ps(ph, [128, 1024], F32, "pT")
                pTb = PBuf()
                pO = ps(ph, [128, 512], F32, "pO")
                pOb = PBuf()
                pZ = [pA[:, 0:512], pA[:, 512:1024]]
                pZb = [bA[0], bA[1]]
                pS = pA[:, 1024:2048]
                pSb = [bA[2], bA[3]]

                winv = win_d.rearrange("(kc p) n -> p kc n", p=128)
                for si in (1, 3, 4, 0, 2):
                    lo, hi = WinSeg[si]
                    QP.start(Win[:, :, lo:hi], winv[:, :, lo:hi], writes=[Winb[si]])
                QP.start(Wout[:], wout_d.rearrange("(kc p) n -> p kc n", p=128), writes=[Woutb])
                QS.start(sinks_s[:], sinks_d, writes=[sconst2])
                out_toks.append(QS.start(kws_o, ck_nat[:, 4:128, :]))
                out_toks.append(QS.start(vws_o, cv_nat[:, 4:128, :]))

                state = {"ti": 0, "z": 0, "wprev": None}
                hsq = [sb(ph, [128, MT], BF16, "hsq") for _ in range(5)]
                hsqb = [Buf() for _ in range(5)]

                def presq(slot, src_ap, W, src_bufs):
                    emit(ACT, lambda: nc.scalar.activation(out=hsq[slot][:, :W], in_=src_ap, func=AF.Square),
                         reads=src_bufs, writes=[hsqb[slot]])

                def zmm(wcol, seg, hs, hsb, W):
                    k = state["z"] % 2
                    state["z"] += 1
                    pe_group([(pZ[k][:, :W], Win[:, kc, wcol:wcol + 128], hs[:, kc, :W]) for kc in range(8)],
                             reads=[Winb[seg]] + hsb, writes=[pZb[k]])
                    return k

                def head_norm(src_ap, W, gap, out_ap, out_bufs, src_bufs, slot=None):
                    sq, sqb, rt, rtb, rstd, rstdb = NS["sq"], NS["sqb"], NS["rt"], NS["rtb"], NS["rstd"], NS["rstdb"]
                    pe_group([(pstat[:, :W], bd64[:], hsq[slot][:, :W])], reads=[hsqb[slot]] + CB, writes=[pstat_b])
                    j = cnt["rt"] % 2
                    cnt["rt"] += 1
                    emit(ACT, lambda: nc.scalar.activation(out=rt[j][:, :W], in_=pstat[:, :W], func=AF.Ln,
                                                           bias=epst[:, 0:1], scale=1.0),
                         reads=[pstat_b] + CB, writes=[rtb[j]])
                    emit(ACT, lambda: nc.scalar.activation(out=rstd[j][:, :W], in_=rt[j][:, :W], func=AF.Exp, scale=-0.5),
                         reads=[rtb[j]], writes=[rstdb[j]])
                    emit(DVE, lambda: nc.vector.scalar_tensor_tensor(
                        out=out_ap, in0=src_ap, scalar=gap, in1=rstd[j][:, :W],
                        op0=ALU.mult, op1=ALU.mult),
                        reads=src_bufs + [rstdb[j]] + CB, writes=out_bufs)

                def norm_only(a, b, ti):
                    W = b - a
                    hs = hbm[ti % 2]
                    norm_tile(a, b, 1, lambda c: hs[:, c, :W], hbmb[ti % 2], pstat, pstat_b)

                def kchunk(a, b, ti):
                    W = b - a
                    hs = hbm[ti % 2]
                    hsb = hbmb[ti % 2]
                    k = zmm(512, 1, hs, hsb, W)
                    emit(ACT, lambda: nc.scalar.copy(out=zk[:, :W], in_=pZ[k][:, :W]), reads=[pZb[k]], writes=[zkb])
                    presq(4, zk[:, :W], W, [zkb])
                    return hs, hsb

                def front0(a, b):
                    ti = state["ti"]
                    state["ti"] += 1
                    norm_only(a, b, ti)
                    return kchunk(a, b, ti)

                def q_chunk1(hs, hsb, W, i):
                    k = zmm(i * 128, 0, hs, hsb, W)
                    emit(ACT, lambda: nc.scalar.copy(out=zq[:, i, :W], in_=pZ[k][:, :W]), reads=[pZb[k]],
                         writes=[zqb[i]])
                    presq(i, zq[:, i, :W], W, [zqb[i]])

                def q_chunks(hs, hsb, W):
                    for i in range(4):
                        q_chunk1(hs, hsb, W, i)

                def u_c1(hs, hsb, W, uview, pzview, ubw, c):
                    k = zmm(1280 + c * 128, 3, hs, hsb, W)
                    emit(ACT, lambda: nc.scalar.copy(out=uview(c), in_=pzview(k)), reads=[pZb[k]], writes=ubw[c])

                def u_h1(hs, hsb, W, uview, pzview, ubw, c):
                    k = zmm(1792 + c * 128, 4, hs, hsb, W)
                    emit(DVE, lambda: nc.vector.tensor_tensor(out=uview(c), in0=uview(c), in1=pzview(k), op=ALU.mult),
                         reads=[pZb[k]] + ubw[c], writes=ubw[c])

                def u_chunks(hs, hsb, W, uview, pzview, ubw):
                    for c in range(4):
                        u_c1(hs, hsb, W, uview, pzview, ubw, c)
                    for c in range(4):
                        u_h1(hs, hsb, W, uview, pzview, ubw, c)

                def k_norm(a, b, kind):
                    W = b - a
                    head_norm(zk[:, :W], W, qkg[:, 1:2], knf[:, :W], [knfb], [zkb], slot=4)
                    blks = [17] if kind == "sample" else list(range(a // 128, b // 128))
                    emit(ACT, lambda: nc.scalar.copy(out=kT_all[:, a:b], in_=knf[:, :W]), reads=[knfb],
                         writes=[kTb[x] for x in blks])

                def conv_only(W, cvo_fn, taps_fn, rb_fn, cs=range(4)):
                    for c in cs:
                        cvo = cvo_fn(c)
                        taps = taps_fn(c)
                        rb = rb_fn(c)
                        CE, ce = (POOL, nc.gpsimd) if CONV_ON_POOL else (DVE, nc.vector)
                        emit(CE, lambda: ce.tensor_scalar(out=cvo, in0=taps[2], scalar1=wconv[:, c, 2:3],
                                                          scalar2=None, op0=ALU.mult),
                             reads=rb + CB, writes=[cvb[c]])
                        for j in (1, 0):
                            emit(CE, lambda: ce.scalar_tensor_tensor(
                                out=cvo, in0=taps[j], scalar=wconv[:, c, j:j + 1], in1=cvo, op0=ALU.mult, op1=ALU.add),
                                reads=rb + CB + [cvb[c]], writes=[cvb[c]])

                def gate_only(hs, hsb, W, par, cs=range(4)):
                    for c in cs:
                        k = zmm(768 + c * 128, 2, hs, hsb, W)
                        emit(DVE, lambda: nc.vector.tensor_tensor(out=mixb2[par][:, 4 + c, :W], in0=pZ[k][:, :W],
                                                                  in1=cv[:, c, :W], op=ALU.mult),
                             reads=[pZb[k], cvb[c]], writes=[mixbb2[par][4 + c]])

                def out_proj(a, b, par, js):
                    W = b - a
                    for j in js:
                        k = state["z"] % 2
                        state["z"] += 1
                        pe_group([(pZ[k][:, :W], Wout[:, m, j * 128:(j + 1) * 128], mixb2[par][:, m, :W]) for m in range(8)],
                                 reads=[Woutb] + mixbb2[par], writes=[pZb[k]])
                        emit(DVE, lambda: nc.vector.tensor_tensor(out=xres[:, j, a:b], in0=pZ[k][:, :W], in1=xres[:, j, a:b],
                                                                  op=ALU.add),
                             reads=[pZb[k]] + xbufs(j, a, b), writes=xbufs(j, a, b))

                with ExitStack() as P:
                    Bhi = sb(P, [128, 8, 256], BF16, "Bhi")
                    Blo = sb(P, [128, 8, 256], BF16, "Blo")
                    Hm = sb(P, [128, 256], BF16, "Hm")
                    Bhib, Blob, Hmb = Buf(), Buf(), Buf()
                    Biasb = Buf()
                    Vt = sb(P, [128, 17, 192], BF16, "Vt")
                    Vtb = [Buf() for _ in range(17)]
                    emit(DVE, lambda: nc.vector.memset(Vt[:], 0.0), writes=Vtb)
                    with ExitStack() as su:
                        tab_sb = sb(su, [33, 8], F32, "tab_sb")
                        tabB = sb(su, [33, 8, 128], F32, "tabB")
                        E2_sb = sb(su, [33, 383], F32, "E2_sb")
                        Ubc_sb = sb(su, [128, 8, 383], F32, "Ubc_sb")
                        Bias = sb(su, [128, 8, 256], F32, "Bias")
                        tb = Buf()
                        eb = Buf()
                        ub = Buf()
                        emit(DVE, lambda: nc.vector.memset(tab_sb[:], 1.0), writes=[tb])
                        QS.start(tab_sb[0:32, :], tabp_d, writes=[tb])
                        QS.start(E2_sb[:], E2_d, writes=[eb])
                        emit(DVE, lambda: nc.vector.tensor_copy(out=tabB[:], in_=tab_sb[:].unsqueeze(2).to_broadcast([33, 8, 128])),
                             reads=[tb], writes=[tb])
                        for h in range(8):
                            k = h % 2
                            pe_group([(pZ[k][:, 0:383], tabB[:, h, :], E2_sb[:])], reads=[tb, eb], writes=[pZb[k]])
                            emit(ACT, lambda: nc.scalar.copy(out=Ubc_sb[:, h, :], in_=pZ[k][:, 0:383]), reads=[pZb[k]], writes=[ub])
                        dsc = Buf()
                        QS.start(Ubc, Ubc_sb[:], reads=[ub], writes=[dsc])
                        src = bass.AP(Ubc_t, 127, [[8 * 383 - 1, 128], [383, 8], [1, 256]])
                        QS.start(Bias[:], src, reads=[dsc], writes=[Biasb])
                        emit(DVE, lambda: nc.vector.tensor_copy(out=Bhi[:], in_=Bias[:]), reads=[Biasb], writes=[Bhib])
                        emit(DVE, lambda: nc.vector.tensor_tensor(out=Blo[:], in0=Bias[:], in1=Bhi[:], op=ALU.subtract),
                             reads=[Biasb, Bhib], writes=[Blob])
                        emit(DVE, lambda: nc.vector.memset(Hm[:], 0.0), writes=[Hmb])
                        emit(DVE, lambda: nc.vector.tensor_scalar(out=Hm[:, 0:128], in0=Hm[:, 0:128], scalar1=hflag[:, 0:1],
                                                                  scalar2=None, op0=ALU.add), reads=[Hmb] + CB, writes=[Hmb])
                        for kv in range(2):
                            for g in range(4):
                                ph_ = 4 * (g // 2) + 2 * kv + (g % 2)
                                src = bass.AP(Ubc_t, ph_ * 383 + 127, [[8 * 383 - 1, 4], [1, 132]])
                                QS.start(Bias_s[g * 4:(g + 1) * 4, kv, :], src, reads=[dsc], writes=[sconst])
                        barrier(queues=(QS,))
                    ubuf = sb(P, [128, 4, MT + 2], F32, "ubuf")
                    ubb = [Buf() for _ in range(4)]
                    vlast = sb(P, [128, 128], F32, "vlast")
                    vlastb = Buf()
                    P2 = [sb(P, [128, 4, 256], BF16, "Pexp") for _ in range(2)]
                    P2b = [[Buf() for _ in range(4)] for _ in range(2)]
                    D2 = [sb(P, [128, 4, 128], BF16, "Dn") for _ in range(2)]
                    D2b = [Buf() for _ in range(2)]
                    PTs2 = [sb(P, [128, 1024], BF16, "PTs") for _ in range(2)]
                    PTsb2 = [Buf() for _ in range(2)]
                    sm2 = [{n: sb(P, [128, 4], F32, n) for n in ("mx", "negm", "tmp4", "es4", "rs4", "den4", "rden4")}
                           for _ in range(2)]
                    smb2 = [{n: Buf() for n in sm2[0]} for _ in range(2)]

                    def attn_A(bi, o, hp, par, hb_):
                        Pe, Peb, sm, smb = P2[hb_], P2b[hb_], sm2[hb_], smb2[hb_]
                        qnb, qnbb = qnb2[par], qnbb2[par]
                        kc0 = 128 * (bi - 1)
                        for ci in range(2):
                            i = 2 * hp + ci
                            for kv in range(2):
                                sl = kv * 2 + ci
                                dst = pS[:, sl * 256:(sl + 1) * 256]
                                mms = [(dst, qnb[kv * 64:(kv + 1) * 64, i, o:o + 128], kT_all[kv * 64:(kv + 1) * 64, kc0:kc0 + 256]),
                                       (dst, ident[:], Bhi[:, 4 * hp + sl, :]),
                                       (dst, ident[:], Blo[:, 4 * hp + sl, :])]
                                rd = [qnbb[i], kTb[bi - 1], kTb[bi], Bhib, Blob] + CB
                                if bi == 1:
                                    mms.append((dst, ident[:], Hm[:]))
                                    rd.append(Hmb)
                                pe_group(mms, reads=rd, writes=[pSb[sl // 2]])
                        emit(DVE, lambda: nc.vector.reduce_max(out=sm["mx"][:], in_=pS.rearrange("p (a b) -> p a b", b=256),
                                                               axis=AX.X),
                             reads=[pSb[0], pSb[1]], writes=[smb["mx"]])
                        emit(DVE, lambda: nc.vector.scalar_tensor_tensor(
                            out=sm["negm"][:], in0=sm["mx"][:], scalar=-1.0, in1=nsinkb[:, 4 * hp:4 * hp + 4],
                            op0=ALU.mult, op1=ALU.min), reads=[smb["mx"]] + CB, writes=[smb["negm"]])
                        emit(DVE, lambda: nc.vector.tensor_tensor(
                            out=sm["tmp4"][:], in0=sinkb[:, 4 * hp:4 * hp + 4], in1=sm["negm"][:], op=ALU.add),
                            reads=[smb["negm"]] + CB, writes=[smb["tmp4"]])
                        for sl in range(4):
                            emit(ACT, lambda: nc.scalar.activation(
                                out=Pe[:, sl, :], in_=pS[:, sl * 256:(sl + 1) * 256], func=AF.Exp,
                                bias=sm["negm"][:, sl:sl + 1], scale=1.0, accum_out=sm["rs4"][:, sl:sl + 1]),
                                reads=[pSb[sl // 2], smb["negm"]], writes=[Peb[sl], smb["rs4"]])
                        emit(ACT, lambda: nc.scalar.activation(out=sm["es4"][:], in_=sm["tmp4"][:], func=AF.Exp),
                             reads=[smb["tmp4"]], writes=[smb["es4"]])

                    def attn_B(bi, o, hp, par, hb_):
                        Pe, Peb, sm, smb = P2[hb_], P2b[hb_], sm2[hb_], smb2[hb_]
                        Dn, Dnb, PTs, PTsb = D2[hb_], D2b[hb_], PTs2[hb_], PTsb2[hb_]
                        emit(DVE, lambda: nc.vector.tensor_tensor(out=sm["den4"][:], in0=sm["rs4"][:], in1=sm["es4"][:],
                                                                  op=ALU.add),
                             reads=[smb["rs4"], smb["es4"]], writes=[smb["den4"]])
                        emit(DVE, lambda: nc.vector.reciprocal(out=sm["rden4"][:], in_=sm["den4"][:]),
                             reads=[smb["den4"]], writes=[smb["rden4"]])
                        emit(DVE, lambda: nc.vector.tensor_tensor(
                            out=Dn[:], in0=ident[:].unsqueeze(1).to_broadcast([128, 4, 128]),
                            in1=sm["rden4"][:].unsqueeze(2).to_broadcast([128, 4, 128]), op=ALU.mult),
                            reads=[smb["rden4"]] + CB, writes=[Dnb])
                        rd = Peb + [Dnb]
                        _pre(PE, rd, [pTb])
                        inst = None
                        for sl in range(4):
                            for kh in range(2):
                                idx = sl * 2 + kh
                                inst = nc.tensor.matmul(pT[:, idx * 128:(idx + 1) * 128], Pe[:, sl, kh * 128:(kh + 1) * 128],
                                                        Dn[:, sl, :], start=True, stop=True)
                        tok_ = PE.mark(inst)
                        _commit(tok_, rd, [pTb])
                        emit(ACT, lambda: nc.scalar.copy(out=PTs[:], in_=pT[:]), reads=[pTb], writes=[PTsb])

                    def attn_B2(bi, o, hp, par, hb_):
                        PTs, PTsb = PTs2[hb_], PTsb2[hb_]
                        for ci in range(2):
                            i = 2 * hp + ci
                            mms = []
                            for kv in range(2):
                                for kh in range(2):
                                    idx = (kv * 2 + ci) * 2 + kh
                                    mms.append((pO[:, i * 128:(i + 1) * 128], Vt[:, bi - 1 + kh, kv * 64:kv * 64 + 128],
                                                PTs[:, idx * 128:(idx + 1) * 128]))
                            pe_group(mms, reads=[PTsb, Vtb[bi - 1], Vtb[bi]], writes=[pOb])

                    def attn_fin(o, par):
                        emit(ACT, lambda: nc.scalar.copy(out=mixb2[par][:, 0:4, o:o + 128],
                                                         in_=pO[:].rearrange("p (a b) -> p a b", b=128)),
                             reads=[pOb], writes=mixbb2[par][0:4])

                    def front_steps(kind, a, b, par, tidx=None, nxt=None):
                        W = b - a
                        box = {}

                        def s0():
                            box["hs"], box["hsb"] = kchunk(a, b, tidx)

                        def s_next():
                            if nxt is not None:
                                norm_only(nxt[1], nxt[2], tidx + 1)

                        def s1():
                            hs, hsb = box["hs"], box["hsb"]
                            for bo in range(W // 128):
                                bi = a // 128 + bo
                                kz = state["z"] % 2
                                state["z"] += 1
                                pe_group([(pZ[kz][:, 0:128], hs[:, kc, bo * 128:(bo + 1) * 128], Win[:, kc, 640:768])
                                          for kc in range(8)], reads=[Winb[1]] + hsb, writes=[pZb[kz]])
                                emit(ACT, lambda: nc.scalar.copy(out=Vt[:, bi, 0:64], in_=pZ[kz][:, 0:64]),
                                     reads=[pZb[kz]], writes=[Vtb[bi]])
                                emit(ACT, lambda: nc.scalar.copy(out=Vt[:, bi, 128:192], in_=pZ[kz][:, 64:128]),
                                     reads=[pZb[kz]], writes=[Vtb[bi]])
                                if bi == 16:
                                    emit(ACT, lambda: nc.scalar.copy(out=vlast[:], in_=pZ[kz][:, 0:128]),
                                         reads=[pZb[kz]], writes=[vlastb])
                                    out_toks.append(QS.start(vl_o, vlast[:], reads=[vlastb]))

                        uv = lambda c: ubuf[:, c, 2:2 + W]
                        pzv = lambda k: pZ[k][:, :W]
                        ubw_ = [[ubb[c]] for c in range(4)]

                        def s3a():
                            if state["wprev"] is not None:
                                wp = state["wprev"]
                                emit(DVE, lambda: nc.vector.tensor_copy(out=ubuf[:, :, 0:2], in_=ubuf[:, :, wp:wp + 2]),
                                     reads=ubb, writes=ubb)
                            else:
                                emit(DVE, lambda: nc.vector.memset(ubuf[:, :, 0:2], 0.0), writes=ubb)
                            state["wprev"] = W

                        def s4():
                            k_norm(a, b, kind)
                            if b == CS:
                                out_toks.append(QS.start(kTl_o, knf[:, W - 128:W], reads=[knfb]))
                                out_toks.append(QS.start(cl_o, ubuf[:, :, W:W + 2], reads=ubb))

                        def mk(f, *args):
                            return lambda: f(*args)
                        steps = [s0, s1]
                        if kind != "halo":
                            steps += [mk(lambda i: q_chunk1(box["hs"], box["hsb"], W, i), i) for i in range(4)]
                        steps.append(s3a)
                        steps += [mk(lambda c: u_c1(box["hs"], box["hsb"], W, uv, pzv, ubw_, c), c) for c in range(4)]
                        steps += [mk(lambda c: u_h1(box["hs"], box["hsb"], W, uv, pzv, ubw_, c), c) for c in range(4)]
                        steps.append(s4)
                        if kind == "halo":
                            return steps + [s_next]
                        steps += [mk(lambda i: head_norm(zq[:, i, :W], W, qkg8[:, 0:1], qnb2[par][:, i, :W], [qnbb2[par][i]], [zqb[i]], slot=i), i)
                                  for i in range(4)]
                        steps += [mk(lambda c: conv_only(W, lambda c_: cv[:, c_, :W],
                                                         lambda c_: [ubuf[:, c_, j:j + W] for j in range(3)],
                                                         lambda c_: [ubb[c_]], cs=[c]), c) for c in range(4)]
                        steps += [mk(lambda c: gate_only(box["hs"], box["hsb"], W, par, cs=[c]), c) for c in range(4)]
                        steps.insert(min(len(steps), KNPOS), s_next)
                        return steps

                    hpc = [0]

                    def back_steps(a, b, par):
                        W = b - a
                        passes = []
                        for bo in range(W // 128):
                            for hp in range(2):
                                passes.append((a // 128 + bo, bo * 128, hp, hpc[0] % 2))
                                hpc[0] += 1
                        steps = []
                        n = len(passes)

                        def mkA(p):
                            return lambda: attn_A(p[0], p[1], p[2], par, p[3])

                        def mkB1(idx):
                            p = passes[idx]
                            return lambda: attn_B(p[0], p[1], p[2], par, p[3])

                        def mkB2(idx):
                            p = passes[idx]

                            def f():
                                attn_B2(p[0], p[1], p[2], par, p[3])
                                if p[2] == 1:
                                    attn_fin(p[1], par)
                            return f
                        steps.append((mkA(passes[0]), SLOT[0]))
                        for idx in range(n):
                            if idx + 1 < n:
                                steps.append((mkA(passes[idx + 1]), SLOT[0]))
                            steps.append((mkB1(idx), SLOT[1]))
                            steps.append((mkB2(idx), SLOT[2]))
                        for j0 in range(0, 8, 2):
                            steps.append(((lambda j0_: (lambda: out_proj(a, b, par, range(j0_, j0_ + 2))))(j0), SLOT[3]))
                        return steps

                    pW = None

                    def warm():
                        for _ in range(NWARM):
                            nc.tensor.matmul(pW[:, :], ident[:], kT_all[:, 0:512], start=True, stop=True)

                    def interleave(bs, fs):
                        out = []
                        j = 0
                        for (st, k) in bs:
                            if NWARM:
                                out.append(warm)
                            out.append(st)
                            for _ in range(k):
                                if j < len(fs):
                                    out.append(fs[j]); j += 1
                        out += fs[j:]
                        return out

                    tiles = [("halo", 0, HALO)] + [("prompt", HALO + i * MT, HALO + (i + 1) * MT) for i in range(NPR // MT)]
                    def nx(t):
                        return tiles[t + 1] if t + 1 < len(tiles) else None
                    norm_only(tiles[0][1], tiles[0][2], 0)
                    for st_ in front_steps(*tiles[0], 0, tidx=0, nxt=nx(0)):
                        st_()
                    for st_ in front_steps(*tiles[1], 1, tidx=1, nxt=nx(1)):
                        st_()
                    for t in range(1, len(tiles)):
                        bs = back_steps(tiles[t][1], tiles[t][2], t % 2)
                        fs = front_steps(*tiles[t + 1], (t + 1) % 2, tidx=t + 1, nxt=nx(t + 1)) if t + 1 < len(tiles) else []
                        for st_ in interleave(bs, fs):
                            st_()
                    barrier()

                with ExitStack() as S_:
                    qnb, qnbb, mixb, mixbb = qnb2[0], qnbb2[0], mixb2[0], mixbb2[0]
                    kTs = sb(S_, [128, NSEQ, 128], BF16, "kTs")
                    Vc = sb(S_, [128, NSEQ, 192], BF16, "Vc")
                    Vn = sb(S_, [4, NSEQ, 192], BF16, "Vn")
                    vsnf = sb(S_, [4, 8, 128], F32, "vsnf")
                    vsnfb = Buf()
                    Vnb = Buf()
                    ubs = sb(S_, [128, 4, NSEQ, 6], F32, "ubs")
                    ubsb = Buf()
                    qs = sb(S_, [128, NSEQ, 16], BF16, "qs")
                    qsb = Buf()
                    NQ = 4
                    NSL = 2 * NQ
                    Ss = sb(S_, [16, NSL, 132], F32, "Ss")
                    Ssb_s = Buf()
                    Pns = sb(S_, [16, NSL, 132], BF16, "Pns")
                    Pnsb = Buf()
                    PTcs = sb(S_, [128, NSL * 16], BF16, "PTcs")
                    PTns = sb(S_, [4, NSL * 16], BF16, "PTns")
                    PTcsb = Buf()
                    ss = {n: sb(S_, [16, NSL], F32, "s" + n) for n in ("mx", "m", "rs", "tmp", "es", "den", "rden")}
                    ssb = {n: Buf() for n in ss}
                    kTsb = Buf()
                    Vcb = Buf()
                    emit(DVE, lambda: nc.vector.memset(Vc[:, :, 64:128], 0.0), writes=[Vcb])
                    emit(DVE, lambda: nc.vector.memset(Vn[:], 0.0), writes=[Vnb])
                    QP.start(kTs[:], ckT_d, writes=[kTsb])
                    QP.start(Vc[:, :, 0:64], cvt_d[:, :, 0:64], writes=[Vcb])
                    QP.start(Vc[:, :, 128:192], cvt_d[:, :, 64:128], writes=[Vcb])
                    st_c = sb(S_, [128, 128], F32, "st_c")
                    st_cb = Buf()
                    cs_c = sb(S_, [128, 128], F32, "cs_c")
                    cs_cb = Buf()
                    ubs_st = Buf()
                    QS.start(st_c[:], scT_d.rearrange("p c q r -> p (c q r)"), writes=[st_cb])
                    emit(DVE, lambda: nc.vector.tensor_copy(out=ubs[:, :, :, 0:2],
                                                            in_=st_c[:].rearrange("p (c q r) -> p c q r", c=4, r=2)),
                         reads=[st_cb], writes=[ubs_st])

                    a, b = CS, NT
                    W = NSM
                    state["ti"] = 0
                    hs, hsb = front0(a, b)
                    pV = pA[0:4, :]
                    for seq in range(NSEQ):
                        pe_group([(pV[:, seq * 128:(seq + 1) * 128], hs[:, kc, seq * 4:(seq + 1) * 4],
                                   Win[:, kc, 640:768]) for kc in range(8)],
                                 reads=[Winb[1]] + hsb, writes=[bA[seq // 4]])
                    pVv = pV.rearrange("p (q n) -> p q n", n=128)
                    emit(ACT, lambda: nc.scalar.copy(out=Vn[:, :, 0:64], in_=pVv[:, :, 0:64]), reads=bA, writes=[Vnb])
                    emit(ACT, lambda: nc.scalar.copy(out=Vn[:, :, 128:192], in_=pVv[:, :, 64:128]), reads=bA, writes=[Vnb])
                    for hh in range(2):
                        emit(ACT, lambda: nc.scalar.copy(out=vsnf[:], in_=pVv[:, hh * 8:(hh + 1) * 8, :]),
                             reads=bA, writes=[vsnfb])
                        out_toks.append(QS.start(vsn_o[:, hh * 8:(hh + 1) * 8, :], vsnf[:], reads=[vsnfb]))
                    q_chunks(hs, hsb, W)
                    u_chunks(hs, hsb, W, lambda c: ubs[:, c, :, 2:6],
                             lambda k: pZ[k][:, 0:NSM].rearrange("p (q s) -> p q s", s=4), [[ubsb]] * 4)
                    k_norm(a, b, "sample")
                    out_toks.append(QS.start(ksn_o, knf[:, :W], reads=[knfb]))
                    emit(DVE, lambda: nc.vector.tensor_copy(out=cs_c[:].rearrange("p (c q r) -> p c q r", c=4, r=2),
                                                            in_=ubs[:, :, :, 4:6]), reads=[ubsb], writes=[cs_cb])
                    out_toks.append(QS.start(csn_o.rearrange("p c q r -> p (c q r)"), cs_c[:], reads=[cs_cb]))
                    for i in range(4):
                        head_norm(zq[:, i, :W], W, qkg8[:, 0:1], qnb[:, i, :W], [qnbb[i]], [zqb[i]], slot=i)
                    conv_only(W, lambda c: cv[:, c, 0:NSM].rearrange("p (q s) -> p q s", s=4),
                              lambda c: [ubs[:, c, :, j:j + 4] for j in range(3)], lambda c: [ubsb, ubs_st])
                    gate_only(hs, hsb, W, 0)
                    emit(DVE, lambda: nc.vector.tensor_copy(
                        out=qs[:].rearrange("p q (g s) -> p q g s", s=4),
                        in_=qnb[:, :, 0:NSM].rearrange("p g (q s) -> p q g s", s=4)),
                        reads=qnbb, writes=[qsb])
                    pSc = pA[0:16, 0:NSL * 128]
                    nbank = (NSL * 128) // 512
                    for part in range(NSEQ // NQ):
                        for si in range(NQ):
                            seq = part * NQ + si
                            for kv in range(2):
                                slot = kv * NQ + si
                                pe_group([(pSc[:, slot * 128:(slot + 1) * 128], qs[kv * 64:(kv + 1) * 64, seq, :],
                                           kTs[kv * 64:(kv + 1) * 64, seq, :])],
                                         reads=[qsb, kTsb], writes=[bA[slot // 4]])
                                pnew, pnewb = ((pO, pOb), (pstat, pstat_b))[kv]
                                pe_group([(pnew[0:16, si * 4:(si + 1) * 4], qs[kv * 64:(kv + 1) * 64, seq, :],
                                           kT_all[kv * 64:(kv + 1) * 64, CS + seq * 4:CS + seq * 4 + 4])],
                                         reads=[qsb, kTb[17]], writes=[pnewb])
                        emit(DVE, lambda: nc.vector.tensor_scalar(
                            out=Ss[:, :, 0:128], in0=pSc.rearrange("p (q n) -> p q n", n=128), scalar1=1.0,
                            scalar2=None, op0=ALU.mult), reads=bA[0:nbank], writes=[Ssb_s])
                        for kv in range(2):
                            pnew, pnewb = ((pO, pOb), (pstat, pstat_b))[kv]
                            emit(DVE, lambda: nc.vector.tensor_scalar(
                                out=Ss[:, kv * NQ:(kv + 1) * NQ, 128:132],
                                in0=pnew[0:16, 0:NQ * 4].rearrange("p (q n) -> p q n", n=4),
                                scalar1=1.0, scalar2=None, op0=ALU.mult), reads=[pnewb], writes=[Ssb_s])
                        for kv in range(2):
                            sv = Ss[:, kv * NQ:(kv + 1) * NQ, :]
                            emit(DVE, lambda: nc.vector.tensor_tensor(
                                out=sv, in0=sv, in1=Bias_s[:, kv, :].unsqueeze(1).to_broadcast([16, NQ, 132]),
                                op=ALU.add), reads=[Ssb_s, sconst], writes=[Ssb_s])
                        emit(DVE, lambda: nc.vector.reduce_max(out=ss["mx"][:], in_=Ss[:], axis=AX.X),
                             reads=[Ssb_s], writes=[ssb["mx"]])
                        sbc = sinks_s[:].unsqueeze(2).to_broadcast([16, 2, NQ])
                        emit(DVE, lambda: nc.vector.tensor_tensor(
                            out=ss["m"][:].rearrange("p (k q) -> p k q", k=2),
                            in0=ss["mx"][:].rearrange("p (k q) -> p k q", k=2), in1=sbc, op=ALU.max),
                            reads=[ssb["mx"], sconst2], writes=[ssb["m"]])
                        emit(DVE, lambda: nc.vector.tensor_tensor(
                            out=Ss[:], in0=Ss[:], in1=ss["m"][:].unsqueeze(2).to_broadcast([16, NSL, 132]),
                            op=ALU.subtract), reads=[Ssb_s, ssb["m"]], writes=[Ssb_s])
                        emit(ACT, lambda: nc.scalar.activation(out=Ss[:], in_=Ss[:], func=AF.Exp),
                             reads=[Ssb_s], writes=[Ssb_s])
                        emit(DVE, lambda: nc.vector.reduce_sum(out=ss["rs"][:], in_=Ss[:], axis=AX.X),
                             reads=[Ssb_s], writes=[ssb["rs"]])
                        emit(DVE, lambda: nc.vector.tensor_tensor(
                            out=ss["tmp"][:].rearrange("p (k q) -> p k q", k=2), in0=sbc,
                            in1=ss["m"][:].rearrange("p (k q) -> p k q", k=2), op=ALU.subtract),
                            reads=[ssb["m"], sconst2], writes=[ssb["tmp"]])
                        emit(ACT, lambda: nc.scalar.activation(out=ss["es"][:], in_=ss["tmp"][:], func=AF.Exp),
                             reads=[ssb["tmp"]], writes=[ssb["es"]])
                        emit(DVE, lambda: nc.vector.tensor_tensor(out=ss["den"][:], in0=ss["rs"][:], in1=ss["es"][:],
                                                                  op=ALU.add),
                             reads=[ssb["rs"], ssb["es"]], writes=[ssb["den"]])
                        emit(DVE, lambda: nc.vector.reciprocal(out=ss["rden"][:], in_=ss["den"][:]),
                             reads=[ssb["den"]], writes=[ssb["rden"]])
                        emit(DVE, lambda: nc.vector.tensor_tensor(
                            out=Pns[:], in0=Ss[:], in1=ss["rden"][:].unsqueeze(2).to_broadcast([16, NSL, 132]),
                            op=ALU.mult), reads=[Ssb_s, ssb["rden"]], writes=[Pnsb])
                        trs = []
                        for slot in range(NSL):
                            trs.append((pT[:, slot * 16:(slot + 1) * 16], Pns[0:16, slot, 0:128]))
                            trs.append((pT[0:4, 512 + slot * 16:512 + (slot + 1) * 16], Pns[0:16, slot, 128:132]))
                        _pre(PE, [Pnsb] + CB, [pTb])
                        inst = None
                        for (o_, i_) in trs:
                            inst = nc.tensor.matmul(o_, i_, ident[0:16, 0:16], start=True, stop=True)
                        tok_ = PE.mark(inst)
                        _commit(tok_, [Pnsb] + CB, [pTb])
                        emit(ACT, lambda: nc.scalar.copy(out=PTcs[:], in_=pT[:, 0:NSL * 16]), reads=[pTb], writes=[PTcsb])
                        emit(ACT, lambda: nc.scalar.copy(out=PTns[:], in_=pT[0:4, 512:512 + NSL * 16]), reads=[pTb],
                             writes=[PTcsb])
                        for si in range(NQ):
                            seq = part * NQ + si
                            mms = []
                            for kv in range(2):
                                slot = kv * NQ + si
                                mms.append((pstat[:, si * 16:(si + 1) * 16], Vc[:, seq, kv * 64:kv * 64 + 128],
                                            PTcs[:, slot * 16:(slot + 1) * 16]))
                                mms.append((pstat[:, si * 16:(si + 1) * 16], Vn[0:4, seq, kv * 64:kv * 64 + 128],
                                            PTns[0:4, slot * 16:(slot + 1) * 16]))
                            pe_group(mms, reads=[PTcsb, Vcb, Vnb], writes=[pstat_b])
                        emit(DVE, lambda: nc.vector.tensor_copy(
                            out=mixb[:, 0:4, part * NQ * 4:(part + 1) * NQ * 4].rearrange("p g (q s) -> p q g s", s=4),
                            in_=pstat[:, 0:NQ * 16].rearrange("p (q g s) -> p q g s", g=4, s=4)),
                            reads=[pstat_b], writes=mixbb[0:4])
                    out_proj(a, b, 0, range(8))
                    barrier()
            return out_toks

        def ple_phase(out_toks):
            with ExitStack() as ph:
                alloc_norm(ph, 512)
                Wpg = sb(ph, [128, 8, D], BF16, "Wpg")
                Wpp = sb(ph, [128, 2, D], BF16, "Wpp")
                pe_b = sb(ph, [128, 2, NOUT], BF16, "pe_b")
                wb_ = Buf()
                hb2 = [sb(ph, [128, 8, 512], BF16, "hb2") for _ in range(2)]
                hb2b = [[Buf() for _ in range(8)] for _ in range(2)]
                sg = [sb(ph, [128, 512], F32, "sg") for _ in range(2)]
                sgb = [Buf() for _ in range(2)]
                tp = [sb(ph, [128, 512], F32, "tp") for _ in range(2)]
                tpb = [Buf() for _ in range(2)]
                pG = [ps(ph, [128, 512], F32, "pG") for _ in range(2)]
                pP = [ps(ph, [128, 512], F32, "pP") for _ in range(2)]
                pstat = ps(ph, [128, 512], F32, "pstat")
                pGb = [PBuf() for _ in range(2)]
                pPb = [PBuf() for _ in range(2)]
                pstat_b = PBuf()
                wb2_ = Buf()
                wb3_ = Buf()
                QP.start(Wpg[:], wpg_d.rearrange("(kc p) n -> p kc n", p=128), writes=[wb_])
                QP.start(Wpp[:], wpp_d.rearrange("(kc p) n -> p kc n", p=128), writes=[wb2_])
                QP.start(pe_b[:], pT_d.rearrange("(kc p) n -> p kc n", p=128), writes=[wb3_])
                yTv = yT.rearrange("(c p) t -> p c t", p=128)
                kk = 0
                def ple_norm(ti):
                    a_, b_ = FFN2_TILES[ti]
                    hs_ = hb2[ti % 2]
                    norm_tile(a_, b_, 3, lambda c: hs_[:, c, :b_ - a_], hb2b[ti % 2], pstat, pstat_b)
                ple_norm(0)
                for ti, (a, b) in enumerate(FFN2_TILES):
                    W = b - a
                    hs = hb2[ti % 2]
                    hsb = hb2b[ti % 2]
                    if ti + 1 < len(FFN2_TILES):
                        ple_norm(ti + 1)
                    for j in range(8):
                        k = kk % 2
                        kk += 1
                        pe_group([(pG[k][:, :W], Wpg[:, kc, j * 128:(j + 1) * 128], hs[:, kc, :W]) for kc in range(8)],
                                 reads=[wb_] + hsb, writes=[pGb[k]])
                        pe_group([(pP[k][:, :W], Wpp[:, m, j * 128:(j + 1) * 128], pe_b[:, m, a - HALO:b - HALO])
                                  for m in range(2)], reads=[wb2_, wb3_], writes=[pPb[k]])
                        emit(ACT, lambda: nc.scalar.activation(out=sg[k][:, :W], in_=pG[k][:, :W], func=AF.Sigmoid),
                             reads=[pGb[k]], writes=[sgb[k]])
                        emit(DVE, lambda: nc.vector.tensor_tensor(out=tp[k][:, :W], in0=sg[k][:, :W], in1=pP[k][:, :W],
                                                                  op=ALU.mult),
                             reads=[sgb[k], pPb[k]], writes=[tpb[k]])
                        emit(DVE, lambda: nc.vector.tensor_tensor(out=xres[:, j, a:b], in0=tp[k][:, :W], in1=xres[:, j, a:b],
                                                                  op=ALU.add),
                             reads=[tpb[k]] + xbufs(j, a, b), writes=xbufs(j, a, b))
                    out_toks.append(QS.start(yTv[:, :, a - HALO:b - HALO], xres[:, :, a:b], reads=xbufs_all(a, b)))
                for t in out_toks:
                    SP.wait(t)
                barrier()

        def dbg_finish():
            yTv = yT.rearrange("(c p) t -> p c t", p=128)
            t = QS.start(yTv, xres[:, :, HALO:NT], reads=xbufs_all(0, NT))
            SP.wait(t)

        if stop == "load":
            dbg_finish()
            return nc
        ffn_phase(w1g, w1u, w1d, 0, FFN1_TILES)
        if stop == "ffn1":
            dbg_finish()
            return nc
        toks = mixer_phase()
        if stop == "mixer":
            dbg_finish()
            return nc
        ffn_phase(w2g, w2u, w2d, 2, FFN2_TILES)
        if stop == "ffn2":
            dbg_finish()
            return nc
        ple_phase(toks)
    return nc


def _rel_bucket_np(d):
    max_exact = 16
    df = np.maximum(d, 1).astype(np.float32)
    val = (np.log(df / np.float32(max_exact)) / np.float32(math.log(128 / 16)) * np.float32(32 - max_exact))
    large = max_exact + val.astype(np.int32)
    large = np.minimum(large, 31)
    return np.where(d < max_exact, d, large)


def _consts():
    E2 = np.zeros((33, 383), np.float32)
    for m in range(383):
        d = 255 - m
        if 0 <= d < 128:
            E2[int(_rel_bucket_np(np.array([d]))[0]), m] = 1.0
        else:
            E2[32, m] = -1e30
    onesD = np.full((128, 128), 1.0 / D, np.float32)
    bd64 = np.zeros((128, 128), np.float32)
    bd64[:64, :64] = 1.0 / 64
    bd64[64:, 64:] = 1.0 / 64
    ident = np.eye(128, dtype=np.float32)
    return E2, onesD, bd64, ident


_NC_CACHE = {}


def kernel(x_prompt, x_sample, p_prompt, p_sample, cache_k, cache_v, state_conv, rel_bias,
           g_ffn1, w1_gate, w1_up, w1_down, g_mix, w_in, q_norm, k_norm, sinks, w_conv, w_out,
           g_ffn2, w2_gate, w2_up, w2_down, g_ple, w_ple_gate, w_ple_proj):
    f = lambda a: np.ascontiguousarray(np.asarray(a, dtype=np.float32))
    x_prompt, x_sample, p_prompt, p_sample = f(x_prompt), f(x_sample), f(p_prompt), f(p_sample)
    cache_k, cache_v, state_conv, rel_bias = f(cache_k), f(cache_v), f(state_conv), f(rel_bias)
    E2, onesD, bd64, ident = _consts()
    qperm = np.concatenate([np.r_[i * 64:(i + 1) * 64, (4 + i) * 64:(5 + i) * 64] for i in range(4)])
    win = f(w_in)[0]
    win_p = f(np.concatenate([win[:, qperm], win[:, 512:]], axis=1))
    wout = f(w_out)[0]
    wout_p = f(np.concatenate([wout[qperm, :], wout[512:, :]], axis=0))
    gv = f(np.stack([f(g_ffn1)[0], f(g_mix)[0], f(g_ffn2)[0], f(g_ple)[0]]).reshape(4, 8, 128).transpose(2, 0, 1))
    qkg = f(np.stack([np.tile(f(q_norm)[0], 2), np.tile(f(k_norm)[0], 2)], axis=1))
    sk = f(sinks)[0]
    sinkb = f(np.broadcast_to(sk[HPERM][None, :], (128, 8)))
    sinks_s = np.zeros((16, 2), np.float32)
    for g in range(4):
        for kv in range(2):
            sinks_s[g * 4:(g + 1) * 4, kv] = sk[kv * 4 + g]
    tabp = f(rel_bias[:, HPERM])
    wconv = f(f(w_conv)[0].reshape(3, 4, 128).transpose(2, 1, 0))
    shared = {
        "tabp": tabp, "E2": E2, "onesD": onesD, "bd64": bd64, "ident": ident, "gv": gv, "qkg": qkg,
        "sinkb": sinkb, "sinks_s": sinks_s, "wconv": wconv,
        "w1g": f(w1_gate)[0], "w1u": f(w1_up)[0], "w1d": f(w1_down)[0], "win": win_p, "wout": wout_p,
        "w2g": f(w2_gate)[0], "w2u": f(w2_up)[0], "w2d": f(w2_down)[0], "wpg": f(w_ple_gate)[0],
        "wpp": f(w_ple_proj)[0],
    }
    in_maps = []
    for c in range(NCORES):
        b, j = divmod(c, 4)
        t0 = j * NPR
        halo = x_prompt[b, t0 - HALO:t0] if j > 0 else np.zeros((HALO, D), np.float32)
        sq = slice(c * NSEQ, (c + 1) * NSEQ)
        xs = x_sample[sq].reshape(NSM, D)
        xT = f(np.concatenate([halo, x_prompt[b, t0:t0 + NPR], xs], axis=0).T)
        pT = f(np.concatenate([p_prompt[0, b, t0:t0 + NPR], p_sample[0, sq].reshape(NSM, DPLE)], axis=0).T)
        ck = cache_k[0, sq].reshape(NSEQ, 128, 128)
        cvv = cache_v[0, sq].reshape(NSEQ, 128, 128)
        sc = state_conv[0, sq]
        scT = f(sc.transpose(2, 0, 1).reshape(4, 128, NSEQ, 2).transpose(1, 0, 2, 3))
        m = dict(shared)
        m.update({
            "xT": xT, "pT": pT, "ckT": f(ck.transpose(2, 0, 1)), "ck_nat": f(ck),
            "cvt": f(cvv.transpose(1, 0, 2)), "cv_nat": f(cvv), "scT": scT,
            "hflag": np.full((128, 1), -1e30 if j == 0 else 0.0, np.float32),
        })
        in_maps.append(m)

    if "nc" not in _NC_CACHE:
        _NC_CACHE["nc"] = build_nc()
    nc = _NC_CACHE["nc"]
    res = run_bass_kernel_spmd(nc, in_maps, core_ids=list(range(NCORES)))
    R = res.results

    B, T = x_prompt.shape[0], x_prompt.shape[1]
    y_prompt = np.zeros((B, T, D), np.float32)
    y_sample = np.zeros((x_sample.shape[0], 4, D), np.float32)
    kwp = np.zeros((1, B, 128, 2, 64), np.float32)
    vwp = np.zeros((1, B, 128, 2, 64), np.float32)
    cvp = np.zeros((1, B, 2, 512), np.float32)
    kws = np.zeros((1, x_sample.shape[0], 128, 2, 64), np.float32)
    vws = np.zeros((1, x_sample.shape[0], 128, 2, 64), np.float32)
    cvs = np.zeros((1, x_sample.shape[0], 2, 512), np.float32)
    for c in range(NCORES):
        b, j = divmod(c, 4)
        t0 = j * NPR
        r = R[c]
        sq = slice(c * NSEQ, (c + 1) * NSEQ)
        y = np.asarray(r["yT"])
        y_prompt[b, t0:t0 + NPR] = y[:, :NPR].T
        y_sample[sq] = y[:, NPR:].T.reshape(NSEQ, 4, D)
        if j == 3:
            kwp[0, b] = np.asarray(r["kTl"]).T.reshape(128, 2, 64)
            vwp[0, b] = np.asarray(r["vl"]).reshape(128, 2, 64)
            cvp[0, b] = np.asarray(r["cl"]).transpose(2, 1, 0).reshape(2, 512)
        kws[0, sq, 0:124] = np.asarray(r["kws_old"]).reshape(NSEQ, 124, 2, 64)
        kws[0, sq, 124:128] = np.asarray(r["ksn"]).T.reshape(NSEQ, 4, 2, 64)
        vws[0, sq, 0:124] = np.asarray(r["vws_old"]).reshape(NSEQ, 124, 2, 64)
        vws[0, sq, 124:128] = np.asarray(r["vsn"]).transpose(1, 0, 2).reshape(NSEQ, 4, 2, 64)
        cvs[0, sq] = np.asarray(r["csn"]).transpose(2, 3, 1, 0).reshape(NSEQ, 2, 512)
    return (y_prompt, y_sample, kwp, vwp, cvp, kws, vws, cvs)
```

```python
import math
from contextlib import ExitStack

import numpy as np
import concourse.bass as bass
import concourse.mybir as mybir
from concourse.bass_utils import run_bass_kernel_spmd

F32 = mybir.dt.float32
BF16 = mybir.dt.bfloat16
AF = mybir.ActivationFunctionType
ALU = mybir.AluOpType
AX = mybir.AxisListType

NCORES = 8
D = 1024
DFF = 2816
NFF = 22
DPLE = 256
INC = 2304
HALO = 128
NPR = 2048
NSM = 64
NT = HALO + NPR + NSM
NOUT = NPR + NSM
CS = HALO + NPR
MT = 256
EPS = 1e-6
NSEQ = 16
HPERM = [0, 1, 4, 5, 2, 3, 6, 7]

FFN_GROUPS = [(0, 5), (5, 10), (10, 14), (14, 18), (18, 22)]
FFN1_TILES = [(0, 128), (128, 640), (640, 1152), (1152, 1664), (1664, 2176), (2176, 2240)]
FFN2_TILES = FFN1_TILES[1:]


class Eng:
    def __init__(self, eng, sem, is_pe=False):
        self.eng = eng
        self.sem = sem
        self.cnt = 0
        self.waited = {}
        self.is_pe = is_pe

    def wait(self, tok):
        if tok is None:
            return
        s, v = tok
        if self.is_pe and s is self.sem:
            return
        if self.waited.get(s.num, 0) >= v:
            return
        self.eng.wait_ge(s, v)
        self.waited[s.num] = v

    def mark(self, inst):
        self.cnt += 1
        inst.then_inc(self.sem, 1)
        return (self.sem, self.cnt)


class _Stop(Exception):
    pass


class Buf:
    __slots__ = ("w", "r", "excl")

    def __init__(self, excl=False):
        self.w = None
        self.r = {}
        self.excl = excl


def PBuf():
    return Buf(excl=True)


def _pre(E, reads, writes):
    for b in reads:
        E.wait(b.w)
        if b.excl:
            for t in list(b.r.values()):
                if t[0] is not E.sem:
                    E.wait(t)
    for b in writes:
        E.wait(b.w)
        for t in list(b.r.values()):
            E.wait(t)


def _commit(tok, reads, writes):
    for b in reads:
        cur = b.r.get(tok[0].num)
        if cur is None or cur[1] < tok[1]:
            b.r[tok[0].num] = tok
    for b in writes:
        b.w = tok
        b.r = {}


def emit(E, fn, reads=(), writes=()):
    _pre(E, reads, writes)
    tok = E.mark(fn())
    _commit(tok, reads, writes)
    return tok


class DmaQ:
    def __init__(self, E, sems):
        self.E = E
        self.slots = [[s, 0] for s in sems]
        self.i = 0

    def start(self, out, in_, reads=(), writes=()):
        E = self.E
        _pre(E, reads, writes)
        slot = self.slots[self.i % len(self.slots)]
        self.i += 1
        if slot[1]:
            E.wait((slot[0], slot[1]))
        inst = E.eng.dma_start(out=out, in_=in_)
        slot[1] += 16
        inst.then_inc(slot[0], 16)
        tok = (slot[0], slot[1])
        _commit(tok, reads, writes)
        return tok

    def outstanding(self):
        return [(s, v) for s, v in self.slots if v]


def build_nc(stop=None):
    nc = bass.Bass("TRN2", target_bir_lowering=False)

    def din(name, shape):
        return nc.dram_tensor(name, list(shape), F32, kind="ExternalInput").ap()

    def dout(name, shape):
        return nc.dram_tensor(name, list(shape), F32, kind="ExternalOutput").ap()

    xT = din("xT", [D, NT])
    pT_d = din("pT", [DPLE, NOUT])
    ckT_d = din("ckT", [128, NSEQ, 128])
    ck_nat = din("ck_nat", [NSEQ, 128, 128])
    cvt_d = din("cvt", [128, NSEQ, 128])
    cv_nat = din("cv_nat", [NSEQ, 128, 128])
    scT_d = din("scT", [128, 4, NSEQ, 2])
    tabp_d = din("tabp", [32, 8])
    E2_d = din("E2", [33, 383])
    onesD_d = din("onesD", [128, 128])
    bd64_d = din("bd64", [128, 128])
    ident_d = din("ident", [128, 128])
    gv_d = din("gv", [128, 4, 8])
    qkg_d = din("qkg", [128, 2])
    sinkb_d = din("sinkb", [128, 8])
    sinks_d = din("sinks_s", [16, 2])
    wconv_d = din("wconv", [128, 4, 3])
    hflag_d = din("hflag", [128, 1])
    w1g = din("w1g", [D, DFF])
    w1u = din("w1u", [D, DFF])
    w1d = din("w1d", [DFF, D])
    win_d = din("win", [D, INC])
    wout_d = din("wout", [D, D])
    w2g = din("w2g", [D, DFF])
    w2u = din("w2u", [D, DFF])
    w2d = din("w2d", [DFF, D])
    wpg_d = din("wpg", [D, D])
    wpp_d = din("wpp", [DPLE, D])

    yT = dout("yT", [D, NOUT])
    kTl_o = dout("kTl", [128, 128])
    vl_o = dout("vl", [128, 128])
    cl_o = dout("cl", [128, 4, 2])
    kws_o = dout("kws_old", [NSEQ, 124, 128])
    vws_o = dout("vws_old", [NSEQ, 124, 128])
    ksn_o = dout("ksn", [128, NSM])
    vsn_o = dout("vsn", [4, NSEQ, 128])
    csn_o = dout("csn", [128, 4, NSEQ, 2])

    Ubc_t = nc.dram_tensor("Ubc", [128, 8, 383], F32, kind="Internal")
    Ubc = Ubc_t.ap()

    uid = [0]

    with ExitStack() as top:
        def sem(name):
            return top.enter_context(nc.semaphore(name))

        PE = Eng(nc.tensor, sem("s_pe"), is_pe=True)
        ACT = Eng(nc.scalar, sem("s_act"))
        DVE = Eng(nc.vector, sem("s_dve"))
        POOL = Eng(nc.gpsimd, sem("s_pool"))
        SP = Eng(nc.sync, sem("s_sp"))
        import os
        NSQ = int(os.environ.get("KNSQ", "10"))
        KSKIP = os.environ.get("KSKIP", "").split(",")
        CONV_ON_POOL = os.environ.get("KCONVPOOL", "0") == "1"
        NORM_ADD_POOL = os.environ.get("KNORMPOOL", "1") == "1"
        NWARM = int(os.environ.get("KNWARM", "0"))
        KNPOS = int(os.environ.get("KNPOS", "10"))
        SLOT = [int(x) for x in os.environ.get("KSLOT", "2,2,1,2").split(",")]
        QS = DmaQ(SP, [sem("qs%d" % i) for i in range(NSQ)])
        QP = DmaQ(POOL, [sem("qp%d" % i) for i in range(NSQ)])
        ENGS = (PE, ACT, DVE, POOL, SP)

        def sb(stack, shape, dt, name="t"):
            uid[0] += 1
            return stack.enter_context(nc.sbuf_tensor("%s_%d" % (name, uid[0]), list(shape), dt))

        def ps(stack, shape, dt, name="p"):
            uid[0] += 1
            return stack.enter_context(nc.psum_tensor("%s_%d" % (name, uid[0]), list(shape), dt))

        QA_REF = []
        X_REST = []

        def barrier(queues=None):
            toks = [(E.sem, E.cnt) for E in (PE, ACT, DVE, POOL) if E.cnt > 0]
            for q_ in (queues if queues is not None else (QS, QP)):
                toks += q_.outstanding()
            if queues is None and QA_REF:
                toks += QA_REF[0].outstanding()
            for E in ENGS:
                for t in toks:
                    E.wait(t)

        def pe_group(mms, reads=(), writes=()):
            _pre(PE, reads, writes)
            n = len(mms)
            inst = None
            for i, (o, l, r) in enumerate(mms):
                inst = nc.tensor.matmul(o, l, r, start=(i == 0), stop=(i == n - 1))
            tok = PE.mark(inst)
            _commit(tok, reads, writes)
            return tok

        def pe_transposes(trs, ident_ap_fn, reads=(), writes=()):
            _pre(PE, reads, writes)
            inst = None
            for (o, i_) in trs:
                inst = nc.tensor.transpose(o, i_, ident_ap_fn(i_))
            tok = PE.mark(inst)
            _commit(tok, reads, writes)
            return tok

        xres = sb(top, [128, 8, NT], F32, "xres")
        xb = [[Buf() for _ in range(18)] for _ in range(8)]

        def xbufs(c, a, b):
            return [xb[c][k] for k in range(a // 128, (b + 127) // 128)]

        def xbufs_all(a, b):
            r = []
            for c in range(8):
                r += xbufs(c, a, b)
            return r

        onesD = sb(top, [128, 128], BF16, "onesD")
        bd64 = sb(top, [128, 128], BF16, "bd64")
        ident = sb(top, [128, 128], BF16, "ident")
        gv = sb(top, [128, 4, 8], F32, "gv")
        qkg = sb(top, [128, 2], F32, "qkg")
        sinkb = sb(top, [128, 8], F32, "sinkb")
        nsinkb = sb(top, [128, 8], F32, "nsinkb")
        wconv = sb(top, [128, 4, 3], F32, "wconv")
        hflag = sb(top, [128, 1], F32, "hflag")
        epst = sb(top, [128, 1], F32, "epst")
        NS = {}

        def alloc_norm(stack, Wn):
            NS["sq"] = [sb(stack, [128, Wn], BF16, "sq") for _ in range(3)]
            NS["sqb"] = [Buf() for _ in range(3)]
            NS["rt"] = [sb(stack, [128, Wn], F32, "rt") for _ in range(2)]
            NS["rtb"] = [Buf() for _ in range(2)]
            NS["rstd"] = [sb(stack, [128, Wn], F32, "rstd") for _ in range(2)]
            NS["rstdb"] = [Buf() for _ in range(2)]
        CB = []

        def newcb():
            b_ = Buf()
            CB.append(b_)
            return b_
        cnt = {"sq": 0, "rt": 0}

        xTv = xT.rearrange("(c p) t -> p c t", p=128)
        QA = DmaQ(ACT, [sem("qa%d" % i) for i in range(8)])
        QA_REF.append(QA)
        for c in range(8):
            q_ = QS if c % 2 == 0 else QA
            q_.start(xres[:, c, 0:640], xTv[:, c, 0:640], writes=xbufs(c, 0, 640))

        def x_rest():
            for c in range(8):
                QS.start(xres[:, c, 640:NT], xTv[:, c, 640:NT], writes=xbufs(c, 640, NT))
        X_REST.append(x_rest)
        for dst, src in ((gv, gv_d), (qkg, qkg_d), (sinkb, sinkb_d), (wconv, wconv_d), (hflag, hflag_d)):
            QS.start(dst[:], src, writes=[newcb()])
        for dst, src in ((onesD, onesD_d), (bd64, bd64_d), (ident, ident_d)):
            QP.start(dst[:], src, writes=[newcb()])
        emit(DVE, lambda: nc.vector.memset(epst[:], EPS), writes=[newcb()])
        qkg8 = sb(top, [128, 1], F32, "qkg8")
        emit(DVE, lambda: nc.vector.tensor_scalar(out=qkg8[:], in0=qkg[:, 0:1], scalar1=0.125, scalar2=None,
                                                  op0=ALU.mult), reads=list(CB), writes=[newcb()])
        emit(DVE, lambda: nc.vector.tensor_scalar(out=nsinkb[:], in0=sinkb[:], scalar1=-1.0, scalar2=None,
                                                  op0=ALU.mult), reads=list(CB), writes=[newcb()])

        def norm_tile(a, b, gsel, out_fn, out_bufs, pstat, pstat_b):
            sq, sqb, rt, rtb, rstd, rstdb = NS["sq"], NS["sqb"], NS["rt"], NS["rtb"], NS["rstd"], NS["rstdb"]
            W = b - a
            for c in range(8):
                i = cnt["sq"] % 3
                cnt["sq"] += 1
                emit(ACT, lambda: nc.scalar.activation(out=sq[i][:, :W], in_=xres[:, c, a:b], func=AF.Square),
                     reads=xbufs(c, a, b), writes=[sqb[i]])
                _pre(PE, [sqb[i]] + CB, [pstat_b] if c == 0 else [])
                inst = nc.tensor.matmul(pstat[:, :W], onesD[:], sq[i][:, :W], start=(c == 0), stop=(c == 7))
                tok = PE.mark(inst)
                _commit(tok, [sqb[i]] + CB, [pstat_b] if c == 7 else [])
            j = cnt["rt"] % 2
            cnt["rt"] += 1
            emit(ACT, lambda: nc.scalar.activation(out=rt[j][:, :W], in_=pstat[:, :W], func=AF.Ln,
                                                   bias=epst[:, 0:1], scale=1.0),
                 reads=[pstat_b] + CB, writes=[rtb[j]])
            emit(ACT, lambda: nc.scalar.activation(out=rstd[j][:, :W], in_=rt[j][:, :W], func=AF.Exp, scale=-0.5),
                 reads=[rtb[j]], writes=[rstdb[j]])
            for c in range(8):
                emit(DVE, lambda: nc.vector.scalar_tensor_tensor(
                    out=out_fn(c), in0=xres[:, c, a:b], scalar=gv[:, gsel, c:c + 1], in1=rstd[j][:, :W],
                    op0=ALU.mult, op1=ALU.mult),
                    reads=xbufs(c, a, b) + [rstdb[j]] + CB, writes=[out_bufs[c]])

        def ffn_phase(wg_d, wu_d, wd_d, gsel, tiles):
            with ExitStack() as ph:
                alloc_norm(ph, 512)
                hb = sb(ph, [128, 8, NT], BF16, "hb")
                hbb = [[Buf() for _ in range(8)] for _ in tiles]
                Wg = [sb(ph, [128, 8, 640], BF16, "Wg") for _ in range(2)]
                Wu = [sb(ph, [128, 8, 640], BF16, "Wu") for _ in range(2)]
                Wd = [sb(ph, [128, 5, 1024], BF16, "Wd") for _ in range(2)]
                Wgb = [Buf() for _ in range(2)]
                Wub = [Buf() for _ in range(2)]
                Wdb = [Buf() for _ in range(2)]
                A = [sb(ph, [128, 5, 512], BF16, "A") for _ in range(2)]
                Ab = [[Buf() for _ in range(5)] for _ in range(2)]
                S = [sb(ph, [128, 512], F32, "S") for _ in range(2)]
                Sb = [Buf() for _ in range(2)]
                pG = [ps(ph, [128, 512], F32, "pG") for _ in range(2)]
                pU = [ps(ph, [128, 512], F32, "pU") for _ in range(2)]
                pY = [ps(ph, [128, 512], F32, "pY") for _ in range(2)]
                pstat = ps(ph, [128, 512], F32, "pstat")
                pGb = [PBuf() for _ in range(2)]
                pUb = [PBuf() for _ in range(2)]
                pYb = [PBuf() for _ in range(2)]
                pstat_b = PBuf()

                def load_group(gi):
                    c0, c1 = FFN_GROUPS[gi]
                    s = gi % 2
                    gw = (c1 - c0) * 128
                    QP.start(Wg[s][:, :, 0:gw], wg_d[:, c0 * 128:c1 * 128].rearrange("(kc p) n -> p kc n", p=128),
                             writes=[Wgb[s]])
                    QP.start(Wu[s][:, :, 0:gw], wu_d[:, c0 * 128:c1 * 128].rearrange("(kc p) n -> p kc n", p=128),
                             writes=[Wub[s]])
                    QP.start(Wd[s][:, 0:c1 - c0, :], wd_d[c0 * 128:c1 * 128, :].rearrange("(g p) n -> p g n", p=128),
                             writes=[Wdb[s]])

                load_group(0)
                if X_REST:
                    X_REST.pop()()
                def ffn_norm(ti):
                    a_, b_ = tiles[ti]
                    norm_tile(a_, b_, gsel, lambda c: hb[:, c, a_:b_], hbb[ti], pstat, pstat_b)
                ffn_norm(0)
                if len(tiles) > 1:
                    ffn_norm(1)
                load_group(1)

                items = [(gi, ti) for gi in range(len(FFN_GROUPS)) for ti in range(len(tiles))]
                kc_cnt = [0]
                y_cnt = [0]

                def stage1(idx):
                    gi, ti = items[idx]
                    if gi == 0 and ti + 2 < len(tiles):
                        ffn_norm(ti + 2)
                    a, b = tiles[ti]
                    W = b - a
                    c0, c1 = FFN_GROUPS[gi]
                    s = gi % 2
                    ai = idx % 2
                    for cl in range(c1 - c0):
                        k = kc_cnt[0] % 2
                        kc_cnt[0] += 1
                        pe_group([(pG[k][:, :W], Wg[s][:, kc, cl * 128:(cl + 1) * 128], hb[:, kc, a:b]) for kc in range(8)],
                                 reads=[Wgb[s]] + hbb[ti], writes=[pGb[k]])
                        pe_group([(pU[k][:, :W], Wu[s][:, kc, cl * 128:(cl + 1) * 128], hb[:, kc, a:b]) for kc in range(8)],
                                 reads=[Wub[s]] + hbb[ti], writes=[pUb[k]])
                        emit(ACT, lambda: nc.scalar.activation(out=S[k][:, :W], in_=pG[k][:, :W], func=AF.Silu),
                             reads=[pGb[k]], writes=[Sb[k]])
                        emit(DVE, lambda: nc.vector.tensor_tensor(out=A[ai][:, cl, :W], in0=S[k][:, :W], in1=pU[k][:, :W],
                                                                  op=ALU.mult),
                             reads=[Sb[k], pUb[k]], writes=[Ab[ai][cl]])

                def stage2(idx):
                    gi, ti = items[idx]
                    a, b = tiles[ti]
                    W = b - a
                    c0, c1 = FFN_GROUPS[gi]
                    s = gi % 2
                    ai = idx % 2
                    n = c1 - c0
                    for j in range(8):
                        k = y_cnt[0] % 2
                        y_cnt[0] += 1
                        pe_group([(pY[k][:, :W], Wd[s][:, cl, j * 128:(j + 1) * 128], A[ai][:, cl, :W]) for cl in range(n)],
                                 reads=[Wdb[s]] + Ab[ai][:n], writes=[pYb[k]])
                        emit(DVE, lambda: nc.vector.scalar_tensor_tensor(
                            out=xres[:, j, a:b], in0=pY[k][:, :W], scalar=0.5, in1=xres[:, j, a:b],
                            op0=ALU.mult, op1=ALU.add),
                            reads=[pYb[k]] + xbufs(j, a, b), writes=xbufs(j, a, b))
                    if ti == len(tiles) - 1 and gi + 2 < len(FFN_GROUPS):
                        load_group(gi + 2)

                stage1(0)
                for idx in range(len(items)):
                    if idx + 1 < len(items):
                        stage1(idx + 1)
                    stage2(idx)
                barrier()

        def mixer_phase():
            out_toks = []
            with ExitStack() as ph:
                alloc_norm(ph, MT)
                Win = sb(ph, [128, 8, INC], BF16, "Win")
                Wout = sb(ph, [128, 8, D], BF16, "Wout")
                WinSeg = [(0, 512), (512, 768), (768, 1280), (1280, 1792), (1792, 2304)]
                Winb = [Buf() for _ in WinSeg]
                Woutb = Buf()
                Bias_s = sb(ph, [16, 2, 132], F32, "Bias_s")
                sinks_s = sb(ph, [16, 2], F32, "sinks_s")
                sconst = Buf()
                sconst2 = Buf()
                kT_all = sb(ph, [128, NT], BF16, "kT_all")
                kTb = [Buf() for _ in range(18)]
                hbm = [sb(ph, [128, 8, MT], BF16, "hbm") for _ in range(2)]
                hbmb = [[Buf() for _ in range(8)] for _ in range(2)]
                zq = sb(ph, [128, 4, MT], F32, "zq")
                zqb = [Buf() for _ in range(4)]
                zk = sb(ph, [128, MT], F32, "zk")
                zkb = Buf()
                cv = sb(ph, [128, 4, MT], F32, "cv")
                cvb = [Buf() for _ in range(4)]
                qnb2 = [sb(ph, [128, 4, MT], BF16, "qnb") for _ in range(2)]
                qnbb2 = [[Buf() for _ in range(4)] for _ in range(2)]
                knf = sb(ph, [128, MT], F32, "knf")
                knfb = Buf()
                mixb2 = [sb(ph, [128, 8, MT], BF16, "mixb") for _ in range(2)]
                mixbb2 = [[Buf() for _ in range(8)] for _ in range(2)]

                pA = ps(ph, [128, 2048], F32, "pA")
                bA = [PBuf() for _ in range(4)]
                pstat = ps(ph, [128, 512], F32, "pstat")
                pstat_b = PBuf()
                pT = ps(ph, [128, 1024], F32, "pT")
                pTb = PBuf()
                pO = ps(ph, [128, 512], F32, "pO")
                pOb = PBuf()
                pZ = [pA[:, 0:512], pA[:, 512:1024]]
                pZb = [bA[0], bA[1]]
                pS = pA[:, 1024:2048]
                pSb = [bA[2], bA[3]]

                winv = win_d.rearrange("(kc p) n -> p kc n", p=128)
                for si in (1, 3, 4, 0, 2):
                    lo, hi = WinSeg[si]
                    QP.start(Win[:, :, lo:hi], winv[:, :, lo:hi], writes=[Winb[si]])
                QP.start(Wout[:], wout_d.rearrange("(kc p) n -> p kc n", p=128), writes=[Woutb])
                QS.start(sinks_s[:], sinks_d, writes=[sconst2])
                out_toks.append(QS.start(kws_o, ck_nat[:, 4:128, :]))
                out_toks.append(QS.start(vws_o, cv_nat[:, 4:128, :]))

                state = {"ti": 0, "z": 0, "wprev": None}
                hsq = [sb(ph, [128, MT], BF16, "hsq") for _ in range(5)]
                hsqb = [Buf() for _ in range(5)]

                def presq(slot, src_ap, W, src_bufs):
                    emit(ACT, lambda: nc.scalar.activation(out=hsq[slot][:, :W], in_=src_ap, func=AF.Square),
                         reads=src_bufs, writes=[hsqb[slot]])

                def zmm(wcol, seg, hs, hsb, W):
                    k = state["z"] % 2
                    state["z"] += 1
                    pe_group([(pZ[k][:, :W], Win[:, kc, wcol:wcol + 128], hs[:, kc, :W]) for kc in range(8)],
                             reads=[Winb[seg]] + hsb, writes=[pZb[k]])
                    return k

                def head_norm(src_ap, W, gap, out_ap, out_bufs, src_bufs, slot=None):
                    sq, sqb, rt, rtb, rstd, rstdb = NS["sq"], NS["sqb"], NS["rt"], NS["rtb"], NS["rstd"], NS["rstdb"]
                    pe_group([(pstat[:, :W], bd64[:], hsq[slot][:, :W])], reads=[hsqb[slot]] + CB, writes=[pstat_b])
                    j = cnt["rt"] % 2
                    cnt["rt"] += 1
                    emit(ACT, lambda: nc.scalar.activation(out=rt[j][:, :W], in_=pstat[:, :W], func=AF.Ln,
                                                           bias=epst[:, 0:1], scale=1.0),
                         reads=[pstat_b] + CB, writes=[rtb[j]])
                    emit(ACT, lambda: nc.scalar.activation(out=rstd[j][:, :W], in_=rt[j][:, :W], func=AF.Exp, scale=-0.5),
                         reads=[rtb[j]], writes=[rstdb[j]])
                    emit(DVE, lambda: nc.vector.scalar_tensor_tensor(
                        out=out_ap, in0=src_ap, scalar=gap, in1=rstd[j][:, :W],
                        op0=ALU.mult, op1=ALU.mult),
                        reads=src_bufs + [rstdb[j]] + CB, writes=out_bufs)

                def norm_only(a, b, ti):
                    W = b - a
                    hs = hbm[ti % 2]
                    norm_tile(a, b, 1, lambda c: hs[:, c, :W], hbmb[ti % 2], pstat, pstat_b)

                def kchunk(a, b, ti):
                    W = b - a
                    hs = hbm[ti % 2]
                    hsb = hbmb[ti % 2]
                    k = zmm(512, 1, hs, hsb, W)
                    emit(ACT, lambda: nc.scalar.copy(out=zk[:, :W], in_=pZ[k][:, :W]), reads=[pZb[k]], writes=[zkb])
                    presq(4, zk[:, :W], W, [zkb])
                    return hs, hsb

                def front0(a, b):
                    ti = state["ti"]
                    state["ti"] += 1
                    norm_only(a, b, ti)
                    return kchunk(a, b, ti)

                def q_chunk1(hs, hsb, W, i):
                    k = zmm(i * 128, 0, hs, hsb, W)
                    emit(ACT, lambda: nc.scalar.copy(out=zq[:, i, :W], in_=pZ[k][:, :W]), reads=[pZb[k]],
                         writes=[zqb[i]])
                    presq(i, zq[:, i, :W], W, [zqb[i]])

                def q_chunks(hs, hsb, W):
                    for i in range(4):
                        q_chunk1(hs, hsb, W, i)

                def u_c1(hs, hsb, W, uview, pzview, ubw, c):
                    k = zmm(1280 + c * 128, 3, hs, hsb, W)
                    emit(ACT, lambda: nc.scalar.copy(out=uview(c), in_=pzview(k)), reads=[pZb[k]], writes=ubw[c])

                def u_h1(hs, hsb, W, uview, pzview, ubw, c):
                    k = zmm(1792 + c * 128, 4, hs, hsb, W)
                    emit(DVE, lambda: nc.vector.tensor_tensor(out=uview(c), in0=uview(c), in1=pzview(k), op=ALU.mult),
                         reads=[pZb[k]] + ubw[c], writes=ubw[c])

                def u_chunks(hs, hsb, W, uview, pzview, ubw):
                    for c in range(4):
                        u_c1(hs, hsb, W, uview, pzview, ubw, c)
                    for c in range(4):
                        u_h1(hs, hsb, W, uview, pzview, ubw, c)

                def k_norm(a, b, kind):
                    W = b - a
                    head_norm(zk[:, :W], W, qkg[:, 1:2], knf[:, :W], [knfb], [zkb], slot=4)
                    blks = [17] if kind == "sample" else list(range(a // 128, b // 128))
                    emit(ACT, lambda: nc.scalar.copy(out=kT_all[:, a:b], in_=knf[:, :W]), reads=[knfb],
                         writes=[kTb[x] for x in blks])

                def conv_only(W, cvo_fn, taps_fn, rb_fn, cs=range(4)):
                    for c in cs:
                        cvo = cvo_fn(c)
                        taps = taps_fn(c)
                        rb = rb_fn(c)
                        CE, ce = (POOL, nc.gpsimd) if CONV_ON_POOL else (DVE, nc.vector)
                        emit(CE, lambda: ce.tensor_scalar(out=cvo, in0=taps[2], scalar1=wconv[:, c, 2:3],
                                                          scalar2=None, op0=ALU.mult),
                             reads=rb + CB, writes=[cvb[c]])
                        for j in (1, 0):
                            emit(CE, lambda: ce.scalar_tensor_tensor(
                                out=cvo, in0=taps[j], scalar=wconv[:, c, j:j + 1], in1=cvo, op0=ALU.mult, op1=ALU.add),
                                reads=rb + CB + [cvb[c]], writes=[cvb[c]])

                def gate_only(hs, hsb, W, par, cs=range(4)):
                    for c in cs:
                        k = zmm(768 + c * 128, 2, hs, hsb, W)
                        emit(DVE, lambda: nc.vector.tensor_tensor(out=mixb2[par][:, 4 + c, :W], in0=pZ[k][:, :W],
                                                                  in1=cv[:, c, :W], op=ALU.mult),
                             reads=[pZb[k], cvb[c]], writes=[mixbb2[par][4 + c]])

                def out_proj(a, b, par, js):
                    W = b - a
                    for j in js:
                        k = state["z"] % 2
                        state["z"] += 1
                        pe_group([(pZ[k][:, :W], Wout[:, m, j * 128:(j + 1) * 128], mixb2[par][:, m, :W]) for m in range(8)],
                                 reads=[Woutb] + mixbb2[par], writes=[pZb[k]])
                        emit(DVE, lambda: nc.vector.tensor_tensor(out=xres[:, j, a:b], in0=pZ[k][:, :W], in1=xres[:, j, a:b],
                                                                  op=ALU.add),
                             reads=[pZb[k]] + xbufs(j, a, b), writes=xbufs(j, a, b))

                with ExitStack() as P:
                    Bhi = sb(P, [128, 8, 256], BF16, "Bhi")
                    Blo = sb(P, [128, 8, 256], BF16, "Blo")
                    Hm = sb(P, [128, 256], BF16, "Hm")
                    Bhib, Blob, Hmb = Buf(), Buf(), Buf()
                    Biasb = Buf()
                    Vt = sb(P, [128, 17, 192], BF16, "Vt")
                    Vtb = [Buf() for _ in range(17)]
                    emit(DVE, lambda: nc.vector.memset(Vt[:], 0.0), writes=Vtb)
                    with ExitStack() as su:
                        tab_sb = sb(su, [33, 8], F32, "tab_sb")
                        tabB = sb(su, [33, 8, 128], F32, "tabB")
                        E2_sb = sb(su, [33, 383], F32, "E2_sb")
                        Ubc_sb = sb(su, [128, 8, 383], F32, "Ubc_sb")
                        Bias = sb(su, [128, 8, 256], F32, "Bias")
                        tb = Buf()
                        eb = Buf()
                        ub = Buf()
                        emit(DVE, lambda: nc.vector.memset(tab_sb[:], 1.0), writes=[tb])
                        QS.start(tab_sb[0:32, :], tabp_d, writes=[tb])
                        QS.start(E2_sb[:], E2_d, writes=[eb])
                        emit(DVE, lambda: nc.vector.tensor_copy(out=tabB[:], in_=tab_sb[:].unsqueeze(2).to_broadcast([33, 8, 128])),
                             reads=[tb], writes=[tb])
                        for h in range(8):
                            k = h % 2
                            pe_group([(pZ[k][:, 0:383], tabB[:, h, :], E2_sb[:])], reads=[tb, eb], writes=[pZb[k]])
                            emit(ACT, lambda: nc.scalar.copy(out=Ubc_sb[:, h, :], in_=pZ[k][:, 0:383]), reads=[pZb[k]], writes=[ub])
                        dsc = Buf()
                        QS.start(Ubc, Ubc_sb[:], reads=[ub], writes=[dsc])
                        src = bass.AP(Ubc_t, 127, [[8 * 383 - 1, 128], [383, 8], [1, 256]])
                        QS.start(Bias[:], src, reads=[dsc], writes=[Biasb])
                        emit(DVE, lambda: nc.vector.tensor_copy(out=Bhi[:], in_=Bias[:]), reads=[Biasb], writes=[Bhib])
                        emit(DVE, lambda: nc.vector.tensor_tensor(out=Blo[:], in0=Bias[:], in1=Bhi[:], op=ALU.subtract),
                             reads=[Biasb, Bhib], writes=[Blob])
                        emit(DVE, lambda: nc.vector.memset(Hm[:], 0.0), writes=[Hmb])
                        emit(DVE, lambda: nc.vector.tensor_scalar(out=Hm[:, 0:128], in0=Hm[:, 0:128], scalar1=hflag[:, 0:1],
                                                                  scalar2=None, op0=ALU.add), reads=[Hmb] + CB, writes=[Hmb])
                        for kv in range(2):
                            for g in range(4):
                                ph_ = 4 * (g // 2) + 2 * kv + (g % 2)
                                src = bass.AP(Ubc_t, ph_ * 383 + 127, [[8 * 383 - 1, 4], [1, 132]])
                                QS.start(Bias_s[g * 4:(g + 1) * 4, kv, :], src, reads=[dsc], writes=[sconst])
                        barrier(queues=(QS,))
                    ubuf = sb(P, [128, 4, MT + 2], F32, "ubuf")
                    ubb = [Buf() for _ in range(4)]
                    vlast = sb(P, [128, 128], F32, "vlast")
                    vlastb = Buf()
                    P2 = [sb(P, [128, 4, 256], BF16, "Pexp") for _ in range(2)]
                    P2b = [[Buf() for _ in range(4)] for _ in range(2)]
                    D2 = [sb(P, [128, 4, 128], BF16, "Dn") for _ in range(2)]
                    D2b = [Buf() for _ in range(2)]
                    PTs2 = [sb(P, [128, 1024], BF16, "PTs") for _ in range(2)]
                    PTsb2 = [Buf() for _ in range(2)]
                    sm2 = [{n: sb(P, [128, 4], F32, n) for n in ("mx", "negm", "tmp4", "es4", "rs4", "den4", "rden4")}
                           for _ in range(2)]
                    smb2 = [{n: Buf() for n in sm2[0]} for _ in range(2)]

                    def attn_A(bi, o, hp, par, hb_):
                        Pe, Peb, sm, smb = P2[hb_], P2b[hb_], sm2[hb_], smb2[hb_]
                        qnb, qnbb = qnb2[par], qnbb2[par]
                        kc0 = 128 * (bi - 1)
                        for ci in range(2):
                            i = 2 * hp + ci
                            for kv in range(2):
                                sl = kv * 2 + ci
                                dst = pS[:, sl * 256:(sl + 1) * 256]
                                mms = [(dst, qnb[kv * 64:(kv + 1) * 64, i, o:o + 128], kT_all[kv * 64:(kv + 1) * 64, kc0:kc0 + 256]),
                                       (dst, ident[:], Bhi[:, 4 * hp + sl, :]),
                                       (dst, ident[:], Blo[:, 4 * hp + sl, :])]
                                rd = [qnbb[i], kTb[bi - 1], kTb[bi], Bhib, Blob] + CB
                                if bi == 1:
                                    mms.append((dst, ident[:], Hm[:]))
                                    rd.append(Hmb)
                                pe_group(mms, reads=rd, writes=[pSb[sl // 2]])
                        emit(DVE, lambda: nc.vector.reduce_max(out=sm["mx"][:], in_=pS.rearrange("p (a b) -> p a b", b=256),
                                                               axis=AX.X),
                             reads=[pSb[0], pSb[1]], writes=[smb["mx"]])
                        emit(DVE, lambda: nc.vector.scalar_tensor_tensor(
                            out=sm["negm"][:], in0=sm["mx"][:], scalar=-1.0, in1=nsinkb[:, 4 * hp:4 * hp + 4],
                            op0=ALU.mult, op1=ALU.min), reads=[smb["mx"]] + CB, writes=[smb["negm"]])
                        emit(DVE, lambda: nc.vector.tensor_tensor(
                            out=sm["tmp4"][:], in0=sinkb[:, 4 * hp:4 * hp + 4], in1=sm["negm"][:], op=ALU.add),
                            reads=[smb["negm"]] + CB, writes=[smb["tmp4"]])
                        for sl in range(4):
                            emit(ACT, lambda: nc.scalar.activation(
                                out=Pe[:, sl, :], in_=pS[:, sl * 256:(sl + 1) * 256], func=AF.Exp,
                                bias=sm["negm"][:, sl:sl + 1], scale=1.0, accum_out=sm["rs4"][:, sl:sl + 1]),
                                reads=[pSb[sl // 2], smb["negm"]], writes=[Peb[sl], smb["rs4"]])
                        emit(ACT, lambda: nc.scalar.activation(out=sm["es4"][:], in_=sm["tmp4"][:], func=AF.Exp),
                             reads=[smb["tmp4"]], writes=[smb["es4"]])

                    def attn_B(bi, o, hp, par, hb_):
                        Pe, Peb, sm, smb = P2[hb_], P2b[hb_], sm2[hb_], smb2[hb_]
                        Dn, Dnb, PTs, PTsb = D2[hb_], D2b[hb_], PTs2[hb_], PTsb2[hb_]
                        emit(DVE, lambda: nc.vector.tensor_tensor(out=sm["den4"][:], in0=sm["rs4"][:], in1=sm["es4"][:],
                                                                  op=ALU.add),
                             reads=[smb["rs4"], smb["es4"]], writes=[smb["den4"]])
                        emit(DVE, lambda: nc.vector.reciprocal(out=sm["rden4"][:], in_=sm["den4"][:]),
                             reads=[smb["den4"]], writes=[smb["rden4"]])
                        emit(DVE, lambda: nc.vector.tensor_tensor(
                            out=Dn[:], in0=ident[:].unsqueeze(1).to_broadcast([128, 4, 128]),
                            in1=sm["rden4"][:].unsqueeze(2).to_broadcast([128, 4, 128]), op=ALU.mult),
                            reads=[smb["rden4"]] + CB, writes=[Dnb])
                        rd = Peb + [Dnb]
                        _pre(PE, rd, [pTb])
                        inst = None
                        for sl in range(4):
                            for kh in range(2):
                                idx = sl * 2 + kh
                                inst = nc.tensor.matmul(pT[:, idx * 128:(idx + 1) * 128], Pe[:, sl, kh * 128:(kh + 1) * 128],
                                                        Dn[:, sl, :], start=True, stop=True)
                        tok_ = PE.mark(inst)
                        _commit(tok_, rd, [pTb])
                        emit(ACT, lambda: nc.scalar.copy(out=PTs[:], in_=pT[:]), reads=[pTb], writes=[PTsb])

                    def attn_B2(bi, o, hp, par, hb_):
                        PTs, PTsb = PTs2[hb_], PTsb2[hb_]
                        for ci in range(2):
                            i = 2 * hp + ci
                            mms = []
                            for kv in range(2):
                                for kh in range(2):
                                    idx = (kv * 2 + ci) * 2 + kh
                                    mms.append((pO[:, i * 128:(i + 1) * 128], Vt[:, bi - 1 + kh, kv * 64:kv * 64 + 128],
                                                PTs[:, idx * 128:(idx + 1) * 128]))
                            pe_group(mms, reads=[PTsb, Vtb[bi - 1], Vtb[bi]], writes=[pOb])

                    def attn_fin(o, par):
                        emit(ACT, lambda: nc.scalar.copy(out=mixb2[par][:, 0:4, o:o + 128],
                                                         in_=pO[:].rearrange("p (a b) -> p a b", b=128)),
                             reads=[pOb], writes=mixbb2[par][0:4])

                    def front_steps(kind, a, b, par, tidx=None, nxt=None):
                        W = b - a
                        box = {}

                        def s0():
                            box["hs"], box["hsb"] = kchunk(a, b, tidx)

                        def s_next():
                            if nxt is not None:
                                norm_only(nxt[1], nxt[2], tidx + 1)

                        def s1():
                            hs, hsb = box["hs"], box["hsb"]
                            for bo in range(W // 128):
                                bi = a // 128 + bo
                                kz = state["z"] % 2
                                state["z"] += 1
                                pe_group([(pZ[kz][:, 0:128], hs[:, kc, bo * 128:(bo + 1) * 128], Win[:, kc, 640:768])
                                          for kc in range(8)], reads=[Winb[1]] + hsb, writes=[pZb[kz]])
                                emit(ACT, lambda: nc.scalar.copy(out=Vt[:, bi, 0:64], in_=pZ[kz][:, 0:64]),
                                     reads=[pZb[kz]], writes=[Vtb[bi]])
                                emit(ACT, lambda: nc.scalar.copy(out=Vt[:, bi, 128:192], in_=pZ[kz][:, 64:128]),
                                     reads=[pZb[kz]], writes=[Vtb[bi]])
                                if bi == 16:
                                    emit(ACT, lambda: nc.scalar.copy(out=vlast[:], in_=pZ[kz][:, 0:128]),
                                         reads=[pZb[kz]], writes=[vlastb])
                                    out_toks.append(QS.start(vl_o, vlast[:], reads=[vlastb]))

                        uv = lambda c: ubuf[:, c, 2:2 + W]
                        pzv = lambda k: pZ[k][:, :W]
                        ubw_ = [[ubb[c]] for c in range(4)]

                        def s3a():
                            if state["wprev"] is not None:
                                wp = state["wprev"]
                                emit(DVE, lambda: nc.vector.tensor_copy(out=ubuf[:, :, 0:2], in_=ubuf[:, :, wp:wp + 2]),
                                     reads=ubb, writes=ubb)
                            else:
                                emit(DVE, lambda: nc.vector.memset(ubuf[:, :, 0:2], 0.0), writes=ubb)
                            state["wprev"] = W

                        def s4():
                            k_norm(a, b, kind)
                            if b == CS:
                                out_toks.append(QS.start(kTl_o, knf[:, W - 128:W], reads=[knfb]))
                                out_toks.append(QS.start(cl_o, ubuf[:, :, W:W + 2], reads=ubb))

                        def mk(f, *args):
                            return lambda: f(*args)
                        steps = [s0, s1]
                        if kind != "halo":
                            steps += [mk(lambda i: q_chunk1(box["hs"], box["hsb"], W, i), i) for i in range(4)]
                        steps.append(s3a)
                        steps += [mk(lambda c: u_c1(box["hs"], box["hsb"], W, uv, pzv, ubw_, c), c) for c in range(4)]
                        steps += [mk(lambda c: u_h1(box["hs"], box["hsb"], W, uv, pzv, ubw_, c), c) for c in range(4)]
                        steps.append(s4)
                        if kind == "halo":
                            return steps + [s_next]
                        steps += [mk(lambda i: head_norm(zq[:, i, :W], W, qkg8[:, 0:1], qnb2[par][:, i, :W], [qnbb2[par][i]], [zqb[i]], slot=i), i)
                                  for i in range(4)]
                        steps += [mk(lambda c: conv_only(W, lambda c_: cv[:, c_, :W],
                                                         lambda c_: [ubuf[:, c_, j:j + W] for j in range(3)],
                                                         lambda c_: [ubb[c_]], cs=[c]), c) for c in range(4)]
                        steps += [mk(lambda c: gate_only(box["hs"], box["hsb"], W, par, cs=[c]), c) for c in range(4)]
                        steps.insert(min(len(steps), KNPOS), s_next)
                        return steps

                    hpc = [0]

                    def back_steps(a, b, par):
                        W = b - a
                        passes = []
                        for bo in range(W // 128):
                            for hp in range(2):
                                passes.append((a // 128 + bo, bo * 128, hp, hpc[0] % 2))
                                hpc[0] += 1
                        steps = []
                        n = len(passes)

                        def mkA(p):
                            return lambda: attn_A(p[0], p[1], p[2], par, p[3])

                        def mkB1(idx):
                            p = passes[idx]
                            return lambda: attn_B(p[0], p[1], p[2], par, p[3])

                        def mkB2(idx):
                            p = passes[idx]

                            def f():
                                attn_B2(p[0], p[1], p[2], par, p[3])
                                if p[2] == 1:
                                    attn_fin(p[1], par)
                            return f
                        steps.append((mkA(passes[0]), SLOT[0]))
                        for idx in range(n):
                            if idx + 1 < n:
                                steps.append((mkA(passes[idx + 1]), SLOT[0]))
                            steps.append((mkB1(idx), SLOT[1]))
                            steps.append((mkB2(idx), SLOT[2]))
                        for j0 in range(0, 8, 2):
                            steps.append(((lambda j0_: (lambda: out_proj(a, b, par, range(j0_, j0_ + 2))))(j0), SLOT[3]))
                        return steps

                    pW = None

                    def warm():
                        for _ in range(NWARM):
                            nc.tensor.matmul(pW[:, :], ident[:], kT_all[:, 0:512], start=True, stop=True)

                    def interleave(bs, fs):
                        out = []
                        j = 0
                        for (st, k) in bs:
                            if NWARM:
                                out.append(warm)
                            out.append(st)
                            for _ in range(k):
                                if j < len(fs):
                                    out.append(fs[j]); j += 1
                        out += fs[j:]
                        return out

                    tiles = [("halo", 0, HALO)] + [("prompt", HALO + i * MT, HALO + (i + 1) * MT) for i in range(NPR // MT)]
                    def nx(t):
                        return tiles[t + 1] if t + 1 < len(tiles) else None
                    norm_only(tiles[0][1], tiles[0][2], 0)
                    for st_ in front_steps(*tiles[0], 0, tidx=0, nxt=nx(0)):
                        st_()
                    for st_ in front_steps(*tiles[1], 1, tidx=1, nxt=nx(1)):
                        st_()
                    for t in range(1, len(tiles)):
                        bs = back_steps(tiles[t][1], tiles[t][2], t % 2)
                        fs = front_steps(*tiles[t + 1], (t + 1) % 2, tidx=t + 1, nxt=nx(t + 1)) if t + 1 < len(tiles) else []
                        for st_ in interleave(bs, fs):
                            st_()
                    barrier()

                with ExitStack() as S_:
                    qnb, qnbb, mixb, mixbb = qnb2[0], qnbb2[0], mixb2[0], mixbb2[0]
                    kTs = sb(S_, [128, NSEQ, 128], BF16, "kTs")
                    Vc = sb(S_, [128, NSEQ, 192], BF16, "Vc")
                    Vn = sb(S_, [4, NSEQ, 192], BF16, "Vn")
                    vsnf = sb(S_, [4, 8, 128], F32, "vsnf")
                    vsnfb = Buf()
                    Vnb = Buf()
                    ubs = sb(S_, [128, 4, NSEQ, 6], F32, "ubs")
                    ubsb = Buf()
                    qs = sb(S_, [128, NSEQ, 16], BF16, "qs")
                    qsb = Buf()
                    NQ = 4
                    NSL = 2 * NQ
                    Ss = sb(S_, [16, NSL, 132], F32, "Ss")
                    Ssb_s = Buf()
                    Pns = sb(S_, [16, NSL, 132], BF16, "Pns")
                    Pnsb = Buf()
                    PTcs = sb(S_, [128, NSL * 16], BF16, "PTcs")
                    PTns = sb(S_, [4, NSL * 16], BF16, "PTns")
                    PTcsb = Buf()
                    ss = {n: sb(S_, [16, NSL], F32, "s" + n) for n in ("mx", "m", "rs", "tmp", "es", "den", "rden")}
                    ssb = {n: Buf() for n in ss}
                    kTsb = Buf()
                    Vcb = Buf()
                    emit(DVE, lambda: nc.vector.memset(Vc[:, :, 64:128], 0.0), writes=[Vcb])
                    emit(DVE, lambda: nc.vector.memset(Vn[:], 0.0), writes=[Vnb])
                    QP.start(kTs[:], ckT_d, writes=[kTsb])
                    QP.start(Vc[:, :, 0:64], cvt_d[:, :, 0:64], writes=[Vcb])
                    QP.start(Vc[:, :, 128:192], cvt_d[:, :, 64:128], writes=[Vcb])
                    st_c = sb(S_, [128, 128], F32, "st_c")
                    st_cb = Buf()
                    cs_c = sb(S_, [128, 128], F32, "cs_c")
                    cs_cb = Buf()
                    ubs_st = Buf()
                    QS.start(st_c[:], scT_d.rearrange("p c q r -> p (c q r)"), writes=[st_cb])
                    emit(DVE, lambda: nc.vector.tensor_copy(out=ubs[:, :, :, 0:2],
                                                            in_=st_c[:].rearrange("p (c q r) -> p c q r", c=4, r=2)),
                         reads=[st_cb], writes=[ubs_st])

                    a, b = CS, NT
                    W = NSM
                    state["ti"] = 0
                    hs, hsb = front0(a, b)
                    pV = pA[0:4, :]
                    for seq in range(NSEQ):
                        pe_group([(pV[:, seq * 128:(seq + 1) * 128], hs[:, kc, seq * 4:(seq + 1) * 4],
                                   Win[:, kc, 640:768]) for kc in range(8)],
                                 reads=[Winb[1]] + hsb, writes=[bA[seq // 4]])
                    pVv = pV.rearrange("p (q n) -> p q n", n=128)
                    emit(ACT, lambda: nc.scalar.copy(out=Vn[:, :, 0:64], in_=pVv[:, :, 0:64]), reads=bA, writes=[Vnb])
                    emit(ACT, lambda: nc.scalar.copy(out=Vn[:, :, 128:192], in_=pVv[:, :, 64:128]), reads=bA, writes=[Vnb])
                    for hh in range(2):
                        emit(ACT, lambda: nc.scalar.copy(out=vsnf[:], in_=pVv[:, hh * 8:(hh + 1) * 8, :]),
                             reads=bA, writes=[vsnfb])
                        out_toks.append(QS.start(vsn_o[:, hh * 8:(hh + 1) * 8, :], vsnf[:], reads=[vsnfb]))
                    q_chunks(hs, hsb, W)
                    u_chunks(hs, hsb, W, lambda c: ubs[:, c, :, 2:6],
                             lambda k: pZ[k][:, 0:NSM].rearrange("p (q s) -> p q s", s=4), [[ubsb]] * 4)
                    k_norm(a, b, "sample")
                    out_toks.append(QS.start(ksn_o, knf[:, :W], reads=[knfb]))
                    emit(DVE, lambda: nc.vector.tensor_copy(out=cs_c[:].rearrange("p (c q r) -> p c q r", c=4, r=2),
                                                            in_=ubs[:, :, :, 4:6]), reads=[ubsb], writes=[cs_cb])
                    out_toks.append(QS.start(csn_o.rearrange("p c q r -> p (c q r)"), cs_c[:], reads=[cs_cb]))
                    for i in range(4):
                        head_norm(zq[:, i, :W], W, qkg8[:, 0:1], qnb[:, i, :W], [qnbb[i]], [zqb[i]], slot=i)
                    conv_only(W, lambda c: cv[:, c, 0:NSM].rearrange("p (q s) -> p q s", s=4),
                              lambda c: [ubs[:, c, :, j:j + 4] for j in range(3)], lambda c: [ubsb, ubs_st])
                    gate_only(hs, hsb, W, 0)
                    emit(DVE, lambda: nc.vector.tensor_copy(
                        out=qs[:].rearrange("p q (g s) -> p q g s", s=4),
                        in_=qnb[:, :, 0:NSM].rearrange("p g (q s) -> p q g s", s=4)),
                        reads=qnbb, writes=[qsb])
                    pSc = pA[0:16, 0:NSL * 128]
                    nbank = (NSL * 128) // 512
                    for part in range(NSEQ // NQ):
                        for si in range(NQ):
                            seq = part * NQ + si
                            for kv in range(2):
                                slot = kv * NQ + si
                                pe_group([(pSc[:, slot * 128:(slot + 1) * 128], qs[kv * 64:(kv + 1) * 64, seq, :],
                                           kTs[kv * 64:(kv + 1) * 64, seq, :])],
                                         reads=[qsb, kTsb], writes=[bA[slot // 4]])
                                pnew, pnewb = ((pO, pOb), (pstat, pstat_b))[kv]
                                pe_group([(pnew[0:16, si * 4:(si + 1) * 4], qs[kv * 64:(kv + 1) * 64, seq, :],
                                           kT_all[kv * 64:(kv + 1) * 64, CS + seq * 4:CS + seq * 4 + 4])],
                                         reads=[qsb, kTb[17]], writes=[pnewb])
                        emit(DVE, lambda: nc.vector.tensor_scalar(
                            out=Ss[:, :, 0:128], in0=pSc.rearrange("p (q n) -> p q n", n=128), scalar1=1.0,
                            scalar2=None, op0=ALU.mult), reads=bA[0:nbank], writes=[Ssb_s])
                        for kv in range(2):
                            pnew, pnewb = ((pO, pOb), (pstat, pstat_b))[kv]
                            emit(DVE, lambda: nc.vector.tensor_scalar(
                                out=Ss[:, kv * NQ:(kv + 1) * NQ, 128:132],
                                in0=pnew[0:16, 0:NQ * 4].rearrange("p (q n) -> p q n", n=4),
                                scalar1=1.0, scalar2=None, op0=ALU.mult), reads=[pnewb], writes=[Ssb_s])
                        for kv in range(2):
                            sv = Ss[:, kv * NQ:(kv + 1) * NQ, :]
                            emit(DVE, lambda: nc.vector.tensor_tensor(
                                out=sv, in0=sv, in1=Bias_s[:, kv, :].unsqueeze(1).to_broadcast([16, NQ, 132]),
                                op=ALU.add), reads=[Ssb_s, sconst], writes=[Ssb_s])
                        emit(DVE, lambda: nc.vector.reduce_max(out=ss["mx"][:], in_=Ss[:], axis=AX.X),
                             reads=[Ssb_s], writes=[ssb["mx"]])
                        sbc = sinks_s[:].unsqueeze(2).to_broadcast([16, 2, NQ])
                        emit(DVE, lambda: nc.vector.tensor_tensor(
                            out=ss["m"][:].rearrange("p (k q) -> p k q", k=2),
                            in0=ss["mx"][:].rearrange("p (k q) -> p k q", k=2), in1=sbc, op=ALU.max),
                            reads=[ssb["mx"], sconst2], writes=[ssb["m"]])
                        emit(DVE, lambda: nc.vector.tensor_tensor(
                            out=Ss[:], in0=Ss[:], in1=ss["m"][:].unsqueeze(2).to_broadcast([16, NSL, 132]),
                            op=ALU.subtract), reads=[Ssb_s, ssb["m"]], writes=[Ssb_s])
                        emit(ACT, lambda: nc.scalar.activation(out=Ss[:], in_=Ss[:], func=AF.Exp),
                             reads=[Ssb_s], writes=[Ssb_s])
                        emit(DVE, lambda: nc.vector.reduce_sum(out=ss["rs"][:], in_=Ss[:], axis=AX.X),
                             reads=[Ssb_s], writes=[ssb["rs"]])
                        emit(DVE, lambda: nc.vector.tensor_tensor(
                            out=ss["tmp"][:].rearrange("p (k q) -> p k q", k=2), in0=sbc,
                            in1=ss["m"][:].rearrange("p (k q) -> p k q", k=2), op=ALU.subtract),
                            reads=[ssb["m"], sconst2], writes=[ssb["tmp"]])
                        emit(ACT, lambda: nc.scalar.activation(out=ss["es"][:], in_=ss["tmp"][:], func=AF.Exp),
                             reads=[ssb["tmp"]], writes=[ssb["es"]])
                        emit(DVE, lambda: nc.vector.tensor_tensor(out=ss["den"][:], in0=ss["rs"][:], in1=ss["es"][:],
                                                                  op=ALU.add),
                             reads=[ssb["rs"], ssb["es"]], writes=[ssb["den"]])
                        emit(DVE, lambda: nc.vector.reciprocal(out=ss["rden"][:], in_=ss["den"][:]),
                             reads=[ssb["den"]], writes=[ssb["rden"]])
                        emit(DVE, lambda: nc.vector.tensor_tensor(
                            out=Pns[:], in0=Ss[:], in1=ss["rden"][:].unsqueeze(2).to_broadcast([16, NSL, 132]),
                            op=ALU.mult), reads=[Ssb_s, ssb["rden"]], writes=[Pnsb])
                        trs = []
                        for slot in range(NSL):
                            trs.append((pT[:, slot * 16:(slot + 1) * 16], Pns[0:16, slot, 0:128]))
                            trs.append((pT[0:4, 512 + slot * 16:512 + (slot + 1) * 16], Pns[0:16, slot, 128:132]))
                        _pre(PE, [Pnsb] + CB, [pTb])
                        inst = None
                        for (o_, i_) in trs:
                            inst = nc.tensor.matmul(o_, i_, ident[0:16, 0:16], start=True, stop=True)
                        tok_ = PE.mark(inst)
                        _commit(tok_, [Pnsb] + CB, [pTb])
                        emit(ACT, lambda: nc.scalar.copy(out=PTcs[:], in_=pT[:, 0:NSL * 16]), reads=[pTb], writes=[PTcsb])
                        emit(ACT, lambda: nc.scalar.copy(out=PTns[:], in_=pT[0:4, 512:512 + NSL * 16]), reads=[pTb],
                             writes=[PTcsb])
                        for si in range(NQ):
                            seq = part * NQ + si
                            mms = []
                            for kv in range(2):
                                slot = kv * NQ + si
                                mms.append((pstat[:, si * 16:(si + 1) * 16], Vc[:, seq, kv * 64:kv * 64 + 128],
                                            PTcs[:, slot * 16:(slot + 1) * 16]))
                                mms.append((pstat[:, si * 16:(si + 1) * 16], Vn[0:4, seq, kv * 64:kv * 64 + 128],
                                            PTns[0:4, slot * 16:(slot + 1) * 16]))
                            pe_group(mms, reads=[PTcsb, Vcb, Vnb], writes=[pstat_b])
                        emit(DVE, lambda: nc.vector.tensor_copy(
                            out=mixb[:, 0:4, part * NQ * 4:(part + 1) * NQ * 4].rearrange("p g (q s) -> p q g s", s=4),
                            in_=pstat[:, 0:NQ * 16].rearrange("p (q g s) -> p q g s", g=4, s=4)),
                            reads=[pstat_b], writes=mixbb[0:4])
                    out_proj(a, b, 0, range(8))
                    barrier()
            return out_toks

        def ple_phase(out_toks):
            with ExitStack() as ph:
                alloc_norm(ph, 512)
                Wpg = sb(ph, [128, 8, D], BF16, "Wpg")
                Wpp = sb(ph, [128, 2, D], BF16, "Wpp")
                pe_b = sb(ph, [128, 2, NOUT], BF16, "pe_b")
                wb_ = Buf()
                hb2 = [sb(ph, [128, 8, 512], BF16, "hb2") for _ in range(2)]
                hb2b = [[Buf() for _ in range(8)] for _ in range(2)]
                sg = [sb(ph, [128, 512], F32, "sg") for _ in range(2)]
                sgb = [Buf() for _ in range(2)]
                tp = [sb(ph, [128, 512], F32, "tp") for _ in range(2)]
                tpb = [Buf() for _ in range(2)]
                pG = [ps(ph, [128, 512], F32, "pG") for _ in range(2)]
                pP = [ps(ph, [128, 512], F32, "pP") for _ in range(2)]
                pstat = ps(ph, [128, 512], F32, "pstat")
                pGb = [PBuf() for _ in range(2)]
                pPb = [PBuf() for _ in range(2)]
                pstat_b = PBuf()
                wb2_ = Buf()
                wb3_ = Buf()
                QP.start(Wpg[:], wpg_d.rearrange("(kc p) n -> p kc n", p=128), writes=[wb_])
                QP.start(Wpp[:], wpp_d.rearrange("(kc p) n -> p kc n", p=128), writes=[wb2_])
                QP.start(pe_b[:], pT_d.rearrange("(kc p) n -> p kc n", p=128), writes=[wb3_])
                yTv = yT.rearrange("(c p) t -> p c t", p=128)
                kk = 0
                def ple_norm(ti):
                    a_, b_ = FFN2_TILES[ti]
                    hs_ = hb2[ti % 2]
                    norm_tile(a_, b_, 3, lambda c: hs_[:, c, :b_ - a_], hb2b[ti % 2], pstat, pstat_b)
                ple_norm(0)
                for ti, (a, b) in enumerate(FFN2_TILES):
                    W = b - a
                    hs = hb2[ti % 2]
                    hsb = hb2b[ti % 2]
                    if ti + 1 < len(FFN2_TILES):
                        ple_norm(ti + 1)
                    for j in range(8):
                        k = kk % 2
                        kk += 1
                        pe_group([(pG[k][:, :W], Wpg[:, kc, j * 128:(j + 1) * 128], hs[:, kc, :W]) for kc in range(8)],
                                 reads=[wb_] + hsb, writes=[pGb[k]])
                        pe_group([(pP[k][:, :W], Wpp[:, m, j * 128:(j + 1) * 128], pe_b[:, m, a - HALO:b - HALO])
                                  for m in range(2)], reads=[wb2_, wb3_], writes=[pPb[k]])
                        emit(ACT, lambda: nc.scalar.activation(out=sg[k][:, :W], in_=pG[k][:, :W], func=AF.Sigmoid),
                             reads=[pGb[k]], writes=[sgb[k]])
                        emit(DVE, lambda: nc.vector.tensor_tensor(out=tp[k][:, :W], in0=sg[k][:, :W], in1=pP[k][:, :W],
                                                                  op=ALU.mult),
                             reads=[sgb[k], pPb[k]], writes=[tpb[k]])
                        emit(DVE, lambda: nc.vector.tensor_tensor(out=xres[:, j, a:b], in0=tp[k][:, :W], in1=xres[:, j, a:b],
                                                                  op=ALU.add),
                             reads=[tpb[k]] + xbufs(j, a, b), writes=xbufs(j, a, b))
                    out_toks.append(QS.start(yTv[:, :, a - HALO:b - HALO], xres[:, :, a:b], reads=xbufs_all(a, b)))
                for t in out_toks:
                    SP.wait(t)
                barrier()

        def dbg_finish():
            yTv = yT.rearrange("(c p) t -> p c t", p=128)
            t = QS.start(yTv, xres[:, :, HALO:NT], reads=xbufs_all(0, NT))
            SP.wait(t)

        if stop == "load":
            dbg_finish()
            return nc
        ffn_phase(w1g, w1u, w1d, 0, FFN1_TILES)
        if stop == "ffn1":
            dbg_finish()
            return nc
        toks = mixer_phase()
        if stop == "mixer":
            dbg_finish()
            return nc
        ffn_phase(w2g, w2u, w2d, 2, FFN2_TILES)
        if stop == "ffn2":
            dbg_finish()
            return nc
        ple_phase(toks)
    return nc


def _rel_bucket_np(d):
    max_exact = 16
    df = np.maximum(d, 1).astype(np.float32)
    val = (np.log(df / np.float32(max_exact)) / np.float32(math.log(128 / 16)) * np.float32(32 - max_exact))
    large = max_exact + val.astype(np.int32)
    large = np.minimum(large, 31)
    return np.where(d < max_exact, d, large)


def _consts():
    E2 = np.zeros((33, 383), np.float32)
    for m in range(383):
        d = 255 - m
        if 0 <= d < 128:
            E2[int(_rel_bucket_np(np.array([d]))[0]), m] = 1.0
        else:
            E2[32, m] = -1e30
    onesD = np.full((128, 128), 1.0 / D, np.float32)
    bd64 = np.zeros((128, 128), np.float32)
    bd64[:64, :64] = 1.0 / 64
    bd64[64:, 64:] = 1.0 / 64
    ident = np.eye(128, dtype=np.float32)
    return E2, onesD, bd64, ident


_NC_CACHE = {}


def kernel(x_prompt, x_sample, p_prompt, p_sample, cache_k, cache_v, state_conv, rel_bias,
           g_ffn1, w1_gate, w1_up, w1_down, g_mix, w_in, q_norm, k_norm, sinks, w_conv, w_out,
           g_ffn2, w2_gate, w2_up, w2_down, g_ple, w_ple_gate, w_ple_proj):
    f = lambda a: np.ascontiguousarray(np.asarray(a, dtype=np.float32))
    x_prompt, x_sample, p_prompt, p_sample = f(x_prompt), f(x_sample), f(p_prompt), f(p_sample)
    cache_k, cache_v, state_conv, rel_bias = f(cache_k), f(cache_v), f(state_conv), f(rel_bias)
    E2, onesD, bd64, ident = _consts()
    qperm = np.concatenate([np.r_[i * 64:(i + 1) * 64, (4 + i) * 64:(5 + i) * 64] for i in range(4)])
    win = f(w_in)[0]
    win_p = f(np.concatenate([win[:, qperm], win[:, 512:]], axis=1))
    wout = f(w_out)[0]
    wout_p = f(np.concatenate([wout[qperm, :], wout[512:, :]], axis=0))
    gv = f(np.stack([f(g_ffn1)[0], f(g_mix)[0], f(g_ffn2)[0], f(g_ple)[0]]).reshape(4, 8, 128).transpose(2, 0, 1))
    qkg = f(np.stack([np.tile(f(q_norm)[0], 2), np.tile(f(k_norm)[0], 2)], axis=1))
    sk = f(sinks)[0]
    sinkb = f(np.broadcast_to(sk[HPERM][None, :], (128, 8)))
    sinks_s = np.zeros((16, 2), np.float32)
    for g in range(4):
        for kv in range(2):
            sinks_s[g * 4:(g + 1) * 4, kv] = sk[kv * 4 + g]
    tabp = f(rel_bias[:, HPERM])
    wconv = f(f(w_conv)[0].reshape(3, 4, 128).transpose(2, 1, 0))
    shared = {
        "tabp": tabp, "E2": E2, "onesD": onesD, "bd64": bd64, "ident": ident, "gv": gv, "qkg": qkg,
        "sinkb": sinkb, "sinks_s": sinks_s, "wconv": wconv,
        "w1g": f(w1_gate)[0], "w1u": f(w1_up)[0], "w1d": f(w1_down)[0], "win": win_p, "wout": wout_p,
        "w2g": f(w2_gate)[0], "w2u": f(w2_up)[0], "w2d": f(w2_down)[0], "wpg": f(w_ple_gate)[0],
        "wpp": f(w_ple_proj)[0],
    }
    in_maps = []
    for c in range(NCORES):
        b, j = divmod(c, 4)
        t0 = j * NPR
        halo = x_prompt[b, t0 - HALO:t0] if j > 0 else np.zeros((HALO, D), np.float32)
        sq = slice(c * NSEQ, (c + 1) * NSEQ)
        xs = x_sample[sq].reshape(NSM, D)
        xT = f(np.concatenate([halo, x_prompt[b, t0:t0 + NPR], xs], axis=0).T)
        pT = f(np.concatenate([p_prompt[0, b, t0:t0 + NPR], p_sample[0, sq].reshape(NSM, DPLE)], axis=0).T)
        ck = cache_k[0, sq].reshape(NSEQ, 128, 128)
        cvv = cache_v[0, sq].reshape(NSEQ, 128, 128)
        sc = state_conv[0, sq]
        scT = f(sc.transpose(2, 0, 1).reshape(4, 128, NSEQ, 2).transpose(1, 0, 2, 3))
        m = dict(shared)
        m.update({
            "xT": xT, "pT": pT, "ckT": f(ck.transpose(2, 0, 1)), "ck_nat": f(ck),
            "cvt": f(cvv.transpose(1, 0, 2)), "cv_nat": f(cvv), "scT": scT,
            "hflag": np.full((128, 1), -1e30 if j == 0 else 0.0, np.float32),
        })
        in_maps.append(m)

    if "nc" not in _NC_CACHE:
        _NC_CACHE["nc"] = build_nc()
    nc = _NC_CACHE["nc"]
    res = run_bass_kernel_spmd(nc, in_maps, core_ids=list(range(NCORES)))
    R = res.results

    B, T = x_prompt.shape[0], x_prompt.shape[1]
    y_prompt = np.zeros((B, T, D), np.float32)
    y_sample = np.zeros((x_sample.shape[0], 4, D), np.float32)
    kwp = np.zeros((1, B, 128, 2, 64), np.float32)
    vwp = np.zeros((1, B, 128, 2, 64), np.float32)
    cvp = np.zeros((1, B, 2, 512), np.float32)
    kws = np.zeros((1, x_sample.shape[0], 128, 2, 64), np.float32)
    vws = np.zeros((1, x_sample.shape[0], 128, 2, 64), np.float32)
    cvs = np.zeros((1, x_sample.shape[0], 2, 512), np.float32)
    for c in range(NCORES):
        b, j = divmod(c, 4)
        t0 = j * NPR
        r = R[c]
        sq = slice(c * NSEQ, (c + 1) * NSEQ)
        y = np.asarray(r["yT"])
        y_prompt[b, t0:t0 + NPR] = y[:, :NPR].T
        y_sample[sq] = y[:, NPR:].T.reshape(NSEQ, 4, D)
        if j == 3:
            kwp[0, b] = np.asarray(r["kTl"]).T.reshape(128, 2, 64)
            vwp[0, b] = np.asarray(r["vl"]).reshape(128, 2, 64)
            cvp[0, b] = np.asarray(r["cl"]).transpose(2, 1, 0).reshape(2, 512)
        kws[0, sq, 0:124] = np.asarray(r["kws_old"]).reshape(NSEQ, 124, 2, 64)
        kws[0, sq, 124:128] = np.asarray(r["ksn"]).T.reshape(NSEQ, 4, 2, 64)
        vws[0, sq, 0:124] = np.asarray(r["vws_old"]).reshape(NSEQ, 124, 2, 64)
        vws[0, sq, 124:128] = np.asarray(r["vsn"]).transpose(1, 0, 2).reshape(NSEQ, 4, 2, 64)
        cvs[0, sq] = np.asarray(r["csn"]).transpose(2, 3, 1, 0).reshape(NSEQ, 2, 512)
    return (y_prompt, y_sample, kwp, vwp, cvp, kws, vws, cvs)
```

```python
import math
from contextlib import ExitStack

import numpy as np
import concourse.bass as bass
import concourse.mybir as mybir
from concourse.bass_utils import run_bass_kernel_spmd

F32 = mybir.dt.float32
BF16 = mybir.dt.bfloat16
AF = mybir.ActivationFunctionType
ALU = mybir.AluOpType
AX = mybir.AxisListType

NCORES = 8
D = 1024
DFF = 2816
NFF = 22
DPLE = 256
INC = 2304
HALO = 128
NPR = 2048
NSM = 64
NT = HALO + NPR + NSM
NOUT = NPR + NSM
CS = HALO + NPR
MT = 256
EPS = 1e-6
NSEQ = 16
HPERM = [0, 1, 4, 5, 2, 3, 6, 7]

FFN_GROUPS = [(0, 5), (5, 10), (10, 14), (14, 18), (18, 22)]
FFN1_TILES = [(0, 128), (128, 640), (640, 1152), (1152, 1664), (1664, 2176), (2176, 2240)]
FFN2_TILES = FFN1_TILES[1:]


class Eng:
    def __init__(self, eng, sem, is_pe=False):
        self.eng = eng
        self.sem = sem
        self.cnt = 0
        self.waited = {}
        self.is_pe = is_pe

    def wait(self, tok):
        if tok is None:
            return
        s, v = tok
        if self.is_pe and s is self.sem:
            return
        if self.waited.get(s.num, 0) >= v:
            return
        self.eng.wait_ge(s, v)
        self.waited[s.num] = v

    def mark(self, inst):
        self.cnt += 1
        inst.then_inc(self.sem, 1)
        return (self.sem, self.cnt)


class _Stop(Exception):
    pass


class Buf:
    __slots__ = ("w", "r", "excl")

    def __init__(self, excl=False):
        self.w = None
        self.r = {}
        self.excl = excl


def PBuf():
    return Buf(excl=True)


def _pre(E, reads, writes):
    for b in reads:
        E.wait(b.w)
        if b.excl:
            for t in list(b.r.values()):
                if t[0] is not E.sem:
                    E.wait(t)
    for b in writes:
        E.wait(b.w)
        for t in list(b.r.values()):
            E.wait(t)


def _commit(tok, reads, writes):
    for b in reads:
        cur = b.r.get(tok[0].num)
        if cur is None or cur[1] < tok[1]:
            b.r[tok[0].num] = tok
    for b in writes:
        b.w = tok
        b.r = {}


def emit(E, fn, reads=(), writes=()):
    _pre(E, reads, writes)
    tok = E.mark(fn())
    _commit(tok, reads, writes)
    return tok


class DmaQ:
    def __init__(self, E, sems):
        self.E = E
        self.slots = [[s, 0] for s in sems]
        self.i = 0

    def start(self, out, in_, reads=(), writes=()):
        E = self.E
        _pre(E, reads, writes)
        slot = self.slots[self.i % len(self.slots)]
        self.i += 1
        if slot[1]:
            E.wait((slot[0], slot[1]))
        inst = E.eng.dma_start(out=out, in_=in_)
        slot[1] += 16
        inst.then_inc(slot[0], 16)
        tok = (slot[0], slot[1])
        _commit(tok, reads, writes)
        return tok

    def outstanding(self):
        return [(s, v) for s, v in self.slots if v]


def build_nc(stop=None):
    nc = bass.Bass("TRN2", target_bir_lowering=False)

    def din(name, shape):
        return nc.dram_tensor(name, list(shape), F32, kind="ExternalInput").ap()

    def dout(name, shape):
        return nc.dram_tensor(name, list(shape), F32, kind="ExternalOutput").ap()

    xT = din("xT", [D, NT])
    pT_d = din("pT", [DPLE, NOUT])
    ckT_d = din("ckT", [128, NSEQ, 128])
    ck_nat = din("ck_nat", [NSEQ, 128, 128])
    cvt_d = din("cvt", [128, NSEQ, 128])
    cv_nat = din("cv_nat", [NSEQ, 128, 128])
    scT_d = din("scT", [128, 4, NSEQ, 2])
    tabp_d = din("tabp", [32, 8])
    E2_d = din("E2", [33, 383])
    onesD_d = din("onesD", [128, 128])
    bd64_d = din("bd64", [128, 128])
    ident_d = din("ident", [128, 128])
    gv_d = din("gv", [128, 4, 8])
    qkg_d = din("qkg", [128, 2])
    sinkb_d = din("sinkb", [128, 8])
    sinks_d = din("sinks_s", [16, 2])
    wconv_d = din("wconv", [128, 4, 3])
    hflag_d = din("hflag", [128, 1])
    w1g = din("w1g", [D, DFF])
    w1u = din("w1u", [D, DFF])
    w1d = din("w1d", [DFF, D])
    win_d = din("win", [D, INC])
    wout_d = din("wout", [D, D])
    w2g = din("w2g", [D, DFF])
    w2u = din("w2u", [D, DFF])
    w2d = din("w2d", [DFF, D])
    wpg_d = din("wpg", [D, D])
    wpp_d = din("wpp", [DPLE, D])

    yT = dout("yT", [D, NOUT])
    kTl_o = dout("kTl", [128, 128])
    vl_o = dout("vl", [128, 128])
    cl_o = dout("cl", [128, 4, 2])
    kws_o = dout("kws_old", [NSEQ, 124, 128])
    vws_o = dout("vws_old", [NSEQ, 124, 128])
    ksn_o = dout("ksn", [128, NSM])
    vsn_o = dout("vsn", [4, NSEQ, 128])
    csn_o = dout("csn", [128, 4, NSEQ, 2])

    Ubc_t = nc.dram_tensor("Ubc", [128, 8, 383], F32, kind="Internal")
    Ubc = Ubc_t.ap()

    uid = [0]

    with ExitStack() as top:
        def sem(name):
            return top.enter_context(nc.semaphore(name))

        PE = Eng(nc.tensor, sem("s_pe"), is_pe=True)
        ACT = Eng(nc.scalar, sem("s_act"))
        DVE = Eng(nc.vector, sem("s_dve"))
        POOL = Eng(nc.gpsimd, sem("s_pool"))
        SP = Eng(nc.sync, sem("s_sp"))
        import os
        NSQ = int(os.environ.get("KNSQ", "10"))
        KSKIP = os.environ.get("KSKIP", "").split(",")
        CONV_ON_POOL = os.environ.get("KCONVPOOL", "0") == "1"
        NORM_ADD_POOL = os.environ.get("KNORMPOOL", "1") == "1"
        NWARM = int(os.environ.get("KNWARM", "0"))
        KNPOS = int(os.environ.get("KNPOS", "10"))
        SLOT = [int(x) for x in os.environ.get("KSLOT", "2,2,1,2").split(",")]
        QS = DmaQ(SP, [sem("qs%d" % i) for i in range(NSQ)])
        QP = DmaQ(POOL, [sem("qp%d" % i) for i in range(NSQ)])
        ENGS = (PE, ACT, DVE, POOL, SP)

        def sb(stack, shape, dt, name="t"):
            uid[0] += 1
            return stack.enter_context(nc.sbuf_tensor("%s_%d" % (name, uid[0]), list(shape), dt))

        def ps(stack, shape, dt, name="p"):
            uid[0] += 1
            return stack.enter_context(nc.psum_tensor("%s_%d" % (name, uid[0]), list(shape), dt))

        QA_REF = []
        X_REST = []
        AFTER_G1 = []
        PLEW = {}

        def barrier(queues=None):
            toks = [(E.sem, E.cnt) for E in (PE, ACT, DVE, POOL) if E.cnt > 0]
            for q_ in (queues if queues is not None else (QS, QP)):
                toks += q_.outstanding()
            if queues is None and QA_REF:
                toks += QA_REF[0].outstanding()
            for E in ENGS:
                for t in toks:
                    E.wait(t)

        def pe_group(mms, reads=(), writes=()):
            _pre(PE, reads, writes)
            n = len(mms)
            inst = None
            for i, (o, l, r) in enumerate(mms):
                inst = nc.tensor.matmul(o, l, r, start=(i == 0), stop=(i == n - 1))
            tok = PE.mark(inst)
            _commit(tok, reads, writes)
            return tok

        def pe_transposes(trs, ident_ap_fn, reads=(), writes=()):
            _pre(PE, reads, writes)
            inst = None
            for (o, i_) in trs:
                inst = nc.tensor.transpose(o, i_, ident_ap_fn(i_))
            tok = PE.mark(inst)
            _commit(tok, reads, writes)
            return tok

        xres = sb(top, [128, 8, NT], F32, "xres")
        xb = [[Buf() for _ in range(18)] for _ in range(8)]

        def xbufs(c, a, b):
            return [xb[c][k] for k in range(a // 128, (b + 127) // 128)]

        def xbufs_all(a, b):
            r = []
            for c in range(8):
                r += xbufs(c, a, b)
            return r

        onesD = sb(top, [128, 128], BF16, "onesD")
        bd64 = sb(top, [128, 128], BF16, "bd64")
        ident = sb(top, [128, 128], BF16, "ident")
        gv = sb(top, [128, 4, 8], F32, "gv")
        qkg = sb(top, [128, 2], F32, "qkg")
        sinkb = sb(top, [128, 8], F32, "sinkb")
        nsinkb = sb(top, [128, 8], F32, "nsinkb")
        wconv = sb(top, [128, 4, 3], F32, "wconv")
        hflag = sb(top, [128, 1], F32, "hflag")
        epst = sb(top, [128, 1], F32, "epst")
        NS = {}

        def alloc_norm(stack, Wn, nb=2):
            NS["sq"] = [sb(stack, [128, Wn], BF16, "sq") for _ in range(3)]
            NS["sqb"] = [Buf() for _ in range(3)]
            NS["rt"] = [sb(stack, [128, Wn], F32, "rt") for _ in range(nb)]
            NS["rtb"] = [Buf() for _ in range(nb)]
            NS["rstd"] = [sb(stack, [128, Wn], F32, "rstd") for _ in range(nb)]
            NS["rstdb"] = [Buf() for _ in range(nb)]
        CB = []

        def newcb():
            b_ = Buf()
            CB.append(b_)
            return b_
        cnt = {"sq": 0, "rt": 0}

        xTv = xT.rearrange("(c p) t -> p c t", p=128)
        QA = DmaQ(ACT, [sem("qa%d" % i) for i in range(8)])
        QA_REF.append(QA)
        for c in range(8):
            q_ = QS if c % 2 == 0 else QA
            q_.start(xres[:, c, 0:640], xTv[:, c, 0:640], writes=xbufs(c, 0, 640))

        def x_rest():
            for c in range(8):
                QS.start(xres[:, c, 640:NT], xTv[:, c, 640:NT], writes=xbufs(c, 640, NT))
        X_REST.append(x_rest)
        for dst, src in ((gv, gv_d), (qkg, qkg_d), (sinkb, sinkb_d), (wconv, wconv_d), (hflag, hflag_d)):
            QS.start(dst[:], src, writes=[newcb()])
        for dst, src in ((onesD, onesD_d), (bd64, bd64_d), (ident, ident_d)):
            QP.start(dst[:], src, writes=[newcb()])
        emit(DVE, lambda: nc.vector.memset(epst[:], EPS), writes=[newcb()])
        qkg8 = sb(top, [128, 1], F32, "qkg8")
        emit(DVE, lambda: nc.vector.tensor_scalar(out=qkg8[:], in0=qkg[:, 0:1], scalar1=0.125, scalar2=None,
                                                  op0=ALU.mult), reads=list(CB), writes=[newcb()])
        emit(DVE, lambda: nc.vector.tensor_scalar(out=nsinkb[:], in0=sinkb[:], scalar1=-1.0, scalar2=None,
                                                  op0=ALU.mult), reads=list(CB), writes=[newcb()])

        def norm_tile(a, b, gsel, out_fn, out_bufs, pstat, pstat_b):
            sq, sqb, rt, rtb, rstd, rstdb = NS["sq"], NS["sqb"], NS["rt"], NS["rtb"], NS["rstd"], NS["rstdb"]
            W = b - a
            for c in range(8):
                i = cnt["sq"] % 3
                cnt["sq"] += 1
                emit(ACT, lambda: nc.scalar.activation(out=sq[i][:, :W], in_=xres[:, c, a:b], func=AF.Square),
                     reads=xbufs(c, a, b), writes=[sqb[i]])
                _pre(PE, [sqb[i]] + CB, [pstat_b] if c == 0 else [])
                inst = nc.tensor.matmul(pstat[:, :W], onesD[:], sq[i][:, :W], start=(c == 0), stop=(c == 7))
                tok = PE.mark(inst)
                _commit(tok, [sqb[i]] + CB, [pstat_b] if c == 7 else [])
            j = cnt["rt"] % len(NS["rt"])
            cnt["rt"] += 1
            emit(ACT, lambda: nc.scalar.activation(out=rt[j][:, :W], in_=pstat[:, :W], func=AF.Ln,
                                                   bias=epst[:, 0:1], scale=1.0),
                 reads=[pstat_b] + CB, writes=[rtb[j]])
            emit(ACT, lambda: nc.scalar.activation(out=rstd[j][:, :W], in_=rt[j][:, :W], func=AF.Exp, scale=-0.5),
                 reads=[rtb[j]], writes=[rstdb[j]])
            for c in range(8):
                emit(DVE, lambda: nc.vector.scalar_tensor_tensor(
                    out=out_fn(c), in0=xres[:, c, a:b], scalar=gv[:, gsel, c:c + 1], in1=rstd[j][:, :W],
                    op0=ALU.mult, op1=ALU.mult),
                    reads=xbufs(c, a, b) + [rstdb[j]] + CB, writes=[out_bufs[c]])

        def ffn_phase(wg_d, wu_d, wd_d, gsel, tiles):
            with ExitStack() as ph:
                alloc_norm(ph, 512, nb=1)
                hb = sb(ph, [128, 8, NT], BF16, "hb")
                hbb = [[Buf() for _ in range(8)] for _ in tiles]
                Wg = [sb(ph, [128, 8, 640], BF16, "Wg") for _ in range(2)]
                Wu = [sb(ph, [128, 8, 640], BF16, "Wu") for _ in range(2)]
                Wd = [sb(ph, [128, 5, 1024], BF16, "Wd") for _ in range(2)]
                Wgb = [Buf() for _ in range(2)]
                Wub = [Buf() for _ in range(2)]
                Wdb = [Buf() for _ in range(2)]
                A = [sb(ph, [128, 5, 512], BF16, "A") for _ in range(2)]
                Ab = [[Buf() for _ in range(5)] for _ in range(2)]
                S = [sb(ph, [128, 512], F32, "S") for _ in range(2)]
                Sb = [Buf() for _ in range(2)]
                pG = [ps(ph, [128, 512], F32, "pG") for _ in range(2)]
                pU = [ps(ph, [128, 512], F32, "pU") for _ in range(2)]
                pY = [ps(ph, [128, 512], F32, "pY") for _ in range(2)]
                pstat = ps(ph, [128, 512], F32, "pstat")
                pGb = [PBuf() for _ in range(2)]
                pUb = [PBuf() for _ in range(2)]
                pYb = [PBuf() for _ in range(2)]
                pstat_b = PBuf()

                def load_group(gi):
                    c0, c1 = FFN_GROUPS[gi]
                    s = gi % 2
                    gw = (c1 - c0) * 128
                    QP.start(Wg[s][:, :, 0:gw], wg_d[:, c0 * 128:c1 * 128].rearrange("(kc p) n -> p kc n", p=128),
                             writes=[Wgb[s]])
                    QP.start(Wu[s][:, :, 0:gw], wu_d[:, c0 * 128:c1 * 128].rearrange("(kc p) n -> p kc n", p=128),
                             writes=[Wub[s]])
                    QP.start(Wd[s][:, 0:c1 - c0, :], wd_d[c0 * 128:c1 * 128, :].rearrange("(g p) n -> p g n", p=128),
                             writes=[Wdb[s]])

                load_group(0)
                if X_REST:
                    X_REST.pop()()
                def ffn_norm(ti):
                    a_, b_ = tiles[ti]
                    norm_tile(a_, b_, gsel, lambda c: hb[:, c, a_:b_], hbb[ti], pstat, pstat_b)
                ffn_norm(0)
                if len(tiles) > 1:
                    ffn_norm(1)
                load_group(1)
                while AFTER_G1:
                    AFTER_G1.pop(0)()

                items = [(gi, ti) for gi in range(len(FFN_GROUPS)) for ti in range(len(tiles))]
                kc_cnt = [0]
                y_cnt = [0]

                def stage1(idx):
                    gi, ti = items[idx]
                    if gi == 0 and ti + 2 < len(tiles):
                        ffn_norm(ti + 2)
                    a, b = tiles[ti]
                    W = b - a
                    c0, c1 = FFN_GROUPS[gi]
                    s = gi % 2
                    ai = idx % 2
                    for cl in range(c1 - c0):
                        k = kc_cnt[0] % 2
                        kc_cnt[0] += 1
                        pe_group([(pG[k][:, :W], Wg[s][:, kc, cl * 128:(cl + 1) * 128], hb[:, kc, a:b]) for kc in range(8)],
                                 reads=[Wgb[s]] + hbb[ti], writes=[pGb[k]])
                        pe_group([(pU[k][:, :W], Wu[s][:, kc, cl * 128:(cl + 1) * 128], hb[:, kc, a:b]) for kc in range(8)],
                                 reads=[Wub[s]] + hbb[ti], writes=[pUb[k]])
                        emit(ACT, lambda: nc.scalar.activation(out=S[k][:, :W], in_=pG[k][:, :W], func=AF.Silu),
                             reads=[pGb[k]], writes=[Sb[k]])
                        emit(DVE, lambda: nc.vector.tensor_tensor(out=A[ai][:, cl, :W], in0=S[k][:, :W], in1=pU[k][:, :W],
                                                                  op=ALU.mult),
                             reads=[Sb[k], pUb[k]], writes=[Ab[ai][cl]])

                def stage2(idx):
                    gi, ti = items[idx]
                    a, b = tiles[ti]
                    W = b - a
                    c0, c1 = FFN_GROUPS[gi]
                    s = gi % 2
                    ai = idx % 2
                    n = c1 - c0
                    for j in range(8):
                        k = y_cnt[0] % 2
                        y_cnt[0] += 1
                        pe_group([(pY[k][:, :W], Wd[s][:, cl, j * 128:(j + 1) * 128], A[ai][:, cl, :W]) for cl in range(n)],
                                 reads=[Wdb[s]] + Ab[ai][:n], writes=[pYb[k]])
                        emit(DVE, lambda: nc.vector.scalar_tensor_tensor(
                            out=xres[:, j, a:b], in0=pY[k][:, :W], scalar=0.5, in1=xres[:, j, a:b],
                            op0=ALU.mult, op1=ALU.add),
                            reads=[pYb[k]] + xbufs(j, a, b), writes=xbufs(j, a, b))
                    if ti == len(tiles) - 1 and gi + 2 < len(FFN_GROUPS):
                        load_group(gi + 2)

                stage1(0)
                for idx in range(len(items)):
                    if idx + 1 < len(items):
                        stage1(idx + 1)
                    stage2(idx)
                barrier()

        def mixer_phase():
            out_toks = []
            with ExitStack() as ph:
                alloc_norm(ph, MT)
                Win = sb(ph, [128, 8, INC], BF16, "Win")
                Wout = sb(ph, [128, 8, D], BF16, "Wout")
                WinSeg = [(0, 512), (512, 768), (768, 1280), (1280, 1792), (1792, 2304)]
                Winb = [Buf() for _ in WinSeg]
                Woutb = Buf()
                Bias_s = sb(ph, [16, 2, 132], F32, "Bias_s")
                sinks_s = sb(ph, [16, 2], F32, "sinks_s")
                sconst = Buf()
                sconst2 = Buf()
                kT_all = sb(ph, [128, NT], BF16, "kT_all")
                kTb = [Buf() for _ in range(18)]
                hbm = [sb(ph, [128, 8, MT], BF16, "hbm") for _ in range(2)]
                hbmb = [[Buf() for _ in range(8)] for _ in range(2)]
                zq = sb(ph, [128, 4, MT], F32, "zq")
                zqb = [Buf() for _ in range(4)]
                zk = sb(ph, [128, MT], F32, "zk")
                zkb = Buf()
                cv = sb(ph, [128, 4, MT], F32, "cv")
                cvb = [Buf() for _ in range(4)]
                qnb2 = [sb(ph, [128, 4, MT], BF16, "qnb") for _ in range(2)]
                qnbb2 = [[Buf() for _ in range(4)] for _ in range(2)]
                knf = sb(ph, [128, MT], F32, "knf")
                knfb = Buf()
                mixb2 = [sb(ph, [128, 8, MT], BF16, "mixb") for _ in range(2)]
                mixbb2 = [[Buf() for _ in range(8)] for _ in range(2)]

                pA = ps(ph, [128, 2048], F32, "pA")
                bA = [PBuf() for _ in range(4)]
                pstat = ps(ph, [128, 512], F32, "pstat")
                pstat_b = PBuf()
                pT = ps(ph, [128, 1024], F32, "pT")
                pTb = PBuf()
                pO = ps(ph, [128, 512], F32, "pO")
                pOb = PBuf()
                pZ = [pA[:, 0:512], pA[:, 512:1024]]
                pZb = [bA[0], bA[1]]
                pS = pA[:, 1024:2048]
                pSb = [bA[2], bA[3]]

                winv = win_d.rearrange("(kc p) n -> p kc n", p=128)
                for si in (1, 3, 4, 0, 2):
                    lo, hi = WinSeg[si]
                    QP.start(Win[:, :, lo:hi], winv[:, :, lo:hi], writes=[Winb[si]])
                QP.start(Wout[:], wout_d.rearrange("(kc p) n -> p kc n", p=128), writes=[Woutb])
                QS.start(sinks_s[:], sinks_d, writes=[sconst2])
                out_toks.append(QS.start(kws_o, ck_nat[:, 4:128, :]))
                out_toks.append(QS.start(vws_o, cv_nat[:, 4:128, :]))

                state = {"ti": 0, "z": 0, "wprev": None}
                hsq = [sb(ph, [128, MT], BF16, "hsq") for _ in range(5)]
                hsqb = [Buf() for _ in range(5)]

                def presq(slot, src_ap, W, src_bufs):
                    emit(ACT, lambda: nc.scalar.activation(out=hsq[slot][:, :W], in_=src_ap, func=AF.Square),
                         reads=src_bufs, writes=[hsqb[slot]])

                def zmm(wcol, seg, hs, hsb, W):
                    k = state["z"] % 2
                    state["z"] += 1
                    pe_group([(pZ[k][:, :W], Win[:, kc, wcol:wcol + 128], hs[:, kc, :W]) for kc in range(8)],
                             reads=[Winb[seg]] + hsb, writes=[pZb[k]])
                    return k

                def head_norm(src_ap, W, gap, out_ap, out_bufs, src_bufs, slot=None):
                    sq, sqb, rt, rtb, rstd, rstdb = NS["sq"], NS["sqb"], NS["rt"], NS["rtb"], NS["rstd"], NS["rstdb"]
                    pe_group([(pstat[:, :W], bd64[:], hsq[slot][:, :W])], reads=[hsqb[slot]] + CB, writes=[pstat_b])
                    j = cnt["rt"] % len(NS["rt"])
                    cnt["rt"] += 1
                    emit(ACT, lambda: nc.scalar.activation(out=rt[j][:, :W], in_=pstat[:, :W], func=AF.Ln,
                                                           bias=epst[:, 0:1], scale=1.0),
                         reads=[pstat_b] + CB, writes=[rtb[j]])
                    emit(ACT, lambda: nc.scalar.activation(out=rstd[j][:, :W], in_=rt[j][:, :W], func=AF.Exp, scale=-0.5),
                         reads=[rtb[j]], writes=[rstdb[j]])
                    emit(DVE, lambda: nc.vector.scalar_tensor_tensor(
                        out=out_ap, in0=src_ap, scalar=gap, in1=rstd[j][:, :W],
                        op0=ALU.mult, op1=ALU.mult),
                        reads=src_bufs + [rstdb[j]] + CB, writes=out_bufs)

                def norm_only(a, b, ti):
                    W = b - a
                    hs = hbm[ti % 2]
                    norm_tile(a, b, 1, lambda c: hs[:, c, :W], hbmb[ti % 2], pstat, pstat_b)

                def kchunk(a, b, ti):
                    W = b - a
                    hs = hbm[ti % 2]
                    hsb = hbmb[ti % 2]
                    k = zmm(512, 1, hs, hsb, W)
                    emit(ACT, lambda: nc.scalar.copy(out=zk[:, :W], in_=pZ[k][:, :W]), reads=[pZb[k]], writes=[zkb])
                    presq(4, zk[:, :W], W, [zkb])
                    return hs, hsb

                def front0(a, b):
                    ti = state["ti"]
                    state["ti"] += 1
                    norm_only(a, b, ti)
                    return kchunk(a, b, ti)

                def q_chunk1(hs, hsb, W, i):
                    k = zmm(i * 128, 0, hs, hsb, W)
                    emit(ACT, lambda: nc.scalar.copy(out=zq[:, i, :W], in_=pZ[k][:, :W]), reads=[pZb[k]],
                         writes=[zqb[i]])
                    presq(i, zq[:, i, :W], W, [zqb[i]])

                def q_chunks(hs, hsb, W):
                    for i in range(4):
                        q_chunk1(hs, hsb, W, i)

                def u_c1(hs, hsb, W, uview, pzview, ubw, c):
                    k = zmm(1280 + c * 128, 3, hs, hsb, W)
                    emit(ACT, lambda: nc.scalar.copy(out=uview(c), in_=pzview(k)), reads=[pZb[k]], writes=ubw[c])

                def u_h1(hs, hsb, W, uview, pzview, ubw, c):
                    k = zmm(1792 + c * 128, 4, hs, hsb, W)
                    emit(DVE, lambda: nc.vector.tensor_tensor(out=uview(c), in0=uview(c), in1=pzview(k), op=ALU.mult),
                         reads=[pZb[k]] + ubw[c], writes=ubw[c])

                def u_chunks(hs, hsb, W, uview, pzview, ubw):
                    for c in range(4):
                        u_c1(hs, hsb, W, uview, pzview, ubw, c)
                    for c in range(4):
                        u_h1(hs, hsb, W, uview, pzview, ubw, c)

                def k_norm(a, b, kind):
                    W = b - a
                    head_norm(zk[:, :W], W, qkg[:, 1:2], knf[:, :W], [knfb], [zkb], slot=4)
                    blks = [17] if kind == "sample" else list(range(a // 128, b // 128))
                    emit(ACT, lambda: nc.scalar.copy(out=kT_all[:, a:b], in_=knf[:, :W]), reads=[knfb],
                         writes=[kTb[x] for x in blks])

                def conv_only(W, cvo_fn, taps_fn, rb_fn, cs=range(4)):
                    for c in cs:
                        cvo = cvo_fn(c)
                        taps = taps_fn(c)
                        rb = rb_fn(c)
                        CE, ce = (POOL, nc.gpsimd) if CONV_ON_POOL else (DVE, nc.vector)
                        emit(CE, lambda: ce.tensor_scalar(out=cvo, in0=taps[2], scalar1=wconv[:, c, 2:3],
                                                          scalar2=None, op0=ALU.mult),
                             reads=rb + CB, writes=[cvb[c]])
                        for j in (1, 0):
                            emit(CE, lambda: ce.scalar_tensor_tensor(
                                out=cvo, in0=taps[j], scalar=wconv[:, c, j:j + 1], in1=cvo, op0=ALU.mult, op1=ALU.add),
                                reads=rb + CB + [cvb[c]], writes=[cvb[c]])

                def gate_only(hs, hsb, W, par, cs=range(4)):
                    for c in cs:
                        k = zmm(768 + c * 128, 2, hs, hsb, W)
                        emit(DVE, lambda: nc.vector.tensor_tensor(out=mixb2[par][:, 4 + c, :W], in0=pZ[k][:, :W],
                                                                  in1=cv[:, c, :W], op=ALU.mult),
                             reads=[pZb[k], cvb[c]], writes=[mixbb2[par][4 + c]])

                def out_proj(a, b, par, js):
                    W = b - a
                    for j in js:
                        k = state["z"] % 2
                        state["z"] += 1
                        pe_group([(pZ[k][:, :W], Wout[:, m, j * 128:(j + 1) * 128], mixb2[par][:, m, :W]) for m in range(8)],
                                 reads=[Woutb] + mixbb2[par], writes=[pZb[k]])
                        emit(DVE, lambda: nc.vector.tensor_tensor(out=xres[:, j, a:b], in0=pZ[k][:, :W], in1=xres[:, j, a:b],
                                                                  op=ALU.add),
                             reads=[pZb[k]] + xbufs(j, a, b), writes=xbufs(j, a, b))

                with ExitStack() as P:
                    Bhi = sb(P, [128, 8, 256], BF16, "Bhi")
                    Blo = sb(P, [128, 8, 256], BF16, "Blo")
                    Hm = sb(P, [128, 256], BF16, "Hm")
                    Bhib, Blob, Hmb = Buf(), Buf(), Buf()
                    Biasb = Buf()
                    Vt = sb(P, [128, 17, 192], BF16, "Vt")
                    Vtb = [Buf() for _ in range(17)]
                    emit(DVE, lambda: nc.vector.memset(Vt[:], 0.0), writes=Vtb)
                    with ExitStack() as su:
                        tab_sb = sb(su, [33, 8], F32, "tab_sb")
                        tabB = sb(su, [33, 8, 128], F32, "tabB")
                        E2_sb = sb(su, [33, 383], F32, "E2_sb")
                        Ubc_sb = sb(su, [128, 8, 383], F32, "Ubc_sb")
                        Bias = sb(su, [128, 8, 256], F32, "Bias")
                        tb = Buf()
                        eb = Buf()
                        ub = Buf()
                        emit(DVE, lambda: nc.vector.memset(tab_sb[:], 1.0), writes=[tb])
                        QS.start(tab_sb[0:32, :], tabp_d, writes=[tb])
                        QS.start(E2_sb[:], E2_d, writes=[eb])
                        emit(DVE, lambda: nc.vector.tensor_copy(out=tabB[:], in_=tab_sb[:].unsqueeze(2).to_broadcast([33, 8, 128])),
                             reads=[tb], writes=[tb])
                        for h in range(8):
                            k = h % 2
                            pe_group([(pZ[k][:, 0:383], tabB[:, h, :], E2_sb[:])], reads=[tb, eb], writes=[pZb[k]])
                            emit(ACT, lambda: nc.scalar.copy(out=Ubc_sb[:, h, :], in_=pZ[k][:, 0:383]), reads=[pZb[k]], writes=[ub])
                        dsc = Buf()
                        QS.start(Ubc, Ubc_sb[:], reads=[ub], writes=[dsc])
                        src = bass.AP(Ubc_t, 127, [[8 * 383 - 1, 128], [383, 8], [1, 256]])
                        QS.start(Bias[:], src, reads=[dsc], writes=[Biasb])
                        emit(DVE, lambda: nc.vector.tensor_copy(out=Bhi[:], in_=Bias[:]), reads=[Biasb], writes=[Bhib])
                        emit(DVE, lambda: nc.vector.tensor_tensor(out=Blo[:], in0=Bias[:], in1=Bhi[:], op=ALU.subtract),
                             reads=[Biasb, Bhib], writes=[Blob])
                        emit(DVE, lambda: nc.vector.memset(Hm[:], 0.0), writes=[Hmb])
                        emit(DVE, lambda: nc.vector.tensor_scalar(out=Hm[:, 0:128], in0=Hm[:, 0:128], scalar1=hflag[:, 0:1],
                                                                  scalar2=None, op0=ALU.add), reads=[Hmb] + CB, writes=[Hmb])
                        for kv in range(2):
                            for g in range(4):
                                ph_ = 4 * (g // 2) + 2 * kv + (g % 2)
                                src = bass.AP(Ubc_t, ph_ * 383 + 127, [[8 * 383 - 1, 4], [1, 132]])
                                QS.start(Bias_s[g * 4:(g + 1) * 4, kv, :], src, reads=[dsc], writes=[sconst])
                        barrier(queues=(QS,))
                    ubuf = sb(P, [128, 4, MT + 2], F32, "ubuf")
                    ubb = [Buf() for _ in range(4)]
                    vlast = sb(P, [128, 128], F32, "vlast")
                    vlastb = Buf()
                    P2 = [sb(P, [128, 4, 256], BF16, "Pexp") for _ in range(2)]
                    P2b = [[Buf() for _ in range(4)] for _ in range(2)]
                    D2 = [sb(P, [128, 4, 128], BF16, "Dn") for _ in range(2)]
                    D2b = [Buf() for _ in range(2)]
                    PTs2 = [sb(P, [128, 1024], BF16, "PTs") for _ in range(2)]
                    PTsb2 = [Buf() for _ in range(2)]
                    sm2 = [{n: sb(P, [128, 4], F32, n) for n in ("mx", "negm", "tmp4", "es4", "rs4", "den4", "rden4")}
                           for _ in range(2)]
                    smb2 = [{n: Buf() for n in sm2[0]} for _ in range(2)]

                    def attn_A(bi, o, hp, par, hb_):
                        Pe, Peb, sm, smb = P2[hb_], P2b[hb_], sm2[hb_], smb2[hb_]
                        qnb, qnbb = qnb2[par], qnbb2[par]
                        kc0 = 128 * (bi - 1)
                        for ci in range(2):
                            i = 2 * hp + ci
                            for kv in range(2):
                                sl = kv * 2 + ci
                                dst = pS[:, sl * 256:(sl + 1) * 256]
                                mms = [(dst, qnb[kv * 64:(kv + 1) * 64, i, o:o + 128], kT_all[kv * 64:(kv + 1) * 64, kc0:kc0 + 256]),
                                       (dst, ident[:], Bhi[:, 4 * hp + sl, :]),
                                       (dst, ident[:], Blo[:, 4 * hp + sl, :])]
                                rd = [qnbb[i], kTb[bi - 1], kTb[bi], Bhib, Blob] + CB
                                if bi == 1:
                                    mms.append((dst, ident[:], Hm[:]))
                                    rd.append(Hmb)
                                pe_group(mms, reads=rd, writes=[pSb[sl // 2]])
                        emit(DVE, lambda: nc.vector.reduce_max(out=sm["mx"][:], in_=pS.rearrange("p (a b) -> p a b", b=256),
                                                               axis=AX.X),
                             reads=[pSb[0], pSb[1]], writes=[smb["mx"]])
                        emit(DVE, lambda: nc.vector.scalar_tensor_tensor(
                            out=sm["negm"][:], in0=sm["mx"][:], scalar=-1.0, in1=nsinkb[:, 4 * hp:4 * hp + 4],
                            op0=ALU.mult, op1=ALU.min), reads=[smb["mx"]] + CB, writes=[smb["negm"]])
                        emit(DVE, lambda: nc.vector.tensor_tensor(
                            out=sm["tmp4"][:], in0=sinkb[:, 4 * hp:4 * hp + 4], in1=sm["negm"][:], op=ALU.add),
                            reads=[smb["negm"]] + CB, writes=[smb["tmp4"]])
                        for sl in range(4):
                            emit(ACT, lambda: nc.scalar.activation(
                                out=Pe[:, sl, :], in_=pS[:, sl * 256:(sl + 1) * 256], func=AF.Exp,
                                bias=sm["negm"][:, sl:sl + 1], scale=1.0, accum_out=sm["rs4"][:, sl:sl + 1]),
                                reads=[pSb[sl // 2], smb["negm"]], writes=[Peb[sl], smb["rs4"]])
                        emit(ACT, lambda: nc.scalar.activation(out=sm["es4"][:], in_=sm["tmp4"][:], func=AF.Exp),
                             reads=[smb["tmp4"]], writes=[smb["es4"]])

                    def attn_B(bi, o, hp, par, hb_):
                        Pe, Peb, sm, smb = P2[hb_], P2b[hb_], sm2[hb_], smb2[hb_]
                        Dn, Dnb, PTs, PTsb = D2[hb_], D2b[hb_], PTs2[hb_], PTsb2[hb_]
                        emit(DVE, lambda: nc.vector.tensor_tensor(out=sm["den4"][:], in0=sm["rs4"][:], in1=sm["es4"][:],
                                                                  op=ALU.add),
                             reads=[smb["rs4"], smb["es4"]], writes=[smb["den4"]])
                        emit(DVE, lambda: nc.vector.reciprocal(out=sm["rden4"][:], in_=sm["den4"][:]),
                             reads=[smb["den4"]], writes=[smb["rden4"]])
                        emit(DVE, lambda: nc.vector.tensor_tensor(
                            out=Dn[:], in0=ident[:].unsqueeze(1).to_broadcast([128, 4, 128]),
                            in1=sm["rden4"][:].unsqueeze(2).to_broadcast([128, 4, 128]), op=ALU.mult),
                            reads=[smb["rden4"]] + CB, writes=[Dnb])
                        rd = Peb + [Dnb]
                        _pre(PE, rd, [pTb])
                        inst = None
                        for sl in range(4):
                            for kh in range(2):
                                idx = sl * 2 + kh
                                inst = nc.tensor.matmul(pT[:, idx * 128:(idx + 1) * 128], Pe[:, sl, kh * 128:(kh + 1) * 128],
                                                        Dn[:, sl, :], start=True, stop=True)
                        tok_ = PE.mark(inst)
                        _commit(tok_, rd, [pTb])
                        emit(ACT, lambda: nc.scalar.copy(out=PTs[:], in_=pT[:]), reads=[pTb], writes=[PTsb])

                    def attn_B2(bi, o, hp, par, hb_):
                        PTs, PTsb = PTs2[hb_], PTsb2[hb_]
                        for ci in range(2):
                            i = 2 * hp + ci
                            mms = []
                            for kv in range(2):
                                for kh in range(2):
                                    idx = (kv * 2 + ci) * 2 + kh
                                    mms.append((pO[:, i * 128:(i + 1) * 128], Vt[:, bi - 1 + kh, kv * 64:kv * 64 + 128],
                                                PTs[:, idx * 128:(idx + 1) * 128]))
                            pe_group(mms, reads=[PTsb, Vtb[bi - 1], Vtb[bi]], writes=[pOb])

                    def attn_fin(o, par):
                        emit(ACT, lambda: nc.scalar.copy(out=mixb2[par][:, 0:4, o:o + 128],
                                                         in_=pO[:].rearrange("p (a b) -> p a b", b=128)),
                             reads=[pOb], writes=mixbb2[par][0:4])

                    def front_steps(kind, a, b, par, tidx=None, nxt=None):
                        W = b - a
                        box = {}

                        def s0():
                            box["hs"], box["hsb"] = kchunk(a, b, tidx)

                        def s_next():
                            if nxt is not None:
                                norm_only(nxt[1], nxt[2], tidx + 1)

                        def s1():
                            hs, hsb = box["hs"], box["hsb"]
                            for bo in range(W // 128):
                                bi = a // 128 + bo
                                kz = state["z"] % 2
                                state["z"] += 1
                                pe_group([(pZ[kz][:, 0:128], hs[:, kc, bo * 128:(bo + 1) * 128], Win[:, kc, 640:768])
                                          for kc in range(8)], reads=[Winb[1]] + hsb, writes=[pZb[kz]])
                                emit(ACT, lambda: nc.scalar.copy(out=Vt[:, bi, 0:64], in_=pZ[kz][:, 0:64]),
                                     reads=[pZb[kz]], writes=[Vtb[bi]])
                                emit(ACT, lambda: nc.scalar.copy(out=Vt[:, bi, 128:192], in_=pZ[kz][:, 64:128]),
                                     reads=[pZb[kz]], writes=[Vtb[bi]])
                                if bi == 16:
                                    emit(ACT, lambda: nc.scalar.copy(out=vlast[:], in_=pZ[kz][:, 0:128]),
                                         reads=[pZb[kz]], writes=[vlastb])
                                    out_toks.append(QS.start(vl_o, vlast[:], reads=[vlastb]))

                        uv = lambda c: ubuf[:, c, 2:2 + W]
                        pzv = lambda k: pZ[k][:, :W]
                        ubw_ = [[ubb[c]] for c in range(4)]

                        def s3a():
                            if state["wprev"] is not None:
                                wp = state["wprev"]
                                emit(DVE, lambda: nc.vector.tensor_copy(out=ubuf[:, :, 0:2], in_=ubuf[:, :, wp:wp + 2]),
                                     reads=ubb, writes=ubb)
                            else:
                                emit(DVE, lambda: nc.vector.memset(ubuf[:, :, 0:2], 0.0), writes=ubb)
                            state["wprev"] = W

                        def s4():
                            k_norm(a, b, kind)
                            if b == CS:
                                out_toks.append(QS.start(kTl_o, knf[:, W - 128:W], reads=[knfb]))
                                out_toks.append(QS.start(cl_o, ubuf[:, :, W:W + 2], reads=ubb))

                        def mk(f, *args):
                            return lambda: f(*args)
                        steps = [s0, s1]
                        if kind != "halo":
                            steps += [mk(lambda i: q_chunk1(box["hs"], box["hsb"], W, i), i) for i in range(4)]
                        steps.append(s3a)
                        steps += [mk(lambda c: u_c1(box["hs"], box["hsb"], W, uv, pzv, ubw_, c), c) for c in range(4)]
                        steps += [mk(lambda c: u_h1(box["hs"], box["hsb"], W, uv, pzv, ubw_, c), c) for c in range(4)]
                        steps.append(s4)
                        if kind == "halo":
                            return steps + [s_next]
                        steps += [mk(lambda i: head_norm(zq[:, i, :W], W, qkg8[:, 0:1], qnb2[par][:, i, :W], [qnbb2[par][i]], [zqb[i]], slot=i), i)
                                  for i in range(4)]
                        steps += [mk(lambda c: conv_only(W, lambda c_: cv[:, c_, :W],
                                                         lambda c_: [ubuf[:, c_, j:j + W] for j in range(3)],
                                                         lambda c_: [ubb[c_]], cs=[c]), c) for c in range(4)]
                        steps += [mk(lambda c: gate_only(box["hs"], box["hsb"], W, par, cs=[c]), c) for c in range(4)]
                        steps.insert(min(len(steps), KNPOS), s_next)
                        return steps

                    hpc = [0]

                    def back_steps(a, b, par):
                        W = b - a
                        passes = []
                        for bo in range(W // 128):
                            for hp in range(2):
                                passes.append((a // 128 + bo, bo * 128, hp, hpc[0] % 2))
                                hpc[0] += 1
                        steps = []
                        n = len(passes)

                        def mkA(p):
                            return lambda: attn_A(p[0], p[1], p[2], par, p[3])

                        def mkB1(idx):
                            p = passes[idx]
                            return lambda: attn_B(p[0], p[1], p[2], par, p[3])

                        def mkB2(idx):
                            p = passes[idx]

                            def f():
                                attn_B2(p[0], p[1], p[2], par, p[3])
                                if p[2] == 1:
                                    attn_fin(p[1], par)
                            return f
                        steps.append((mkA(passes[0]), SLOT[0]))
                        for idx in range(n):
                            if idx + 1 < n:
                                steps.append((mkA(passes[idx + 1]), SLOT[0]))
                            steps.append((mkB1(idx), SLOT[1]))
                            steps.append((mkB2(idx), SLOT[2]))
                        for j0 in range(0, 8, 2):
                            steps.append(((lambda j0_: (lambda: out_proj(a, b, par, range(j0_, j0_ + 2))))(j0), SLOT[3]))
                        return steps

                    pW = None

                    def warm():
                        for _ in range(NWARM):
                            nc.tensor.matmul(pW[:, :], ident[:], kT_all[:, 0:512], start=True, stop=True)

                    def interleave(bs, fs):
                        out = []
                        j = 0
                        for (st, k) in bs:
                            if NWARM:
                                out.append(warm)
                            out.append(st)
                            for _ in range(k):
                                if j < len(fs):
                                    out.append(fs[j]); j += 1
                        out += fs[j:]
                        return out

                    tiles = [("halo", 0, HALO)] + [("prompt", HALO + i * MT, HALO + (i + 1) * MT) for i in range(NPR // MT)]
                    def nx(t):
                        return tiles[t + 1] if t + 1 < len(tiles) else None
                    norm_only(tiles[0][1], tiles[0][2], 0)
                    for st_ in front_steps(*tiles[0], 0, tidx=0, nxt=nx(0)):
                        st_()
                    for st_ in front_steps(*tiles[1], 1, tidx=1, nxt=nx(1)):
                        st_()
                    for t in range(1, len(tiles)):
                        bs = back_steps(tiles[t][1], tiles[t][2], t % 2)
                        fs = front_steps(*tiles[t + 1], (t + 1) % 2, tidx=t + 1, nxt=nx(t + 1)) if t + 1 < len(tiles) else []
                        for st_ in interleave(bs, fs):
                            st_()
                    barrier()

                with ExitStack() as S_:
                    qnb, qnbb, mixb, mixbb = qnb2[0], qnbb2[0], mixb2[0], mixbb2[0]
                    kTs = sb(S_, [128, NSEQ, 128], BF16, "kTs")
                    Vc = sb(S_, [128, NSEQ, 192], BF16, "Vc")
                    Vn = sb(S_, [4, NSEQ, 192], BF16, "Vn")
                    vsnf = sb(S_, [4, 8, 128], F32, "vsnf")
                    vsnfb = Buf()
                    Vnb = Buf()
                    ubs = sb(S_, [128, 4, NSEQ, 6], F32, "ubs")
                    ubsb = Buf()
                    qs = sb(S_, [128, NSEQ, 16], BF16, "qs")
                    qsb = Buf()
                    NQ = 4
                    NSL = 2 * NQ
                    Ss = sb(S_, [16, NSL, 132], F32, "Ss")
                    Ssb_s = Buf()
                    Pns = sb(S_, [16, NSL, 132], BF16, "Pns")
                    Pnsb = Buf()
                    PTcs = sb(S_, [128, NSL * 16], BF16, "PTcs")
                    PTns = sb(S_, [4, NSL * 16], BF16, "PTns")
                    PTcsb = Buf()
                    ss = {n: sb(S_, [16, NSL], F32, "s" + n) for n in ("mx", "m", "rs", "tmp", "es", "den", "rden")}
                    ssb = {n: Buf() for n in ss}
                    kTsb = Buf()
                    Vcb = Buf()
                    emit(DVE, lambda: nc.vector.memset(Vc[:, :, 64:128], 0.0), writes=[Vcb])
                    emit(DVE, lambda: nc.vector.memset(Vn[:], 0.0), writes=[Vnb])
                    QP.start(kTs[:], ckT_d, writes=[kTsb])
                    QP.start(Vc[:, :, 0:64], cvt_d[:, :, 0:64], writes=[Vcb])
                    QP.start(Vc[:, :, 128:192], cvt_d[:, :, 64:128], writes=[Vcb])
                    st_c = sb(S_, [128, 128], F32, "st_c")
                    st_cb = Buf()
                    cs_c = sb(S_, [128, 128], F32, "cs_c")
                    cs_cb = Buf()
                    ubs_st = Buf()
                    QS.start(st_c[:], scT_d.rearrange("p c q r -> p (c q r)"), writes=[st_cb])
                    emit(DVE, lambda: nc.vector.tensor_copy(out=ubs[:, :, :, 0:2],
                                                            in_=st_c[:].rearrange("p (c q r) -> p c q r", c=4, r=2)),
                         reads=[st_cb], writes=[ubs_st])

                    a, b = CS, NT
                    W = NSM
                    state["ti"] = 0
                    hs, hsb = front0(a, b)
                    pV = pA[0:4, :]
                    for seq in range(NSEQ):
                        pe_group([(pV[:, seq * 128:(seq + 1) * 128], hs[:, kc, seq * 4:(seq + 1) * 4],
                                   Win[:, kc, 640:768]) for kc in range(8)],
                                 reads=[Winb[1]] + hsb, writes=[bA[seq // 4]])
                    pVv = pV.rearrange("p (q n) -> p q n", n=128)
                    emit(ACT, lambda: nc.scalar.copy(out=Vn[:, :, 0:64], in_=pVv[:, :, 0:64]), reads=bA, writes=[Vnb])
                    emit(ACT, lambda: nc.scalar.copy(out=Vn[:, :, 128:192], in_=pVv[:, :, 64:128]), reads=bA, writes=[Vnb])
                    for hh in range(2):
                        emit(ACT, lambda: nc.scalar.copy(out=vsnf[:], in_=pVv[:, hh * 8:(hh + 1) * 8, :]),
                             reads=bA, writes=[vsnfb])
                        out_toks.append(QS.start(vsn_o[:, hh * 8:(hh + 1) * 8, :], vsnf[:], reads=[vsnfb]))
                    q_chunks(hs, hsb, W)
                    u_chunks(hs, hsb, W, lambda c: ubs[:, c, :, 2:6],
                             lambda k: pZ[k][:, 0:NSM].rearrange("p (q s) -> p q s", s=4), [[ubsb]] * 4)
                    k_norm(a, b, "sample")
                    out_toks.append(QS.start(ksn_o, knf[:, :W], reads=[knfb]))
                    emit(DVE, lambda: nc.vector.tensor_copy(out=cs_c[:].rearrange("p (c q r) -> p c q r", c=4, r=2),
                                                            in_=ubs[:, :, :, 4:6]), reads=[ubsb], writes=[cs_cb])
                    out_toks.append(QS.start(csn_o.rearrange("p c q r -> p (c q r)"), cs_c[:], reads=[cs_cb]))
                    for i in range(4):
                        head_norm(zq[:, i, :W], W, qkg8[:, 0:1], qnb[:, i, :W], [qnbb[i]], [zqb[i]], slot=i)
                    conv_only(W, lambda c: cv[:, c, 0:NSM].rearrange("p (q s) -> p q s", s=4),
                              lambda c: [ubs[:, c, :, j:j + 4] for j in range(3)], lambda c: [ubsb, ubs_st])
                    gate_only(hs, hsb, W, 0)
                    emit(DVE, lambda: nc.vector.tensor_copy(
                        out=qs[:].rearrange("p q (g s) -> p q g s", s=4),
                        in_=qnb[:, :, 0:NSM].rearrange("p g (q s) -> p q g s", s=4)),
                        reads=qnbb, writes=[qsb])
                    pSc = pA[0:16, 0:NSL * 128]
                    nbank = (NSL * 128) // 512
                    for part in range(NSEQ // NQ):
                        for si in range(NQ):
                            seq = part * NQ + si
                            for kv in range(2):
                                slot = kv * NQ + si
                                pe_group([(pSc[:, slot * 128:(slot + 1) * 128], qs[kv * 64:(kv + 1) * 64, seq, :],
                                           kTs[kv * 64:(kv + 1) * 64, seq, :])],
                                         reads=[qsb, kTsb], writes=[bA[slot // 4]])
                                pnew, pnewb = ((pO, pOb), (pstat, pstat_b))[kv]
                                pe_group([(pnew[0:16, si * 4:(si + 1) * 4], qs[kv * 64:(kv + 1) * 64, seq, :],
                                           kT_all[kv * 64:(kv + 1) * 64, CS + seq * 4:CS + seq * 4 + 4])],
                                         reads=[qsb, kTb[17]], writes=[pnewb])
                        emit(DVE, lambda: nc.vector.tensor_scalar(
                            out=Ss[:, :, 0:128], in0=pSc.rearrange("p (q n) -> p q n", n=128), scalar1=1.0,
                            scalar2=None, op0=ALU.mult), reads=bA[0:nbank], writes=[Ssb_s])
                        for kv in range(2):
                            pnew, pnewb = ((pO, pOb), (pstat, pstat_b))[kv]
                            emit(DVE, lambda: nc.vector.tensor_scalar(
                                out=Ss[:, kv * NQ:(kv + 1) * NQ, 128:132],
                                in0=pnew[0:16, 0:NQ * 4].rearrange("p (q n) -> p q n", n=4),
                                scalar1=1.0, scalar2=None, op0=ALU.mult), reads=[pnewb], writes=[Ssb_s])
                        for kv in range(2):
                            sv = Ss[:, kv * NQ:(kv + 1) * NQ, :]
                            emit(DVE, lambda: nc.vector.tensor_tensor(
                                out=sv, in0=sv, in1=Bias_s[:, kv, :].unsqueeze(1).to_broadcast([16, NQ, 132]),
                                op=ALU.add), reads=[Ssb_s, sconst], writes=[Ssb_s])
                        emit(DVE, lambda: nc.vector.reduce_max(out=ss["mx"][:], in_=Ss[:], axis=AX.X),
                             reads=[Ssb_s], writes=[ssb["mx"]])
                        sbc = sinks_s[:].unsqueeze(2).to_broadcast([16, 2, NQ])
                        emit(DVE, lambda: nc.vector.tensor_tensor(
                            out=ss["m"][:].rearrange("p (k q) -> p k q", k=2),
                            in0=ss["mx"][:].rearrange("p (k q) -> p k q", k=2), in1=sbc, op=ALU.max),
                            reads=[ssb["mx"], sconst2], writes=[ssb["m"]])
                        emit(DVE, lambda: nc.vector.tensor_tensor(
                            out=Ss[:], in0=Ss[:], in1=ss["m"][:].unsqueeze(2).to_broadcast([16, NSL, 132]),
                            op=ALU.subtract), reads=[Ssb_s, ssb["m"]], writes=[Ssb_s])
                        emit(ACT, lambda: nc.scalar.activation(out=Ss[:], in_=Ss[:], func=AF.Exp),
                             reads=[Ssb_s], writes=[Ssb_s])
                        emit(DVE, lambda: nc.vector.reduce_sum(out=ss["rs"][:], in_=Ss[:], axis=AX.X),
                             reads=[Ssb_s], writes=[ssb["rs"]])
                        emit(DVE, lambda: nc.vector.tensor_tensor(
                            out=ss["tmp"][:].rearrange("p (k q) -> p k q", k=2), in0=sbc,
                            in1=ss["m"][:].rearrange("p (k q) -> p k q", k=2), op=ALU.subtract),
                            reads=[ssb["m"], sconst2], writes=[ssb["tmp"]])
                        emit(ACT, lambda: nc.scalar.activation(out=ss["es"][:], in_=ss["tmp"][:], func=AF.Exp),
                             reads=[ssb["tmp"]], writes=[ssb["es"]])
                        emit(DVE, lambda: nc.vector.tensor_tensor(out=ss["den"][:], in0=ss["rs"][:], in1=ss["es"][:],
                                                                  op=ALU.add),
                             reads=[ssb["rs"], ssb["es"]], writes=[ssb["den"]])
                        emit(DVE, lambda: nc.vector.reciprocal(out=ss["rden"][:], in_=ss["den"][:]),
                             reads=[ssb["den"]], writes=[ssb["rden"]])
                        emit(DVE, lambda: nc.vector.tensor_tensor(
                            out=Pns[:], in0=Ss[:], in1=ss["rden"][:].unsqueeze(2).to_broadcast([16, NSL, 132]),
                            op=ALU.mult), reads=[Ssb_s, ssb["rden"]], writes=[Pnsb])
                        trs = []
                        for slot in range(NSL):
                            trs.append((pT[:, slot * 16:(slot + 1) * 16], Pns[0:16, slot, 0:128]))
                            trs.append((pT[0:4, 512 + slot * 16:512 + (slot + 1) * 16], Pns[0:16, slot, 128:132]))
                        _pre(PE, [Pnsb] + CB, [pTb])
                        inst = None
                        for (o_, i_) in trs:
                            inst = nc.tensor.matmul(o_, i_, ident[0:16, 0:16], start=True, stop=True)
                        tok_ = PE.mark(inst)
                        _commit(tok_, [Pnsb] + CB, [pTb])
                        emit(ACT, lambda: nc.scalar.copy(out=PTcs[:], in_=pT[:, 0:NSL * 16]), reads=[pTb], writes=[PTcsb])
                        emit(ACT, lambda: nc.scalar.copy(out=PTns[:], in_=pT[0:4, 512:512 + NSL * 16]), reads=[pTb],
                             writes=[PTcsb])
                        for si in range(NQ):
                            seq = part * NQ + si
                            mms = []
                            for kv in range(2):
                                slot = kv * NQ + si
                                mms.append((pstat[:, si * 16:(si + 1) * 16], Vc[:, seq, kv * 64:kv * 64 + 128],
                                            PTcs[:, slot * 16:(slot + 1) * 16]))
                                mms.append((pstat[:, si * 16:(si + 1) * 16], Vn[0:4, seq, kv * 64:kv * 64 + 128],
                                            PTns[0:4, slot * 16:(slot + 1) * 16]))
                            pe_group(mms, reads=[PTcsb, Vcb, Vnb], writes=[pstat_b])
                        emit(DVE, lambda: nc.vector.tensor_copy(
                            out=mixb[:, 0:4, part * NQ * 4:(part + 1) * NQ * 4].rearrange("p g (q s) -> p q g s", s=4),
                            in_=pstat[:, 0:NQ * 16].rearrange("p (q g s) -> p q g s", g=4, s=4)),
                            reads=[pstat_b], writes=mixbb[0:4])
                    out_proj(a, b, 0, range(8))
                    barrier()
            return out_toks

        def ple_phase(out_toks):
            with ExitStack() as ph:
                alloc_norm(ph, 512)
                Wpg = sb(ph, [128, 8, D], BF16, "Wpg")
                Wpp, pe_b = PLEW["Wpp"], PLEW["pe_b"]
                wb_ = Buf()
                hb2 = [sb(ph, [128, 8, 512], BF16, "hb2") for _ in range(2)]
                hb2b = [[Buf() for _ in range(8)] for _ in range(2)]
                sg = [sb(ph, [128, 512], F32, "sg") for _ in range(2)]
                sgb = [Buf() for _ in range(2)]
                tp = [sb(ph, [128, 512], F32, "tp") for _ in range(2)]
                tpb = [Buf() for _ in range(2)]
                pG = [ps(ph, [128, 512], F32, "pG") for _ in range(2)]
                pP = [ps(ph, [128, 512], F32, "pP") for _ in range(2)]
                pstat = ps(ph, [128, 512], F32, "pstat")
                pGb = [PBuf() for _ in range(2)]
                pPb = [PBuf() for _ in range(2)]
                pstat_b = PBuf()
                QP.start(Wpg[:], wpg_d.rearrange("(kc p) n -> p kc n", p=128), writes=[wb_])
                wb2_, wb3_ = PLEW["b2"], PLEW["b3"]
                yTv = yT.rearrange("(c p) t -> p c t", p=128)
                kk = 0
                def ple_norm(ti):
                    a_, b_ = FFN2_TILES[ti]
                    hs_ = hb2[ti % 2]
                    norm_tile(a_, b_, 3, lambda c: hs_[:, c, :b_ - a_], hb2b[ti % 2], pstat, pstat_b)
                ple_norm(0)
                for ti, (a, b) in enumerate(FFN2_TILES):
                    W = b - a
                    hs = hb2[ti % 2]
                    hsb = hb2b[ti % 2]
                    if ti + 1 < len(FFN2_TILES):
                        ple_norm(ti + 1)
                    for j in range(8):
                        k = kk % 2
                        kk += 1
                        pe_group([(pG[k][:, :W], Wpg[:, kc, j * 128:(j + 1) * 128], hs[:, kc, :W]) for kc in range(8)],
                                 reads=[wb_] + hsb, writes=[pGb[k]])
                        pe_group([(pP[k][:, :W], Wpp[:, m, j * 128:(j + 1) * 128], pe_b[:, m, a - HALO:b - HALO])
                                  for m in range(2)], reads=[wb2_, wb3_], writes=[pPb[k]])
                        emit(ACT, lambda: nc.scalar.activation(out=sg[k][:, :W], in_=pG[k][:, :W], func=AF.Sigmoid),
                             reads=[pGb[k]], writes=[sgb[k]])
                        emit(DVE, lambda: nc.vector.tensor_tensor(out=tp[k][:, :W], in0=sg[k][:, :W], in1=pP[k][:, :W],
                                                                  op=ALU.mult),
                             reads=[sgb[k], pPb[k]], writes=[tpb[k]])
                        emit(DVE, lambda: nc.vector.tensor_tensor(out=xres[:, j, a:b], in0=tp[k][:, :W], in1=xres[:, j, a:b],
                                                                  op=ALU.add),
                             reads=[tpb[k]] + xbufs(j, a, b), writes=xbufs(j, a, b))
                    out_toks.append(QS.start(yTv[:, :, a - HALO:b - HALO], xres[:, :, a:b], reads=xbufs_all(a, b)))
                for t in out_toks:
                    SP.wait(t)
                barrier()

        def dbg_finish():
            yTv = yT.rearrange("(c p) t -> p c t", p=128)
            t = QS.start(yTv, xres[:, :, HALO:NT], reads=xbufs_all(0, NT))
            SP.wait(t)

        if stop == "load":
            dbg_finish()
            return nc
        ffn_phase(w1g, w1u, w1d, 0, FFN1_TILES)
        if stop == "ffn1":
            dbg_finish()
            return nc
        toks = mixer_phase()
        if stop == "mixer":
            dbg_finish()
            return nc
        PLEW["Wpp"] = sb(top, [128, 2, D], BF16, "Wpp")
        PLEW["pe_b"] = sb(top, [128, 2, NOUT], BF16, "pe_b")
        PLEW["b2"], PLEW["b3"] = Buf(), Buf()

        def ple_prefetch():
            QP.start(PLEW["Wpp"][:], wpp_d.rearrange("(kc p) n -> p kc n", p=128), writes=[PLEW["b2"]])
            QP.start(PLEW["pe_b"][:], pT_d.rearrange("(kc p) n -> p kc n", p=128), writes=[PLEW["b3"]])
        AFTER_G1.append(ple_prefetch)
        ffn_phase(w2g, w2u, w2d, 2, FFN2_TILES)
        if stop == "ffn2":
            dbg_finish()
            return nc
        ple_phase(toks)
    return nc


def _rel_bucket_np(d):
    max_exact = 16
    df = np.maximum(d, 1).astype(np.float32)
    val = (np.log(df / np.float32(max_exact)) / np.float32(math.log(128 / 16)) * np.float32(32 - max_exact))
    large = max_exact + val.astype(np.int32)
    large = np.minimum(large, 31)
    return np.where(d < max_exact, d, large)


def _consts():
    E2 = np.zeros((33, 383), np.float32)
    for m in range(383):
        d = 255 - m
        if 0 <= d < 128:
            E2[int(_rel_bucket_np(np.array([d]))[0]), m] = 1.0
        else:
            E2[32, m] = -1e30
    onesD = np.full((128, 128), 1.0 / D, np.float32)
    bd64 = np.zeros((128, 128), np.float32)
    bd64[:64, :64] = 1.0 / 64
    bd64[64:, 64:] = 1.0 / 64
    ident = np.eye(128, dtype=np.float32)
    return E2, onesD, bd64, ident


_NC_CACHE = {}


def kernel(x_prompt, x_sample, p_prompt, p_sample, cache_k, cache_v, state_conv, rel_bias,
           g_ffn1, w1_gate, w1_up, w1_down, g_mix, w_in, q_norm, k_norm, sinks, w_conv, w_out,
           g_ffn2, w2_gate, w2_up, w2_down, g_ple, w_ple_gate, w_ple_proj):
    f = lambda a: np.ascontiguousarray(np.asarray(a, dtype=np.float32))
    x_prompt, x_sample, p_prompt, p_sample = f(x_prompt), f(x_sample), f(p_prompt), f(p_sample)
    cache_k, cache_v, state_conv, rel_bias = f(cache_k), f(cache_v), f(state_conv), f(rel_bias)
    E2, onesD, bd64, ident = _consts()
    qperm = np.concatenate([np.r_[i * 64:(i + 1) * 64, (4 + i) * 64:(5 + i) * 64] for i in range(4)])
    win = f(w_in)[0]
    win_p = f(np.concatenate([win[:, qperm], win[:, 512:]], axis=1))
    wout = f(w_out)[0]
    wout_p = f(np.concatenate([wout[qperm, :], wout[512:, :]], axis=0))
    gv = f(np.stack([f(g_ffn1)[0], f(g_mix)[0], f(g_ffn2)[0], f(g_ple)[0]]).reshape(4, 8, 128).transpose(2, 0, 1))
    qkg = f(np.stack([np.tile(f(q_norm)[0], 2), np.tile(f(k_norm)[0], 2)], axis=1))
    sk = f(sinks)[0]
    sinkb = f(np.broadcast_to(sk[HPERM][None, :], (128, 8)))
    sinks_s = np.zeros((16, 2), np.float32)
    for g in range(4):
        for kv in range(2):
            sinks_s[g * 4:(g + 1) * 4, kv] = sk[kv * 4 + g]
    tabp = f(rel_bias[:, HPERM])
    wconv = f(f(w_conv)[0].reshape(3, 4, 128).transpose(2, 1, 0))
    shared = {
        "tabp": tabp, "E2": E2, "onesD": onesD, "bd64": bd64, "ident": ident, "gv": gv, "qkg": qkg,
        "sinkb": sinkb, "sinks_s": sinks_s, "wconv": wconv,
        "w1g": f(w1_gate)[0], "w1u": f(w1_up)[0], "w1d": f(w1_down)[0], "win": win_p, "wout": wout_p,
        "w2g": f(w2_gate)[0], "w2u": f(w2_up)[0], "w2d": f(w2_down)[0], "wpg": f(w_ple_gate)[0],
        "wpp": f(w_ple_proj)[0],
    }
    in_maps = []
    for c in range(NCORES):
        b, j = divmod(c, 4)
        t0 = j * NPR
        halo = x_prompt[b, t0 - HALO:t0] if j > 0 else np.zeros((HALO, D), np.float32)
        sq = slice(c * NSEQ, (c + 1) * NSEQ)
        xs = x_sample[sq].reshape(NSM, D)
        xT = f(np.concatenate([halo, x_prompt[b, t0:t0 + NPR], xs], axis=0).T)
        pT = f(np.concatenate([p_prompt[0, b, t0:t0 + NPR], p_sample[0, sq].reshape(NSM, DPLE)], axis=0).T)
        ck = cache_k[0, sq].reshape(NSEQ, 128, 128)
        cvv = cache_v[0, sq].reshape(NSEQ, 128, 128)
        sc = state_conv[0, sq]
        scT = f(sc.transpose(2, 0, 1).reshape(4, 128, NSEQ, 2).transpose(1, 0, 2, 3))
        m = dict(shared)
        m.update({
            "xT": xT, "pT": pT, "ckT": f(ck.transpose(2, 0, 1)), "ck_nat": f(ck),
            "cvt": f(cvv.transpose(1, 0, 2)), "cv_nat": f(cvv), "scT": scT,
            "hflag": np.full((128, 1), -1e30 if j == 0 else 0.0, np.float32),
        })
        in_maps.append(m)

    if "nc" not in _NC_CACHE:
        _NC_CACHE["nc"] = build_nc()
    nc = _NC_CACHE["nc"]
    res = run_bass_kernel_spmd(nc, in_maps, core_ids=list(range(NCORES)))
    R = res.results

    B, T = x_prompt.shape[0], x_prompt.shape[1]
    y_prompt = np.zeros((B, T, D), np.float32)
    y_sample = np.zeros((x_sample.shape[0], 4, D), np.float32)
    kwp = np.zeros((1, B, 128, 2, 64), np.float32)
    vwp = np.zeros((1, B, 128, 2, 64), np.float32)
    cvp = np.zeros((1, B, 2, 512), np.float32)
    kws = np.zeros((1, x_sample.shape[0], 128, 2, 64), np.float32)
    vws = np.zeros((1, x_sample.shape[0], 128, 2, 64), np.float32)
    cvs = np.zeros((1, x_sample.shape[0], 2, 512), np.float32)
    for c in range(NCORES):
        b, j = divmod(c, 4)
        t0 = j * NPR
        r = R[c]
        sq = slice(c * NSEQ, (c + 1) * NSEQ)
        y = np.asarray(r["yT"])
        y_prompt[b, t0:t0 + NPR] = y[:, :NPR].T
        y_sample[sq] = y[:, NPR:].T.reshape(NSEQ, 4, D)
        if j == 3:
            kwp[0, b] = np.asarray(r["kTl"]).T.reshape(128, 2, 64)
            vwp[0, b] = np.asarray(r["vl"]).reshape(128, 2, 64)
            cvp[0, b] = np.asarray(r["cl"]).transpose(2, 1, 0).reshape(2, 512)
        kws[0, sq, 0:124] = np.asarray(r["kws_old"]).reshape(NSEQ, 124, 2, 64)
        kws[0, sq, 124:128] = np.asarray(r["ksn"]).T.reshape(NSEQ, 4, 2, 64)
        vws[0, sq, 0:124] = np.asarray(r["vws_old"]).reshape(NSEQ, 124, 2, 64)
        vws[0, sq, 124:128] = np.asarray(r["vsn"]).transpose(1, 0, 2).reshape(NSEQ, 4, 2, 64)
        cvs[0, sq] = np.asarray(r["csn"]).transpose(2, 3, 1, 0).reshape(NSEQ, 2, 512)
    return (y_prompt, y_sample, kwp, vwp, cvp, kws, vws, cvs)
```

```python
import math
from contextlib import ExitStack

import numpy as np
import concourse.bass as bass
import concourse.mybir as mybir
from concourse.bass_utils import run_bass_kernel_spmd

F32 = mybir.dt.float32
BF16 = mybir.dt.bfloat16
AF = mybir.ActivationFunctionType
ALU = mybir.AluOpType
AX = mybir.AxisListType

NCORES = 8
D = 1024
DFF = 2816
NFF = 22
DPLE = 256
INC = 2304
HALO = 128
NPR = 2048
NSM = 64
NT = HALO + NPR + NSM
NOUT = NPR + NSM
CS = HALO + NPR
MT = 256
EPS = 1e-6
NSEQ = 16
HPERM = [0, 1, 4, 5, 2, 3, 6, 7]

FFN_GROUPS = [(0, 5), (5, 10), (10, 14), (14, 18), (18, 22)]
FFN1_TILES = [(0, 128), (128, 640), (640, 1152), (1152, 1664), (1664, 2176), (2176, 2240)]
FFN2_TILES = FFN1_TILES[1:]


class Eng:
    def __init__(self, eng, sem, is_pe=False):
        self.eng = eng
        self.sem = sem
        self.cnt = 0
        self.waited = {}
        self.is_pe = is_pe

    def wait(self, tok):
        if tok is None:
            return
        s, v = tok
        if self.is_pe and s is self.sem:
            return
        if self.waited.get(s.num, 0) >= v:
            return
        self.eng.wait_ge(s, v)
        self.waited[s.num] = v

    def mark(self, inst):
        self.cnt += 1
        inst.then_inc(self.sem, 1)
        return (self.sem, self.cnt)


class _Stop(Exception):
    pass


class Buf:
    __slots__ = ("w", "r", "excl")

    def __init__(self, excl=False):
        self.w = None
        self.r = {}
        self.excl = excl


def PBuf():
    return Buf(excl=True)


def _pre(E, reads, writes):
    for b in reads:
        E.wait(b.w)
        if b.excl:
            for t in list(b.r.values()):
                if t[0] is not E.sem:
                    E.wait(t)
    for b in writes:
        E.wait(b.w)
        for t in list(b.r.values()):
            E.wait(t)


def _commit(tok, reads, writes):
    for b in reads:
        cur = b.r.get(tok[0].num)
        if cur is None or cur[1] < tok[1]:
            b.r[tok[0].num] = tok
    for b in writes:
        b.w = tok
        b.r = {}


def emit(E, fn, reads=(), writes=()):
    _pre(E, reads, writes)
    tok = E.mark(fn())
    _commit(tok, reads, writes)
    return tok


class DmaQ:
    def __init__(self, E, sems):
        self.E = E
        self.slots = [[s, 0] for s in sems]
        self.i = 0

    def start(self, out, in_, reads=(), writes=()):
        E = self.E
        _pre(E, reads, writes)
        slot = self.slots[self.i % len(self.slots)]
        self.i += 1
        if slot[1]:
            E.wait((slot[0], slot[1]))
        inst = E.eng.dma_start(out=out, in_=in_)
        slot[1] += 16
        inst.then_inc(slot[0], 16)
        tok = (slot[0], slot[1])
        _commit(tok, reads, writes)
        return tok

    def outstanding(self):
        return [(s, v) for s, v in self.slots if v]


def build_nc(stop=None):
    nc = bass.Bass("TRN2", target_bir_lowering=False)

    def din(name, shape):
        return nc.dram_tensor(name, list(shape), F32, kind="ExternalInput").ap()

    def dout(name, shape):
        return nc.dram_tensor(name, list(shape), F32, kind="ExternalOutput").ap()

    xT = din("xT", [D, NT])
    pT_d = din("pT", [DPLE, NOUT])
    ckT_d = din("ckT", [128, NSEQ, 128])
    ck_nat = din("ck_nat", [NSEQ, 128, 128])
    cvt_d = din("cvt", [128, NSEQ, 128])
    cv_nat = din("cv_nat", [NSEQ, 128, 128])
    scT_d = din("scT", [128, 4, NSEQ, 2])
    tabp_d = din("tabp", [32, 8])
    E2_d = din("E2", [33, 383])
    onesD_d = din("onesD", [128, 128])
    bd64_d = din("bd64", [128, 128])
    ident_d = din("ident", [128, 128])
    gv_d = din("gv", [128, 4, 8])
    qkg_d = din("qkg", [128, 2])
    sinkb_d = din("sinkb", [128, 8])
    sinks_d = din("sinks_s", [16, 2])
    wconv_d = din("wconv", [128, 4, 3])
    hflag_d = din("hflag", [128, 1])
    w1g = din("w1g", [D, DFF])
    w1u = din("w1u", [D, DFF])
    w1d = din("w1d", [DFF, D])
    win_d = din("win", [D, INC])
    wout_d = din("wout", [D, D])
    w2g = din("w2g", [D, DFF])
    w2u = din("w2u", [D, DFF])
    w2d = din("w2d", [DFF, D])
    wpg_d = din("wpg", [D, D])
    wpp_d = din("wpp", [DPLE, D])

    yT = dout("yT", [D, NOUT])
    kTl_o = dout("kTl", [128, 128])
    vl_o = dout("vl", [128, 128])
    cl_o = dout("cl", [128, 4, 2])
    kws_o = dout("kws_old", [NSEQ, 124, 128])
    vws_o = dout("vws_old", [NSEQ, 124, 128])
    ksn_o = dout("ksn", [128, NSM])
    vsn_o = dout("vsn", [4, NSEQ, 128])
    csn_o = dout("csn", [128, 4, NSEQ, 2])

    Ubc_t = nc.dram_tensor("Ubc", [128, 8, 383], F32, kind="Internal")
    Ubc = Ubc_t.ap()

    uid = [0]

    with ExitStack() as top:
        def sem(name):
            return top.enter_context(nc.semaphore(name))

        PE = Eng(nc.tensor, sem("s_pe"), is_pe=True)
        ACT = Eng(nc.scalar, sem("s_act"))
        DVE = Eng(nc.vector, sem("s_dve"))
        POOL = Eng(nc.gpsimd, sem("s_pool"))
        SP = Eng(nc.sync, sem("s_sp"))
        import os
        NSQ = int(os.environ.get("KNSQ", "10"))
        KSKIP = os.environ.get("KSKIP", "").split(",")
        CONV_ON_POOL = os.environ.get("KCONVPOOL", "0") == "1"
        NORM_ADD_POOL = os.environ.get("KNORMPOOL", "1") == "1"
        NWARM = int(os.environ.get("KNWARM", "0"))
        KNPOS = int(os.environ.get("KNPOS", "10"))
        SLOT = [int(x) for x in os.environ.get("KSLOT", "2,2,1,2").split(",")]
        QS = DmaQ(SP, [sem("qs%d" % i) for i in range(NSQ)])
        QP = DmaQ(POOL, [sem("qp%d" % i) for i in range(NSQ)])
        ENGS = (PE, ACT, DVE, POOL, SP)

        def sb(stack, shape, dt, name="t"):
            uid[0] += 1
            return stack.enter_context(nc.sbuf_tensor("%s_%d" % (name, uid[0]), list(shape), dt))

        def ps(stack, shape, dt, name="p"):
            uid[0] += 1
            return stack.enter_context(nc.psum_tensor("%s_%d" % (name, uid[0]), list(shape), dt))

        QA_REF = []
        X_REST = []
        AFTER_G1 = []
        PLEW = {}

        def barrier(queues=None):
            toks = [(E.sem, E.cnt) for E in (PE, ACT, DVE, POOL) if E.cnt > 0]
            for q_ in (queues if queues is not None else (QS, QP)):
                toks += q_.outstanding()
            if queues is None and QA_REF:
                toks += QA_REF[0].outstanding()
            for E in ENGS:
                for t in toks:
                    E.wait(t)

        def pe_group(mms, reads=(), writes=()):
            _pre(PE, reads, writes)
            n = len(mms)
            inst = None
            for i, (o, l, r) in enumerate(mms):
                inst = nc.tensor.matmul(o, l, r, start=(i == 0), stop=(i == n - 1))
            tok = PE.mark(inst)
            _commit(tok, reads, writes)
            return tok

        def pe_transposes(trs, ident_ap_fn, reads=(), writes=()):
            _pre(PE, reads, writes)
            inst = None
            for (o, i_) in trs:
                inst = nc.tensor.transpose(o, i_, ident_ap_fn(i_))
            tok = PE.mark(inst)
            _commit(tok, reads, writes)
            return tok

        xres = sb(top, [128, 8, NT], F32, "xres")
        xb = [[Buf() for _ in range(18)] for _ in range(8)]

        def xbufs(c, a, b):
            return [xb[c][k] for k in range(a // 128, (b + 127) // 128)]

        def xbufs_all(a, b):
            r = []
            for c in range(8):
                r += xbufs(c, a, b)
            return r

        onesD = sb(top, [128, 128], BF16, "onesD")
        bd64 = sb(top, [128, 128], BF16, "bd64")
        ident = sb(top, [128, 128], BF16, "ident")
        gv = sb(top, [128, 4, 8], F32, "gv")
        qkg = sb(top, [128, 2], F32, "qkg")
        sinkb = sb(top, [128, 8], F32, "sinkb")
        nsinkb = sb(top, [128, 8], F32, "nsinkb")
        wconv = sb(top, [128, 4, 3], F32, "wconv")
        hflag = sb(top, [128, 1], F32, "hflag")
        epst = sb(top, [128, 1], F32, "epst")
        NS = {}

        def alloc_norm(stack, Wn, nb=2):
            NS["sq"] = [sb(stack, [128, Wn], BF16, "sq") for _ in range(3)]
            NS["sqb"] = [Buf() for _ in range(3)]
            NS["rt"] = [sb(stack, [128, Wn], F32, "rt") for _ in range(nb)]
            NS["rtb"] = [Buf() for _ in range(nb)]
            NS["rstd"] = [sb(stack, [128, Wn], F32, "rstd") for _ in range(nb)]
            NS["rstdb"] = [Buf() for _ in range(nb)]
        CB = []

        def newcb():
            b_ = Buf()
            CB.append(b_)
            return b_
        cnt = {"sq": 0, "rt": 0}

        xTv = xT.rearrange("(c p) t -> p c t", p=128)
        QA = DmaQ(ACT, [sem("qa%d" % i) for i in range(8)])
        QA_REF.append(QA)
        for c in range(8):
            q_ = QS if c % 2 == 0 else QA
            q_.start(xres[:, c, 0:640], xTv[:, c, 0:640], writes=xbufs(c, 0, 640))

        def x_rest():
            for c in range(8):
                QS.start(xres[:, c, 640:NT], xTv[:, c, 640:NT], writes=xbufs(c, 640, NT))
        X_REST.append(x_rest)
        for dst, src in ((gv, gv_d), (qkg, qkg_d), (sinkb, sinkb_d), (wconv, wconv_d), (hflag, hflag_d)):
            QS.start(dst[:], src, writes=[newcb()])
        for dst, src in ((onesD, onesD_d), (bd64, bd64_d), (ident, ident_d)):
            QP.start(dst[:], src, writes=[newcb()])
        emit(DVE, lambda: nc.vector.memset(epst[:], EPS), writes=[newcb()])
        qkg8 = sb(top, [128, 1], F32, "qkg8")
        emit(DVE, lambda: nc.vector.tensor_scalar(out=qkg8[:], in0=qkg[:, 0:1], scalar1=0.125, scalar2=None,
                                                  op0=ALU.mult), reads=list(CB), writes=[newcb()])
        emit(DVE, lambda: nc.vector.tensor_scalar(out=nsinkb[:], in0=sinkb[:], scalar1=-1.0, scalar2=None,
                                                  op0=ALU.mult), reads=list(CB), writes=[newcb()])

        def norm_tile(a, b, gsel, out_fn, out_bufs, pstat, pstat_b):
            sq, sqb, rt, rtb, rstd, rstdb = NS["sq"], NS["sqb"], NS["rt"], NS["rtb"], NS["rstd"], NS["rstdb"]
            W = b - a
            for c in range(8):
                i = cnt["sq"] % 3
                cnt["sq"] += 1
                emit(ACT, lambda: nc.scalar.activation(out=sq[i][:, :W], in_=xres[:, c, a:b], func=AF.Square),
                     reads=xbufs(c, a, b), writes=[sqb[i]])
                _pre(PE, [sqb[i]] + CB, [pstat_b] if c == 0 else [])
                inst = nc.tensor.matmul(pstat[:, :W], onesD[:], sq[i][:, :W], start=(c == 0), stop=(c == 7))
                tok = PE.mark(inst)
                _commit(tok, [sqb[i]] + CB, [pstat_b] if c == 7 else [])
            j = cnt["rt"] % len(NS["rt"])
            cnt["rt"] += 1
            emit(ACT, lambda: nc.scalar.activation(out=rt[j][:, :W], in_=pstat[:, :W], func=AF.Ln,
                                                   bias=epst[:, 0:1], scale=1.0),
                 reads=[pstat_b] + CB, writes=[rtb[j]])
            emit(ACT, lambda: nc.scalar.activation(out=rstd[j][:, :W], in_=rt[j][:, :W], func=AF.Exp, scale=-0.5),
                 reads=[rtb[j]], writes=[rstdb[j]])
            for c in range(8):
                emit(DVE, lambda: nc.vector.scalar_tensor_tensor(
                    out=out_fn(c), in0=xres[:, c, a:b], scalar=gv[:, gsel, c:c + 1], in1=rstd[j][:, :W],
                    op0=ALU.mult, op1=ALU.mult),
                    reads=xbufs(c, a, b) + [rstdb[j]] + CB, writes=[out_bufs[c]])

        def ffn_phase(wg_d, wu_d, wd_d, gsel, tiles):
            with ExitStack() as ph:
                alloc_norm(ph, 512, nb=1)
                hb = sb(ph, [128, 8, NT], BF16, "hb")
                hbb = [[Buf() for _ in range(8)] for _ in tiles]
                Wg = [sb(ph, [128, 8, 640], BF16, "Wg") for _ in range(2)]
                Wu = [sb(ph, [128, 8, 640], BF16, "Wu") for _ in range(2)]
                Wd = [sb(ph, [128, 5, 1024], BF16, "Wd") for _ in range(2)]
                Wgb = [Buf() for _ in range(2)]
                Wub = [Buf() for _ in range(2)]
                Wdb = [Buf() for _ in range(2)]
                A = [sb(ph, [128, 5, 512], BF16, "A") for _ in range(2)]
                Ab = [[Buf() for _ in range(5)] for _ in range(2)]
                S = [sb(ph, [128, 512], F32, "S") for _ in range(2)]
                Sb = [Buf() for _ in range(2)]
                pG = [ps(ph, [128, 512], F32, "pG") for _ in range(2)]
                pU = [ps(ph, [128, 512], F32, "pU") for _ in range(2)]
                pY = [ps(ph, [128, 512], F32, "pY") for _ in range(2)]
                pstat = ps(ph, [128, 512], F32, "pstat")
                pGb = [PBuf() for _ in range(2)]
                pUb = [PBuf() for _ in range(2)]
                pYb = [PBuf() for _ in range(2)]
                pstat_b = PBuf()

                def load_group(gi):
                    c0, c1 = FFN_GROUPS[gi]
                    s = gi % 2
                    gw = (c1 - c0) * 128
                    QP.start(Wg[s][:, :, 0:gw], wg_d[:, c0 * 128:c1 * 128].rearrange("(kc p) n -> p kc n", p=128),
                             writes=[Wgb[s]])
                    QP.start(Wu[s][:, :, 0:gw], wu_d[:, c0 * 128:c1 * 128].rearrange("(kc p) n -> p kc n", p=128),
                             writes=[Wub[s]])
                    QP.start(Wd[s][:, 0:c1 - c0, :], wd_d[c0 * 128:c1 * 128, :].rearrange("(g p) n -> p g n", p=128),
                             writes=[Wdb[s]])

                load_group(0)
                if X_REST:
                    X_REST.pop()()
                def ffn_norm(ti):
                    a_, b_ = tiles[ti]
                    norm_tile(a_, b_, gsel, lambda c: hb[:, c, a_:b_], hbb[ti], pstat, pstat_b)
                ffn_norm(0)
                if len(tiles) > 1:
                    ffn_norm(1)
                load_group(1)
                while AFTER_G1:
                    AFTER_G1.pop(0)()

                items = [(gi, ti) for gi in range(len(FFN_GROUPS)) for ti in range(len(tiles))]
                kc_cnt = [0]
                y_cnt = [0]

                def stage1(idx):
                    gi, ti = items[idx]
                    if gi == 0 and ti + 2 < len(tiles):
                        ffn_norm(ti + 2)
                    a, b = tiles[ti]
                    W = b - a
                    c0, c1 = FFN_GROUPS[gi]
                    s = gi % 2
                    ai = idx % 2
                    for cl in range(c1 - c0):
                        k = kc_cnt[0] % 2
                        kc_cnt[0] += 1
                        pe_group([(pG[k][:, :W], Wg[s][:, kc, cl * 128:(cl + 1) * 128], hb[:, kc, a:b]) for kc in range(8)],
                                 reads=[Wgb[s]] + hbb[ti], writes=[pGb[k]])
                        pe_group([(pU[k][:, :W], Wu[s][:, kc, cl * 128:(cl + 1) * 128], hb[:, kc, a:b]) for kc in range(8)],
                                 reads=[Wub[s]] + hbb[ti], writes=[pUb[k]])
                        emit(ACT, lambda: nc.scalar.activation(out=S[k][:, :W], in_=pG[k][:, :W], func=AF.Silu),
                             reads=[pGb[k]], writes=[Sb[k]])
                        emit(DVE, lambda: nc.vector.tensor_tensor(out=A[ai][:, cl, :W], in0=S[k][:, :W], in1=pU[k][:, :W],
                                                                  op=ALU.mult),
                             reads=[Sb[k], pUb[k]], writes=[Ab[ai][cl]])

                def stage2(idx):
                    gi, ti = items[idx]
                    a, b = tiles[ti]
                    W = b - a
                    c0, c1 = FFN_GROUPS[gi]
                    s = gi % 2
                    ai = idx % 2
                    n = c1 - c0
                    for j in range(8):
                        k = y_cnt[0] % 2
                        y_cnt[0] += 1
                        pe_group([(pY[k][:, :W], Wd[s][:, cl, j * 128:(j + 1) * 128], A[ai][:, cl, :W]) for cl in range(n)],
                                 reads=[Wdb[s]] + Ab[ai][:n], writes=[pYb[k]])
                        emit(DVE, lambda: nc.vector.scalar_tensor_tensor(
                            out=xres[:, j, a:b], in0=pY[k][:, :W], scalar=0.5, in1=xres[:, j, a:b],
                            op0=ALU.mult, op1=ALU.add),
                            reads=[pYb[k]] + xbufs(j, a, b), writes=xbufs(j, a, b))
                    if ti == len(tiles) - 1 and gi + 2 < len(FFN_GROUPS):
                        load_group(gi + 2)

                stage1(0)
                for idx in range(len(items)):
                    if idx + 1 < len(items):
                        stage1(idx + 1)
                    stage2(idx)
                barrier()

        def mixer_phase():
            out_toks = []
            with ExitStack() as ph:
                alloc_norm(ph, MT)
                Win = sb(ph, [128, 8, INC], BF16, "Win")
                Wout = sb(ph, [128, 8, D], BF16, "Wout")
                WinSeg = [(0, 512), (512, 768), (768, 1280), (1280, 1792), (1792, 2304)]
                Winb = [Buf() for _ in WinSeg]
                Woutb = Buf()
                Bias_s = sb(ph, [16, 2, 132], F32, "Bias_s")
                sinks_s = sb(ph, [16, 2], F32, "sinks_s")
                sconst = Buf()
                sconst2 = Buf()
                kT_all = sb(ph, [128, NT], BF16, "kT_all")
                kTb = [Buf() for _ in range(18)]
                hbm = [sb(ph, [128, 8, MT], BF16, "hbm") for _ in range(2)]
                hbmb = [[Buf() for _ in range(8)] for _ in range(2)]
                zq = sb(ph, [128, 4, MT], F32, "zq")
                zqb = [Buf() for _ in range(4)]
                zk = sb(ph, [128, MT], F32, "zk")
                zkb = Buf()
                cv = sb(ph, [128, 4, MT], F32, "cv")
                cvb = [Buf() for _ in range(4)]
                qnb2 = [sb(ph, [128, 4, MT], BF16, "qnb") for _ in range(2)]
                qnbb2 = [[Buf() for _ in range(4)] for _ in range(2)]
                knf = sb(ph, [128, MT], F32, "knf")
                knfb = Buf()
                mixb2 = [sb(ph, [128, 8, MT], BF16, "mixb") for _ in range(2)]
                mixbb2 = [[Buf() for _ in range(8)] for _ in range(2)]

                pA = ps(ph, [128, 2048], F32, "pA")
                bA = [PBuf() for _ in range(4)]
                pstat = ps(ph, [128, 512], F32, "pstat")
                pstat_b = PBuf()
                pT = ps(ph, [128, 1024], F32, "pT")
                pTb = PBuf()
                pO = ps(ph, [128, 512], F32, "pO")
                pOb = PBuf()
                pZ = [pA[:, 0:512], pA[:, 512:1024]]
                pZb = [bA[0], bA[1]]
                pS = pA[:, 1024:2048]
                pSb = [bA[2], bA[3]]

                winv = win_d.rearrange("(kc p) n -> p kc n", p=128)
                for si in (1, 3, 4, 0, 2):
                    lo, hi = WinSeg[si]
                    QP.start(Win[:, :, lo:hi], winv[:, :, lo:hi], writes=[Winb[si]])
                QP.start(Wout[:], wout_d.rearrange("(kc p) n -> p kc n", p=128), writes=[Woutb])
                QS.start(sinks_s[:], sinks_d, writes=[sconst2])
                out_toks.append(QS.start(kws_o, ck_nat[:, 4:128, :]))
                out_toks.append(QS.start(vws_o, cv_nat[:, 4:128, :]))

                state = {"ti": 0, "z": 0, "wprev": None}
                hsq = [sb(ph, [128, MT], BF16, "hsq") for _ in range(5)]
                hsqb = [Buf() for _ in range(5)]

                def presq(slot, src_ap, W, src_bufs):
                    emit(ACT, lambda: nc.scalar.activation(out=hsq[slot][:, :W], in_=src_ap, func=AF.Square),
                         reads=src_bufs, writes=[hsqb[slot]])

                def zmm(wcol, seg, hs, hsb, W):
                    k = state["z"] % 2
                    state["z"] += 1
                    pe_group([(pZ[k][:, :W], Win[:, kc, wcol:wcol + 128], hs[:, kc, :W]) for kc in range(8)],
                             reads=[Winb[seg]] + hsb, writes=[pZb[k]])
                    return k

                def head_norm(src_ap, W, gap, out_ap, out_bufs, src_bufs, slot=None):
                    sq, sqb, rt, rtb, rstd, rstdb = NS["sq"], NS["sqb"], NS["rt"], NS["rtb"], NS["rstd"], NS["rstdb"]
                    pe_group([(pstat[:, :W], bd64[:], hsq[slot][:, :W])], reads=[hsqb[slot]] + CB, writes=[pstat_b])
                    j = cnt["rt"] % len(NS["rt"])
                    cnt["rt"] += 1
                    emit(ACT, lambda: nc.scalar.activation(out=rt[j][:, :W], in_=pstat[:, :W], func=AF.Ln,
                                                           bias=epst[:, 0:1], scale=1.0),
                         reads=[pstat_b] + CB, writes=[rtb[j]])
                    emit(ACT, lambda: nc.scalar.activation(out=rstd[j][:, :W], in_=rt[j][:, :W], func=AF.Exp, scale=-0.5),
                         reads=[rtb[j]], writes=[rstdb[j]])
                    emit(DVE, lambda: nc.vector.scalar_tensor_tensor(
                        out=out_ap, in0=src_ap, scalar=gap, in1=rstd[j][:, :W],
                        op0=ALU.mult, op1=ALU.mult),
                        reads=src_bufs + [rstdb[j]] + CB, writes=out_bufs)

                def norm_only(a, b, ti):
                    W = b - a
                    hs = hbm[ti % 2]
                    norm_tile(a, b, 1, lambda c: hs[:, c, :W], hbmb[ti % 2], pstat, pstat_b)

                def kchunk(a, b, ti):
                    W = b - a
                    hs = hbm[ti % 2]
                    hsb = hbmb[ti % 2]
                    k = zmm(512, 1, hs, hsb, W)
                    emit(ACT, lambda: nc.scalar.copy(out=zk[:, :W], in_=pZ[k][:, :W]), reads=[pZb[k]], writes=[zkb])
                    presq(4, zk[:, :W], W, [zkb])
                    return hs, hsb

                def front0(a, b):
                    ti = state["ti"]
                    state["ti"] += 1
                    norm_only(a, b, ti)
                    return kchunk(a, b, ti)

                def q_chunk1(hs, hsb, W, i):
                    k = zmm(i * 128, 0, hs, hsb, W)
                    emit(ACT, lambda: nc.scalar.copy(out=zq[:, i, :W], in_=pZ[k][:, :W]), reads=[pZb[k]],
                         writes=[zqb[i]])
                    presq(i, zq[:, i, :W], W, [zqb[i]])

                def q_chunks(hs, hsb, W):
                    for i in range(4):
                        q_chunk1(hs, hsb, W, i)

                def u_c1(hs, hsb, W, uview, pzview, ubw, c):
                    k = zmm(1280 + c * 128, 3, hs, hsb, W)
                    emit(ACT, lambda: nc.scalar.copy(out=uview(c), in_=pzview(k)), reads=[pZb[k]], writes=ubw[c])

                def u_h1(hs, hsb, W, uview, pzview, ubw, c):
                    k = zmm(1792 + c * 128, 4, hs, hsb, W)
                    emit(DVE, lambda: nc.vector.tensor_tensor(out=uview(c), in0=uview(c), in1=pzview(k), op=ALU.mult),
                         reads=[pZb[k]] + ubw[c], writes=ubw[c])

                def u_chunks(hs, hsb, W, uview, pzview, ubw):
                    for c in range(4):
                        u_c1(hs, hsb, W, uview, pzview, ubw, c)
                    for c in range(4):
                        u_h1(hs, hsb, W, uview, pzview, ubw, c)

                def k_norm(a, b, kind):
                    W = b - a
                    head_norm(zk[:, :W], W, qkg[:, 1:2], knf[:, :W], [knfb], [zkb], slot=4)
                    blks = [17] if kind == "sample" else list(range(a // 128, b // 128))
                    emit(ACT, lambda: nc.scalar.copy(out=kT_all[:, a:b], in_=knf[:, :W]), reads=[knfb],
                         writes=[kTb[x] for x in blks])

                def conv_only(W, cvo_fn, taps_fn, rb_fn, cs=range(4)):
                    for c in cs:
                        cvo = cvo_fn(c)
                        taps = taps_fn(c)
                        rb = rb_fn(c)
                        CE, ce = (POOL, nc.gpsimd) if CONV_ON_POOL else (DVE, nc.vector)
                        emit(CE, lambda: ce.tensor_scalar(out=cvo, in0=taps[2], scalar1=wconv[:, c, 2:3],
                                                          scalar2=None, op0=ALU.mult),
                             reads=rb + CB, writes=[cvb[c]])
                        for j in (1, 0):
                            emit(CE, lambda: ce.scalar_tensor_tensor(
                                out=cvo, in0=taps[j], scalar=wconv[:, c, j:j + 1], in1=cvo, op0=ALU.mult, op1=ALU.add),
                                reads=rb + CB + [cvb[c]], writes=[cvb[c]])

                def gate_only(hs, hsb, W, par, cs=range(4)):
                    for c in cs:
                        k = zmm(768 + c * 128, 2, hs, hsb, W)
                        emit(DVE, lambda: nc.vector.tensor_tensor(out=mixb2[par][:, 4 + c, :W], in0=pZ[k][:, :W],
                                                                  in1=cv[:, c, :W], op=ALU.mult),
                             reads=[pZb[k], cvb[c]], writes=[mixbb2[par][4 + c]])

                def out_proj(a, b, par, js):
                    W = b - a
                    for j in js:
                        k = state["z"] % 2
                        state["z"] += 1
                        pe_group([(pZ[k][:, :W], Wout[:, m, j * 128:(j + 1) * 128], mixb2[par][:, m, :W]) for m in range(8)],
                                 reads=[Woutb] + mixbb2[par], writes=[pZb[k]])
                        emit(DVE, lambda: nc.vector.tensor_tensor(out=xres[:, j, a:b], in0=pZ[k][:, :W], in1=xres[:, j, a:b],
                                                                  op=ALU.add),
                             reads=[pZb[k]] + xbufs(j, a, b), writes=xbufs(j, a, b))

                with ExitStack() as P:
                    Bhi = sb(P, [128, 8, 256], BF16, "Bhi")
                    Blo = sb(P, [128, 8, 256], BF16, "Blo")
                    Hm = sb(P, [128, 256], BF16, "Hm")
                    Bhib, Blob, Hmb = Buf(), Buf(), Buf()
                    Biasb = Buf()
                    Vt = sb(P, [128, 17, 192], BF16, "Vt")
                    Vtb = [Buf() for _ in range(17)]
                    emit(DVE, lambda: nc.vector.memset(Vt[:], 0.0), writes=Vtb)
                    with ExitStack() as su:
                        tab_sb = sb(su, [33, 8], F32, "tab_sb")
                        tabB = sb(su, [33, 8, 128], F32, "tabB")
                        E2_sb = sb(su, [33, 383], F32, "E2_sb")
                        Ubc_sb = sb(su, [128, 8, 383], F32, "Ubc_sb")
                        Bias = sb(su, [128, 8, 256], F32, "Bias")
                        tb = Buf()
                        eb = Buf()
                        ub = Buf()
                        emit(DVE, lambda: nc.vector.memset(tab_sb[:], 1.0), writes=[tb])
                        QS.start(tab_sb[0:32, :], tabp_d, writes=[tb])
                        QS.start(E2_sb[:], E2_d, writes=[eb])
                        emit(DVE, lambda: nc.vector.tensor_copy(out=tabB[:], in_=tab_sb[:].unsqueeze(2).to_broadcast([33, 8, 128])),
                             reads=[tb], writes=[tb])
                        for h in range(8):
                            k = h % 2
                            pe_group([(pZ[k][:, 0:383], tabB[:, h, :], E2_sb[:])], reads=[tb, eb], writes=[pZb[k]])
                            emit(ACT, lambda: nc.scalar.copy(out=Ubc_sb[:, h, :], in_=pZ[k][:, 0:383]), reads=[pZb[k]], writes=[ub])
                        dsc = Buf()
                        QS.start(Ubc, Ubc_sb[:], reads=[ub], writes=[dsc])
                        src = bass.AP(Ubc_t, 127, [[8 * 383 - 1, 128], [383, 8], [1, 256]])
                        QS.start(Bias[:], src, reads=[dsc], writes=[Biasb])
                        emit(DVE, lambda: nc.vector.tensor_copy(out=Bhi[:], in_=Bias[:]), reads=[Biasb], writes=[Bhib])
                        emit(DVE, lambda: nc.vector.tensor_tensor(out=Blo[:], in0=Bias[:], in1=Bhi[:], op=ALU.subtract),
                             reads=[Biasb, Bhib], writes=[Blob])
                        emit(DVE, lambda: nc.vector.memset(Hm[:], 0.0), writes=[Hmb])
                        emit(DVE, lambda: nc.vector.tensor_scalar(out=Hm[:, 0:128], in0=Hm[:, 0:128], scalar1=hflag[:, 0:1],
                                                                  scalar2=None, op0=ALU.add), reads=[Hmb] + CB, writes=[Hmb])
                        for kv in range(2):
                            for g in range(4):
                                ph_ = 4 * (g // 2) + 2 * kv + (g % 2)
                                src = bass.AP(Ubc_t, ph_ * 383 + 127, [[8 * 383 - 1, 4], [1, 132]])
                                QS.start(Bias_s[g * 4:(g + 1) * 4, kv, :], src, reads=[dsc], writes=[sconst])
                        barrier(queues=(QS,))
                    ubuf = sb(P, [128, 4, MT + 2], F32, "ubuf")
                    ubb = [Buf() for _ in range(4)]
                    vlast = sb(P, [128, 128], F32, "vlast")
                    vlastb = Buf()
                    P2 = [sb(P, [128, 4, 256], BF16, "Pexp") for _ in range(2)]
                    P2b = [[Buf() for _ in range(4)] for _ in range(2)]
                    D2 = [sb(P, [128, 4, 128], BF16, "Dn") for _ in range(2)]
                    D2b = [Buf() for _ in range(2)]
                    PTs2 = [sb(P, [128, 1024], BF16, "PTs") for _ in range(2)]
                    PTsb2 = [Buf() for _ in range(2)]
                    sm2 = [{n: sb(P, [128, 4], F32, n) for n in ("mx", "negm", "tmp4", "es4", "rs4", "den4", "rden4")}
                           for _ in range(2)]
                    smb2 = [{n: Buf() for n in sm2[0]} for _ in range(2)]

                    def attn_A(bi, o, hp, par, hb_):
                        Pe, Peb, sm, smb = P2[hb_], P2b[hb_], sm2[hb_], smb2[hb_]
                        qnb, qnbb = qnb2[par], qnbb2[par]
                        kc0 = 128 * (bi - 1)
                        for ci in range(2):
                            i = 2 * hp + ci
                            for kv in range(2):
                                sl = kv * 2 + ci
                                dst = pS[:, sl * 256:(sl + 1) * 256]
                                mms = [(dst, qnb[kv * 64:(kv + 1) * 64, i, o:o + 128], kT_all[kv * 64:(kv + 1) * 64, kc0:kc0 + 256]),
                                       (dst, ident[:], Bhi[:, 4 * hp + sl, :]),
                                       (dst, ident[:], Blo[:, 4 * hp + sl, :])]
                                rd = [qnbb[i], kTb[bi - 1], kTb[bi], Bhib, Blob] + CB
                                if bi == 1:
                                    mms.append((dst, ident[:], Hm[:]))
                                    rd.append(Hmb)
                                pe_group(mms, reads=rd, writes=[pSb[sl // 2]])
                        emit(DVE, lambda: nc.vector.reduce_max(out=sm["mx"][:], in_=pS.rearrange("p (a b) -> p a b", b=256),
                                                               axis=AX.X),
                             reads=[pSb[0], pSb[1]], writes=[smb["mx"]])
                        emit(DVE, lambda: nc.vector.scalar_tensor_tensor(
                            out=sm["negm"][:], in0=sm["mx"][:], scalar=-1.0, in1=nsinkb[:, 4 * hp:4 * hp + 4],
                            op0=ALU.mult, op1=ALU.min), reads=[smb["mx"]] + CB, writes=[smb["negm"]])
                        emit(DVE, lambda: nc.vector.tensor_tensor(
                            out=sm["tmp4"][:], in0=sinkb[:, 4 * hp:4 * hp + 4], in1=sm["negm"][:], op=ALU.add),
                            reads=[smb["negm"]] + CB, writes=[smb["tmp4"]])
                        for sl in range(4):
                            emit(ACT, lambda: nc.scalar.activation(
                                out=Pe[:, sl, :], in_=pS[:, sl * 256:(sl + 1) * 256], func=AF.Exp,
                                bias=sm["negm"][:, sl:sl + 1], scale=1.0, accum_out=sm["rs4"][:, sl:sl + 1]),
                                reads=[pSb[sl // 2], smb["negm"]], writes=[Peb[sl], smb["rs4"]])
                        emit(ACT, lambda: nc.scalar.activation(out=sm["es4"][:], in_=sm["tmp4"][:], func=AF.Exp),
                             reads=[smb["tmp4"]], writes=[smb["es4"]])

                    def attn_B(bi, o, hp, par, hb_):
                        Pe, Peb, sm, smb = P2[hb_], P2b[hb_], sm2[hb_], smb2[hb_]
                        Dn, Dnb, PTs, PTsb = D2[hb_], D2b[hb_], PTs2[hb_], PTsb2[hb_]
                        emit(DVE, lambda: nc.vector.tensor_tensor(out=sm["den4"][:], in0=sm["rs4"][:], in1=sm["es4"][:],
                                                                  op=ALU.add),
                             reads=[smb["rs4"], smb["es4"]], writes=[smb["den4"]])
                        emit(DVE, lambda: nc.vector.reciprocal(out=sm["rden4"][:], in_=sm["den4"][:]),
                             reads=[smb["den4"]], writes=[smb["rden4"]])
                        emit(DVE, lambda: nc.vector.tensor_tensor(
                            out=Dn[:], in0=ident[:].unsqueeze(1).to_broadcast([128, 4, 128]),
                            in1=sm["rden4"][:].unsqueeze(2).to_broadcast([128, 4, 128]), op=ALU.mult),
                            reads=[smb["rden4"]] + CB, writes=[Dnb])
                        rd = Peb + [Dnb]
                        _pre(PE, rd, [pTb])
                        inst = None
                        for sl in range(4):
                            for kh in range(2):
                                idx = sl * 2 + kh
                                inst = nc.tensor.matmul(pT[:, idx * 128:(idx + 1) * 128], Pe[:, sl, kh * 128:(kh + 1) * 128],
                                                        Dn[:, sl, :], start=True, stop=True)
                        tok_ = PE.mark(inst)
                        _commit(tok_, rd, [pTb])
                        emit(ACT, lambda: nc.scalar.copy(out=PTs[:], in_=pT[:]), reads=[pTb], writes=[PTsb])

                    def attn_B2(bi, o, hp, par, hb_):
                        PTs, PTsb = PTs2[hb_], PTsb2[hb_]
                        for ci in range(2):
                            i = 2 * hp + ci
                            mms = []
                            for kv in range(2):
                                for kh in range(2):
                                    idx = (kv * 2 + ci) * 2 + kh
                                    mms.append((pO[:, i * 128:(i + 1) * 128], Vt[:, bi - 1 + kh, kv * 64:kv * 64 + 128],
                                                PTs[:, idx * 128:(idx + 1) * 128]))
                            pe_group(mms, reads=[PTsb, Vtb[bi - 1], Vtb[bi]], writes=[pOb])

                    def attn_fin(o, par):
                        emit(ACT, lambda: nc.scalar.copy(out=mixb2[par][:, 0:4, o:o + 128],
                                                         in_=pO[:].rearrange("p (a b) -> p a b", b=128)),
                             reads=[pOb], writes=mixbb2[par][0:4])

                    def front_steps(kind, a, b, par, tidx=None, nxt=None):
                        W = b - a
                        box = {}

                        def s0():
                            box["hs"], box["hsb"] = kchunk(a, b, tidx)

                        def s_next():
                            if nxt is not None:
                                norm_only(nxt[1], nxt[2], tidx + 1)

                        def s1():
                            hs, hsb = box["hs"], box["hsb"]
                            for bo in range(W // 128):
                                bi = a // 128 + bo
                                kz = state["z"] % 2
                                state["z"] += 1
                                pe_group([(pZ[kz][:, 0:128], hs[:, kc, bo * 128:(bo + 1) * 128], Win[:, kc, 640:768])
                                          for kc in range(8)], reads=[Winb[1]] + hsb, writes=[pZb[kz]])
                                emit(ACT, lambda: nc.scalar.copy(out=Vt[:, bi, 0:64], in_=pZ[kz][:, 0:64]),
                                     reads=[pZb[kz]], writes=[Vtb[bi]])
                                emit(ACT, lambda: nc.scalar.copy(out=Vt[:, bi, 128:192], in_=pZ[kz][:, 64:128]),
                                     reads=[pZb[kz]], writes=[Vtb[bi]])
                                if bi == 16:
                                    emit(ACT, lambda: nc.scalar.copy(out=vlast[:], in_=pZ[kz][:, 0:128]),
                                         reads=[pZb[kz]], writes=[vlastb])
                                    out_toks.append(QS.start(vl_o, vlast[:], reads=[vlastb]))

                        uv = lambda c: ubuf[:, c, 2:2 + W]
                        pzv = lambda k: pZ[k][:, :W]
                        ubw_ = [[ubb[c]] for c in range(4)]

                        def s3a():
                            if state["wprev"] is not None:
                                wp = state["wprev"]
                                emit(DVE, lambda: nc.vector.tensor_copy(out=ubuf[:, :, 0:2], in_=ubuf[:, :, wp:wp + 2]),
                                     reads=ubb, writes=ubb)
                            else:
                                emit(DVE, lambda: nc.vector.memset(ubuf[:, :, 0:2], 0.0), writes=ubb)
                            state["wprev"] = W

                        def s4():
                            k_norm(a, b, kind)
                            if b == CS:
                                out_toks.append(QS.start(kTl_o, knf[:, W - 128:W], reads=[knfb]))
                                out_toks.append(QS.start(cl_o, ubuf[:, :, W:W + 2], reads=ubb))

                        def mk(f, *args):
                            return lambda: f(*args)
                        steps = [s0, s1]
                        if kind != "halo":
                            steps += [mk(lambda i: q_chunk1(box["hs"], box["hsb"], W, i), i) for i in range(4)]
                        steps.append(s3a)
                        steps += [mk(lambda c: u_c1(box["hs"], box["hsb"], W, uv, pzv, ubw_, c), c) for c in range(4)]
                        steps += [mk(lambda c: u_h1(box["hs"], box["hsb"], W, uv, pzv, ubw_, c), c) for c in range(4)]
                        steps.append(s4)
                        if kind == "halo":
                            return steps + [s_next]
                        steps += [mk(lambda i: head_norm(zq[:, i, :W], W, qkg8[:, 0:1], qnb2[par][:, i, :W], [qnbb2[par][i]], [zqb[i]], slot=i), i)
                                  for i in range(4)]
                        steps += [mk(lambda c: conv_only(W, lambda c_: cv[:, c_, :W],
                                                         lambda c_: [ubuf[:, c_, j:j + W] for j in range(3)],
                                                         lambda c_: [ubb[c_]], cs=[c]), c) for c in range(4)]
                        steps += [mk(lambda c: gate_only(box["hs"], box["hsb"], W, par, cs=[c]), c) for c in range(4)]
                        steps.insert(min(len(steps), KNPOS), s_next)
                        return steps

                    hpc = [0]

                    def back_steps(a, b, par):
                        W = b - a
                        passes = []
                        for bo in range(W // 128):
                            for hp in range(2):
                                passes.append((a // 128 + bo, bo * 128, hp, hpc[0] % 2))
                                hpc[0] += 1
                        steps = []
                        n = len(passes)

                        def mkA(p):
                            return lambda: attn_A(p[0], p[1], p[2], par, p[3])

                        def mkB1(idx):
                            p = passes[idx]
                            return lambda: attn_B(p[0], p[1], p[2], par, p[3])

                        def mkB2(idx):
                            p = passes[idx]

                            def f():
                                attn_B2(p[0], p[1], p[2], par, p[3])
                                if p[2] == 1:
                                    attn_fin(p[1], par)
                            return f
                        steps.append((mkA(passes[0]), SLOT[0]))
                        for idx in range(n):
                            if idx + 1 < n:
                                steps.append((mkA(passes[idx + 1]), SLOT[0]))
                            steps.append((mkB1(idx), SLOT[1]))
                            steps.append((mkB2(idx), SLOT[2]))
                        for j0 in range(0, 8, 2):
                            steps.append(((lambda j0_: (lambda: out_proj(a, b, par, range(j0_, j0_ + 2))))(j0), SLOT[3]))
                        return steps

                    pW = None

                    def warm():
                        for _ in range(NWARM):
                            nc.tensor.matmul(pW[:, :], ident[:], kT_all[:, 0:512], start=True, stop=True)

                    def interleave(bs, fs):
                        out = []
                        j = 0
                        for (st, k) in bs:
                            if NWARM:
                                out.append(warm)
                            out.append(st)
                            for _ in range(k):
                                if j < len(fs):
                                    out.append(fs[j]); j += 1
                        out += fs[j:]
                        return out

                    tiles = [("halo", 0, HALO)] + [("prompt", HALO + i * MT, HALO + (i + 1) * MT) for i in range(NPR // MT)]
                    def nx(t):
                        return tiles[t + 1] if t + 1 < len(tiles) else None
                    norm_only(tiles[0][1], tiles[0][2], 0)
                    for st_ in front_steps(*tiles[0], 0, tidx=0, nxt=nx(0)):
                        st_()
                    for st_ in front_steps(*tiles[1], 1, tidx=1, nxt=nx(1)):
                        st_()
                    for t in range(1, len(tiles)):
                        bs = back_steps(tiles[t][1], tiles[t][2], t % 2)
                        fs = front_steps(*tiles[t + 1], (t + 1) % 2, tidx=t + 1, nxt=nx(t + 1)) if t + 1 < len(tiles) else []
                        for st_ in interleave(bs, fs):
                            st_()
                    barrier()

                with ExitStack() as S_:
                    qnb, qnbb, mixb, mixbb = qnb2[0], qnbb2[0], mixb2[0], mixbb2[0]
                    kTs = sb(S_, [128, NSEQ, 128], BF16, "kTs")
                    Vc = sb(S_, [128, NSEQ, 192], BF16, "Vc")
                    Vn = sb(S_, [4, NSEQ, 192], BF16, "Vn")
                    vsnf = sb(S_, [4, 8, 128], F32, "vsnf")
                    vsnfb = Buf()
                    Vnb = Buf()
                    ubs = sb(S_, [128, 4, NSEQ, 6], F32, "ubs")
                    ubsb = Buf()
                    qs = sb(S_, [128, NSEQ, 16], BF16, "qs")
                    qsb = Buf()
                    NQ = 8
                    NSL = 2 * NQ
                    Ss = sb(S_, [16, NSL, 132], F32, "Ss")
                    Ssb_s = Buf()
                    Pns = sb(S_, [16, NSL, 132], BF16, "Pns")
                    Pnsb = Buf()
                    PTcs = sb(S_, [128, NSL * 16], BF16, "PTcs")
                    PTns = sb(S_, [4, NSL * 16], BF16, "PTns")
                    PTcsb = Buf()
                    ss = {n: sb(S_, [16, NSL], F32, "s" + n) for n in ("mx", "m", "rs", "tmp", "es", "den", "rden")}
                    ssb = {n: Buf() for n in ss}
                    kTsb = Buf()
                    Vcb = Buf()
                    emit(DVE, lambda: nc.vector.memset(Vc[:, :, 64:128], 0.0), writes=[Vcb])
                    emit(DVE, lambda: nc.vector.memset(Vn[:], 0.0), writes=[Vnb])
                    QP.start(kTs[:], ckT_d, writes=[kTsb])
                    QP.start(Vc[:, :, 0:64], cvt_d[:, :, 0:64], writes=[Vcb])
                    QP.start(Vc[:, :, 128:192], cvt_d[:, :, 64:128], writes=[Vcb])
                    st_c = sb(S_, [128, 128], F32, "st_c")
                    st_cb = Buf()
                    cs_c = sb(S_, [128, 128], F32, "cs_c")
                    cs_cb = Buf()
                    ubs_st = Buf()
                    QS.start(st_c[:], scT_d.rearrange("p c q r -> p (c q r)"), writes=[st_cb])
                    emit(DVE, lambda: nc.vector.tensor_copy(out=ubs[:, :, :, 0:2],
                                                            in_=st_c[:].rearrange("p (c q r) -> p c q r", c=4, r=2)),
                         reads=[st_cb], writes=[ubs_st])

                    a, b = CS, NT
                    W = NSM
                    state["ti"] = 0
                    hs, hsb = front0(a, b)
                    pV = pA[0:4, :]
                    for seq in range(NSEQ):
                        pe_group([(pV[:, seq * 128:(seq + 1) * 128], hs[:, kc, seq * 4:(seq + 1) * 4],
                                   Win[:, kc, 640:768]) for kc in range(8)],
                                 reads=[Winb[1]] + hsb, writes=[bA[seq // 4]])
                    pVv = pV.rearrange("p (q n) -> p q n", n=128)
                    emit(ACT, lambda: nc.scalar.copy(out=Vn[:, :, 0:64], in_=pVv[:, :, 0:64]), reads=bA, writes=[Vnb])
                    emit(ACT, lambda: nc.scalar.copy(out=Vn[:, :, 128:192], in_=pVv[:, :, 64:128]), reads=bA, writes=[Vnb])
                    for hh in range(2):
                        emit(ACT, lambda: nc.scalar.copy(out=vsnf[:], in_=pVv[:, hh * 8:(hh + 1) * 8, :]),
                             reads=bA, writes=[vsnfb])
                        out_toks.append(QS.start(vsn_o[:, hh * 8:(hh + 1) * 8, :], vsnf[:], reads=[vsnfb]))
                    q_chunks(hs, hsb, W)
                    u_chunks(hs, hsb, W, lambda c: ubs[:, c, :, 2:6],
                             lambda k: pZ[k][:, 0:NSM].rearrange("p (q s) -> p q s", s=4), [[ubsb]] * 4)
                    k_norm(a, b, "sample")
                    out_toks.append(QS.start(ksn_o, knf[:, :W], reads=[knfb]))
                    emit(DVE, lambda: nc.vector.tensor_copy(out=cs_c[:].rearrange("p (c q r) -> p c q r", c=4, r=2),
                                                            in_=ubs[:, :, :, 4:6]), reads=[ubsb], writes=[cs_cb])
                    out_toks.append(QS.start(csn_o.rearrange("p c q r -> p (c q r)"), cs_c[:], reads=[cs_cb]))
                    for i in range(4):
                        head_norm(zq[:, i, :W], W, qkg8[:, 0:1], qnb[:, i, :W], [qnbb[i]], [zqb[i]], slot=i)
                    conv_only(W, lambda c: cv[:, c, 0:NSM].rearrange("p (q s) -> p q s", s=4),
                              lambda c: [ubs[:, c, :, j:j + 4] for j in range(3)], lambda c: [ubsb, ubs_st])
                    gate_only(hs, hsb, W, 0)
                    emit(DVE, lambda: nc.vector.tensor_copy(
                        out=qs[:].rearrange("p q (g s) -> p q g s", s=4),
                        in_=qnb[:, :, 0:NSM].rearrange("p g (q s) -> p q g s", s=4)),
                        reads=qnbb, writes=[qsb])
                    pSc = pA[0:16, 0:NSL * 128]
                    nbank = (NSL * 128) // 512
                    for part in range(NSEQ // NQ):
                        for si in range(NQ):
                            seq = part * NQ + si
                            for kv in range(2):
                                slot = kv * NQ + si
                                pe_group([(pSc[:, slot * 128:(slot + 1) * 128], qs[kv * 64:(kv + 1) * 64, seq, :],
                                           kTs[kv * 64:(kv + 1) * 64, seq, :])],
                                         reads=[qsb, kTsb], writes=[bA[slot // 4]])
                                pnew, pnewb = ((pO, pOb), (pstat, pstat_b))[kv]
                                pe_group([(pnew[0:16, si * 4:(si + 1) * 4], qs[kv * 64:(kv + 1) * 64, seq, :],
                                           kT_all[kv * 64:(kv + 1) * 64, CS + seq * 4:CS + seq * 4 + 4])],
                                         reads=[qsb, kTb[17]], writes=[pnewb])
                        emit(DVE, lambda: nc.vector.tensor_scalar(
                            out=Ss[:, :, 0:128], in0=pSc.rearrange("p (q n) -> p q n", n=128), scalar1=1.0,
                            scalar2=None, op0=ALU.mult), reads=bA[0:nbank], writes=[Ssb_s])
                        for kv in range(2):
                            pnew, pnewb = ((pO, pOb), (pstat, pstat_b))[kv]
                            emit(DVE, lambda: nc.vector.tensor_scalar(
                                out=Ss[:, kv * NQ:(kv + 1) * NQ, 128:132],
                                in0=pnew[0:16, 0:NQ * 4].rearrange("p (q n) -> p q n", n=4),
                                scalar1=1.0, scalar2=None, op0=ALU.mult), reads=[pnewb], writes=[Ssb_s])
                        for kv in range(2):
                            sv = Ss[:, kv * NQ:(kv + 1) * NQ, :]
                            emit(DVE, lambda: nc.vector.tensor_tensor(
                                out=sv, in0=sv, in1=Bias_s[:, kv, :].unsqueeze(1).to_broadcast([16, NQ, 132]),
                                op=ALU.add), reads=[Ssb_s, sconst], writes=[Ssb_s])
                        emit(DVE, lambda: nc.vector.reduce_max(out=ss["mx"][:], in_=Ss[:], axis=AX.X),
                             reads=[Ssb_s], writes=[ssb["mx"]])
                        sbc = sinks_s[:].unsqueeze(2).to_broadcast([16, 2, NQ])
                        emit(DVE, lambda: nc.vector.tensor_tensor(
                            out=ss["m"][:].rearrange("p (k q) -> p k q", k=2),
                            in0=ss["mx"][:].rearrange("p (k q) -> p k q", k=2), in1=sbc, op=ALU.max),
                            reads=[ssb["mx"], sconst2], writes=[ssb["m"]])
                        emit(DVE, lambda: nc.vector.tensor_tensor(
                            out=Ss[:], in0=Ss[:], in1=ss["m"][:].unsqueeze(2).to_broadcast([16, NSL, 132]),
                            op=ALU.subtract), reads=[Ssb_s, ssb["m"]], writes=[Ssb_s])
                        emit(ACT, lambda: nc.scalar.activation(out=Ss[:], in_=Ss[:], func=AF.Exp),
                             reads=[Ssb_s], writes=[Ssb_s])
                        emit(DVE, lambda: nc.vector.reduce_sum(out=ss["rs"][:], in_=Ss[:], axis=AX.X),
                             reads=[Ssb_s], writes=[ssb["rs"]])
                        emit(DVE, lambda: nc.vector.tensor_tensor(
                            out=ss["tmp"][:].rearrange("p (k q) -> p k q", k=2), in0=sbc,
                            in1=ss["m"][:].rearrange("p (k q) -> p k q", k=2), op=ALU.subtract),
                            reads=[ssb["m"], sconst2], writes=[ssb["tmp"]])
                        emit(ACT, lambda: nc.scalar.activation(out=ss["es"][:], in_=ss["tmp"][:], func=AF.Exp),
                             reads=[ssb["tmp"]], writes=[ssb["es"]])
                        emit(DVE, lambda: nc.vector.tensor_tensor(out=ss["den"][:], in0=ss["rs"][:], in1=ss["es"][:],
                                                                  op=ALU.add),
                             reads=[ssb["rs"], ssb["es"]], writes=[ssb["den"]])
                        emit(DVE, lambda: nc.vector.reciprocal(out=ss["rden"][:], in_=ss["den"][:]),
                             reads=[ssb["den"]], writes=[ssb["rden"]])
                        emit(DVE, lambda: nc.vector.tensor_tensor(
                            out=Pns[:], in0=Ss[:], in1=ss["rden"][:].unsqueeze(2).to_broadcast([16, NSL, 132]),
                            op=ALU.mult), reads=[Ssb_s, ssb["rden"]], writes=[Pnsb])
                        trs = []
                        for slot in range(NSL):
                            trs.append((pT[:, slot * 16:(slot + 1) * 16], Pns[0:16, slot, 0:128]))
                            trs.append((pT[0:4, 512 + slot * 16:512 + (slot + 1) * 16], Pns[0:16, slot, 128:132]))
                        _pre(PE, [Pnsb] + CB, [pTb])
                        inst = None
                        for (o_, i_) in trs:
                            inst = nc.tensor.matmul(o_, i_, ident[0:16, 0:16], start=True, stop=True)
                        tok_ = PE.mark(inst)
                        _commit(tok_, [Pnsb] + CB, [pTb])
                        emit(ACT, lambda: nc.scalar.copy(out=PTcs[:], in_=pT[:, 0:NSL * 16]), reads=[pTb], writes=[PTcsb])
                        emit(ACT, lambda: nc.scalar.copy(out=PTns[:], in_=pT[0:4, 512:512 + NSL * 16]), reads=[pTb],
                             writes=[PTcsb])
                        for si in range(NQ):
                            seq = part * NQ + si
                            mms = []
                            for kv in range(2):
                                slot = kv * NQ + si
                                mms.append((pstat[:, si * 16:(si + 1) * 16], Vc[:, seq, kv * 64:kv * 64 + 128],
                                            PTcs[:, slot * 16:(slot + 1) * 16]))
                                mms.append((pstat[:, si * 16:(si + 1) * 16], Vn[0:4, seq, kv * 64:kv * 64 + 128],
                                            PTns[0:4, slot * 16:(slot + 1) * 16]))
                            pe_group(mms, reads=[PTcsb, Vcb, Vnb], writes=[pstat_b])
                        emit(DVE, lambda: nc.vector.tensor_copy(
                            out=mixb[:, 0:4, part * NQ * 4:(part + 1) * NQ * 4].rearrange("p g (q s) -> p q g s", s=4),
                            in_=pstat[:, 0:NQ * 16].rearrange("p (q g s) -> p q g s", g=4, s=4)),
                            reads=[pstat_b], writes=mixbb[0:4])
                    out_proj(a, b, 0, range(8))
                    barrier()
            return out_toks

        def ple_phase(out_toks):
            with ExitStack() as ph:
                alloc_norm(ph, 512)
                Wpg = sb(ph, [128, 8, D], BF16, "Wpg")
                Wpp, pe_b = PLEW["Wpp"], PLEW["pe_b"]
                wb_ = Buf()
                hb2 = [sb(ph, [128, 8, 512], BF16, "hb2") for _ in range(2)]
                hb2b = [[Buf() for _ in range(8)] for _ in range(2)]
                sg = [sb(ph, [128, 512], F32, "sg") for _ in range(2)]
                sgb = [Buf() for _ in range(2)]
                tp = [sb(ph, [128, 512], F32, "tp") for _ in range(2)]
                tpb = [Buf() for _ in range(2)]
                pG = [ps(ph, [128, 512], F32, "pG") for _ in range(2)]
                pP = [ps(ph, [128, 512], F32, "pP") for _ in range(2)]
                pstat = ps(ph, [128, 512], F32, "pstat")
                pGb = [PBuf() for _ in range(2)]
                pPb = [PBuf() for _ in range(2)]
                pstat_b = PBuf()
                QP.start(Wpg[:], wpg_d.rearrange("(kc p) n -> p kc n", p=128), writes=[wb_])
                wb2_, wb3_ = PLEW["b2"], PLEW["b3"]
                yTv = yT.rearrange("(c p) t -> p c t", p=128)
                kk = 0
                def ple_norm(ti):
                    a_, b_ = FFN2_TILES[ti]
                    hs_ = hb2[ti % 2]
                    norm_tile(a_, b_, 3, lambda c: hs_[:, c, :b_ - a_], hb2b[ti % 2], pstat, pstat_b)
                ple_norm(0)
                for ti, (a, b) in enumerate(FFN2_TILES):
                    W = b - a
                    hs = hb2[ti % 2]
                    hsb = hb2b[ti % 2]
                    if ti + 1 < len(FFN2_TILES):
                        ple_norm(ti + 1)
                    for j in range(8):
                        k = kk % 2
                        kk += 1
                        pe_group([(pG[k][:, :W], Wpg[:, kc, j * 128:(j + 1) * 128], hs[:, kc, :W]) for kc in range(8)],
                                 reads=[wb_] + hsb, writes=[pGb[k]])
                        pe_group([(pP[k][:, :W], Wpp[:, m, j * 128:(j + 1) * 128], pe_b[:, m, a - HALO:b - HALO])
                                  for m in range(2)], reads=[wb2_, wb3_], writes=[pPb[k]])
                        emit(ACT, lambda: nc.scalar.activation(out=sg[k][:, :W], in_=pG[k][:, :W], func=AF.Sigmoid),
                             reads=[pGb[k]], writes=[sgb[k]])
                        emit(DVE, lambda: nc.vector.tensor_tensor(out=tp[k][:, :W], in0=sg[k][:, :W], in1=pP[k][:, :W],
                                                                  op=ALU.mult),
                             reads=[sgb[k], pPb[k]], writes=[tpb[k]])
                        emit(DVE, lambda: nc.vector.tensor_tensor(out=xres[:, j, a:b], in0=tp[k][:, :W], in1=xres[:, j, a:b],
                                                                  op=ALU.add),
                             reads=[tpb[k]] + xbufs(j, a, b), writes=xbufs(j, a, b))
                    out_toks.append(QS.start(yTv[:, :, a - HALO:b - HALO], xres[:, :, a:b], reads=xbufs_all(a, b)))
                for t in out_toks:
                    SP.wait(t)
                barrier()

        def dbg_finish():
            yTv = yT.rearrange("(c p) t -> p c t", p=128)
            t = QS.start(yTv, xres[:, :, HALO:NT], reads=xbufs_all(0, NT))
            SP.wait(t)

        if stop == "load":
            dbg_finish()
            return nc
        ffn_phase(w1g, w1u, w1d, 0, FFN1_TILES)
        if stop == "ffn1":
            dbg_finish()
            return nc
        toks = mixer_phase()
        if stop == "mixer":
            dbg_finish()
            return nc
        PLEW["Wpp"] = sb(top, [128, 2, D], BF16, "Wpp")
        PLEW["pe_b"] = sb(top, [128, 2, NOUT], BF16, "pe_b")
        PLEW["b2"], PLEW["b3"] = Buf(), Buf()

        def ple_prefetch():
            QP.start(PLEW["Wpp"][:], wpp_d.rearrange("(kc p) n -> p kc n", p=128), writes=[PLEW["b2"]])
            QP.start(PLEW["pe_b"][:], pT_d.rearrange("(kc p) n -> p kc n", p=128), writes=[PLEW["b3"]])
        AFTER_G1.append(ple_prefetch)
        ffn_phase(w2g, w2u, w2d, 2, FFN2_TILES)
        if stop == "ffn2":
            dbg_finish()
            return nc
        ple_phase(toks)
    return nc


def _rel_bucket_np(d):
    max_exact = 16
    df = np.maximum(d, 1).astype(np.float32)
    val = (np.log(df / np.float32(max_exact)) / np.float32(math.log(128 / 16)) * np.float32(32 - max_exact))
    large = max_exact + val.astype(np.int32)
    large = np.minimum(large, 31)
    return np.where(d < max_exact, d, large)


def _consts():
    E2 = np.zeros((33, 383), np.float32)
    for m in range(383):
        d = 255 - m
        if 0 <= d < 128:
            E2[int(_rel_bucket_np(np.array([d]))[0]), m] = 1.0
        else:
            E2[32, m] = -1e30
    onesD = np.full((128, 128), 1.0 / D, np.float32)
    bd64 = np.zeros((128, 128), np.float32)
    bd64[:64, :64] = 1.0 / 64
    bd64[64:, 64:] = 1.0 / 64
    ident = np.eye(128, dtype=np.float32)
    return E2, onesD, bd64, ident


_NC_CACHE = {}


def kernel(x_prompt, x_sample, p_prompt, p_sample, cache_k, cache_v, state_conv, rel_bias,
           g_ffn1, w1_gate, w1_up, w1_down, g_mix, w_in, q_norm, k_norm, sinks, w_conv, w_out,
           g_ffn2, w2_gate, w2_up, w2_down, g_ple, w_ple_gate, w_ple_proj):
    f = lambda a: np.ascontiguousarray(np.asarray(a, dtype=np.float32))
    x_prompt, x_sample, p_prompt, p_sample = f(x_prompt), f(x_sample), f(p_prompt), f(p_sample)
    cache_k, cache_v, state_conv, rel_bias = f(cache_k), f(cache_v), f(state_conv), f(rel_bias)
    E2, onesD, bd64, ident = _consts()
    qperm = np.concatenate([np.r_[i * 64:(i + 1) * 64, (4 + i) * 64:(5 + i) * 64] for i in range(4)])
    win = f(w_in)[0]
    win_p = f(np.concatenate([win[:, qperm], win[:, 512:]], axis=1))
    wout = f(w_out)[0]
    wout_p = f(np.concatenate([wout[qperm, :], wout[512:, :]], axis=0))
    gv = f(np.stack([f(g_ffn1)[0], f(g_mix)[0], f(g_ffn2)[0], f(g_ple)[0]]).reshape(4, 8, 128).transpose(2, 0, 1))
    qkg = f(np.stack([np.tile(f(q_norm)[0], 2), np.tile(f(k_norm)[0], 2)], axis=1))
    sk = f(sinks)[0]
    sinkb = f(np.broadcast_to(sk[HPERM][None, :], (128, 8)))
    sinks_s = np.zeros((16, 2), np.float32)
    for g in range(4):
        for kv in range(2):
            sinks_s[g * 4:(g + 1) * 4, kv] = sk[kv * 4 + g]
    tabp = f(rel_bias[:, HPERM])
    wconv = f(f(w_conv)[0].reshape(3, 4, 128).transpose(2, 1, 0))
    shared = {
        "tabp": tabp, "E2": E2, "onesD": onesD, "bd64": bd64, "ident": ident, "gv": gv, "qkg": qkg,
        "sinkb": sinkb, "sinks_s": sinks_s, "wconv": wconv,
        "w1g": f(w1_gate)[0], "w1u": f(w1_up)[0], "w1d": f(w1_down)[0], "win": win_p, "wout": wout_p,
        "w2g": f(w2_gate)[0], "w2u": f(w2_up)[0], "w2d": f(w2_down)[0], "wpg": f(w_ple_gate)[0],
        "wpp": f(w_ple_proj)[0],
    }
    in_maps = []
    for c in range(NCORES):
        b, j = divmod(c, 4)
        t0 = j * NPR
        halo = x_prompt[b, t0 - HALO:t0] if j > 0 else np.zeros((HALO, D), np.float32)
        sq = slice(c * NSEQ, (c + 1) * NSEQ)
        xs = x_sample[sq].reshape(NSM, D)
        xT = f(np.concatenate([halo, x_prompt[b, t0:t0 + NPR], xs], axis=0).T)
        pT = f(np.concatenate([p_prompt[0, b, t0:t0 + NPR], p_sample[0, sq].reshape(NSM, DPLE)], axis=0).T)
        ck = cache_k[0, sq].reshape(NSEQ, 128, 128)
        cvv = cache_v[0, sq].reshape(NSEQ, 128, 128)
        sc = state_conv[0, sq]
        scT = f(sc.transpose(2, 0, 1).reshape(4, 128, NSEQ, 2).transpose(1, 0, 2, 3))
        m = dict(shared)
        m.update({
            "xT": xT, "pT": pT, "ckT": f(ck.transpose(2, 0, 1)), "ck_nat": f(ck),
            "cvt": f(cvv.transpose(1, 0, 2)), "cv_nat": f(cvv), "scT": scT,
            "hflag": np.full((128, 1), -1e30 if j == 0 else 0.0, np.float32),
        })
        in_maps.append(m)

    if "nc" not in _NC_CACHE:
        _NC_CACHE["nc"] = build_nc()
    nc = _NC_CACHE["nc"]
    res = run_bass_kernel_spmd(nc, in_maps, core_ids=list(range(NCORES)))
    R = res.results

    B, T = x_prompt.shape[0], x_prompt.shape[1]
    y_prompt = np.zeros((B, T, D), np.float32)
    y_sample = np.zeros((x_sample.shape[0], 4, D), np.float32)
    kwp = np.zeros((1, B, 128, 2, 64), np.float32)
    vwp = np.zeros((1, B, 128, 2, 64), np.float32)
    cvp = np.zeros((1, B, 2, 512), np.float32)
    kws = np.zeros((1, x_sample.shape[0], 128, 2, 64), np.float32)
    vws = np.zeros((1, x_sample.shape[0], 128, 2, 64), np.float32)
    cvs = np.zeros((1, x_sample.shape[0], 2, 512), np.float32)
    for c in range(NCORES):
        b, j = divmod(c, 4)
        t0 = j * NPR
        r = R[c]
        sq = slice(c * NSEQ, (c + 1) * NSEQ)
        y = np.asarray(r["yT"])
        y_prompt[b, t0:t0 + NPR] = y[:, :NPR].T
        y_sample[sq] = y[:, NPR:].T.reshape(NSEQ, 4, D)
        if j == 3:
            kwp[0, b] = np.asarray(r["kTl"]).T.reshape(128, 2, 64)
            vwp[0, b] = np.asarray(r["vl"]).reshape(128, 2, 64)
            cvp[0, b] = np.asarray(r["cl"]).transpose(2, 1, 0).reshape(2, 512)
        kws[0, sq, 0:124] = np.asarray(r["kws_old"]).reshape(NSEQ, 124, 2, 64)
        kws[0, sq, 124:128] = np.asarray(r["ksn"]).T.reshape(NSEQ, 4, 2, 64)
        vws[0, sq, 0:124] = np.asarray(r["vws_old"]).reshape(NSEQ, 124, 2, 64)
        vws[0, sq, 124:128] = np.asarray(r["vsn"]).transpose(1, 0, 2).reshape(NSEQ, 4, 2, 64)
        cvs[0, sq] = np.asarray(r["csn"]).transpose(2, 3, 1, 0).reshape(NSEQ, 2, 512)
    return (y_prompt, y_sample, kwp, vwp, cvp, kws, vws, cvs)
```

```python
import math
from contextlib import ExitStack

import numpy as np
import concourse.bass as bass
import concourse.mybir as mybir
from concourse.bass_utils import run_bass_kernel_spmd

F32 = mybir.dt.float32
BF16 = mybir.dt.bfloat16
AF = mybir.ActivationFunctionType
ALU = mybir.AluOpType
AX = mybir.AxisListType

NCORES = 8
D = 1024
DFF = 2816
NFF = 22
DPLE = 256
INC = 2304
HALO = 128
NPR = 2048
NSM = 64
NT = HALO + NPR + NSM
NOUT = NPR + NSM
CS = HALO + NPR
MT = 256
EPS = 1e-6
NSEQ = 16
HPERM = [0, 1, 4, 5, 2, 3, 6, 7]

FFN_GROUPS = [(0, 5), (5, 10), (10, 14), (14, 18), (18, 22)]
FFN1_TILES = [(0, 128), (128, 640), (640, 1152), (1152, 1664), (1664, 2176), (2176, 2240)]
FFN2_TILES = FFN1_TILES[1:]


class Eng:
    def __init__(self, eng, sem, is_pe=False):
        self.eng = eng
        self.sem = sem
        self.cnt = 0
        self.waited = {}
        self.is_pe = is_pe

    def wait(self, tok):
        if tok is None:
            return
        s, v = tok
        if self.is_pe and s is self.sem:
            return
        if self.waited.get(s.num, 0) >= v:
            return
        self.eng.wait_ge(s, v)
        self.waited[s.num] = v

    def mark(self, inst):
        self.cnt += 1
        inst.then_inc(self.sem, 1)
        return (self.sem, self.cnt)


class _Stop(Exception):
    pass


class Buf:
    __slots__ = ("w", "r", "excl")

    def __init__(self, excl=False):
        self.w = None
        self.r = {}
        self.excl = excl


def PBuf():
    return Buf(excl=True)


def _pre(E, reads, writes):
    for b in reads:
        E.wait(b.w)
        if b.excl:
            for t in list(b.r.values()):
                if t[0] is not E.sem:
                    E.wait(t)
    for b in writes:
        E.wait(b.w)
        for t in list(b.r.values()):
            E.wait(t)


def _commit(tok, reads, writes):
    for b in reads:
        cur = b.r.get(tok[0].num)
        if cur is None or cur[1] < tok[1]:
            b.r[tok[0].num] = tok
    for b in writes:
        b.w = tok
        b.r = {}


def emit(E, fn, reads=(), writes=()):
    _pre(E, reads, writes)
    tok = E.mark(fn())
    _commit(tok, reads, writes)
    return tok


class DmaQ:
    def __init__(self, E, sems):
        self.E = E
        self.slots = [[s, 0] for s in sems]
        self.i = 0

    def start(self, out, in_, reads=(), writes=()):
        E = self.E
        _pre(E, reads, writes)
        slot = self.slots[self.i % len(self.slots)]
        self.i += 1
        if slot[1]:
            E.wait((slot[0], slot[1]))
        inst = E.eng.dma_start(out=out, in_=in_)
        slot[1] += 16
        inst.then_inc(slot[0], 16)
        tok = (slot[0], slot[1])
        _commit(tok, reads, writes)
        return tok

    def outstanding(self):
        return [(s, v) for s, v in self.slots if v]


def build_nc(stop=None):
    nc = bass.Bass("TRN2", target_bir_lowering=False)

    def din(name, shape):
        return nc.dram_tensor(name, list(shape), F32, kind="ExternalInput").ap()

    def dout(name, shape):
        return nc.dram_tensor(name, list(shape), F32, kind="ExternalOutput").ap()

    xT = din("xT", [D, NT])
    pT_d = din("pT", [DPLE, NOUT])
    ckT_d = din("ckT", [128, NSEQ, 128])
    ck_nat = din("ck_nat", [NSEQ, 128, 128])
    cvt_d = din("cvt", [128, NSEQ, 128])
    cv_nat = din("cv_nat", [NSEQ, 128, 128])
    scT_d = din("scT", [128, 4, NSEQ, 2])
    tabp_d = din("tabp", [32, 8])
    E2_d = din("E2", [33, 383])
    onesD_d = din("onesD", [128, 128])
    bd64_d = din("bd64", [128, 128])
    ident_d = din("ident", [128, 128])
    gv_d = din("gv", [128, 4, 8])
    qkg_d = din("qkg", [128, 2])
    sinkb_d = din("sinkb", [128, 8])
    sinks_d = din("sinks_s", [16, 2])
    wconv_d = din("wconv", [128, 4, 3])
    hflag_d = din("hflag", [128, 1])
    w1g = din("w1g", [D, DFF])
    w1u = din("w1u", [D, DFF])
    w1d = din("w1d", [DFF, D])
    win_d = din("win", [D, INC])
    wout_d = din("wout", [D, D])
    w2g = din("w2g", [D, DFF])
    w2u = din("w2u", [D, DFF])
    w2d = din("w2d", [DFF, D])
    wpg_d = din("wpg", [D, D])
    wpp_d = din("wpp", [DPLE, D])

    yT = dout("yT", [D, NOUT])
    kTl_o = dout("kTl", [128, 128])
    vl_o = dout("vl", [128, 128])
    cl_o = dout("cl", [128, 4, 2])
    kws_o = dout("kws_old", [NSEQ, 124, 128])
    vws_o = dout("vws_old", [NSEQ, 124, 128])
    ksn_o = dout("ksn", [128, NSM])
    vsn_o = dout("vsn", [4, NSEQ, 128])
    csn_o = dout("csn", [128, 4, NSEQ, 2])

    Ubc_t = nc.dram_tensor("Ubc", [128, 8, 383], F32, kind="Internal")
    Ubc = Ubc_t.ap()

    uid = [0]

    with ExitStack() as top:
        def sem(name):
            return top.enter_context(nc.semaphore(name))

        PE = Eng(nc.tensor, sem("s_pe"), is_pe=True)
        ACT = Eng(nc.scalar, sem("s_act"))
        DVE = Eng(nc.vector, sem("s_dve"))
        POOL = Eng(nc.gpsimd, sem("s_pool"))
        SP = Eng(nc.sync, sem("s_sp"))
        import os
        NSQ = int(os.environ.get("KNSQ", "10"))
        KSKIP = os.environ.get("KSKIP", "").split(",")
        CONV_ON_POOL = os.environ.get("KCONVPOOL", "0") == "1"
        NORM_ADD_POOL = os.environ.get("KNORMPOOL", "1") == "1"
        NWARM = int(os.environ.get("KNWARM", "0"))
        KNPOS = int(os.environ.get("KNPOS", "9"))
        SLOT = [int(x) for x in os.environ.get("KSLOT", "2,2,1,2").split(",")]
        QS = DmaQ(SP, [sem("qs%d" % i) for i in range(NSQ)])
        QP = DmaQ(POOL, [sem("qp%d" % i) for i in range(NSQ)])
        ENGS = (PE, ACT, DVE, POOL, SP)

        def sb(stack, shape, dt, name="t"):
            uid[0] += 1
            return stack.enter_context(nc.sbuf_tensor("%s_%d" % (name, uid[0]), list(shape), dt))

        def ps(stack, shape, dt, name="p"):
            uid[0] += 1
            return stack.enter_context(nc.psum_tensor("%s_%d" % (name, uid[0]), list(shape), dt))

        QA_REF = []
        X_REST = []
        AFTER_G1 = []
        PLEW = {}

        def barrier(queues=None):
            toks = [(E.sem, E.cnt) for E in (PE, ACT, DVE, POOL) if E.cnt > 0]
            for q_ in (queues if queues is not None else (QS, QP)):
                toks += q_.outstanding()
            if queues is None and QA_REF:
                toks += QA_REF[0].outstanding()
            for E in ENGS:
                for t in toks:
                    E.wait(t)

        def pe_group(mms, reads=(), writes=()):
            _pre(PE, reads, writes)
            n = len(mms)
            inst = None
            for i, (o, l, r) in enumerate(mms):
                inst = nc.tensor.matmul(o, l, r, start=(i == 0), stop=(i == n - 1))
            tok = PE.mark(inst)
            _commit(tok, reads, writes)
            return tok

        def pe_transposes(trs, ident_ap_fn, reads=(), writes=()):
            _pre(PE, reads, writes)
            inst = None
            for (o, i_) in trs:
                inst = nc.tensor.transpose(o, i_, ident_ap_fn(i_))
            tok = PE.mark(inst)
            _commit(tok, reads, writes)
            return tok

        xres = sb(top, [128, 8, NT], F32, "xres")
        xb = [[Buf() for _ in range(18)] for _ in range(8)]

        def xbufs(c, a, b):
            return [xb[c][k] for k in range(a // 128, (b + 127) // 128)]

        def xbufs_all(a, b):
            r = []
            for c in range(8):
                r += xbufs(c, a, b)
            return r

        onesD = sb(top, [128, 128], BF16, "onesD")
        bd64 = sb(top, [128, 128], BF16, "bd64")
        ident = sb(top, [128, 128], BF16, "ident")
        gv = sb(top, [128, 4, 8], F32, "gv")
        qkg = sb(top, [128, 2], F32, "qkg")
        sinkb = sb(top, [128, 8], F32, "sinkb")
        nsinkb = sb(top, [128, 8], F32, "nsinkb")
        wconv = sb(top, [128, 4, 3], F32, "wconv")
        hflag = sb(top, [128, 1], F32, "hflag")
        epst = sb(top, [128, 1], F32, "epst")
        NS = {}

        def alloc_norm(stack, Wn, nb=2):
            NS["sq"] = [sb(stack, [128, Wn], BF16, "sq") for _ in range(3)]
            NS["sqb"] = [Buf() for _ in range(3)]
            NS["rt"] = [sb(stack, [128, Wn], F32, "rt") for _ in range(nb)]
            NS["rtb"] = [Buf() for _ in range(nb)]
            NS["rstd"] = [sb(stack, [128, Wn], F32, "rstd") for _ in range(nb)]
            NS["rstdb"] = [Buf() for _ in range(nb)]
        CB = []

        def newcb():
            b_ = Buf()
            CB.append(b_)
            return b_
        cnt = {"sq": 0, "rt": 0}

        xTv = xT.rearrange("(c p) t -> p c t", p=128)
        QA = DmaQ(ACT, [sem("qa%d" % i) for i in range(8)])
        QA_REF.append(QA)
        for c in range(8):
            q_ = QS if c % 2 == 0 else QA
            q_.start(xres[:, c, 0:640], xTv[:, c, 0:640], writes=xbufs(c, 0, 640))

        def x_rest():
            for c in range(8):
                QS.start(xres[:, c, 640:NT], xTv[:, c, 640:NT], writes=xbufs(c, 640, NT))
        X_REST.append(x_rest)
        for dst, src in ((gv, gv_d), (qkg, qkg_d), (sinkb, sinkb_d), (wconv, wconv_d), (hflag, hflag_d)):
            QS.start(dst[:], src, writes=[newcb()])
        for dst, src in ((onesD, onesD_d), (bd64, bd64_d), (ident, ident_d)):
            QP.start(dst[:], src, writes=[newcb()])
        emit(DVE, lambda: nc.vector.memset(epst[:], EPS), writes=[newcb()])
        qkg8 = sb(top, [128, 1], F32, "qkg8")
        emit(DVE, lambda: nc.vector.tensor_scalar(out=qkg8[:], in0=qkg[:, 0:1], scalar1=0.125, scalar2=None,
                                                  op0=ALU.mult), reads=list(CB), writes=[newcb()])
        emit(DVE, lambda: nc.vector.tensor_scalar(out=nsinkb[:], in0=sinkb[:], scalar1=-1.0, scalar2=None,
                                                  op0=ALU.mult), reads=list(CB), writes=[newcb()])

        def norm_tile(a, b, gsel, out_fn, out_bufs, pstat, pstat_b):
            sq, sqb, rt, rtb, rstd, rstdb = NS["sq"], NS["sqb"], NS["rt"], NS["rtb"], NS["rstd"], NS["rstdb"]
            W = b - a
            for c in range(8):
                i = cnt["sq"] % 3
                cnt["sq"] += 1
                emit(ACT, lambda: nc.scalar.activation(out=sq[i][:, :W], in_=xres[:, c, a:b], func=AF.Square),
                     reads=xbufs(c, a, b), writes=[sqb[i]])
                _pre(PE, [sqb[i]] + CB, [pstat_b] if c == 0 else [])
                inst = nc.tensor.matmul(pstat[:, :W], onesD[:], sq[i][:, :W], start=(c == 0), stop=(c == 7))
                tok = PE.mark(inst)
                _commit(tok, [sqb[i]] + CB, [pstat_b] if c == 7 else [])
            j = cnt["rt"] % len(NS["rt"])
            cnt["rt"] += 1
            emit(ACT, lambda: nc.scalar.activation(out=rt[j][:, :W], in_=pstat[:, :W], func=AF.Ln,
                                                   bias=epst[:, 0:1], scale=1.0),
                 reads=[pstat_b] + CB, writes=[rtb[j]])
            emit(ACT, lambda: nc.scalar.activation(out=rstd[j][:, :W], in_=rt[j][:, :W], func=AF.Exp, scale=-0.5),
                 reads=[rtb[j]], writes=[rstdb[j]])
            for c in range(8):
                emit(DVE, lambda: nc.vector.scalar_tensor_tensor(
                    out=out_fn(c), in0=xres[:, c, a:b], scalar=gv[:, gsel, c:c + 1], in1=rstd[j][:, :W],
                    op0=ALU.mult, op1=ALU.mult),
                    reads=xbufs(c, a, b) + [rstdb[j]] + CB, writes=[out_bufs[c]])

        def ffn_phase(wg_d, wu_d, wd_d, gsel, tiles):
            with ExitStack() as ph:
                alloc_norm(ph, 512, nb=1)
                hb = sb(ph, [128, 8, NT], BF16, "hb")
                hbb = [[Buf() for _ in range(8)] for _ in tiles]
                Wg = [sb(ph, [128, 8, 640], BF16, "Wg") for _ in range(2)]
                Wu = [sb(ph, [128, 8, 640], BF16, "Wu") for _ in range(2)]
                Wd = [sb(ph, [128, 5, 1024], BF16, "Wd") for _ in range(2)]
                Wgb = [Buf() for _ in range(2)]
                Wub = [Buf() for _ in range(2)]
                Wdb = [Buf() for _ in range(2)]
                A = [sb(ph, [128, 5, 512], BF16, "A") for _ in range(2)]
                Ab = [[Buf() for _ in range(5)] for _ in range(2)]
                S = [sb(ph, [128, 512], F32, "S") for _ in range(2)]
                Sb = [Buf() for _ in range(2)]
                pG = [ps(ph, [128, 512], F32, "pG") for _ in range(2)]
                pU = [ps(ph, [128, 512], F32, "pU") for _ in range(2)]
                pY = [ps(ph, [128, 512], F32, "pY") for _ in range(2)]
                pstat = ps(ph, [128, 512], F32, "pstat")
                pGb = [PBuf() for _ in range(2)]
                pUb = [PBuf() for _ in range(2)]
                pYb = [PBuf() for _ in range(2)]
                pstat_b = PBuf()

                def load_group(gi):
                    c0, c1 = FFN_GROUPS[gi]
                    s = gi % 2
                    gw = (c1 - c0) * 128
                    QP.start(Wg[s][:, :, 0:gw], wg_d[:, c0 * 128:c1 * 128].rearrange("(kc p) n -> p kc n", p=128),
                             writes=[Wgb[s]])
                    QP.start(Wu[s][:, :, 0:gw], wu_d[:, c0 * 128:c1 * 128].rearrange("(kc p) n -> p kc n", p=128),
                             writes=[Wub[s]])
                    QP.start(Wd[s][:, 0:c1 - c0, :], wd_d[c0 * 128:c1 * 128, :].rearrange("(g p) n -> p g n", p=128),
                             writes=[Wdb[s]])

                load_group(0)
                if X_REST:
                    X_REST.pop()()
                def ffn_norm(ti):
                    a_, b_ = tiles[ti]
                    norm_tile(a_, b_, gsel, lambda c: hb[:, c, a_:b_], hbb[ti], pstat, pstat_b)
                ffn_norm(0)
                if len(tiles) > 1:
                    ffn_norm(1)
                load_group(1)
                while AFTER_G1:
                    AFTER_G1.pop(0)()

                items = [(gi, ti) for gi in range(len(FFN_GROUPS)) for ti in range(len(tiles))]
                kc_cnt = [0]
                y_cnt = [0]

                def stage1(idx):
                    gi, ti = items[idx]
                    if gi == 0 and ti + 2 < len(tiles):
                        ffn_norm(ti + 2)
                    a, b = tiles[ti]
                    W = b - a
                    c0, c1 = FFN_GROUPS[gi]
                    s = gi % 2
                    ai = idx % 2
                    for cl in range(c1 - c0):
                        k = kc_cnt[0] % 2
                        kc_cnt[0] += 1
                        pe_group([(pG[k][:, :W], Wg[s][:, kc, cl * 128:(cl + 1) * 128], hb[:, kc, a:b]) for kc in range(8)],
                                 reads=[Wgb[s]] + hbb[ti], writes=[pGb[k]])
                        pe_group([(pU[k][:, :W], Wu[s][:, kc, cl * 128:(cl + 1) * 128], hb[:, kc, a:b]) for kc in range(8)],
                                 reads=[Wub[s]] + hbb[ti], writes=[pUb[k]])
                        emit(ACT, lambda: nc.scalar.activation(out=S[k][:, :W], in_=pG[k][:, :W], func=AF.Silu),
                             reads=[pGb[k]], writes=[Sb[k]])
                        emit(DVE, lambda: nc.vector.tensor_tensor(out=A[ai][:, cl, :W], in0=S[k][:, :W], in1=pU[k][:, :W],
                                                                  op=ALU.mult),
                             reads=[Sb[k], pUb[k]], writes=[Ab[ai][cl]])

                def stage2(idx):
                    gi, ti = items[idx]
                    a, b = tiles[ti]
                    W = b - a
                    c0, c1 = FFN_GROUPS[gi]
                    s = gi % 2
                    ai = idx % 2
                    n = c1 - c0
                    for j in range(8):
                        k = y_cnt[0] % 2
                        y_cnt[0] += 1
                        pe_group([(pY[k][:, :W], Wd[s][:, cl, j * 128:(j + 1) * 128], A[ai][:, cl, :W]) for cl in range(n)],
                                 reads=[Wdb[s]] + Ab[ai][:n], writes=[pYb[k]])
                        emit(DVE, lambda: nc.vector.scalar_tensor_tensor(
                            out=xres[:, j, a:b], in0=pY[k][:, :W], scalar=0.5, in1=xres[:, j, a:b],
                            op0=ALU.mult, op1=ALU.add),
                            reads=[pYb[k]] + xbufs(j, a, b), writes=xbufs(j, a, b))
                    if ti == len(tiles) - 1 and gi + 2 < len(FFN_GROUPS):
                        load_group(gi + 2)

                stage1(0)
                for idx in range(len(items)):
                    if idx + 1 < len(items):
                        stage1(idx + 1)
                    stage2(idx)
                barrier()

        def mixer_phase():
            out_toks = []
            with ExitStack() as ph:
                alloc_norm(ph, MT)
                Win = sb(ph, [128, 8, INC], BF16, "Win")
                Wout = sb(ph, [128, 8, D], BF16, "Wout")
                WinSeg = [(0, 512), (512, 768), (768, 1280), (1280, 1792), (1792, 2304)]
                Winb = [Buf() for _ in WinSeg]
                Woutb = Buf()
                Bias_s = sb(ph, [16, 2, 132], F32, "Bias_s")
                sinks_s = sb(ph, [16, 2], F32, "sinks_s")
                sconst = Buf()
                sconst2 = Buf()
                kT_all = sb(ph, [128, NT], BF16, "kT_all")
                kTb = [Buf() for _ in range(18)]
                hbm = [sb(ph, [128, 8, MT], BF16, "hbm") for _ in range(2)]
                hbmb = [[Buf() for _ in range(8)] for _ in range(2)]
                zq = sb(ph, [128, 4, MT], F32, "zq")
                zqb = [Buf() for _ in range(4)]
                zk = sb(ph, [128, MT], F32, "zk")
                zkb = Buf()
                cv = sb(ph, [128, 4, MT], F32, "cv")
                cvb = [Buf() for _ in range(4)]
                qnb2 = [sb(ph, [128, 4, MT], BF16, "qnb") for _ in range(2)]
                qnbb2 = [[Buf() for _ in range(4)] for _ in range(2)]
                knf = sb(ph, [128, MT], F32, "knf")
                knfb = Buf()
                mixb2 = [sb(ph, [128, 8, MT], BF16, "mixb") for _ in range(2)]
                mixbb2 = [[Buf() for _ in range(8)] for _ in range(2)]

                pA = ps(ph, [128, 2048], F32, "pA")
                bA = [PBuf() for _ in range(4)]
                pstat = ps(ph, [128, 512], F32, "pstat")
                pstat_b = PBuf()
                pT = ps(ph, [128, 1024], F32, "pT")
                pTb = PBuf()
                pO = ps(ph, [128, 512], F32, "pO")
                pOb = PBuf()
                pZ = [pA[:, 0:512], pA[:, 512:1024]]
                pZb = [bA[0], bA[1]]
                pS = pA[:, 1024:2048]
                pSb = [bA[2], bA[3]]

                winv = win_d.rearrange("(kc p) n -> p kc n", p=128)
                for si in (1, 3, 4, 0, 2):
                    lo, hi = WinSeg[si]
                    QP.start(Win[:, :, lo:hi], winv[:, :, lo:hi], writes=[Winb[si]])
                QP.start(Wout[:], wout_d.rearrange("(kc p) n -> p kc n", p=128), writes=[Woutb])
                QS.start(sinks_s[:], sinks_d, writes=[sconst2])
                out_toks.append(QS.start(kws_o, ck_nat[:, 4:128, :]))
                out_toks.append(QS.start(vws_o, cv_nat[:, 4:128, :]))

                state = {"ti": 0, "z": 0, "wprev": None}
                hsq = [sb(ph, [128, MT], BF16, "hsq") for _ in range(5)]
                hsqb = [Buf() for _ in range(5)]

                def presq(slot, src_ap, W, src_bufs):
                    emit(ACT, lambda: nc.scalar.activation(out=hsq[slot][:, :W], in_=src_ap, func=AF.Square),
                         reads=src_bufs, writes=[hsqb[slot]])

                def zmm(wcol, seg, hs, hsb, W):
                    k = state["z"] % 2
                    state["z"] += 1
                    pe_group([(pZ[k][:, :W], Win[:, kc, wcol:wcol + 128], hs[:, kc, :W]) for kc in range(8)],
                             reads=[Winb[seg]] + hsb, writes=[pZb[k]])
                    return k

                def head_norm(src_ap, W, gap, out_ap, out_bufs, src_bufs, slot=None):
                    sq, sqb, rt, rtb, rstd, rstdb = NS["sq"], NS["sqb"], NS["rt"], NS["rtb"], NS["rstd"], NS["rstdb"]
                    pe_group([(pstat[:, :W], bd64[:], hsq[slot][:, :W])], reads=[hsqb[slot]] + CB, writes=[pstat_b])
                    j = cnt["rt"] % len(NS["rt"])
                    cnt["rt"] += 1
                    emit(ACT, lambda: nc.scalar.activation(out=rt[j][:, :W], in_=pstat[:, :W], func=AF.Ln,
                                                           bias=epst[:, 0:1], scale=1.0),
                         reads=[pstat_b] + CB, writes=[rtb[j]])
                    emit(ACT, lambda: nc.scalar.activation(out=rstd[j][:, :W], in_=rt[j][:, :W], func=AF.Exp, scale=-0.5),
                         reads=[rtb[j]], writes=[rstdb[j]])
                    emit(DVE, lambda: nc.vector.scalar_tensor_tensor(
                        out=out_ap, in0=src_ap, scalar=gap, in1=rstd[j][:, :W],
                        op0=ALU.mult, op1=ALU.mult),
                        reads=src_bufs + [rstdb[j]] + CB, writes=out_bufs)

                def norm_only(a, b, ti):
                    W = b - a
                    hs = hbm[ti % 2]
                    norm_tile(a, b, 1, lambda c: hs[:, c, :W], hbmb[ti % 2], pstat, pstat_b)

                def kchunk(a, b, ti):
                    W = b - a
                    hs = hbm[ti % 2]
                    hsb = hbmb[ti % 2]
                    k = zmm(512, 1, hs, hsb, W)
                    emit(ACT, lambda: nc.scalar.copy(out=zk[:, :W], in_=pZ[k][:, :W]), reads=[pZb[k]], writes=[zkb])
                    presq(4, zk[:, :W], W, [zkb])
                    return hs, hsb

                def front0(a, b):
                    ti = state["ti"]
                    state["ti"] += 1
                    norm_only(a, b, ti)
                    return kchunk(a, b, ti)

                def q_chunk1(hs, hsb, W, i):
                    k = zmm(i * 128, 0, hs, hsb, W)
                    emit(ACT, lambda: nc.scalar.copy(out=zq[:, i, :W], in_=pZ[k][:, :W]), reads=[pZb[k]],
                         writes=[zqb[i]])
                    presq(i, zq[:, i, :W], W, [zqb[i]])

                def q_chunks(hs, hsb, W):
                    for i in range(4):
                        q_chunk1(hs, hsb, W, i)

                def u_c1(hs, hsb, W, uview, pzview, ubw, c):
                    k = zmm(1280 + c * 128, 3, hs, hsb, W)
                    emit(ACT, lambda: nc.scalar.copy(out=uview(c), in_=pzview(k)), reads=[pZb[k]], writes=ubw[c])

                def u_h1(hs, hsb, W, uview, pzview, ubw, c):
                    k = zmm(1792 + c * 128, 4, hs, hsb, W)
                    emit(DVE, lambda: nc.vector.tensor_tensor(out=uview(c), in0=uview(c), in1=pzview(k), op=ALU.mult),
                         reads=[pZb[k]] + ubw[c], writes=ubw[c])

                def u_chunks(hs, hsb, W, uview, pzview, ubw):
                    for c in range(4):
                        u_c1(hs, hsb, W, uview, pzview, ubw, c)
                    for c in range(4):
                        u_h1(hs, hsb, W, uview, pzview, ubw, c)

                def k_norm(a, b, kind):
                    W = b - a
                    head_norm(zk[:, :W], W, qkg[:, 1:2], knf[:, :W], [knfb], [zkb], slot=4)
                    blks = [17] if kind == "sample" else list(range(a // 128, b // 128))
                    emit(ACT, lambda: nc.scalar.copy(out=kT_all[:, a:b], in_=knf[:, :W]), reads=[knfb],
                         writes=[kTb[x] for x in blks])

                def conv_only(W, cvo_fn, taps_fn, rb_fn, cs=range(4)):
                    for c in cs:
                        cvo = cvo_fn(c)
                        taps = taps_fn(c)
                        rb = rb_fn(c)
                        CE, ce = (POOL, nc.gpsimd) if CONV_ON_POOL else (DVE, nc.vector)
                        emit(CE, lambda: ce.tensor_scalar(out=cvo, in0=taps[2], scalar1=wconv[:, c, 2:3],
                                                          scalar2=None, op0=ALU.mult),
                             reads=rb + CB, writes=[cvb[c]])
                        for j in (1, 0):
                            emit(CE, lambda: ce.scalar_tensor_tensor(
                                out=cvo, in0=taps[j], scalar=wconv[:, c, j:j + 1], in1=cvo, op0=ALU.mult, op1=ALU.add),
                                reads=rb + CB + [cvb[c]], writes=[cvb[c]])

                def gate_only(hs, hsb, W, par, cs=range(4)):
                    for c in cs:
                        k = zmm(768 + c * 128, 2, hs, hsb, W)
                        emit(DVE, lambda: nc.vector.tensor_tensor(out=mixb2[par][:, 4 + c, :W], in0=pZ[k][:, :W],
                                                                  in1=cv[:, c, :W], op=ALU.mult),
                             reads=[pZb[k], cvb[c]], writes=[mixbb2[par][4 + c]])

                def out_proj(a, b, par, js):
                    W = b - a
                    for j in js:
                        k = state["z"] % 2
                        state["z"] += 1
                        pe_group([(pZ[k][:, :W], Wout[:, m, j * 128:(j + 1) * 128], mixb2[par][:, m, :W]) for m in range(8)],
                                 reads=[Woutb] + mixbb2[par], writes=[pZb[k]])
                        emit(DVE, lambda: nc.vector.tensor_tensor(out=xres[:, j, a:b], in0=pZ[k][:, :W], in1=xres[:, j, a:b],
                                                                  op=ALU.add),
                             reads=[pZb[k]] + xbufs(j, a, b), writes=xbufs(j, a, b))

                with ExitStack() as P:
                    Bhi = sb(P, [128, 8, 256], BF16, "Bhi")
                    Blo = sb(P, [128, 8, 256], BF16, "Blo")
                    Hm = sb(P, [128, 256], BF16, "Hm")
                    Bhib, Blob, Hmb = Buf(), Buf(), Buf()
                    Biasb = Buf()
                    Vt = sb(P, [128, 17, 192], BF16, "Vt")
                    Vtb = [Buf() for _ in range(17)]
                    emit(DVE, lambda: nc.vector.memset(Vt[:], 0.0), writes=Vtb)
                    with ExitStack() as su:
                        tab_sb = sb(su, [33, 8], F32, "tab_sb")
                        tabB = sb(su, [33, 8, 128], F32, "tabB")
                        E2_sb = sb(su, [33, 383], F32, "E2_sb")
                        Ubc_sb = sb(su, [128, 8, 383], F32, "Ubc_sb")
                        Bias = sb(su, [128, 8, 256], F32, "Bias")
                        tb = Buf()
                        eb = Buf()
                        ub = Buf()
                        emit(DVE, lambda: nc.vector.memset(tab_sb[:], 1.0), writes=[tb])
                        QS.start(tab_sb[0:32, :], tabp_d, writes=[tb])
                        QS.start(E2_sb[:], E2_d, writes=[eb])
                        emit(DVE, lambda: nc.vector.tensor_copy(out=tabB[:], in_=tab_sb[:].unsqueeze(2).to_broadcast([33, 8, 128])),
                             reads=[tb], writes=[tb])
                        for h in range(8):
                            k = h % 2
                            pe_group([(pZ[k][:, 0:383], tabB[:, h, :], E2_sb[:])], reads=[tb, eb], writes=[pZb[k]])
                            emit(ACT, lambda: nc.scalar.copy(out=Ubc_sb[:, h, :], in_=pZ[k][:, 0:383]), reads=[pZb[k]], writes=[ub])
                        dsc = Buf()
                        QS.start(Ubc, Ubc_sb[:], reads=[ub], writes=[dsc])
                        src = bass.AP(Ubc_t, 127, [[8 * 383 - 1, 128], [383, 8], [1, 256]])
                        QS.start(Bias[:], src, reads=[dsc], writes=[Biasb])
                        emit(DVE, lambda: nc.vector.tensor_copy(out=Bhi[:], in_=Bias[:]), reads=[Biasb], writes=[Bhib])
                        emit(DVE, lambda: nc.vector.tensor_tensor(out=Blo[:], in0=Bias[:], in1=Bhi[:], op=ALU.subtract),
                             reads=[Biasb, Bhib], writes=[Blob])
                        emit(DVE, lambda: nc.vector.memset(Hm[:], 0.0), writes=[Hmb])
                        emit(DVE, lambda: nc.vector.tensor_scalar(out=Hm[:, 0:128], in0=Hm[:, 0:128], scalar1=hflag[:, 0:1],
                                                                  scalar2=None, op0=ALU.add), reads=[Hmb] + CB, writes=[Hmb])
                        for kv in range(2):
                            for g in range(4):
                                ph_ = 4 * (g // 2) + 2 * kv + (g % 2)
                                src = bass.AP(Ubc_t, ph_ * 383 + 127, [[8 * 383 - 1, 4], [1, 132]])
                                QS.start(Bias_s[g * 4:(g + 1) * 4, kv, :], src, reads=[dsc], writes=[sconst])
                        barrier(queues=(QS,))
                    ubuf = sb(P, [128, 4, MT + 2], F32, "ubuf")
                    ubb = [Buf() for _ in range(4)]
                    vlast = sb(P, [128, 128], F32, "vlast")
                    vlastb = Buf()
                    P2 = [sb(P, [128, 4, 256], BF16, "Pexp") for _ in range(2)]
                    P2b = [[Buf() for _ in range(4)] for _ in range(2)]
                    D2 = [sb(P, [128, 4, 128], BF16, "Dn") for _ in range(2)]
                    D2b = [Buf() for _ in range(2)]
                    PTs2 = [sb(P, [128, 1024], BF16, "PTs") for _ in range(2)]
                    PTsb2 = [Buf() for _ in range(2)]
                    sm2 = [{n: sb(P, [128, 4], F32, n) for n in ("mx", "negm", "tmp4", "es4", "rs4", "den4", "rden4")}
                           for _ in range(2)]
                    smb2 = [{n: Buf() for n in sm2[0]} for _ in range(2)]

                    def attn_A(bi, o, hp, par, hb_):
                        Pe, Peb, sm, smb = P2[hb_], P2b[hb_], sm2[hb_], smb2[hb_]
                        qnb, qnbb = qnb2[par], qnbb2[par]
                        kc0 = 128 * (bi - 1)
                        for ci in range(2):
                            i = 2 * hp + ci
                            for kv in range(2):
                                sl = kv * 2 + ci
                                dst = pS[:, sl * 256:(sl + 1) * 256]
                                mms = [(dst, qnb[kv * 64:(kv + 1) * 64, i, o:o + 128], kT_all[kv * 64:(kv + 1) * 64, kc0:kc0 + 256]),
                                       (dst, ident[:], Bhi[:, 4 * hp + sl, :]),
                                       (dst, ident[:], Blo[:, 4 * hp + sl, :])]
                                rd = [qnbb[i], kTb[bi - 1], kTb[bi], Bhib, Blob] + CB
                                if bi == 1:
                                    mms.append((dst, ident[:], Hm[:]))
                                    rd.append(Hmb)
                                pe_group(mms, reads=rd, writes=[pSb[sl // 2]])
                        emit(DVE, lambda: nc.vector.reduce_max(out=sm["mx"][:], in_=pS.rearrange("p (a b) -> p a b", b=256),
                                                               axis=AX.X),
                             reads=[pSb[0], pSb[1]], writes=[smb["mx"]])
                        emit(DVE, lambda: nc.vector.scalar_tensor_tensor(
                            out=sm["negm"][:], in0=sm["mx"][:], scalar=-1.0, in1=nsinkb[:, 4 * hp:4 * hp + 4],
                            op0=ALU.mult, op1=ALU.min), reads=[smb["mx"]] + CB, writes=[smb["negm"]])
                        emit(DVE, lambda: nc.vector.tensor_tensor(
                            out=sm["tmp4"][:], in0=sinkb[:, 4 * hp:4 * hp + 4], in1=sm["negm"][:], op=ALU.add),
                            reads=[smb["negm"]] + CB, writes=[smb["tmp4"]])
                        for sl in range(4):
                            emit(ACT, lambda: nc.scalar.activation(
                                out=Pe[:, sl, :], in_=pS[:, sl * 256:(sl + 1) * 256], func=AF.Exp,
                                bias=sm["negm"][:, sl:sl + 1], scale=1.0, accum_out=sm["rs4"][:, sl:sl + 1]),
                                reads=[pSb[sl // 2], smb["negm"]], writes=[Peb[sl], smb["rs4"]])
                        emit(ACT, lambda: nc.scalar.activation(out=sm["es4"][:], in_=sm["tmp4"][:], func=AF.Exp),
                             reads=[smb["tmp4"]], writes=[smb["es4"]])

                    def attn_B(bi, o, hp, par, hb_):
                        Pe, Peb, sm, smb = P2[hb_], P2b[hb_], sm2[hb_], smb2[hb_]
                        Dn, Dnb, PTs, PTsb = D2[hb_], D2b[hb_], PTs2[hb_], PTsb2[hb_]
                        emit(DVE, lambda: nc.vector.tensor_tensor(out=sm["den4"][:], in0=sm["rs4"][:], in1=sm["es4"][:],
                                                                  op=ALU.add),
                             reads=[smb["rs4"], smb["es4"]], writes=[smb["den4"]])
                        emit(DVE, lambda: nc.vector.reciprocal(out=sm["rden4"][:], in_=sm["den4"][:]),
                             reads=[smb["den4"]], writes=[smb["rden4"]])
                        emit(DVE, lambda: nc.vector.tensor_tensor(
                            out=Dn[:], in0=ident[:].unsqueeze(1).to_broadcast([128, 4, 128]),
                            in1=sm["rden4"][:].unsqueeze(2).to_broadcast([128, 4, 128]), op=ALU.mult),
                            reads=[smb["rden4"]] + CB, writes=[Dnb])
                        rd = Peb + [Dnb]
                        _pre(PE, rd, [pTb])
                        inst = None
                        for sl in range(4):
                            for kh in range(2):
                                idx = sl * 2 + kh
                                inst = nc.tensor.matmul(pT[:, idx * 128:(idx + 1) * 128], Pe[:, sl, kh * 128:(kh + 1) * 128],
                                                        Dn[:, sl, :], start=True, stop=True)
                        tok_ = PE.mark(inst)
                        _commit(tok_, rd, [pTb])
                        emit(ACT, lambda: nc.scalar.copy(out=PTs[:], in_=pT[:]), reads=[pTb], writes=[PTsb])

                    def attn_B2(bi, o, hp, par, hb_):
                        PTs, PTsb = PTs2[hb_], PTsb2[hb_]
                        for ci in range(2):
                            i = 2 * hp + ci
                            mms = []
                            for kv in range(2):
                                for kh in range(2):
                                    idx = (kv * 2 + ci) * 2 + kh
                                    mms.append((pO[:, i * 128:(i + 1) * 128], Vt[:, bi - 1 + kh, kv * 64:kv * 64 + 128],
                                                PTs[:, idx * 128:(idx + 1) * 128]))
                            pe_group(mms, reads=[PTsb, Vtb[bi - 1], Vtb[bi]], writes=[pOb])

                    def attn_fin(o, par):
                        emit(ACT, lambda: nc.scalar.copy(out=mixb2[par][:, 0:4, o:o + 128],
                                                         in_=pO[:].rearrange("p (a b) -> p a b", b=128)),
                             reads=[pOb], writes=mixbb2[par][0:4])

                    def front_steps(kind, a, b, par, tidx=None, nxt=None):
                        W = b - a
                        box = {}

                        def s0():
                            box["hs"], box["hsb"] = kchunk(a, b, tidx)

                        def s_next():
                            if nxt is not None:
                                norm_only(nxt[1], nxt[2], tidx + 1)

                        def s1():
                            hs, hsb = box["hs"], box["hsb"]
                            for bo in range(W // 128):
                                bi = a // 128 + bo
                                kz = state["z"] % 2
                                state["z"] += 1
                                pe_group([(pZ[kz][:, 0:128], hs[:, kc, bo * 128:(bo + 1) * 128], Win[:, kc, 640:768])
                                          for kc in range(8)], reads=[Winb[1]] + hsb, writes=[pZb[kz]])
                                emit(ACT, lambda: nc.scalar.copy(out=Vt[:, bi, 0:64], in_=pZ[kz][:, 0:64]),
                                     reads=[pZb[kz]], writes=[Vtb[bi]])
                                emit(ACT, lambda: nc.scalar.copy(out=Vt[:, bi, 128:192], in_=pZ[kz][:, 64:128]),
                                     reads=[pZb[kz]], writes=[Vtb[bi]])
                                if bi == 16:
                                    emit(ACT, lambda: nc.scalar.copy(out=vlast[:], in_=pZ[kz][:, 0:128]),
                                         reads=[pZb[kz]], writes=[vlastb])
                                    out_toks.append(QS.start(vl_o, vlast[:], reads=[vlastb]))

                        uv = lambda c: ubuf[:, c, 2:2 + W]
                        pzv = lambda k: pZ[k][:, :W]
                        ubw_ = [[ubb[c]] for c in range(4)]

                        def s3a():
                            if state["wprev"] is not None:
                                wp = state["wprev"]
                                emit(DVE, lambda: nc.vector.tensor_copy(out=ubuf[:, :, 0:2], in_=ubuf[:, :, wp:wp + 2]),
                                     reads=ubb, writes=ubb)
                            else:
                                emit(DVE, lambda: nc.vector.memset(ubuf[:, :, 0:2], 0.0), writes=ubb)
                            state["wprev"] = W

                        def s4():
                            k_norm(a, b, kind)
                            if b == CS:
                                out_toks.append(QS.start(kTl_o, knf[:, W - 128:W], reads=[knfb]))
                                out_toks.append(QS.start(cl_o, ubuf[:, :, W:W + 2], reads=ubb))

                        def mk(f, *args):
                            return lambda: f(*args)
                        steps = [s0, s1]
                        if kind != "halo":
                            steps += [mk(lambda i: q_chunk1(box["hs"], box["hsb"], W, i), i) for i in range(4)]
                        steps.append(s3a)
                        steps += [mk(lambda c: u_c1(box["hs"], box["hsb"], W, uv, pzv, ubw_, c), c) for c in range(4)]
                        steps += [mk(lambda c: u_h1(box["hs"], box["hsb"], W, uv, pzv, ubw_, c), c) for c in range(4)]
                        steps.append(s4)
                        if kind == "halo":
                            return steps + [s_next]
                        steps += [mk(lambda i: head_norm(zq[:, i, :W], W, qkg8[:, 0:1], qnb2[par][:, i, :W], [qnbb2[par][i]], [zqb[i]], slot=i), i)
                                  for i in range(4)]
                        steps += [mk(lambda c: conv_only(W, lambda c_: cv[:, c_, :W],
                                                         lambda c_: [ubuf[:, c_, j:j + W] for j in range(3)],
                                                         lambda c_: [ubb[c_]], cs=[c]), c) for c in range(4)]
                        steps += [mk(lambda c: gate_only(box["hs"], box["hsb"], W, par, cs=[c]), c) for c in range(4)]
                        steps.insert(min(len(steps), KNPOS), s_next)
                        return steps

                    hpc = [0]

                    def back_steps(a, b, par):
                        W = b - a
                        passes = []
                        for bo in range(W // 128):
                            for hp in range(2):
                                passes.append((a // 128 + bo, bo * 128, hp, hpc[0] % 2))
                                hpc[0] += 1
                        steps = []
                        n = len(passes)

                        def mkA(p):
                            return lambda: attn_A(p[0], p[1], p[2], par, p[3])

                        def mkB1(idx):
                            p = passes[idx]
                            return lambda: attn_B(p[0], p[1], p[2], par, p[3])

                        def mkB2(idx):
                            p = passes[idx]

                            def f():
                                attn_B2(p[0], p[1], p[2], par, p[3])
                                if p[2] == 1:
                                    attn_fin(p[1], par)
                            return f
                        steps.append((mkA(passes[0]), SLOT[0]))
                        for idx in range(n):
                            if idx + 1 < n:
                                steps.append((mkA(passes[idx + 1]), SLOT[0]))
                            steps.append((mkB1(idx), SLOT[1]))
                            steps.append((mkB2(idx), SLOT[2]))
                        for j0 in range(0, 8, 2):
                            steps.append(((lambda j0_: (lambda: out_proj(a, b, par, range(j0_, j0_ + 2))))(j0), SLOT[3]))
                        return steps

                    pW = None

                    def warm():
                        for _ in range(NWARM):
                            nc.tensor.matmul(pW[:, :], ident[:], kT_all[:, 0:512], start=True, stop=True)

                    def interleave(bs, fs):
                        out = []
                        j = 0
                        for (st, k) in bs:
                            if NWARM:
                                out.append(warm)
                            out.append(st)
                            for _ in range(k):
                                if j < len(fs):
                                    out.append(fs[j]); j += 1
                        out += fs[j:]
                        return out

                    tiles = [("halo", 0, HALO)] + [("prompt", HALO + i * MT, HALO + (i + 1) * MT) for i in range(NPR // MT)]
                    def nx(t):
                        return tiles[t + 1] if t + 1 < len(tiles) else None
                    norm_only(tiles[0][1], tiles[0][2], 0)
                    for st_ in front_steps(*tiles[0], 0, tidx=0, nxt=nx(0)):
                        st_()
                    for st_ in front_steps(*tiles[1], 1, tidx=1, nxt=nx(1)):
                        st_()
                    for t in range(1, len(tiles)):
                        bs = back_steps(tiles[t][1], tiles[t][2], t % 2)
                        fs = front_steps(*tiles[t + 1], (t + 1) % 2, tidx=t + 1, nxt=nx(t + 1)) if t + 1 < len(tiles) else []
                        for st_ in interleave(bs, fs):
                            st_()
                    barrier()

                with ExitStack() as S_:
                    qnb, qnbb, mixb, mixbb = qnb2[0], qnbb2[0], mixb2[0], mixbb2[0]
                    kTs = sb(S_, [128, NSEQ, 128], BF16, "kTs")
                    Vc = sb(S_, [128, NSEQ, 192], BF16, "Vc")
                    Vn = sb(S_, [4, NSEQ, 192], BF16, "Vn")
                    vsnf = sb(S_, [4, 8, 128], F32, "vsnf")
                    vsnfb = Buf()
                    Vnb = Buf()
                    ubs = sb(S_, [128, 4, NSEQ, 6], F32, "ubs")
                    ubsb = Buf()
                    qs = sb(S_, [128, NSEQ, 16], BF16, "qs")
                    qsb = Buf()
                    NQ = 8
                    NSL = 2 * NQ
                    Ss = sb(S_, [16, NSL, 132], F32, "Ss")
                    Ssb_s = Buf()
                    Pns = sb(S_, [16, NSL, 132], BF16, "Pns")
                    Pnsb = Buf()
                    PTcs = sb(S_, [128, NSL * 16], BF16, "PTcs")
                    PTns = sb(S_, [4, NSL * 16], BF16, "PTns")
                    PTcsb = Buf()
                    ss = {n: sb(S_, [16, NSL], F32, "s" + n) for n in ("mx", "m", "rs", "tmp", "es", "den", "rden")}
                    ssb = {n: Buf() for n in ss}
                    kTsb = Buf()
                    Vcb = Buf()
                    emit(DVE, lambda: nc.vector.memset(Vc[:, :, 64:128], 0.0), writes=[Vcb])
                    emit(DVE, lambda: nc.vector.memset(Vn[:], 0.0), writes=[Vnb])
                    QP.start(kTs[:], ckT_d, writes=[kTsb])
                    QP.start(Vc[:, :, 0:64], cvt_d[:, :, 0:64], writes=[Vcb])
                    QP.start(Vc[:, :, 128:192], cvt_d[:, :, 64:128], writes=[Vcb])
                    st_c = sb(S_, [128, 128], F32, "st_c")
                    st_cb = Buf()
                    cs_c = sb(S_, [128, 128], F32, "cs_c")
                    cs_cb = Buf()
                    ubs_st = Buf()
                    QS.start(st_c[:], scT_d.rearrange("p c q r -> p (c q r)"), writes=[st_cb])
                    emit(DVE, lambda: nc.vector.tensor_copy(out=ubs[:, :, :, 0:2],
                                                            in_=st_c[:].rearrange("p (c q r) -> p c q r", c=4, r=2)),
                         reads=[st_cb], writes=[ubs_st])

                    a, b = CS, NT
                    W = NSM
                    state["ti"] = 0
                    hs, hsb = front0(a, b)
                    pV = pA[0:4, :]
                    for seq in range(NSEQ):
                        pe_group([(pV[:, seq * 128:(seq + 1) * 128], hs[:, kc, seq * 4:(seq + 1) * 4],
                                   Win[:, kc, 640:768]) for kc in range(8)],
                                 reads=[Winb[1]] + hsb, writes=[bA[seq // 4]])
                    pVv = pV.rearrange("p (q n) -> p q n", n=128)
                    emit(ACT, lambda: nc.scalar.copy(out=Vn[:, :, 0:64], in_=pVv[:, :, 0:64]), reads=bA, writes=[Vnb])
                    emit(ACT, lambda: nc.scalar.copy(out=Vn[:, :, 128:192], in_=pVv[:, :, 64:128]), reads=bA, writes=[Vnb])
                    for hh in range(2):
                        emit(ACT, lambda: nc.scalar.copy(out=vsnf[:], in_=pVv[:, hh * 8:(hh + 1) * 8, :]),
                             reads=bA, writes=[vsnfb])
                        out_toks.append(QS.start(vsn_o[:, hh * 8:(hh + 1) * 8, :], vsnf[:], reads=[vsnfb]))
                    q_chunks(hs, hsb, W)
                    u_chunks(hs, hsb, W, lambda c: ubs[:, c, :, 2:6],
                             lambda k: pZ[k][:, 0:NSM].rearrange("p (q s) -> p q s", s=4), [[ubsb]] * 4)
                    k_norm(a, b, "sample")
                    out_toks.append(QS.start(ksn_o, knf[:, :W], reads=[knfb]))
                    emit(DVE, lambda: nc.vector.tensor_copy(out=cs_c[:].rearrange("p (c q r) -> p c q r", c=4, r=2),
                                                            in_=ubs[:, :, :, 4:6]), reads=[ubsb], writes=[cs_cb])
                    out_toks.append(QS.start(csn_o.rearrange("p c q r -> p (c q r)"), cs_c[:], reads=[cs_cb]))
                    for i in range(4):
                        head_norm(zq[:, i, :W], W, qkg8[:, 0:1], qnb[:, i, :W], [qnbb[i]], [zqb[i]], slot=i)
                    conv_only(W, lambda c: cv[:, c, 0:NSM].rearrange("p (q s) -> p q s", s=4),
                              lambda c: [ubs[:, c, :, j:j + 4] for j in range(3)], lambda c: [ubsb, ubs_st])
                    gate_only(hs, hsb, W, 0)
                    emit(DVE, lambda: nc.vector.tensor_copy(
                        out=qs[:].rearrange("p q (g s) -> p q g s", s=4),
                        in_=qnb[:, :, 0:NSM].rearrange("p g (q s) -> p q g s", s=4)),
                        reads=qnbb, writes=[qsb])
                    pSc = pA[0:16, 0:NSL * 128]
                    nbank = (NSL * 128) // 512
                    for part in range(NSEQ // NQ):
                        for si in range(NQ):
                            seq = part * NQ + si
                            for kv in range(2):
                                slot = kv * NQ + si
                                pe_group([(pSc[:, slot * 128:(slot + 1) * 128], qs[kv * 64:(kv + 1) * 64, seq, :],
                                           kTs[kv * 64:(kv + 1) * 64, seq, :])],
                                         reads=[qsb, kTsb], writes=[bA[slot // 4]])
                                pnew, pnewb = ((pO, pOb), (pstat, pstat_b))[kv]
                                pe_group([(pnew[0:16, si * 4:(si + 1) * 4], qs[kv * 64:(kv + 1) * 64, seq, :],
                                           kT_all[kv * 64:(kv + 1) * 64, CS + seq * 4:CS + seq * 4 + 4])],
                                         reads=[qsb, kTb[17]], writes=[pnewb])
                        emit(DVE, lambda: nc.vector.tensor_scalar(
                            out=Ss[:, :, 0:128], in0=pSc.rearrange("p (q n) -> p q n", n=128), scalar1=1.0,
                            scalar2=None, op0=ALU.mult), reads=bA[0:nbank], writes=[Ssb_s])
                        for kv in range(2):
                            pnew, pnewb = ((pO, pOb), (pstat, pstat_b))[kv]
                            emit(DVE, lambda: nc.vector.tensor_scalar(
                                out=Ss[:, kv * NQ:(kv + 1) * NQ, 128:132],
                                in0=pnew[0:16, 0:NQ * 4].rearrange("p (q n) -> p q n", n=4),
                                scalar1=1.0, scalar2=None, op0=ALU.mult), reads=[pnewb], writes=[Ssb_s])
                        for kv in range(2):
                            sv = Ss[:, kv * NQ:(kv + 1) * NQ, :]
                            emit(DVE, lambda: nc.vector.tensor_tensor(
                                out=sv, in0=sv, in1=Bias_s[:, kv, :].unsqueeze(1).to_broadcast([16, NQ, 132]),
                                op=ALU.add), reads=[Ssb_s, sconst], writes=[Ssb_s])
                        emit(DVE, lambda: nc.vector.reduce_max(out=ss["mx"][:], in_=Ss[:], axis=AX.X),
                             reads=[Ssb_s], writes=[ssb["mx"]])
                        sbc = sinks_s[:].unsqueeze(2).to_broadcast([16, 2, NQ])
                        emit(DVE, lambda: nc.vector.tensor_tensor(
                            out=ss["m"][:].rearrange("p (k q) -> p k q", k=2),
                            in0=ss["mx"][:].rearrange("p (k q) -> p k q", k=2), in1=sbc, op=ALU.max),
                            reads=[ssb["mx"], sconst2], writes=[ssb["m"]])
                        emit(DVE, lambda: nc.vector.tensor_tensor(
                            out=Ss[:], in0=Ss[:], in1=ss["m"][:].unsqueeze(2).to_broadcast([16, NSL, 132]),
                            op=ALU.subtract), reads=[Ssb_s, ssb["m"]], writes=[Ssb_s])
                        emit(ACT, lambda: nc.scalar.activation(out=Ss[:], in_=Ss[:], func=AF.Exp),
                             reads=[Ssb_s], writes=[Ssb_s])
                        emit(DVE, lambda: nc.vector.reduce_sum(out=ss["rs"][:], in_=Ss[:], axis=AX.X),
                             reads=[Ssb_s], writes=[ssb["rs"]])
                        emit(DVE, lambda: nc.vector.tensor_tensor(
                            out=ss["tmp"][:].rearrange("p (k q) -> p k q", k=2), in0=sbc,
                            in1=ss["m"][:].rearrange("p (k q) -> p k q", k=2), op=ALU.subtract),
                            reads=[ssb["m"], sconst2], writes=[ssb["tmp"]])
                        emit(ACT, lambda: nc.scalar.activation(out=ss["es"][:], in_=ss["tmp"][:], func=AF.Exp),
                             reads=[ssb["tmp"]], writes=[ssb["es"]])
                        emit(DVE, lambda: nc.vector.tensor_tensor(out=ss["den"][:], in0=ss["rs"][:], in1=ss["es"][:],
                                                                  op=ALU.add),
                             reads=[ssb["rs"], ssb["es"]], writes=[ssb["den"]])
                        emit(DVE, lambda: nc.vector.reciprocal(out=ss["rden"][:], in_=ss["den"][:]),
                             reads=[ssb["den"]], writes=[ssb["rden"]])
                        emit(DVE, lambda: nc.vector.tensor_tensor(
                            out=Pns[:], in0=Ss[:], in1=ss["rden"][:].unsqueeze(2).to_broadcast([16, NSL, 132]),
                            op=ALU.mult), reads=[Ssb_s, ssb["rden"]], writes=[Pnsb])
                        trs = []
                        for slot in range(NSL):
                            trs.append((pT[:, slot * 16:(slot + 1) * 16], Pns[0:16, slot, 0:128]))
                            trs.append((pT[0:4, 512 + slot * 16:512 + (slot + 1) * 16], Pns[0:16, slot, 128:132]))
                        _pre(PE, [Pnsb] + CB, [pTb])
                        inst = None
                        for (o_, i_) in trs:
                            inst = nc.tensor.matmul(o_, i_, ident[0:16, 0:16], start=True, stop=True)
                        tok_ = PE.mark(inst)
                        _commit(tok_, [Pnsb] + CB, [pTb])
                        emit(ACT, lambda: nc.scalar.copy(out=PTcs[:], in_=pT[:, 0:NSL * 16]), reads=[pTb], writes=[PTcsb])
                        emit(ACT, lambda: nc.scalar.copy(out=PTns[:], in_=pT[0:4, 512:512 + NSL * 16]), reads=[pTb],
                             writes=[PTcsb])
                        for si in range(NQ):
                            seq = part * NQ + si
                            mms = []
                            for kv in range(2):
                                slot = kv * NQ + si
                                mms.append((pstat[:, si * 16:(si + 1) * 16], Vc[:, seq, kv * 64:kv * 64 + 128],
                                            PTcs[:, slot * 16:(slot + 1) * 16]))
                                mms.append((pstat[:, si * 16:(si + 1) * 16], Vn[0:4, seq, kv * 64:kv * 64 + 128],
                                            PTns[0:4, slot * 16:(slot + 1) * 16]))
                            pe_group(mms, reads=[PTcsb, Vcb, Vnb], writes=[pstat_b])
                        emit(DVE, lambda: nc.vector.tensor_copy(
                            out=mixb[:, 0:4, part * NQ * 4:(part + 1) * NQ * 4].rearrange("p g (q s) -> p q g s", s=4),
                            in_=pstat[:, 0:NQ * 16].rearrange("p (q g s) -> p q g s", g=4, s=4)),
                            reads=[pstat_b], writes=mixbb[0:4])
                    out_proj(a, b, 0, range(8))
                    barrier()
            return out_toks

        def ple_phase(out_toks):
            with ExitStack() as ph:
                alloc_norm(ph, 512)
                Wpg = sb(ph, [128, 8, D], BF16, "Wpg")
                Wpp, pe_b = PLEW["Wpp"], PLEW["pe_b"]
                wb_ = Buf()
                hb2 = [sb(ph, [128, 8, 512], BF16, "hb2") for _ in range(2)]
                hb2b = [[Buf() for _ in range(8)] for _ in range(2)]
                sg = [sb(ph, [128, 512], F32, "sg") for _ in range(2)]
                sgb = [Buf() for _ in range(2)]
                tp = [sb(ph, [128, 512], F32, "tp") for _ in range(2)]
                tpb = [Buf() for _ in range(2)]
                pG = [ps(ph, [128, 512], F32, "pG") for _ in range(2)]
                pP = [ps(ph, [128, 512], F32, "pP") for _ in range(2)]
                pstat = ps(ph, [128, 512], F32, "pstat")
                pGb = [PBuf() for _ in range(2)]
                pPb = [PBuf() for _ in range(2)]
                pstat_b = PBuf()
                QP.start(Wpg[:], wpg_d.rearrange("(kc p) n -> p kc n", p=128), writes=[wb_])
                wb2_, wb3_ = PLEW["b2"], PLEW["b3"]
                yTv = yT.rearrange("(c p) t -> p c t", p=128)
                kk = 0
                def ple_norm(ti):
                    a_, b_ = FFN2_TILES[ti]
                    hs_ = hb2[ti % 2]
                    norm_tile(a_, b_, 3, lambda c: hs_[:, c, :b_ - a_], hb2b[ti % 2], pstat, pstat_b)
                ple_norm(0)
                for ti, (a, b) in enumerate(FFN2_TILES):
                    W = b - a
                    hs = hb2[ti % 2]
                    hsb = hb2b[ti % 2]
                    if ti + 1 < len(FFN2_TILES):
                        ple_norm(ti + 1)
                    for j in range(8):
                        k = kk % 2
                        kk += 1
                        pe_group([(pG[k][:, :W], Wpg[:, kc, j * 128:(j + 1) * 128], hs[:, kc, :W]) for kc in range(8)],
                                 reads=[wb_] + hsb, writes=[pGb[k]])
                        pe_group([(pP[k][:, :W], Wpp[:, m, j * 128:(j + 1) * 128], pe_b[:, m, a - HALO:b - HALO])
                                  for m in range(2)], reads=[wb2_, wb3_], writes=[pPb[k]])
                        emit(ACT, lambda: nc.scalar.activation(out=sg[k][:, :W], in_=pG[k][:, :W], func=AF.Sigmoid),
                             reads=[pGb[k]], writes=[sgb[k]])
                        emit(DVE, lambda: nc.vector.tensor_tensor(out=tp[k][:, :W], in0=sg[k][:, :W], in1=pP[k][:, :W],
                                                                  op=ALU.mult),
                             reads=[sgb[k], pPb[k]], writes=[tpb[k]])
                        emit(DVE, lambda: nc.vector.tensor_tensor(out=xres[:, j, a:b], in0=tp[k][:, :W], in1=xres[:, j, a:b],
                                                                  op=ALU.add),
                             reads=[tpb[k]] + xbufs(j, a, b), writes=xbufs(j, a, b))
                    out_toks.append(QS.start(yTv[:, :, a - HALO:b - HALO], xres[:, :, a:b], reads=xbufs_all(a, b)))
                for t in out_toks:
                    SP.wait(t)
                barrier()

        def dbg_finish():
            yTv = yT.rearrange("(c p) t -> p c t", p=128)
            t = QS.start(yTv, xres[:, :, HALO:NT], reads=xbufs_all(0, NT))
            SP.wait(t)

        if stop == "load":
            dbg_finish()
            return nc
        ffn_phase(w1g, w1u, w1d, 0, FFN1_TILES)
        if stop == "ffn1":
            dbg_finish()
            return nc
        toks = mixer_phase()
        if stop == "mixer":
            dbg_finish()
            return nc
        PLEW["Wpp"] = sb(top, [128, 2, D], BF16, "Wpp")
        PLEW["pe_b"] = sb(top, [128, 2, NOUT], BF16, "pe_b")
        PLEW["b2"], PLEW["b3"] = Buf(), Buf()

        def ple_prefetch():
            QP.start(PLEW["Wpp"][:], wpp_d.rearrange("(kc p) n -> p kc n", p=128), writes=[PLEW["b2"]])
            QP.start(PLEW["pe_b"][:], pT_d.rearrange("(kc p) n -> p kc n", p=128), writes=[PLEW["b3"]])
        AFTER_G1.append(ple_prefetch)
        ffn_phase(w2g, w2u, w2d, 2, FFN2_TILES)
        if stop == "ffn2":
            dbg_finish()
            return nc
        ple_phase(toks)
    return nc


def _rel_bucket_np(d):
    max_exact = 16
    df = np.maximum(d, 1).astype(np.float32)
    val = (np.log(df / np.float32(max_exact)) / np.float32(math.log(128 / 16)) * np.float32(32 - max_exact))
    large = max_exact + val.astype(np.int32)
    large = np.minimum(large, 31)
    return np.where(d < max_exact, d, large)


def _consts():
    E2 = np.zeros((33, 383), np.float32)
    for m in range(383):
        d = 255 - m
        if 0 <= d < 128:
            E2[int(_rel_bucket_np(np.array([d]))[0]), m] = 1.0
        else:
            E2[32, m] = -1e30
    onesD = np.full((128, 128), 1.0 / D, np.float32)
    bd64 = np.zeros((128, 128), np.float32)
    bd64[:64, :64] = 1.0 / 64
    bd64[64:, 64:] = 1.0 / 64
    ident = np.eye(128, dtype=np.float32)
    return E2, onesD, bd64, ident


_NC_CACHE = {}


def kernel(x_prompt, x_sample, p_prompt, p_sample, cache_k, cache_v, state_conv, rel_bias,
           g_ffn1, w1_gate, w1_up, w1_down, g_mix, w_in, q_norm, k_norm, sinks, w_conv, w_out,
           g_ffn2, w2_gate, w2_up, w2_down, g_ple, w_ple_gate, w_ple_proj):
    f = lambda a: np.ascontiguousarray(np.asarray(a, dtype=np.float32))
    x_prompt, x_sample, p_prompt, p_sample = f(x_prompt), f(x_sample), f(p_prompt), f(p_sample)
    cache_k, cache_v, state_conv, rel_bias = f(cache_k), f(cache_v), f(state_conv), f(rel_bias)
    E2, onesD, bd64, ident = _consts()
    qperm = np.concatenate([np.r_[i * 64:(i + 1) * 64, (4 + i) * 64:(5 + i) * 64] for i in range(4)])
    win = f(w_in)[0]
    win_p = f(np.concatenate([win[:, qperm], win[:, 512:]], axis=1))
    wout = f(w_out)[0]
    wout_p = f(np.concatenate([wout[qperm, :], wout[512:, :]], axis=0))
    gv = f(np.stack([f(g_ffn1)[0], f(g_mix)[0], f(g_ffn2)[0], f(g_ple)[0]]).reshape(4, 8, 128).transpose(2, 0, 1))
    qkg = f(np.stack([np.tile(f(q_norm)[0], 2), np.tile(f(k_norm)[0], 2)], axis=1))
    sk = f(sinks)[0]
    sinkb = f(np.broadcast_to(sk[HPERM][None, :], (128, 8)))
    sinks_s = np.zeros((16, 2), np.float32)
    for g in range(4):
        for kv in range(2):
            sinks_s[g * 4:(g + 1) * 4, kv] = sk[kv * 4 + g]
    tabp = f(rel_bias[:, HPERM])
    wconv = f(f(w_conv)[0].reshape(3, 4, 128).transpose(2, 1, 0))
    shared = {
        "tabp": tabp, "E2": E2, "onesD": onesD, "bd64": bd64, "ident": ident, "gv": gv, "qkg": qkg,
        "sinkb": sinkb, "sinks_s": sinks_s, "wconv": wconv,
        "w1g": f(w1_gate)[0], "w1u": f(w1_up)[0], "w1d": f(w1_down)[0], "win": win_p, "wout": wout_p,
        "w2g": f(w2_gate)[0], "w2u": f(w2_up)[0], "w2d": f(w2_down)[0], "wpg": f(w_ple_gate)[0],
        "wpp": f(w_ple_proj)[0],
    }
    in_maps = []
    for c in range(NCORES):
        b, j = divmod(c, 4)
        t0 = j * NPR
        halo = x_prompt[b, t0 - HALO:t0] if j > 0 else np.zeros((HALO, D), np.float32)
        sq = slice(c * NSEQ, (c + 1) * NSEQ)
        xs = x_sample[sq].reshape(NSM, D)
        xT = f(np.concatenate([halo, x_prompt[b, t0:t0 + NPR], xs], axis=0).T)
        pT = f(np.concatenate([p_prompt[0, b, t0:t0 + NPR], p_sample[0, sq].reshape(NSM, DPLE)], axis=0).T)
        ck = cache_k[0, sq].reshape(NSEQ, 128, 128)
        cvv = cache_v[0, sq].reshape(NSEQ, 128, 128)
        sc = state_conv[0, sq]
        scT = f(sc.transpose(2, 0, 1).reshape(4, 128, NSEQ, 2).transpose(1, 0, 2, 3))
        m = dict(shared)
        m.update({
            "xT": xT, "pT": pT, "ckT": f(ck.transpose(2, 0, 1)), "ck_nat": f(ck),
            "cvt": f(cvv.transpose(1, 0, 2)), "cv_nat": f(cvv), "scT": scT,
            "hflag": np.full((128, 1), -1e30 if j == 0 else 0.0, np.float32),
        })
        in_maps.append(m)

    if "nc" not in _NC_CACHE:
        _NC_CACHE["nc"] = build_nc()
    nc = _NC_CACHE["nc"]
    res = run_bass_kernel_spmd(nc, in_maps, core_ids=list(range(NCORES)))
    R = res.results

    B, T = x_prompt.shape[0], x_prompt.shape[1]
    y_prompt = np.zeros((B, T, D), np.float32)
    y_sample = np.zeros((x_sample.shape[0], 4, D), np.float32)
    kwp = np.zeros((1, B, 128, 2, 64), np.float32)
    vwp = np.zeros((1, B, 128, 2, 64), np.float32)
    cvp = np.zeros((1, B, 2, 512), np.float32)
    kws = np.zeros((1, x_sample.shape[0], 128, 2, 64), np.float32)
    vws = np.zeros((1, x_sample.shape[0], 128, 2, 64), np.float32)
    cvs = np.zeros((1, x_sample.shape[0], 2, 512), np.float32)
    for c in range(NCORES):
        b, j = divmod(c, 4)
        t0 = j * NPR
        r = R[c]
        sq = slice(c * NSEQ, (c + 1) * NSEQ)
        y = np.asarray(r["yT"])
        y_prompt[b, t0:t0 + NPR] = y[:, :NPR].T
        y_sample[sq] = y[:, NPR:].T.reshape(NSEQ, 4, D)
        if j == 3:
            kwp[0, b] = np.asarray(r["kTl"]).T.reshape(128, 2, 64)
            vwp[0, b] = np.asarray(r["vl"]).reshape(128, 2, 64)
            cvp[0, b] = np.asarray(r["cl"]).transpose(2, 1, 0).reshape(2, 512)
        kws[0, sq, 0:124] = np.asarray(r["kws_old"]).reshape(NSEQ, 124, 2, 64)
        kws[0, sq, 124:128] = np.asarray(r["ksn"]).T.reshape(NSEQ, 4, 2, 64)
        vws[0, sq, 0:124] = np.asarray(r["vws_old"]).reshape(NSEQ, 124, 2, 64)
        vws[0, sq, 124:128] = np.asarray(r["vsn"]).transpose(1, 0, 2).reshape(NSEQ, 4, 2, 64)
        cvs[0, sq] = np.asarray(r["csn"]).transpose(2, 3, 1, 0).reshape(NSEQ, 2, 512)
    return (y_prompt, y_sample, kwp, vwp, cvp, kws, vws, cvs)
```
